# Optimizing a Trainium2 kernel written in Bass

```python
import math
import jax, jax.numpy as jnp
from jax import lax
import numpy as np

D_MODEL = 1024
BATCH = 16
SEQ = 2048
DEPTH = 2

RET_HEADS = 4
RET_HEAD_DIM = 128
ATT_HEADS = 4
ATT_HEAD_DIM = 128
IDX_HEADS = 8
IDX_DIM = 64
RET_WIDTH = RET_HEADS * RET_HEAD_DIM
ATT_WIDTH = ATT_HEADS * ATT_HEAD_DIM
MIX_WIDTH = RET_WIDTH + ATT_WIDTH
FFN_HIDDEN = -(-(8 * D_MODEL) // (3 * 256)) * 256
TOPK_MAX = 256
ROPE_THETA = 10000.0
RET_CHUNK = 128
IDX_BLOCK = 128
ATT_BLOCK = 32
LN_EPS = 1e-5
DEEPNORM_ALPHA = (2.0 * DEPTH) ** 0.25
DEEPNORM_BETA = (8.0 * DEPTH) ** -0.25

IN_SIZES = (RET_WIDTH, RET_WIDTH, RET_WIDTH, RET_WIDTH,
            ATT_WIDTH, ATT_WIDTH, ATT_WIDTH,
            IDX_HEADS * IDX_DIM, IDX_DIM, IDX_HEADS)
IN_COLS = sum(IN_SIZES)
IN_SPLITS = tuple(int(s) for s in np.cumsum(IN_SIZES)[:-1])

kernel_name = "hybrid_retention_dsa_deepnorm"


def rope(x, positions):
    d = x.shape[-1]
    inv_freq = ROPE_THETA ** (-jnp.arange(0, d, 2, dtype=jnp.float32) / d)
    ang = positions.astype(jnp.float32)[..., None] * inv_freq
    cos = jnp.cos(ang)[:, :, None, :].astype(x.dtype)
    sin = jnp.sin(ang)[:, :, None, :].astype(x.dtype)
    x1, x2 = x[..., : d // 2], x[..., d // 2:]
    return jnp.concatenate([x1 * cos - x2 * sin, x1 * sin + x2 * cos], axis=-1)


def layer_norm(x, g, b):
    xf = x.astype(jnp.float32)
    mu = jnp.mean(xf, axis=-1, keepdims=True)
    var = jnp.mean(jnp.square(xf - mu), axis=-1, keepdims=True)
    y = (xf - mu) * lax.rsqrt(var + LN_EPS)
    return (y * g.astype(jnp.float32) + b.astype(jnp.float32)).astype(x.dtype)


def retention_chunkwise(q, k, v):
    B, S, H, d = q.shape
    C = RET_CHUNK
    N = S // C
    log_g = jnp.log(1.0 - 2.0 ** (-5.0 - jnp.arange(H, dtype=jnp.float32)))
    pos = jnp.arange(C, dtype=jnp.float32)
    rel = pos[:, None] - pos[None, :]
    decay_intra = jnp.where(rel >= 0,
                            jnp.exp(log_g[:, None, None] * jnp.maximum(rel, 0.0)),
                            0.0)
    decay_q = jnp.exp(log_g[:, None] * (pos + 1.0))
    decay_k = jnp.exp(log_g[:, None] * (C - 1.0 - pos))
    decay_chunk = jnp.exp(log_g * C)

    def to_chunks(t):
        return t.astype(jnp.float32).reshape(B, N, C, H, d).transpose(1, 0, 3, 2, 4)

    qc, kc, vc = to_chunks(q), to_chunks(k) * (d ** -0.5), to_chunks(v)
    scores = jnp.einsum('nbhid,nbhjd->nbhij', qc, kc) * decay_intra
    inner = jnp.einsum('nbhij,nbhjd->nbhid', scores, vc)
    kv = jnp.einsum('nbhjd,nbhje->nbhde', kc * decay_k[:, :, None], vc)

    def step(state, kv_n):
        return decay_chunk[:, None, None] * state + kv_n, state

    _, prev = lax.scan(step, jnp.zeros((B, H, d, d), jnp.float32), kv)
    cross = jnp.einsum('nbhid,nbhde->nbhie', qc, prev) * decay_q[:, :, None]
    out = inner + cross
    return out.transpose(1, 0, 3, 2, 4).reshape(B, S, H, d)


def indexer_topk(q_idx, k_idx, w_idx, topk):
    B, S, HI, DI = q_idx.shape
    nblk = S // IDX_BLOCK
    qf = q_idx.astype(jnp.float32) * (DI ** -0.5)
    kf = k_idx.astype(jnp.float32)
    wf = w_idx.astype(jnp.float32) * (HI ** -0.5)
    key_pos = jnp.arange(S)
    q_blocks = qf.reshape(B, nblk, IDX_BLOCK, HI, DI).swapaxes(0, 1)
    w_blocks = wf.reshape(B, nblk, IDX_BLOCK, HI).swapaxes(0, 1)
    q_pos = jnp.arange(S).reshape(nblk, IDX_BLOCK)

    def block(args):
        qb, wb, pb = args
        logits = jax.nn.relu(jnp.einsum('bqhd,bsd->bqhs', qb, kf))
        score = jnp.einsum('bqh,bqhs->bqs', wb, logits)
        score = jnp.where(key_pos[None, None, :] <= pb[None, :, None], score, -jnp.inf)
        _, sel = lax.top_k(score, topk)
        return sel.astype(jnp.int32)

    sel = lax.map(block, (q_blocks, w_blocks, q_pos))
    return sel.swapaxes(0, 1).reshape(B, S, topk)


def sparse_attention(q, k, v, sel):
    B, S, H, dh = q.shape
    K = sel.shape[-1]
    nblk = S // ATT_BLOCK
    q_blocks = q.reshape(B, nblk, ATT_BLOCK, H, dh).swapaxes(0, 1)
    s_blocks = sel.reshape(B, nblk, ATT_BLOCK, K).swapaxes(0, 1)
    q_pos = jnp.arange(S).reshape(nblk, ATT_BLOCK)
    gather = jax.vmap(lambda t, i: t[i])

    def block(args):
        qb, sb, pb = args
        kb = gather(k, sb)
        vb = gather(v, sb)
        s = jnp.einsum('bqhd,bqkhd->bhqk', qb, kb).astype(jnp.float32) * (dh ** -0.5)
        valid = sb <= pb[None, :, None]
        s = jnp.where(valid[:, None, :, :], s, -jnp.inf)
        p = jax.nn.softmax(s, axis=-1).astype(vb.dtype)
        return jnp.einsum('bhqk,bqkhd->bqhd', p, vb)

    out = lax.map(block, (q_blocks, s_blocks, q_pos))
    return out.swapaxes(0, 1).reshape(B, S, H, dh)


def hybrid_layer(x, positions, w_in, ret_gn_gain, w_out, ln_mix_gain, ln_mix_bias,
                 w_gate_up, w_down, ln_ffn_gain, ln_ffn_bias):
    B, S, _ = x.shape
    topk = min(TOPK_MAX, S // 4)
    proj = x @ w_in
    rq, rk, rv, rg, aq, ak, av, iq, ik, iw = jnp.split(proj, IN_SPLITS, axis=-1)

    rq = rope(rq.reshape(B, S, RET_HEADS, RET_HEAD_DIM), positions)
    rk = rope(rk.reshape(B, S, RET_HEADS, RET_HEAD_DIM), positions)
    rv = rv.reshape(B, S, RET_HEADS, RET_HEAD_DIM)
    ret = retention_chunkwise(rq, rk, rv)
    mu = jnp.mean(ret, axis=-1, keepdims=True)
    var = jnp.mean(jnp.square(ret - mu), axis=-1, keepdims=True)
    ret = ((ret - mu) * lax.rsqrt(var + LN_EPS)).reshape(B, S, RET_WIDTH)
    ret = (ret * ret_gn_gain.astype(jnp.float32)).astype(x.dtype)
    ret = jax.nn.silu(rg) * ret

    aq = rope(aq.reshape(B, S, ATT_HEADS, ATT_HEAD_DIM), positions)
    ak = rope(ak.reshape(B, S, ATT_HEADS, ATT_HEAD_DIM), positions)
    av = av.reshape(B, S, ATT_HEADS, ATT_HEAD_DIM)
    iq = rope(iq.reshape(B, S, IDX_HEADS, IDX_DIM), positions)
    ik = rope(ik[:, :, None, :], positions)[:, :, 0, :]
    sel = indexer_topk(iq, ik, iw, topk)
    att = sparse_attention(aq, ak, av, sel).reshape(B, S, ATT_WIDTH)

    mix = jnp.concatenate([ret, att], axis=-1) @ w_out
    x = layer_norm(DEEPNORM_ALPHA * x + mix, ln_mix_gain, ln_mix_bias)

    gate, up = jnp.split(x @ w_gate_up, 2, axis=-1)
    ffn = (jax.nn.silu(gate) * up) @ w_down
    return layer_norm(DEEPNORM_ALPHA * x + ffn, ln_ffn_gain, ln_ffn_bias)


def setup_inputs(seed: int = 0) -> dict:
    key = jax.random.key(seed)
    ks = jax.random.split(key, 12)
    f32 = jnp.float32
    x = jax.random.normal(ks[0], (BATCH, SEQ, D_MODEL), f32)
    start = jax.random.randint(ks[1], (BATCH, 1), 0, 4096, dtype=jnp.int32)
    positions = (start + jnp.arange(SEQ, dtype=jnp.int32)[None, :]).astype(jnp.int32)
    w_in = jax.random.normal(ks[2], (DEPTH, D_MODEL, IN_COLS), f32) * D_MODEL ** -0.5
    ret_gn_gain = 1.0 + 0.02 * jax.random.normal(ks[3], (DEPTH, RET_WIDTH), f32)
    w_out = jax.random.normal(ks[4], (DEPTH, MIX_WIDTH, D_MODEL), f32) * (MIX_WIDTH ** -0.5 * DEEPNORM_BETA)
    ln_mix_gain = 1.0 + 0.02 * jax.random.normal(ks[5], (DEPTH, D_MODEL), f32)
    ln_mix_bias = 0.02 * jax.random.normal(ks[6], (DEPTH, D_MODEL), f32)
    w_gate_up = jax.random.normal(ks[7], (DEPTH, D_MODEL, 2 * FFN_HIDDEN), f32) * D_MODEL ** -0.5
    w_down = jax.random.normal(ks[8], (DEPTH, FFN_HIDDEN, D_MODEL), f32) * (FFN_HIDDEN ** -0.5 * DEEPNORM_BETA)
    ln_ffn_gain = 1.0 + 0.02 * jax.random.normal(ks[9], (DEPTH, D_MODEL), f32)
    ln_ffn_bias = 0.02 * jax.random.normal(ks[10], (DEPTH, D_MODEL), f32)
    return {"x": x, "positions": positions, "w_in": w_in, "ret_gn_gain": ret_gn_gain,
            "w_out": w_out, "ln_mix_gain": ln_mix_gain, "ln_mix_bias": ln_mix_bias,
            "w_gate_up": w_gate_up, "w_down": w_down,
            "ln_ffn_gain": ln_ffn_gain, "ln_ffn_bias": ln_ffn_bias}


def reference(x, positions, w_in, ret_gn_gain, w_out, ln_mix_gain, ln_mix_bias,
              w_gate_up, w_down, ln_ffn_gain, ln_ffn_bias):
    for layer in range(DEPTH):
        x = hybrid_layer(x, positions, w_in[layer], ret_gn_gain[layer], w_out[layer],
                         ln_mix_gain[layer], ln_mix_bias[layer], w_gate_up[layer],
                         w_down[layer], ln_ffn_gain[layer], ln_ffn_bias[layer])
    return x
```

```python
import math
from contextlib import ExitStack
import numpy as np
import concourse.bass as bass
import concourse.mybir as mybir
from concourse.bass_utils import run_bass_kernel_spmd

F32 = mybir.dt.float32
BF16 = mybir.dt.bfloat16
I32 = mybir.dt.int32
ALU = mybir.AluOpType
AF = mybir.ActivationFunctionType
AX = mybir.AxisListType

D = 1024
KC = 8
HID = 2816
HC = 22
IN_COLS = 4168
OFF_RQ, OFF_RK, OFF_RV, OFF_RG, OFF_AQ, OFF_AK, OFF_AV, OFF_IQ, OFF_IK, OFF_IW = (
    0, 512, 1024, 1536, 2048, 2560, 3072, 3584, 4096, 4160)
LN_EPS = 1e-5
MAGIC = 12582912.0
TWO_PI = 2.0 * math.pi
C1 = 6.28125
C2 = TWO_PI - C1
PI_LO = 3.1415925
NEG_BIG = -1.0e30
MASK_NEG = -30000.0

C_ID, C_TRI, C_CB, C_RT, C_INVF, C_P2, C_END = 0, 128, 256, 384, 404, 468, 500


class Res:
    __slots__ = ("name", "w", "r", "sem", "cnt")

    def __init__(self, name):
        self.name = name
        self.w = None
        self.r = {}
        self.sem = None
        self.cnt = 0


class Sched:
    ENG = ("pe", "act", "dve", "pool", "sp")

    def __init__(self, nc, es, needed):
        self.nc = nc
        self.es = es
        self.rec = needed is None
        self.needed = set() if self.rec else needed
        self.idx = {e: 0 for e in self.ENG}
        self.sig = {e: 0 for e in self.ENG}
        self.cntof = {}
        self.waited = {}
        self.eng = dict(pe=nc.tensor, act=nc.scalar, dve=nc.vector, pool=nc.gpsimd, sp=nc.sync)
        self.sem = {}
        self.nsem = 0
        if not self.rec:
            for e in self.ENG:
                self.sem[e] = es.enter_context(nc.semaphore("sem_" + e))

    def _wait(self, e, tok):
        if tok is None:
            return
        if tok[0] == "eng":
            _, pe_, i = tok
            if pe_ == e and e in ("pe", "sp"):
                return
            if self.rec:
                self.needed.add((pe_, i))
                return
            val = self.cntof[(pe_, i)]
            key = (e, pe_)
            semh = self.sem[pe_]
        else:
            _, res, val = tok
            if self.rec:
                return
            key = (e, "dma", res.name)
            semh = res.sem
        if self.waited.get(key, 0) >= val:
            return
        self.waited[key] = val
        self.eng[e].wait_ge(semh, val)

    def _deps(self, e, reads, writes):
        for r in reads:
            self._wait(e, r.w)
        for w in writes:
            self._wait(e, w.w)
            for t in list(w.r.values()):
                self._wait(e, t)

    def op(self, e, fn, reads=(), writes=()):
        self._deps(e, reads, writes)
        i = self.idx[e]
        self.idx[e] += 1
        tok = ("eng", e, i)
        if not self.rec:
            ins = fn()
            if (e, i) in self.needed:
                ins.then_inc(self.sem[e], 1)
                self.sig[e] += 1
                self.cntof[(e, i)] = self.sig[e]
        for r in reads:
            r.r[e] = tok
        for w in writes:
            w.w = tok
            w.r = {}
        return tok

    def _getsem(self, res):
        if res.sem is None and not self.rec:
            res.sem = self.es.enter_context(self.nc.semaphore("dsem_%d" % self.nsem))
            self.nsem += 1

    def dma(self, q, out_ap, in_ap, reads=(), writes=(), semres=None):
        self._deps(q, reads, writes)
        self.idx[q] += 1
        tr = writes[0] if writes else semres
        self._getsem(tr)
        tr.cnt += 16
        tok = ("dma", tr, tr.cnt)
        if not self.rec:
            self.eng[q].dma_start(out=out_ap, in_=in_ap).then_inc(tr.sem, 16)
        for w in writes:
            w.w = tok
            w.r = {}
        for r in reads:
            r.r["dma_" + tr.name] = tok
        return tok

    def barrier(self, allres):
        toks = []
        for r in allres:
            if r.w is not None:
                toks.append(r.w)
            toks.extend(r.r.values())
        for e in ("pe", "act", "dve", "pool", "sp"):
            for t in toks:
                self._wait(e, t)


class _Stop(Exception):
    pass


class Cfg:
    stop = None

    def __init__(self, S=2048, NSEQ=2, DEPTH=2, NITER=20):
        self.S = S
        self.NSEQ = NSEQ
        self.DEPTH = DEPTH
        self.NB = S // 128
        self.TOPK = min(256, S // 4)
        self.NITER = NITER
        self.GT = min(512, S)


def make_consts(cfg):
    c = np.zeros((128, C_END), np.float32)
    p = np.arange(128)
    c[:, C_ID:C_ID + 128] = np.eye(128, dtype=np.float32)
    c[:, C_TRI:C_TRI + 128] = (p[None, :] >= p[:, None]).astype(np.float32)
    c[:, C_CB:C_CB + 128] = np.where(p[None, :] <= p[:, None], 0.0, NEG_BIG)
    for h in range(4):
        lg = math.log(1.0 - 2.0 ** (-5.0 - h))
        dq = np.exp(lg * (p + 1.0))
        c[:, C_RT + h] = dq
        c[:, C_RT + 4 + h] = np.exp(-lg * (p + 1.0)) * 128 ** -0.5
        c[:, C_RT + 8 + h] = np.exp(lg * (127.0 - p)) * 128 ** -0.5
        c[:, C_RT + 12 + h] = dq * dq
        c[:, C_RT + 16 + h] = (1.0 - 2.0 ** (-5.0 - h)) ** 128
    invf = (10000.0 ** (-np.arange(0, 128, 2, dtype=np.float32) / 128)).astype(np.float32)
    c[:, C_INVF:C_INVF + 64] = invf[None, :]
    c[:, C_P2:C_P2 + 32] = (2.0 ** (-np.arange(32, dtype=np.float64)))[None, :]
    return c


def pipeline(gens, depth):
    gens = list(gens)
    active = []
    nxt = 0
    while nxt < len(gens) or active:
        if nxt < len(gens) and len(active) < depth:
            active.append(gens[nxt])
            nxt += 1
        for g in list(active):
            try:
                next(g)
            except StopIteration:
                active.remove(g)


def build(cfg):
    S, NSEQ, DEPTH, NB, TOPK, NITER, GT = cfg.S, cfg.NSEQ, cfg.DEPTH, cfg.NB, cfg.TOPK, cfg.NITER, cfg.GT
    nc = bass.Bass("TRN2", target_bir_lowering=False)
    dt = nc.dram_tensor
    x_d = dt("x", [NSEQ, S, D], F32, kind="ExternalInput").ap()
    pos_d = dt("pos", [NSEQ, 128, NB], I32, kind="ExternalInput").ap()
    wkv_d = dt("w_kv", [DEPTH, 128, KC, 1096], F32, kind="ExternalInput").ap()
    wq_d = dt("w_q", [DEPTH, 128, KC, 1024], F32, kind="ExternalInput").ap()
    wr_d = dt("w_r", [DEPTH, 128, KC, 2048], F32, kind="ExternalInput").ap()
    wout_d = dt("w_o", [DEPTH, 128, KC, 1024], F32, kind="ExternalInput").ap()
    wgu_d = dt("w_gu", [DEPTH, HC // 2, 128, KC * 512], F32, kind="ExternalInput").ap()
    wd_d = dt("w_d", [DEPTH, 128, HC, 1024], F32, kind="ExternalInput").ap()
    gret_d = dt("g_ret", [DEPTH, 128, 512], F32, kind="ExternalInput").ap()
    gmix_d = dt("g_mix", [DEPTH, 128, 2048], F32, kind="ExternalInput").ap()
    gffn_d = dt("g_ffn", [DEPTH, 128, 2048], F32, kind="ExternalInput").ap()
    cst_d = dt("cst", [128, C_END], F32, kind="ExternalInput").ap()
    y_d = dt("y", [NSEQ, S, D], F32, kind="ExternalOutput").ap()
    wgu_bf = dt("wgu_bf", [DEPTH, HC // 2, 128, KC * 512], BF16, kind="Internal").ap()

    ARENA = 63 * 1024
    with ExitStack() as es:
        sb = lambda name, shape, dtp: es.enter_context(nc.sbuf_tensor(name, shape, dtp))
        x_sb = sb("x_sb", [128, NB, D], F32)
        cs_cos = sb("cs_cos", [128, NB, 64], F32)
        cs_sin = sb("cs_sin", [128, NB, 64], F32)
        cst = sb("cst_sb", [128, C_END], F32)
        ident = sb("ident", [128, 128], BF16)
        posi = sb("posi", [128, NB], I32)
        posf = sb("posf", [128, NB], F32)
        wsb = sb("wsb", [128, NB, 8], F32)
        sm = sb("sm", [128, 256], F32)
        steps = sb("steps", [128, 2, 32], F32)
        nstp = sb("nstp", [128, 32], F32)
        biasc_t = sb("biasc", [128, 32], F32)
        sb_late = lambda name, shape, dtp: biasc_t
        ang = sb("ang", [128, 4, 64], F32)
        arena = sb("arena", [128, ARENA], BF16)
        ps = es.enter_context(nc.psum_tensor("ps", [128, 4096], F32))

        def bankf(i, n=1):
            return ps[:, i * 512:(i + n) * 512]

        def bankb(i):
            return ps[:, i * 512:(i + 1) * 512].bitcast(BF16)

        class Carver:
            def __init__(self, off=0):
                self.off = off

            def take(self, n_bf16, dtype=BF16):
                n = (n_bf16 + 15) // 16 * 16
                ap = arena[:, self.off:self.off + n_bf16]
                self.off += n
                assert self.off <= ARENA, "arena overflow %d" % self.off
                if dtype == F32:
                    ap = ap.bitcast(F32)
                return ap

        def emit(s):
            R = {}

            def res(name):
                if name not in R:
                    R[name] = Res(name)
                return R[name]

            E = s.eng
            r_cst, r_ident, r_cs, r_pos, r_wsb = res("cst"), res("ident"), res("cs"), res("pos"), res("wsb")
            r_x = [res("x%d" % b) for b in range(NB)]
            r_bk = [res("bank%d" % i) for i in range(8)]
            r_stb_ = [res("store%d" % b) for b in range(NB)]
            r_smc = res("smc")

            def tt(e, out, in0, in1, op, reads, writes):
                return s.op(e, lambda: E[e].tensor_tensor(out=out, in0=in0, in1=in1, op=op), reads, writes)

            def ts(e, out, in0, s1, s2, op0, op1, reads, writes, accum=None):
                if op1 is None:
                    return s.op(e, lambda: E[e].tensor_scalar(out=out, in0=in0, scalar1=s1, scalar2=None, op0=op0), reads, writes)
                if accum is not None:
                    return s.op(e, lambda: E[e].tensor_scalar(out=out, in0=in0, scalar1=s1, scalar2=s2, op0=op0, op1=op1, accum_out=accum), reads, writes)
                return s.op(e, lambda: E[e].tensor_scalar(out=out, in0=in0, scalar1=s1, scalar2=s2, op0=op0, op1=op1), reads, writes)

            def stt(e, out, in0, sc, in1, op0, op1, reads, writes):
                return s.op(e, lambda: E[e].scalar_tensor_tensor(out=out, in0=in0, scalar=sc, in1=in1, op0=op0, op1=op1), reads, writes)

            def act(out, in_, func, reads, writes, bias=None, scale=None):
                kw = {}
                if bias is not None:
                    kw["bias"] = bias
                if scale is not None:
                    kw["scale"] = scale
                return s.op("act", lambda: nc.scalar.activation(out=out, in_=in_, func=func, **kw), reads, writes)

            def copy(e, out, in_, reads, writes):
                if e == "act":
                    return s.op("act", lambda: nc.scalar.copy(out=out, in_=in_), reads, writes)
                return s.op(e, lambda: E[e].tensor_copy(out=out, in_=in_), reads, writes)

            def mm(out, pairs, reads, writes):
                def f():
                    ins = None
                    n = len(pairs)
                    for i, (l, r) in enumerate(pairs):
                        ins = nc.tensor.matmul(out, lhsT=l, rhs=r, start=(i == 0), stop=(i == n - 1))
                    return ins
                return s.op("pe", f, reads, writes)

            def tr(out, in_, reads, writes):
                return s.op("pe", lambda: nc.tensor.transpose(out=out, in_=in_, identity=ident[:]), list(reads) + [r_ident], writes)

            s.dma("sp", cst[:], cst_d, writes=[r_cst])
            copy("dve", ident[:], cst[:, C_ID:C_ID + 128], [r_cst], [r_ident])
            cb = cst[:, C_CB:C_CB + 128]
            tri = cst[:, C_TRI:C_TRI + 128]
            invf = cst[:, C_INVF:C_INVF + 64]
            p2 = cst[:, C_P2:C_P2 + 32]
            s.op("dve", lambda: nc.vector.memset(sm[:, 0:8], -0.5), [], [r_smc])
            s.op("dve", lambda: nc.vector.memset(sm[:, 8:9], -1.0e29), [], [r_smc])
            biasc = sb_late("biasc", [128, 32], F32)
            BIASC = {}
            for tbq in range(NB):
                Lq = (tbq + 1) * 128
                s.op("dve", lambda tbq=tbq, Lq=Lq: nc.vector.memset(biasc[:, tbq:tbq + 1], float(-(2 * TOPK - Lq - 1))), [], [r_smc])
                BIASC[Lq] = biasc[:, tbq:tbq + 1]
            NEGH4 = sm[:, 0:4]
            NEGH = sm[:, 0:1]
            TAUALL = sm[:, 8:9]
            r_conv = [res("conv%d" % l) for l in range(DEPTH)]
            for l in range(DEPTH):
                for pr in range(HC // 2):
                    s.dma("pool", wgu_bf[l, pr], wgu_d[l, pr], semres=r_conv[l])
                r_conv[l].w = ("dma", r_conv[l], r_conv[l].cnt)

            xrot = [0]

            def make_xT(tb, xbf, xT, r_xbf, r_xT, bk):
                e = "act" if xrot[0] % 2 == 0 else "dve"
                xrot[0] += 1
                copy(e, xbf, x_sb[:, tb, :], [r_x[tb]], [r_xbf])
                pb = bankb(bk)
                for k in range(KC):
                    tr(pb[:, k * 128:(k + 1) * 128], xbf[:, k * 128:(k + 1) * 128], [r_xbf], [r_bk[bk]])
                copy("act", xT, pb[:, :], [r_bk[bk]], [r_xT])

            def rope(src, r_src, G, half, cosap, sinap, out, r_out, tA, tB, r_tA, r_tB):
                n = G * 2 * half
                s3 = src.rearrange("p (g d) -> p g d", d=half)
                cbb = cosap.unsqueeze(1).to_broadcast([128, 2 * G, half])
                sbb = sinap.unsqueeze(1).to_broadcast([128, 2 * G, half])
                tA3 = tA[:, 0:n].rearrange("p (g d) -> p g d", d=half)
                tB3 = tB[:, 0:n].rearrange("p (g d) -> p g d", d=half)
                tt("dve", tA3, s3, cbb, ALU.mult, list(r_src) + [r_cs], [r_tA])
                tt("dve", tB3, s3, sbb, ALU.mult, list(r_src) + [r_cs], [r_tB])
                tA4 = tA[:, 0:n].rearrange("p (g t d) -> p g t d", t=2, d=half)
                tB4 = tB[:, 0:n].rearrange("p (g t d) -> p g t d", t=2, d=half)
                o4 = out.rearrange("p (g t d) -> p g t d", t=2, d=half)
                tt("pool", o4[:, :, 0, :], tA4[:, :, 0, :], tB4[:, :, 1, :], ALU.subtract, [r_tA, r_tB], [r_out])
                tt("pool", o4[:, :, 1, :], tA4[:, :, 1, :], tB4[:, :, 0, :], ALU.add, [r_tA, r_tB], [r_out])

            lncnt = [0]

            def layer_norm_block(tb, ypair, r_y, gbt, r_gbt):
                par = lncnt[0] % 2
                lncnt[0] += 1
                c0 = 16 + par * 24
                r_s = res("smln%d" % par)
                xb = x_sb[:, tb, :]
                for hh in range(2):
                    stt("dve", xb[:, hh * 512:(hh + 1) * 512], xb[:, hh * 512:(hh + 1) * 512], ALPHA,
                        ypair[hh], ALU.mult, ALU.add, [r_y[hh]], [r_x[tb]])
                st6 = sm[:, c0:c0 + 12].rearrange("p (a b) -> p a b", a=2)
                for hh in range(2):
                    s.op("dve", lambda hh=hh: nc.vector.bn_stats(out=st6[:, hh, :], in_=xb[:, hh * 512:(hh + 1) * 512]), [r_x[tb]], [r_s])
                s.op("dve", lambda: nc.vector.bn_aggr(out=sm[:, c0 + 12:c0 + 14], in_=st6), [r_s], [r_s])
                ts("dve", sm[:, c0 + 14:c0 + 15], sm[:, c0 + 13:c0 + 14], LN_EPS, None, ALU.add, None, [r_s], [r_s])
                tt("pool", sm[:, c0 + 15:c0 + 16], sm[:, c0 + 14:c0 + 15], NEGH, ALU.pow, [r_s, r_smc], [r_s])
                stt("dve", sm[:, c0 + 16:c0 + 17], sm[:, c0 + 12:c0 + 13], -1.0, sm[:, c0 + 15:c0 + 16], ALU.mult, ALU.mult, [r_s], [r_s])
                act(xb, xb, AF.Identity, [r_x[tb], r_s], [r_x[tb]], bias=sm[:, c0 + 16:c0 + 17], scale=sm[:, c0 + 15:c0 + 16])
                tt("dve", xb, xb, gbt[:, 0:1024], ALU.mult, [r_x[tb], r_gbt], [r_x[tb]])
                tt("pool", xb, xb, gbt[:, 1024:2048], ALU.add, [r_x[tb], r_gbt], [r_x[tb]])

            ALPHA = (2.0 * DEPTH) ** 0.25
            ATT_SCALE = 128 ** -0.5
            W_SCALE = (8 ** -0.5) * (64 ** -0.5)

            def chk(name):
                if cfg.stop == name:
                    raise _Stop()

            for sq in range(NSEQ):
              try:
                xv = x_d[sq].rearrange("(b p) d -> p b d", p=128)
                for b in range(NB):
                    s.dma("sp", x_sb[:, b, :], xv[:, b, :], writes=[r_x[b]])
                s.dma("sp", posi[:], pos_d[sq], writes=[r_pos])
                copy("dve", posf[:], posi[:], [r_pos], [r_pos])
                r_ang = res("ang")
                for b in range(NB):
                    A_, K_, Rs, Rc = ang[:, 0, :], ang[:, 1, :], ang[:, 2, :], ang[:, 3, :]
                    ts("dve", A_, invf, posf[:, b:b + 1], None, ALU.mult, None, [r_cst, r_pos], [r_ang])
                    for which, dst in ((0, Rs), (1, Rc)):
                        src = A_
                        if which == 1:
                            ts("dve", Rc, A_, math.pi / 2, None, ALU.add, None, [r_ang], [r_ang])
                            src = Rc
                        ts("dve", K_, src, 1.0 / TWO_PI, MAGIC, ALU.mult, ALU.add, [r_ang], [r_ang])
                        ts("dve", K_, K_, MAGIC, None, ALU.subtract, None, [r_ang], [r_ang])
                        stt("dve", dst, K_, -C1, src, ALU.mult, ALU.add, [r_ang], [r_ang])
                        stt("dve", dst, K_, -C2, dst, ALU.mult, ALU.add, [r_ang], [r_ang])
                        ts("dve", dst, dst, PI_LO, -PI_LO, ALU.min, ALU.max, [r_ang], [r_ang])
                    act(cs_sin[:, b, :], Rs, AF.Sin, [r_ang], [r_cs])
                    act(cs_cos[:, b, :], Rc, AF.Sin, [r_ang], [r_cs])
                chk('setup')
                for l in range(DEPTH):
                    cv = Carver()
                    attTall = cv.take(4 * S).rearrange("p (k t) -> p k t", k=4)
                    PERSIST = cv.off
                    KT = cv.take(4 * S).rearrange("p (h t) -> p h t", h=4)
                    Vg = cv.take(NB * 4 * 130).rearrange("p (b h e) -> p b h e", b=NB, h=4)
                    ikT = cv.take(S)
                    wA = cv.take(KC * 1096)
                    xbf_ = cv.take(1024)
                    xT_ = cv.take(1024)
                    xbf = [xbf_, xbf_]
                    xT = [xT_, xT_]
                    tA = cv.take(1024, F32)
                    tB = cv.take(1024, F32)
                    rbuf = [cv.take(1024, F32) for _ in range(2)]
                    qiq_ = cv.take(1024)
                    qiq = [qiq_, qiq_]
                    dgw = cv.take(2048, F32)
                    attb_ = cv.take(512)
                    attb = [attb_, attb_]
                    qT = [cv.take(512).rearrange("p (h t) -> p h t", h=4) for _ in range(3)]
                    iqT = [cv.take(512).rearrange("p (h t) -> p h t", h=4) for _ in range(2)]
                    Ibuf2 = [cv.take(2 * S, F32) for _ in range(2)]
                    jk = cv.take(16)
                    mb = [cv.take(S) for _ in range(2)]
                    ebuf = cv.take(S)
                    PTs_ = cv.take(S)
                    PTs = [PTs_, PTs_]
                    r_attTall = [res("attTall%d" % b) for b in range(NB)]
                    r_KT = [res("KT%d" % b) for b in range(NB)]
                    r_V = [res("V%d" % b) for b in range(NB)]
                    r_ikT = [res("ikT%d" % b) for b in range(NB)]
                    r_xbf = [res("xbf0"), res("xbf0")]
                    r_xT = [res("xT0"), res("xT0")]
                    r_tA, r_tB = res("tA"), res("tB")
                    r_wA = res("wA")
                    r_I2, r_e = [res("I0"), res("I1")], res("e")
                    r_jk = res("jk")
                    r_mb = [res("mb0"), res("mb1")]
                    r_PTs = [res("PTs0"), res("PTs0")]
                    r_rb = [res("rb0"), res("rb1")]
                    r_qbf = [res("qbf0"), res("qbf0")]
                    r_iqb = [res("iqb0"), res("iqb0")]
                    r_dgw, r_jk2, r_sb2 = res("dgw"), res("jk2"), res("sb2")
                    r_attb = [res("attb0"), res("attb0")]
                    r_kb = r_qbf
                    r_ikb = r_iqb
                    r_qT = [res("qT0"), res("qT1"), res("qT2")]
                    r_iqT = [res("iqT0"), res("iqT1")]

                    wKV = wA[:, 0:KC * 1096].rearrange("p (k c) -> p k c", k=KC)
                    r_wKVg = [res("wKVg%d" % i) for i in range(3)]
                    r_wQg = [res("wQg%d" % i) for i in range(2)]
                    for i, (c0, c1) in enumerate(((0, 512), (512, 1024), (1024, 1096))):
                        s.dma("pool", wKV[:, :, c0:c1], wkv_d[l][:, :, c0:c1], writes=[r_wKVg[i]] + r_wQg)
                    for tb in range(NB):
                        s.op("pool", lambda tb=tb: nc.gpsimd.memset(Vg[:, tb, :, 128:130], 1.0), [], [r_V[tb]])

                    def blkB(tb):
                        p = tb % 2
                        b0 = 4 * p
                        make_xT(tb, xbf[p], xT[p], r_xbf[p], r_xT[p], b0)
                        xt, r_xt = xT[p], r_xT[p]
                        mm(bankf(b0 + 1), [(xt[:, k * 128:(k + 1) * 128], wKV[:, k, 0:512]) for k in range(KC)], [r_xt, r_wKVg[0]], [r_bk[b0 + 1]])
                        mm(bankf(b0 + 2), [(xt[:, k * 128:(k + 1) * 128], wKV[:, k, 512:1024]) for k in range(KC)], [r_xt, r_wKVg[1]], [r_bk[b0 + 2]])
                        mm(bankf(b0 + 3)[:, 0:72], [(xt[:, k * 128:(k + 1) * 128], wKV[:, k, 1024:1096]) for k in range(KC)], [r_xt, r_wKVg[2]], [r_bk[b0 + 3]])
                        yield
                        kbf = qiq[p][:, 0:512]
                        ikb = qiq[p][:, 512:640]
                        rope(bankf(b0 + 1), [r_bk[b0 + 1]], 4, 64, cs_cos[:, tb, :], cs_sin[:, tb, :], kbf, r_kb[p], tA, tB, r_tA, r_tB)
                        copy("act", Vg[:, tb, :, 0:128], bankf(b0 + 2).rearrange("p (h e) -> p h e", h=4), [r_bk[b0 + 2]], [r_V[tb]])
                        rope(bankf(b0 + 3)[:, 0:64], [r_bk[b0 + 3]], 1, 32, cs_cos[:, tb, 0:64:2], cs_sin[:, tb, 0:64:2], ikb[:, 0:64], r_ikb[p], tA, tB, r_tA, r_tB)
                        copy("pool", ikb[:, 64:128], ikb[:, 0:64], [r_ikb[p]], [r_ikb[p]])
                        ts("dve", wsb[:, tb, :], bankf(b0 + 3)[:, 64:72], W_SCALE, None, ALU.mult, None, [r_bk[b0 + 3]], [r_wsb])
                        yield
                        pb = bankb(b0)
                        for h in range(4):
                            tr(pb[:, h * 128:(h + 1) * 128], kbf[:, h * 128:(h + 1) * 128], [r_kb[p]], [r_bk[b0]])
                        tr(pb[:, 512:640], ikb, [r_ikb[p]], [r_bk[b0]])
                        copy("act", KT[:, :, tb * 128:(tb + 1) * 128], pb[:, 0:512].rearrange("p (h t) -> p h t", h=4), [r_bk[b0]], [r_KT[tb]])
                        copy("act", ikT[:, tb * 128:(tb + 1) * 128], pb[:, 512:640], [r_bk[b0]], [r_ikT[tb]])

                    pipeline([blkB(tb) for tb in range(NB)], 2)
                    chk('B')
                    wQ = wA[:, 0:KC * 1024].rearrange("p (k c) -> p k c", k=KC)
                    for i in range(2):
                        s.dma("pool", wQ[:, :, i * 512:(i + 1) * 512], wq_d[l][:, :, i * 512:(i + 1) * 512], writes=[r_wQg[i]] + r_wKVg)
                    mb_free = [True, True]

                    def frontC(tb):
                        p = tb % 2
                        q3 = tb % 3
                        Ib = Ibuf2[p]
                        r_I = r_I2[p]
                        L = (tb + 1) * 128
                        r_sb = res("smbis%d" % p)
                        r_sa = res("smatt%d" % p)
                        cB = 64 + p * 16
                        cA = 96 + p * 16
                        stp = steps[:, p, :]
                        make_xT(tb, xbf[p], xT[p], r_xbf[p], r_xT[p], 0)
                        xt, r_xt = xT[p], r_xT[p]
                        mm(bankf(1), [(xt[:, k * 128:(k + 1) * 128], wQ[:, k, 0:512]) for k in range(KC)], [r_xt, r_wQg[0]], [r_bk[1]])
                        mm(bankf(2), [(xt[:, k * 128:(k + 1) * 128], wQ[:, k, 512:1024]) for k in range(KC)], [r_xt, r_wQg[1]], [r_bk[2]])
                        qbf = qiq[p][:, 0:512]
                        iqb = qiq[p][:, 512:1024]
                        rope(bankf(1), [r_bk[1]], 4, 64, cs_cos[:, tb, :], cs_sin[:, tb, :], qbf, r_qbf[p], tA, tB, r_tA, r_tB)
                        rope(bankf(2), [r_bk[2]], 8, 32, cs_cos[:, tb, 0:64:2], cs_sin[:, tb, 0:64:2], iqb, r_iqb[p], tA, tB, r_tA, r_tB)
                        pb = bankb(0)
                        for h in range(4):
                            tr(pb[:, h * 128:(h + 1) * 128], qbf[:, h * 128:(h + 1) * 128], [r_qbf[p]], [r_bk[0]])
                        for h in range(4):
                            tr(pb[:, 512 + h * 128:512 + (h + 1) * 128], iqb[:, h * 128:(h + 1) * 128], [r_iqb[p]], [r_bk[0]])
                        copy("act", qT[q3], pb[:, 0:512].rearrange("p (h t) -> p h t", h=4), [r_bk[0]], [r_qT[q3]])
                        copy("act", iqT[p], pb[:, 512:1024].rearrange("p (h t) -> p h t", h=4), [r_bk[0]], [r_iqT[p]])
                        yield
                        nch = (L + 511) // 512
                        cnt = 0
                        for c in range(nch):
                            n = min(512, L - 512 * c)
                            kres = r_ikT[4 * c:4 * c + (n + 127) // 128]
                            for h in range(8):
                                m_, base = h // 2, 64 * (h % 2)
                                bk = 1 + cnt % 2
                                rb, r_rb_ = rbuf[cnt % 2], r_rb[cnt % 2]
                                cnt += 1
                                mm(bankf(bk)[:, 0:n], [(iqT[p][base:base + 64, m_, :], ikT[base:base + 64, c * 512:c * 512 + n])], [r_iqT[p]] + kres, [r_bk[bk]])
                                act(rb[:, 0:n], bankf(bk)[:, 0:n], AF.Relu, [r_bk[bk]], [r_rb_])
                                if h == 0:
                                    ts("dve", Ib[:, c * 512:c * 512 + n], rb[:, 0:n], wsb[:, tb, 0:1], None, ALU.mult, None, [r_rb_, r_wsb], [r_I])
                                else:
                                    stt("dve", Ib[:, c * 512:c * 512 + n], rb[:, 0:n], wsb[:, tb, h:h + 1], Ib[:, c * 512:c * 512 + n], ALU.mult, ALU.add, [r_rb_, r_wsb], [r_I])
                            yield
                        tt("dve", Ib[:, tb * 128:L], Ib[:, tb * 128:L], cb, ALU.add, [r_cst], [r_I])
                        if L > TOPK:
                            MX, MN, RG, T, CNT, DD = [sm[:, cB + i:cB + i + 1] for i in range(6)]
                            s.op("dve", lambda: nc.vector.tensor_reduce(out=MX, in_=Ib[:, 0:L], axis=AX.X, op=ALU.max), [r_I], [r_sb])
                            s.op("dve", lambda: nc.vector.tensor_reduce(out=MN, in_=Ib[:, 0:L - 128], axis=AX.X, op=ALU.min), [r_I], [r_sb])
                            tt("dve", RG, MX, MN, ALU.subtract, [r_sb], [r_sb])
                            ts("dve", RG, RG, 1.0 + 1.0 / 1024, None, ALU.mult, None, [r_sb], [r_sb])
                            tt("dve", T, MX, RG, ALU.subtract, [r_sb], [r_sb])
                            ts("dve", stp[:, 0:NITER + 2], p2[:, 0:NITER + 2], RG, None, ALU.mult, None, [r_sb, r_cst], [r_sb])
                            stt("dve", T, stp[:, 1:2], 1.0, T, ALU.mult, ALU.add, [r_sb], [r_sb])
                            if p == 1:
                                ts("dve", nstp[:, 0:NITER + 2], stp[:, 0:NITER + 2], -0.5, None, ALU.mult, None, [r_sb], [r_sb])
                            junk = jk[:, 0:1].to_broadcast([128, L])
                            if p == 0:
                                for i in range(NITER):
                                    ts("dve", junk, Ib[:, 0:L], T, None, ALU.is_gt, ALU.add, [r_I, r_sb], [r_jk], accum=CNT)
                                    s.op("dve", lambda: nc.vector.tensor_scalar(out=DD, in0=CNT, scalar1=TOPK - 0.5, scalar2=-0.5, op0=ALU.is_gt, op1=ALU.add), [r_jk], [r_sb])
                                    stt("dve", T, DD, stp[:, i + 1:i + 2], T, ALU.mult, ALU.add, [r_sb], [r_sb])
                                    if i % 4 == 3:
                                        yield
                            else:
                                NT = sm[:, cB + 6:cB + 7]
                                DS = sm[:, cB + 7:cB + 8]
                                junk2 = jk[:, 1:2].to_broadcast([128, L])
                                ts("dve", NT, T, -1.0, None, ALU.mult, None, [r_sb], [r_sb])
                                for i in range(NITER):
                                    s.op("act", lambda: nc.scalar.activation(out=junk2, in_=Ib[:, 0:L], func=AF.Sign, bias=NT, scale=1.0, accum_out=CNT), [r_I, r_sb], [r_jk2])
                                    s.op("act", lambda: nc.scalar.activation(out=DS, in_=CNT, func=AF.Sign, bias=BIASC[L], scale=1.0), [r_jk2, r_smc], [r_sb2])
                                    s.op("act", lambda i=i: nc.scalar.activation(out=NT, in_=DS, func=AF.Identity, bias=NT, scale=nstp[:, i + 1:i + 2]), [r_sb2, r_sb], [r_sb])
                                    if i % 4 == 3:
                                        yield
                                ts("dve", T, NT, -1.0, None, ALU.mult, None, [r_sb], [r_sb])
                            tt("dve", T, T, stp[:, NITER + 1:NITER + 2], ALU.subtract, [r_sb], [r_sb])
                            TAU = T
                            r_tau = r_sb
                        else:
                            TAU = TAUALL
                            r_tau = r_smc
                        while not mb_free[p]:
                            yield
                        mb_free[p] = False
                        ts("dve", mb[p][:, 0:L], Ib[:, 0:L], TAU, MASK_NEG, ALU.is_le, ALU.mult, [r_I, r_tau], [r_mb[p]])

                    def backC(tb):
                        p = tb % 2
                        q3 = tb % 3
                        L = (tb + 1) * 128
                        nch = (L + 511) // 512
                        cA = 96 + p * 16
                        for h in range(4):
                            hb = 4 + 2 * (h % 2) if nch <= 2 else 4
                            sc_ = ps[:, hb * 512:hb * 512 + L]
                            for c in range(nch):
                                n = min(512, L - 512 * c)
                                kres = r_KT[4 * c:4 * c + (n + 127) // 128]
                                mm(bankf(hb + c)[:, 0:n], [(qT[q3][:, h, :], KT[:, h, c * 512:c * 512 + n]), (ident[:], mb[p][:, c * 512:c * 512 + n])],
                                   [r_qT[q3], r_mb[p], r_ident] + kres, [r_bk[hb + c]])
                            rsc = r_bk[hb:hb + nch]
                            MXa, NBa, RDa = [sm[:, cA + 3 * (h % 2) + i:cA + 3 * (h % 2) + i + 1] for i in range(3)]
                            r_sah = res("smatt%d_%d" % (p, h % 2))
                            s.op("dve", lambda: nc.vector.tensor_reduce(out=MXa, in_=sc_, axis=AX.X, op=ALU.max), rsc, [r_sah])
                            ts("dve", NBa, MXa, -ATT_SCALE, None, ALU.mult, None, [r_sah], [r_sah])
                            act(ebuf[:, 0:L], sc_, AF.Exp, rsc + [r_sah], [r_e], bias=NBa, scale=ATT_SCALE)
                            yield
                            pbb = bankb(3)
                            pts = PTs[h % 2]
                            for half in range((tb + 8) // 8):
                                nblk = min(8, tb + 1 - 8 * half)
                                for j in range(nblk):
                                    sbk = half * 8 + j
                                    tr(pbb[:, j * 128:(j + 1) * 128], ebuf[:, sbk * 128:(sbk + 1) * 128], [r_e], [r_bk[3]])
                                copy("act", pts[:, half * 1024:half * 1024 + nblk * 128], pbb[:, 0:nblk * 128], [r_bk[3]], [r_PTs[h % 2]])
                            mm(bankf(hb)[:, 0:129], [(pts[:, sbk * 128:(sbk + 1) * 128], Vg[:, sbk, h, 0:129]) for sbk in range(tb + 1)],
                               [r_PTs[h % 2]] + r_V[0:tb + 1], [r_bk[hb]])
                            s.op("dve", lambda: nc.vector.reciprocal(out=RDa, in_=bankf(hb)[:, 128:129]), [r_bk[hb]], [r_sah])
                            ts("dve", attb[p][:, h * 128:(h + 1) * 128], bankf(hb)[:, 0:128], RDa, None, ALU.mult, None, [r_bk[hb], r_sah], [r_attb[p]])
                            yield
                        pbb = bankb(3)
                        for h in range(4):
                            tr(pbb[:, h * 128:(h + 1) * 128], attb[p][:, h * 128:(h + 1) * 128], [r_attb[p]], [r_bk[3]])
                        copy("act", attTall[:, :, tb * 128:(tb + 1) * 128], pbb[:, 0:512].rearrange("p (h t) -> p h t", h=4), [r_bk[3]], [r_attTall[tb]])

                    fronts = []
                    ready = []
                    back = None
                    done_back = set()
                    next_tb = 0
                    while next_tb < NB or fronts or ready or back is not None:
                        while (next_tb < NB and len(fronts) < 2
                               and all(t % 2 != next_tb % 2 for t, _ in fronts)
                               and (next_tb < 3 or (next_tb - 3) in done_back)):
                            fronts.append((next_tb, frontC(next_tb)))
                            next_tb += 1
                        if back is None and ready:
                            tbb = ready.pop(0)
                            back = (tbb, backC(tbb))
                        if back is not None:
                            try:
                                next(back[1])
                            except StopIteration:
                                mb_free[back[0] % 2] = True
                                done_back.add(back[0])
                                back = None
                        for item in list(fronts):
                            try:
                                next(item[1])
                            except StopIteration:
                                fronts.remove(item)
                                ready.append(item[0])
                    chk('C')
                    s.barrier(list(R.values()))
                    cv = Carver(PERSIST)
                    mixT = cv.take(4 * S).rearrange("p (k t) -> p k t", k=4)
                    PERSIST2 = cv.off
                    wR = cv.take(KC * 2048).rearrange("p (k c) -> p k c", k=KC)
                    xbf = [cv.take(1024) for _ in range(3)]
                    xT = [cv.take(1024) for _ in range(3)]
                    tA = cv.take(2048, F32)
                    tB = cv.take(2048, F32)
                    qk_r = cv.take(2048, F32)
                    qkb = [cv.take(1024) for _ in range(3)]
                    kdec = [cv.take(512) for _ in range(3)]
                    vbf = [cv.take(512) for _ in range(3)]
                    gsl = [cv.take(1024, F32) for _ in range(3)]
                    qkT = [cv.take(1024) for _ in range(3)]
                    stb = [cv.take(512) for _ in range(3)]
                    yn = cv.take(1024, F32)
                    y3 = [cv.take(512) for _ in range(3)]
                    state_f = cv.take(1024, F32)
                    state_bf = cv.take(512)
                    gbA = cv.take(1024, F32)
                    r_mixT = [res("mixT%d" % b) for b in range(NB)]
                    r_wR, r_gbA = res("wR"), res("gbA")
                    r_qk, r_yn = res("qk_r"), res("yn")
                    r_qkb = [res("qkb%d" % i) for i in range(3)]
                    r_kdec = [res("kdec%d" % i) for i in range(3)]
                    r_vbf = [res("vbf%d" % i) for i in range(3)]
                    r_gsl = [res("gsl%d" % i) for i in range(3)]
                    r_qkT = [res("qkT%d" % i) for i in range(3)]
                    r_stb = [res("stb%d" % i) for i in range(3)]
                    r_y3 = [res("y3%d" % i) for i in range(3)]
                    r_stf, r_stbf = res("state_f"), res("state_bf")
                    r_xbfA = [res("xbfA%d" % i) for i in range(3)]
                    r_xTA = [res("xTA%d" % i) for i in range(3)]
                    r_tA2, r_tB2 = res("tA2"), res("tB2")
                    s.dma("sp", gbA[:, 0:512], gret_d[l], writes=[r_gbA])
                    r_wRg = [res("wRg%d" % i) for i in range(4)]
                    for i in range(4):
                        s.dma("pool", wR[:, :, i * 512:(i + 1) * 512], wr_d[l][:, :, i * 512:(i + 1) * 512], writes=[r_wRg[i]])
                    s.op("dve", lambda: nc.vector.memset(state_f, 0.0), [], [r_stf])
                    s.op("pool", lambda: nc.gpsimd.memset(state_bf, 0.0), [], [r_stbf])
                    dct = cst[:, C_RT + 16:C_RT + 20].unsqueeze(2).to_broadcast([128, 4, 128])
                    kps = cst[:, C_RT + 4:C_RT + 8].unsqueeze(2).to_broadcast([128, 4, 128])
                    dks = cst[:, C_RT + 8:C_RT + 12].unsqueeze(2).to_broadcast([128, 4, 128])

                    def blkA(tb):
                        p = tb % 3
                        r_sg = res("smgn%d" % (tb % 2))
                        cG = 128 + (tb % 2) * 48
                        make_xT(tb, xbf[p], xT[p], r_xbfA[p], r_xTA[p], 0)
                        xt, r_xt = xT[p], r_xTA[p]
                        for j in range(4):
                            mm(bankf(1 + j), [(xt[:, k * 128:(k + 1) * 128], wR[:, k, j * 512:(j + 1) * 512]) for k in range(KC)], [r_xt, r_wRg[j]], [r_bk[1 + j]])
                        yield
                        rope(bankf(1, 2), [r_bk[1], r_bk[2]], 8, 64, cs_cos[:, tb, :], cs_sin[:, tb, :], qk_r, r_qk, tA, tB, r_tA2, r_tB2)
                        copy("act", vbf[p], bankf(3), [r_bk[3]], [r_vbf[p]])
                        act(gsl[p], bankf(4), AF.Silu, [r_bk[4]], [r_gsl[p]])
                        copy("pool", qkb[p][:, 0:512], qk_r[:, 0:512], [r_qk], [r_qkb[p]])
                        k3 = qk_r[:, 512:1024].rearrange("p (h d) -> p h d", h=4)
                        tt("dve", qkb[p][:, 512:1024].rearrange("p (h d) -> p h d", h=4), k3, kps, ALU.mult, [r_qk, r_cst], [r_qkb[p]])
                        tt("pool", kdec[p].rearrange("p (h d) -> p h d", h=4), k3, dks, ALU.mult, [r_qk, r_cst], [r_kdec[p]])
                        pb = bankb(0)
                        for j in range(8):
                            tr(pb[:, j * 128:(j + 1) * 128], qkb[p][:, j * 128:(j + 1) * 128], [r_qkb[p]], [r_bk[0]])
                        copy("act", qkT[p], pb[:, :], [r_bk[0]], [r_qkT[p]])
                        yield
                        for h in range(4):
                            mm(bankf(5)[:, h * 128:(h + 1) * 128], [(qkT[p][:, 512 + h * 128:512 + (h + 1) * 128], qkT[p][:, h * 128:(h + 1) * 128])], [r_qkT[p]], [r_bk[5]])
                        tt("dve", stb[p].rearrange("p (h d) -> p h d", h=4), bankf(5).rearrange("p (h d) -> p h d", h=4),
                           tri.unsqueeze(1).to_broadcast([128, 4, 128]), ALU.mult, [r_bk[5], r_cst], [r_stb[p]])
                        for h in range(4):
                            sl = slice(h * 128, (h + 1) * 128)
                            mm(bankf(6)[:, sl], [(stb[p][:, sl], vbf[p][:, sl]), (qkT[p][:, sl], state_bf[:, sl])], [r_stb[p], r_vbf[p], r_qkT[p], r_stbf], [r_bk[6]])
                        for h in range(4):
                            sl = slice(h * 128, (h + 1) * 128)
                            mm(bankf(7)[:, sl], [(kdec[p][:, sl], vbf[p][:, sl])], [r_kdec[p], r_vbf[p]], [r_bk[7]])
                        tt("pool", state_f.rearrange("p (h d) -> p h d", h=4), state_f.rearrange("p (h d) -> p h d", h=4), dct, ALU.mult, [r_cst], [r_stf])
                        tt("dve", state_f, state_f, bankf(7), ALU.add, [r_bk[7]], [r_stf])
                        copy("pool", state_bf, state_f, [r_stf], [r_stbf])
                        yield
                        st6 = sm[:, cG:cG + 24].rearrange("p (h a) -> p h a", h=4)
                        mv = sm[:, cG + 24:cG + 32].rearrange("p (h a) -> p h a", h=4)
                        for h in range(4):
                            s.op("dve", lambda h=h: nc.vector.bn_stats(out=st6[:, h, :], in_=bankf(6)[:, h * 128:(h + 1) * 128]), [r_bk[6]], [r_sg])
                        for h in range(4):
                            s.op("dve", lambda h=h: nc.vector.bn_aggr(out=mv[:, h, :], in_=st6[:, h, :]), [r_sg], [r_sg])
                        A4 = sm[:, cG + 32:cG + 36]
                        RS4 = sm[:, cG + 36:cG + 40]
                        SC4 = sm[:, cG + 40:cG + 44]
                        NB4 = sm[:, cG + 44:cG + 48]
                        tt("dve", A4, mv[:, :, 1], cst[:, C_RT + 12:C_RT + 16], ALU.mult, [r_sg, r_cst], [r_sg])
                        ts("dve", A4, A4, LN_EPS, None, ALU.add, None, [r_sg], [r_sg])
                        tt("pool", RS4, A4, NEGH4, ALU.pow, [r_sg, r_smc], [r_sg])
                        tt("dve", SC4, RS4, cst[:, C_RT:C_RT + 4], ALU.mult, [r_sg, r_cst], [r_sg])
                        stt("dve", NB4, mv[:, :, 0], -1.0, SC4, ALU.mult, ALU.mult, [r_sg], [r_sg])
                        for h in range(4):
                            sl = slice(h * 128, (h + 1) * 128)
                            act(yn[:, sl], bankf(6)[:, sl], AF.Identity, [r_bk[6], r_sg], [r_yn], bias=NB4[:, h:h + 1], scale=SC4[:, h:h + 1])
                        tt("dve", yn, yn, gbA[:, 0:512], ALU.mult, [r_yn, r_gbA], [r_yn])
                        tt("pool", y3[p], yn, gsl[p], ALU.mult, [r_yn, r_gsl[p]], [r_y3[p]])
                        pb = bankb(0)
                        for h in range(4):
                            tr(pb[:, h * 128:(h + 1) * 128], y3[p][:, h * 128:(h + 1) * 128], [r_y3[p]], [r_bk[0]])
                        copy("act", mixT[:, :, tb * 128:(tb + 1) * 128], pb[:, 0:512].rearrange("p (h t) -> p h t", h=4), [r_bk[0]], [r_mixT[tb]])

                    pipeline([blkA(tb) for tb in range(NB)], 3)
                    chk('A')
                    s.barrier(list(R.values()))
                    cv = Carver(PERSIST2)
                    wO = cv.take(KC * 1024).rearrange("p (k c) -> p k c", k=KC)
                    gbm = cv.take(4096, F32)
                    r_wO, r_gbm = res("wO"), res("gbm")
                    r_wOg = [res("wOg%d" % i) for i in range(2)]
                    for i in range(2):
                        s.dma("pool", wO[:, :, i * 512:(i + 1) * 512], wout_d[l][:, :, i * 512:(i + 1) * 512], writes=[r_wOg[i]])
                    s.dma("sp", gbm[:, :], gmix_d[l], writes=[r_gbm])
                    for tb in range(NB):
                        b0 = 2 * (tb % 2)
                        for hh in range(2):
                            pairs = [(mixT[:, k, tb * 128:(tb + 1) * 128], wO[:, k, hh * 512:(hh + 1) * 512]) for k in range(4)]
                            pairs += [(attTall[:, k, tb * 128:(tb + 1) * 128], wO[:, 4 + k, hh * 512:(hh + 1) * 512]) for k in range(4)]
                            mm(bankf(b0 + hh), pairs, [r_mixT[tb], r_attTall[tb], r_wOg[hh]], [r_bk[b0 + hh]])
                        layer_norm_block(tb, [bankf(b0), bankf(b0 + 1)], [r_bk[b0], r_bk[b0 + 1]], gbm, r_gbm)
                    chk('C2')
                    s.barrier(list(R.values()))
                    cv = Carver()
                    NG = S // GT
                    nblk = GT // 128
                    ncol = GT // 512
                    x1T = [cv.take(KC * GT).rearrange("p (k t) -> p k t", k=KC) for _ in range(2)]
                    hT = cv.take(HC * GT).rearrange("p (c t) -> p c t", c=HC)
                    wD = cv.take(HC * 1024).rearrange("p (c n) -> p c n", c=HC)
                    wG = [cv.take(KC * 512).rearrange("p (k c) -> p k c", k=KC) for _ in range(3)]
                    xbf2 = [cv.take(1024) for _ in range(2)]
                    sgb = [cv.take(1024, F32) for _ in range(2)]
                    gbf = cv.take(4096, F32)
                    r_x1T = [[res("x1T%d_%d" % (q, i)) for i in range(nblk)] for q in range(2)]
                    r_hT = [res("hT%d_%d" % (c, t)) for c in range(HC) for t in range(ncol)]
                    r_wD = [res("wD%d" % i) for i in range(4)]
                    r_wG = [res("wG%d" % i) for i in range(3)]
                    r_xbf2 = [res("xbf2_0"), res("xbf2_1")]
                    r_sgl = [res("sg0"), res("sg1")]
                    r_gbf = res("gbf")
                    wdq = [(0, 6), (6, 12), (12, 18), (18, 22)]
                    for i, (c0, c1) in enumerate(wdq):
                        s.dma("pool", wD[:, c0:c1, :], wd_d[l][:, c0:c1, :], writes=[r_wD[i]])
                    s.dma("sp", gbf[:, :], gffn_d[l], writes=[r_gbf])
                    NP = HC // 2
                    gu = [0]

                    def load_gu(pr):
                        slot = gu[0] % 3
                        gu[0] += 1
                        s.dma("sp", wG[slot].rearrange("p k c -> p (k c)"), wgu_bf[l, pr], reads=[r_conv[l]], writes=[r_wG[slot]])
                        return slot

                    def prep_x1T(g):
                        q = g % 2
                        for j in range(nblk):
                            tb = g * nblk + j
                            i2 = j % 2
                            copy("act" if j % 2 == 0 else "dve", xbf2[i2], x_sb[:, tb, :], [r_x[tb]], [r_xbf2[i2]])
                            pb = bankb(i2)
                            for k in range(KC):
                                tr(pb[:, k * 128:(k + 1) * 128], xbf2[i2][:, k * 128:(k + 1) * 128], [r_xbf2[i2]], [r_bk[i2]])
                            copy("act", x1T[q][:, :, j * 128:(j + 1) * 128], pb[:, :].rearrange("p (k t) -> p k t", k=KC), [r_bk[i2]], [r_x1T[q][j]])

                    prep_x1T(0)
                    for g in range(NG):
                        q = g % 2
                        slots = [load_gu(0), load_gu(1)]
                        for pr in range(NP):
                            if pr + 2 < NP:
                                slots.append(load_gu(pr + 2))
                            slot = slots[pr]
                            for ci in range(2):
                                c = 2 * pr + ci
                                for th in range(ncol):
                                    xr = r_x1T[q][4 * th:4 * th + 4]
                                    qq = (c * ncol + th) % 2
                                    bG, bU = 2 + 2 * qq, 3 + 2 * qq
                                    mm(bankf(bG), [(wG[slot][:, k, ci * 256:ci * 256 + 128], x1T[q][:, k, th * 512:(th + 1) * 512]) for k in range(KC)], xr + [r_wG[slot]], [r_bk[bG]])
                                    mm(bankf(bU), [(wG[slot][:, k, ci * 256 + 128:ci * 256 + 256], x1T[q][:, k, th * 512:(th + 1) * 512]) for k in range(KC)], xr + [r_wG[slot]], [r_bk[bU]])
                                    act(sgb[qq], bankf(bG), AF.Silu, [r_bk[bG]], [r_sgl[qq]])
                                    tt("dve", hT[:, c, th * 512:(th + 1) * 512], sgb[qq], bankf(bU), ALU.mult, [r_sgl[qq], r_bk[bU]], [r_hT[c * ncol + th]])
                        if g + 1 < NG:
                            prep_x1T(g + 1)
                        for j in range(nblk):
                            tb = g * nblk + j
                            th = j // 4
                            hres = [r_hT[c * ncol + th] for c in range(HC)]
                            for hh in range(2):
                                mm(bankf(6 + hh), [(hT[:, c, j * 128:(j + 1) * 128], wD[:, c, hh * 512:(hh + 1) * 512]) for c in range(HC)],
                                   hres + r_wD, [r_bk[6 + hh]])
                            layer_norm_block(tb, [bankf(6), bankf(7)], [r_bk[6], r_bk[7]], gbf, r_gbf)
                    s.barrier(list(R.values()))
              except _Stop:
                pass
              if True:
                yv = y_d[sq].rearrange("(b p) d -> p b d", p=128)
                for b in range(NB):
                    s.dma("sp", yv[:, b, :], x_sb[:, b, :], reads=[r_x[b]], semres=r_stb_[b])
            for b in range(NB):
                s._wait("sp", ("dma", r_stb_[b], r_stb_[b].cnt))

        s1 = Sched(nc, es, None)
        emit(s1)
        s2 = Sched(nc, es, s1.needed)
        emit(s2)
    return nc


_CACHE = {}


def _prep_inputs(cfg, x, positions, w_in, ret_gn_gain, w_out, ln_mix_gain, ln_mix_bias,
                 w_gate_up, w_down, ln_ffn_gain, ln_ffn_bias, ncores):
    DEPTH = cfg.DEPTH
    f = lambda a: np.ascontiguousarray(np.asarray(a, dtype=np.float32))
    g_ret = np.ascontiguousarray(np.broadcast_to(f(ret_gn_gain)[:, None, :], (DEPTH, 128, 512)))
    g_mix = np.ascontiguousarray(np.broadcast_to(
        np.concatenate([f(ln_mix_gain), f(ln_mix_bias)], axis=1)[:, None, :], (DEPTH, 128, 2048)))
    g_ffn = np.ascontiguousarray(np.broadcast_to(
        np.concatenate([f(ln_ffn_gain), f(ln_ffn_bias)], axis=1)[:, None, :], (DEPTH, 128, 2048)))
    cst = make_consts(cfg)
    w_in = f(w_in)

    def tile_k(w):
        return np.ascontiguousarray(w.reshape(DEPTH, KC, 128, w.shape[-1]).transpose(0, 2, 1, 3))

    w_kv = tile_k(np.concatenate([w_in[:, :, OFF_AK:OFF_AK + 1024], w_in[:, :, OFF_IK:OFF_IK + 72]], axis=2))
    w_q = tile_k(np.concatenate([w_in[:, :, OFF_AQ:OFF_AQ + 512], w_in[:, :, OFF_IQ:OFF_IQ + 512]], axis=2))
    w_r = tile_k(w_in[:, :, 0:2048])
    w_o = tile_k(f(w_out))
    wgu = f(w_gate_up)
    g4 = wgu[:, :, 0:HID].reshape(DEPTH, KC, 128, HC // 2, 2, 128)
    u4 = wgu[:, :, HID:2 * HID].reshape(DEPTH, KC, 128, HC // 2, 2, 128)
    gu = np.stack([g4, u4], axis=5)
    gu = gu.transpose(0, 3, 2, 1, 4, 5, 6)
    w_gu = np.ascontiguousarray(gu.reshape(DEPTH, HC // 2, 128, KC * 512))
    w_d = np.ascontiguousarray(f(w_down).reshape(DEPTH, HC, 128, 1024).transpose(0, 2, 1, 3))
    x = f(x)
    pos = np.asarray(positions).astype(np.int32)
    shared = {"w_kv": w_kv, "w_q": w_q, "w_r": w_r, "w_o": w_o, "w_gu": w_gu, "w_d": w_d,
              "g_ret": g_ret, "g_mix": g_mix, "g_ffn": g_ffn, "cst": cst}
    maps = []
    for c in range(ncores):
        xs = x[c * cfg.NSEQ:(c + 1) * cfg.NSEQ]
        ps_ = pos[c * cfg.NSEQ:(c + 1) * cfg.NSEQ]
        ps_ = np.ascontiguousarray(ps_.reshape(cfg.NSEQ, cfg.NB, 128).transpose(0, 2, 1))
        m = dict(shared)
        m["x"] = np.ascontiguousarray(xs)
        m["pos"] = ps_
        maps.append(m)
    return maps


def run(cfg, ncores, **inputs):
    key = (cfg.S, cfg.NSEQ, cfg.DEPTH, cfg.NITER, cfg.stop)
    if key not in _CACHE:
        _CACHE[key] = build(cfg)
    nc = _CACHE[key]
    maps = _prep_inputs(cfg, ncores=ncores, **inputs)
    res = run_bass_kernel_spmd(nc, maps, core_ids=list(range(ncores)))
    return np.concatenate([r["y"] for r in res.results], axis=0)


def kernel(x, positions, w_in, ret_gn_gain, w_out, ln_mix_gain, ln_mix_bias,
           w_gate_up, w_down, ln_ffn_gain, ln_ffn_bias):
    cfg = Cfg(S=2048, NSEQ=2, DEPTH=2, NITER=17)
    out = run(cfg, 8, x=x, positions=positions, w_in=w_in, ret_gn_gain=ret_gn_gain, w_out=w_out,
              ln_mix_gain=ln_mix_gain, ln_mix_bias=ln_mix_bias, w_gate_up=w_gate_up, w_down=w_down,
              ln_ffn_gain=ln_ffn_gain, ln_ffn_bias=ln_ffn_bias)
    return out.astype(np.float32)
```

```python
import math
from contextlib import ExitStack
import numpy as np
import concourse.bass as bass
import concourse.mybir as mybir
from concourse.bass_utils import run_bass_kernel_spmd

F32 = mybir.dt.float32
BF16 = mybir.dt.bfloat16
I32 = mybir.dt.int32
ALU = mybir.AluOpType
AF = mybir.ActivationFunctionType
AX = mybir.AxisListType

D = 1024
KC = 8
HID = 2816
HC = 22
IN_COLS = 4168
OFF_RQ, OFF_RK, OFF_RV, OFF_RG, OFF_AQ, OFF_AK, OFF_AV, OFF_IQ, OFF_IK, OFF_IW = (
    0, 512, 1024, 1536, 2048, 2560, 3072, 3584, 4096, 4160)
LN_EPS = 1e-5
MAGIC = 12582912.0
TWO_PI = 2.0 * math.pi
C1 = 6.28125
C2 = TWO_PI - C1
PI_LO = 3.1415925
NEG_BIG = -1.0e30
MASK_NEG = -30000.0

C_ID, C_TRI, C_CB, C_RT, C_INVF, C_P2, C_END = 0, 128, 256, 384, 404, 468, 500


class Res:
    __slots__ = ("name", "w", "r", "sem", "cnt")

    def __init__(self, name):
        self.name = name
        self.w = None
        self.r = {}
        self.sem = None
        self.cnt = 0


class Sched:
    ENG = ("pe", "act", "dve", "pool", "sp")

    def __init__(self, nc, es, needed):
        self.nc = nc
        self.es = es
        self.rec = needed is None
        self.needed = set() if self.rec else needed
        self.idx = {e: 0 for e in self.ENG}
        self.sig = {e: 0 for e in self.ENG}
        self.cntof = {}
        self.waited = {}
        self.eng = dict(pe=nc.tensor, act=nc.scalar, dve=nc.vector, pool=nc.gpsimd, sp=nc.sync)
        self.sem = {}
        self.nsem = 0
        if not self.rec:
            for e in self.ENG:
                self.sem[e] = es.enter_context(nc.semaphore("sem_" + e))

    def _wait(self, e, tok):
        if tok is None:
            return
        if tok[0] == "eng":
            _, pe_, i = tok
            if pe_ == e and e in ("pe", "sp"):
                return
            if self.rec:
                self.needed.add((pe_, i))
                return
            val = self.cntof[(pe_, i)]
            key = (e, pe_)
            semh = self.sem[pe_]
        else:
            _, res, val = tok
            if self.rec:
                return
            key = (e, "dma", res.name)
            semh = res.sem
        if self.waited.get(key, 0) >= val:
            return
        self.waited[key] = val
        self.eng[e].wait_ge(semh, val)

    def _deps(self, e, reads, writes):
        for r in reads:
            self._wait(e, r.w)
        for w in writes:
            self._wait(e, w.w)
            for t in list(w.r.values()):
                self._wait(e, t)

    def op(self, e, fn, reads=(), writes=()):
        self._deps(e, reads, writes)
        i = self.idx[e]
        self.idx[e] += 1
        tok = ("eng", e, i)
        if not self.rec:
            ins = fn()
            if (e, i) in self.needed:
                ins.then_inc(self.sem[e], 1)
                self.sig[e] += 1
                self.cntof[(e, i)] = self.sig[e]
        for r in reads:
            r.r[e] = tok
        for w in writes:
            w.w = tok
            w.r = {}
        return tok

    def _getsem(self, res):
        if res.sem is None and not self.rec:
            res.sem = self.es.enter_context(self.nc.semaphore("dsem_%d" % self.nsem))
            self.nsem += 1

    def dma(self, q, out_ap, in_ap, reads=(), writes=(), semres=None):
        self._deps(q, reads, writes)
        self.idx[q] += 1
        tr = writes[0] if writes else semres
        self._getsem(tr)
        tr.cnt += 16
        tok = ("dma", tr, tr.cnt)
        if not self.rec:
            self.eng[q].dma_start(out=out_ap, in_=in_ap).then_inc(tr.sem, 16)
        for w in writes:
            w.w = tok
            w.r = {}
        for r in reads:
            r.r["dma_" + tr.name] = tok
        return tok

    def barrier(self, allres):
        toks = []
        for r in allres:
            if r.w is not None:
                toks.append(r.w)
            toks.extend(r.r.values())
        for e in ("pe", "act", "dve", "pool", "sp"):
            for t in toks:
                self._wait(e, t)


class _Stop(Exception):
    pass


class Cfg:
    stop = None

    def __init__(self, S=2048, NSEQ=2, DEPTH=2, NITER=20):
        self.S = S
        self.NSEQ = NSEQ
        self.DEPTH = DEPTH
        self.NB = S // 128
        self.TOPK = min(256, S // 4)
        self.NITER = NITER
        self.GT = min(512, S)


def make_consts(cfg):
    c = np.zeros((128, C_END), np.float32)
    p = np.arange(128)
    c[:, C_ID:C_ID + 128] = np.eye(128, dtype=np.float32)
    c[:, C_TRI:C_TRI + 128] = (p[None, :] >= p[:, None]).astype(np.float32)
    c[:, C_CB:C_CB + 128] = np.where(p[None, :] <= p[:, None], 0.0, NEG_BIG)
    for h in range(4):
        lg = math.log(1.0 - 2.0 ** (-5.0 - h))
        dq = np.exp(lg * (p + 1.0))
        c[:, C_RT + h] = dq
        c[:, C_RT + 4 + h] = np.exp(-lg * (p + 1.0)) * 128 ** -0.5
        c[:, C_RT + 8 + h] = np.exp(lg * (127.0 - p)) * 128 ** -0.5
        c[:, C_RT + 12 + h] = dq * dq
        c[:, C_RT + 16 + h] = (1.0 - 2.0 ** (-5.0 - h)) ** 128
    invf = (10000.0 ** (-np.arange(0, 128, 2, dtype=np.float32) / 128)).astype(np.float32)
    c[:, C_INVF:C_INVF + 64] = invf[None, :]
    c[:, C_P2:C_P2 + 32] = (2.0 ** (-np.arange(32, dtype=np.float64)))[None, :]
    return c


def pipeline(gens, depth):
    gens = list(gens)
    active = []
    nxt = 0
    while nxt < len(gens) or active:
        if nxt < len(gens) and len(active) < depth:
            active.append(gens[nxt])
            nxt += 1
        for g in list(active):
            try:
                next(g)
            except StopIteration:
                active.remove(g)


def build(cfg):
    S, NSEQ, DEPTH, NB, TOPK, NITER, GT = cfg.S, cfg.NSEQ, cfg.DEPTH, cfg.NB, cfg.TOPK, cfg.NITER, cfg.GT
    nc = bass.Bass("TRN2", target_bir_lowering=False)
    dt = nc.dram_tensor
    x_d = dt("x", [NSEQ, S, D], F32, kind="ExternalInput").ap()
    pos_d = dt("pos", [NSEQ, 128, NB], I32, kind="ExternalInput").ap()
    wkv_d = dt("w_kv", [DEPTH, 128, KC, 1096], F32, kind="ExternalInput").ap()
    wq_d = dt("w_q", [DEPTH, 128, KC, 1024], F32, kind="ExternalInput").ap()
    wr_d = dt("w_r", [DEPTH, 128, KC, 2048], F32, kind="ExternalInput").ap()
    wout_d = dt("w_o", [DEPTH, 128, KC, 1024], F32, kind="ExternalInput").ap()
    wgu_d = dt("w_gu", [DEPTH, HC // 2, 128, KC * 512], F32, kind="ExternalInput").ap()
    wd_d = dt("w_d", [DEPTH, 128, HC, 1024], F32, kind="ExternalInput").ap()
    gret_d = dt("g_ret", [DEPTH, 128, 512], F32, kind="ExternalInput").ap()
    gmix_d = dt("g_mix", [DEPTH, 128, 2048], F32, kind="ExternalInput").ap()
    gffn_d = dt("g_ffn", [DEPTH, 128, 2048], F32, kind="ExternalInput").ap()
    cst_d = dt("cst", [128, C_END], F32, kind="ExternalInput").ap()
    y_d = dt("y", [NSEQ, S, D], F32, kind="ExternalOutput").ap()
    wgu_bf = dt("wgu_bf", [DEPTH, HC // 2, 128, KC * 512], BF16, kind="Internal").ap()

    ARENA = 63 * 1024
    with ExitStack() as es:
        sb = lambda name, shape, dtp: es.enter_context(nc.sbuf_tensor(name, shape, dtp))
        x_sb = sb("x_sb", [128, NB, D], F32)
        cs_cos = sb("cs_cos", [128, NB, 64], F32)
        cs_sin = sb("cs_sin", [128, NB, 64], F32)
        cst = sb("cst_sb", [128, C_END], F32)
        ident = sb("ident", [128, 128], BF16)
        posi = sb("posi", [128, NB], I32)
        posf = sb("posf", [128, NB], F32)
        wsb = sb("wsb", [128, NB, 8], F32)
        sm = sb("sm", [128, 256], F32)
        steps = sb("steps", [128, 2, 32], F32)
        nstp = sb("nstp", [128, 32], F32)
        biasc_t = sb("biasc", [128, 32], F32)
        sb_late = lambda name, shape, dtp: biasc_t
        ang = sb("ang", [128, 4, 64], F32)
        arena = sb("arena", [128, ARENA], BF16)
        ps = es.enter_context(nc.psum_tensor("ps", [128, 4096], F32))

        def bankf(i, n=1):
            return ps[:, i * 512:(i + n) * 512]

        def bankb(i):
            return ps[:, i * 512:(i + 1) * 512].bitcast(BF16)

        class Carver:
            def __init__(self, off=0):
                self.off = off

            def take(self, n_bf16, dtype=BF16):
                n = (n_bf16 + 15) // 16 * 16
                ap = arena[:, self.off:self.off + n_bf16]
                self.off += n
                assert self.off <= ARENA, "arena overflow %d" % self.off
                if dtype == F32:
                    ap = ap.bitcast(F32)
                return ap

        def emit(s):
            R = {}

            def res(name):
                if name not in R:
                    R[name] = Res(name)
                return R[name]

            E = s.eng
            r_cst, r_ident, r_cs, r_pos, r_wsb = res("cst"), res("ident"), res("cs"), res("pos"), res("wsb")
            r_x = [res("x%d" % b) for b in range(NB)]
            r_bk = [res("bank%d" % i) for i in range(8)]
            r_stb_ = [res("store%d" % b) for b in range(NB)]
            r_smc = res("smc")

            def tt(e, out, in0, in1, op, reads, writes):
                return s.op(e, lambda: E[e].tensor_tensor(out=out, in0=in0, in1=in1, op=op), reads, writes)

            def ts(e, out, in0, s1, s2, op0, op1, reads, writes, accum=None):
                if op1 is None:
                    return s.op(e, lambda: E[e].tensor_scalar(out=out, in0=in0, scalar1=s1, scalar2=None, op0=op0), reads, writes)
                if accum is not None:
                    return s.op(e, lambda: E[e].tensor_scalar(out=out, in0=in0, scalar1=s1, scalar2=s2, op0=op0, op1=op1, accum_out=accum), reads, writes)
                return s.op(e, lambda: E[e].tensor_scalar(out=out, in0=in0, scalar1=s1, scalar2=s2, op0=op0, op1=op1), reads, writes)

            def stt(e, out, in0, sc, in1, op0, op1, reads, writes):
                return s.op(e, lambda: E[e].scalar_tensor_tensor(out=out, in0=in0, scalar=sc, in1=in1, op0=op0, op1=op1), reads, writes)

            def act(out, in_, func, reads, writes, bias=None, scale=None):
                kw = {}
                if bias is not None:
                    kw["bias"] = bias
                if scale is not None:
                    kw["scale"] = scale
                return s.op("act", lambda: nc.scalar.activation(out=out, in_=in_, func=func, **kw), reads, writes)

            def copy(e, out, in_, reads, writes):
                if e == "act":
                    return s.op("act", lambda: nc.scalar.copy(out=out, in_=in_), reads, writes)
                return s.op(e, lambda: E[e].tensor_copy(out=out, in_=in_), reads, writes)

            def mm(out, pairs, reads, writes):
                def f():
                    ins = None
                    n = len(pairs)
                    for i, (l, r) in enumerate(pairs):
                        ins = nc.tensor.matmul(out, lhsT=l, rhs=r, start=(i == 0), stop=(i == n - 1))
                    return ins
                return s.op("pe", f, reads, writes)

            def tr(out, in_, reads, writes):
                return s.op("pe", lambda: nc.tensor.transpose(out=out, in_=in_, identity=ident[:]), list(reads) + [r_ident], writes)

            s.dma("sp", cst[:], cst_d, writes=[r_cst])
            copy("dve", ident[:], cst[:, C_ID:C_ID + 128], [r_cst], [r_ident])
            cb = cst[:, C_CB:C_CB + 128]
            tri = cst[:, C_TRI:C_TRI + 128]
            invf = cst[:, C_INVF:C_INVF + 64]
            p2 = cst[:, C_P2:C_P2 + 32]
            s.op("dve", lambda: nc.vector.memset(sm[:, 0:8], -0.5), [], [r_smc])
            s.op("dve", lambda: nc.vector.memset(sm[:, 8:9], -1.0e29), [], [r_smc])
            biasc = sb_late("biasc", [128, 32], F32)
            BIASC = {}
            for tbq in range(NB):
                Lq = (tbq + 1) * 128
                s.op("dve", lambda tbq=tbq, Lq=Lq: nc.vector.memset(biasc[:, tbq:tbq + 1], float(-(2 * TOPK - Lq - 1))), [], [r_smc])
                BIASC[Lq] = biasc[:, tbq:tbq + 1]
            NEGH4 = sm[:, 0:4]
            NEGH = sm[:, 0:1]
            TAUALL = sm[:, 8:9]
            r_conv = [res("conv%d" % l) for l in range(DEPTH)]
            for l in range(DEPTH):
                for pr in range(HC // 2):
                    s.dma("pool", wgu_bf[l, pr], wgu_d[l, pr], semres=r_conv[l])
                r_conv[l].w = ("dma", r_conv[l], r_conv[l].cnt)

            xrot = [0]

            def make_xT(tb, xbf, xT, r_xbf, r_xT, bk):
                e = "act" if xrot[0] % 2 == 0 else "dve"
                xrot[0] += 1
                copy(e, xbf, x_sb[:, tb, :], [r_x[tb]], [r_xbf])
                pb = bankb(bk)
                for k in range(KC):
                    tr(pb[:, k * 128:(k + 1) * 128], xbf[:, k * 128:(k + 1) * 128], [r_xbf], [r_bk[bk]])
                copy("act", xT, pb[:, :], [r_bk[bk]], [r_xT])

            def rope(src, r_src, G, half, cosap, sinap, out, r_out, tA, tB, r_tA, r_tB):
                n = G * 2 * half
                s3 = src.rearrange("p (g d) -> p g d", d=half)
                cbb = cosap.unsqueeze(1).to_broadcast([128, 2 * G, half])
                sbb = sinap.unsqueeze(1).to_broadcast([128, 2 * G, half])
                tA3 = tA[:, 0:n].rearrange("p (g d) -> p g d", d=half)
                tB3 = tB[:, 0:n].rearrange("p (g d) -> p g d", d=half)
                tt("dve", tA3, s3, cbb, ALU.mult, list(r_src) + [r_cs], [r_tA])
                tt("dve", tB3, s3, sbb, ALU.mult, list(r_src) + [r_cs], [r_tB])
                tA4 = tA[:, 0:n].rearrange("p (g t d) -> p g t d", t=2, d=half)
                tB4 = tB[:, 0:n].rearrange("p (g t d) -> p g t d", t=2, d=half)
                o4 = out.rearrange("p (g t d) -> p g t d", t=2, d=half)
                tt("pool", o4[:, :, 0, :], tA4[:, :, 0, :], tB4[:, :, 1, :], ALU.subtract, [r_tA, r_tB], [r_out])
                tt("pool", o4[:, :, 1, :], tA4[:, :, 1, :], tB4[:, :, 0, :], ALU.add, [r_tA, r_tB], [r_out])

            lncnt = [0]

            def layer_norm_block(tb, ypair, r_y, gbt, r_gbt):
                par = lncnt[0] % 2
                lncnt[0] += 1
                c0 = 16 + par * 24
                r_s = res("smln%d" % par)
                xb = x_sb[:, tb, :]
                for hh in range(2):
                    stt("dve", xb[:, hh * 512:(hh + 1) * 512], xb[:, hh * 512:(hh + 1) * 512], ALPHA,
                        ypair[hh], ALU.mult, ALU.add, [r_y[hh]], [r_x[tb]])
                st6 = sm[:, c0:c0 + 12].rearrange("p (a b) -> p a b", a=2)
                for hh in range(2):
                    s.op("dve", lambda hh=hh: nc.vector.bn_stats(out=st6[:, hh, :], in_=xb[:, hh * 512:(hh + 1) * 512]), [r_x[tb]], [r_s])
                s.op("dve", lambda: nc.vector.bn_aggr(out=sm[:, c0 + 12:c0 + 14], in_=st6), [r_s], [r_s])
                ts("dve", sm[:, c0 + 14:c0 + 15], sm[:, c0 + 13:c0 + 14], LN_EPS, None, ALU.add, None, [r_s], [r_s])
                tt("pool", sm[:, c0 + 15:c0 + 16], sm[:, c0 + 14:c0 + 15], NEGH, ALU.pow, [r_s, r_smc], [r_s])
                stt("dve", sm[:, c0 + 16:c0 + 17], sm[:, c0 + 12:c0 + 13], -1.0, sm[:, c0 + 15:c0 + 16], ALU.mult, ALU.mult, [r_s], [r_s])
                act(xb, xb, AF.Identity, [r_x[tb], r_s], [r_x[tb]], bias=sm[:, c0 + 16:c0 + 17], scale=sm[:, c0 + 15:c0 + 16])
                tt("dve", xb, xb, gbt[:, 0:1024], ALU.mult, [r_x[tb], r_gbt], [r_x[tb]])
                tt("pool", xb, xb, gbt[:, 1024:2048], ALU.add, [r_x[tb], r_gbt], [r_x[tb]])

            ALPHA = (2.0 * DEPTH) ** 0.25
            ATT_SCALE = 128 ** -0.5
            W_SCALE = (8 ** -0.5) * (64 ** -0.5)

            def chk(name):
                if cfg.stop == name:
                    raise _Stop()

            for sq in range(NSEQ):
              try:
                xv = x_d[sq].rearrange("(b p) d -> p b d", p=128)
                for b in range(NB):
                    s.dma("sp", x_sb[:, b, :], xv[:, b, :], writes=[r_x[b]])
                s.dma("sp", posi[:], pos_d[sq], writes=[r_pos])
                copy("dve", posf[:], posi[:], [r_pos], [r_pos])
                r_ang = res("ang")
                for b in range(NB):
                    A_, K_, Rs, Rc = ang[:, 0, :], ang[:, 1, :], ang[:, 2, :], ang[:, 3, :]
                    ts("dve", A_, invf, posf[:, b:b + 1], None, ALU.mult, None, [r_cst, r_pos], [r_ang])
                    for which, dst in ((0, Rs), (1, Rc)):
                        src = A_
                        if which == 1:
                            ts("dve", Rc, A_, math.pi / 2, None, ALU.add, None, [r_ang], [r_ang])
                            src = Rc
                        ts("dve", K_, src, 1.0 / TWO_PI, MAGIC, ALU.mult, ALU.add, [r_ang], [r_ang])
                        ts("dve", K_, K_, MAGIC, None, ALU.subtract, None, [r_ang], [r_ang])
                        stt("dve", dst, K_, -C1, src, ALU.mult, ALU.add, [r_ang], [r_ang])
                        stt("dve", dst, K_, -C2, dst, ALU.mult, ALU.add, [r_ang], [r_ang])
                        ts("dve", dst, dst, PI_LO, -PI_LO, ALU.min, ALU.max, [r_ang], [r_ang])
                    act(cs_sin[:, b, :], Rs, AF.Sin, [r_ang], [r_cs])
                    act(cs_cos[:, b, :], Rc, AF.Sin, [r_ang], [r_cs])
                chk('setup')
                for l in range(DEPTH):
                    cv = Carver()
                    attTall = cv.take(4 * S).rearrange("p (k t) -> p k t", k=4)
                    PERSIST = cv.off
                    KT = cv.take(4 * S).rearrange("p (h t) -> p h t", h=4)
                    Vg = cv.take(NB * 4 * 130).rearrange("p (b h e) -> p b h e", b=NB, h=4)
                    ikT = cv.take(S)
                    wA = cv.take(KC * 1096)
                    xbf_ = cv.take(1024)
                    xT_ = cv.take(1024)
                    xbf = [xbf_, xbf_]
                    xT = [xT_, xT_]
                    tA = cv.take(1024, F32)
                    tB = cv.take(1024, F32)
                    rbuf = [cv.take(1024, F32) for _ in range(2)]
                    qiq_ = cv.take(1024)
                    qiq = [qiq_, qiq_]
                    ebuf2 = cv.take(S)
                    attb_ = cv.take(512)
                    attb = [attb_, attb_]
                    qT = [cv.take(512).rearrange("p (h t) -> p h t", h=4) for _ in range(3)]
                    iqT = [cv.take(512).rearrange("p (h t) -> p h t", h=4) for _ in range(2)]
                    Ibuf2 = [cv.take(2 * S, F32) for _ in range(2)]
                    jk = cv.take(16)
                    mb = [cv.take(S) for _ in range(2)]
                    ebuf = cv.take(S)
                    PTs_ = cv.take(S)
                    PTs = [PTs_, PTs_]
                    r_attTall = [res("attTall%d" % b) for b in range(NB)]
                    r_KT = [res("KT%d" % b) for b in range(NB)]
                    r_V = [res("V%d" % b) for b in range(NB)]
                    r_ikT = [res("ikT%d" % b) for b in range(NB)]
                    r_xbf = [res("xbf0"), res("xbf0")]
                    r_xT = [res("xT0"), res("xT0")]
                    r_tA, r_tB = res("tA"), res("tB")
                    r_wA = res("wA")
                    r_I2, r_e = [res("I0"), res("I1")], res("e")
                    r_jk = res("jk")
                    r_mb = [res("mb0"), res("mb1")]
                    r_PTs = [res("PTs0"), res("PTs0")]
                    r_rb = [res("rb0"), res("rb1")]
                    r_qbf = [res("qbf0"), res("qbf0")]
                    r_iqb = [res("iqb0"), res("iqb0")]
                    r_e2, r_jk2, r_sb2 = res("e2"), res("jk2"), res("sb2")
                    r_attb = [res("attb0"), res("attb0")]
                    r_kb = r_qbf
                    r_ikb = r_iqb
                    r_qT = [res("qT0"), res("qT1"), res("qT2")]
                    r_iqT = [res("iqT0"), res("iqT1")]

                    wKV = wA[:, 0:KC * 1096].rearrange("p (k c) -> p k c", k=KC)
                    r_wKVg = [res("wKVg%d" % i) for i in range(3)]
                    r_wQg = [res("wQg%d" % i) for i in range(2)]
                    for i, (c0, c1) in enumerate(((0, 512), (512, 1024), (1024, 1096))):
                        s.dma("pool", wKV[:, :, c0:c1], wkv_d[l][:, :, c0:c1], writes=[r_wKVg[i]] + r_wQg)
                    for tb in range(NB):
                        s.op("pool", lambda tb=tb: nc.gpsimd.memset(Vg[:, tb, :, 128:130], 1.0), [], [r_V[tb]])

                    def blkB(tb):
                        p = tb % 2
                        b0 = 4 * p
                        make_xT(tb, xbf[p], xT[p], r_xbf[p], r_xT[p], b0)
                        xt, r_xt = xT[p], r_xT[p]
                        mm(bankf(b0 + 1), [(xt[:, k * 128:(k + 1) * 128], wKV[:, k, 0:512]) for k in range(KC)], [r_xt, r_wKVg[0]], [r_bk[b0 + 1]])
                        mm(bankf(b0 + 2), [(xt[:, k * 128:(k + 1) * 128], wKV[:, k, 512:1024]) for k in range(KC)], [r_xt, r_wKVg[1]], [r_bk[b0 + 2]])
                        mm(bankf(b0 + 3)[:, 0:72], [(xt[:, k * 128:(k + 1) * 128], wKV[:, k, 1024:1096]) for k in range(KC)], [r_xt, r_wKVg[2]], [r_bk[b0 + 3]])
                        yield
                        kbf = qiq[p][:, 0:512]
                        ikb = qiq[p][:, 512:640]
                        rope(bankf(b0 + 1), [r_bk[b0 + 1]], 4, 64, cs_cos[:, tb, :], cs_sin[:, tb, :], kbf, r_kb[p], tA, tB, r_tA, r_tB)
                        copy("act", Vg[:, tb, :, 0:128], bankf(b0 + 2).rearrange("p (h e) -> p h e", h=4), [r_bk[b0 + 2]], [r_V[tb]])
                        rope(bankf(b0 + 3)[:, 0:64], [r_bk[b0 + 3]], 1, 32, cs_cos[:, tb, 0:64:2], cs_sin[:, tb, 0:64:2], ikb[:, 0:64], r_ikb[p], tA, tB, r_tA, r_tB)
                        copy("pool", ikb[:, 64:128], ikb[:, 0:64], [r_ikb[p]], [r_ikb[p]])
                        ts("dve", wsb[:, tb, :], bankf(b0 + 3)[:, 64:72], W_SCALE, None, ALU.mult, None, [r_bk[b0 + 3]], [r_wsb])
                        yield
                        pb = bankb(b0)
                        for h in range(4):
                            tr(pb[:, h * 128:(h + 1) * 128], kbf[:, h * 128:(h + 1) * 128], [r_kb[p]], [r_bk[b0]])
                        tr(pb[:, 512:640], ikb, [r_ikb[p]], [r_bk[b0]])
                        copy("act", KT[:, :, tb * 128:(tb + 1) * 128], pb[:, 0:512].rearrange("p (h t) -> p h t", h=4), [r_bk[b0]], [r_KT[tb]])
                        copy("act", ikT[:, tb * 128:(tb + 1) * 128], pb[:, 512:640], [r_bk[b0]], [r_ikT[tb]])

                    pipeline([blkB(tb) for tb in range(NB)], 2)
                    chk('B')
                    wQ = wA[:, 0:KC * 1024].rearrange("p (k c) -> p k c", k=KC)
                    for i in range(2):
                        s.dma("pool", wQ[:, :, i * 512:(i + 1) * 512], wq_d[l][:, :, i * 512:(i + 1) * 512], writes=[r_wQg[i]] + r_wKVg)
                    mb_free = [True, True]

                    def frontC(tb):
                        p = tb % 2
                        q3 = tb % 3
                        Ib = Ibuf2[p]
                        r_I = r_I2[p]
                        L = (tb + 1) * 128
                        r_sb = res("smbis%d" % p)
                        r_sa = res("smatt%d" % p)
                        cB = 64 + p * 16
                        cA = 96 + p * 16
                        stp = steps[:, p, :]
                        make_xT(tb, xbf[p], xT[p], r_xbf[p], r_xT[p], 0)
                        xt, r_xt = xT[p], r_xT[p]
                        mm(bankf(1), [(xt[:, k * 128:(k + 1) * 128], wQ[:, k, 0:512]) for k in range(KC)], [r_xt, r_wQg[0]], [r_bk[1]])
                        mm(bankf(2), [(xt[:, k * 128:(k + 1) * 128], wQ[:, k, 512:1024]) for k in range(KC)], [r_xt, r_wQg[1]], [r_bk[2]])
                        qbf = qiq[p][:, 0:512]
                        iqb = qiq[p][:, 512:1024]
                        rope(bankf(1), [r_bk[1]], 4, 64, cs_cos[:, tb, :], cs_sin[:, tb, :], qbf, r_qbf[p], tA, tB, r_tA, r_tB)
                        rope(bankf(2), [r_bk[2]], 8, 32, cs_cos[:, tb, 0:64:2], cs_sin[:, tb, 0:64:2], iqb, r_iqb[p], tA, tB, r_tA, r_tB)
                        pb = bankb(0)
                        for h in range(4):
                            tr(pb[:, h * 128:(h + 1) * 128], qbf[:, h * 128:(h + 1) * 128], [r_qbf[p]], [r_bk[0]])
                        for h in range(4):
                            tr(pb[:, 512 + h * 128:512 + (h + 1) * 128], iqb[:, h * 128:(h + 1) * 128], [r_iqb[p]], [r_bk[0]])
                        copy("act", qT[q3], pb[:, 0:512].rearrange("p (h t) -> p h t", h=4), [r_bk[0]], [r_qT[q3]])
                        copy("act", iqT[p], pb[:, 512:1024].rearrange("p (h t) -> p h t", h=4), [r_bk[0]], [r_iqT[p]])
                        yield
                        nch = (L + 511) // 512
                        cnt = 0
                        for c in range(nch):
                            n = min(512, L - 512 * c)
                            kres = r_ikT[4 * c:4 * c + (n + 127) // 128]
                            for h in range(8):
                                m_, base = h // 2, 64 * (h % 2)
                                bk = 1 + cnt % 2
                                rb, r_rb_ = rbuf[cnt % 2], r_rb[cnt % 2]
                                cnt += 1
                                mm(bankf(bk)[:, 0:n], [(iqT[p][base:base + 64, m_, :], ikT[base:base + 64, c * 512:c * 512 + n])], [r_iqT[p]] + kres, [r_bk[bk]])
                                act(rb[:, 0:n], bankf(bk)[:, 0:n], AF.Relu, [r_bk[bk]], [r_rb_])
                                if h == 0:
                                    ts("dve", Ib[:, c * 512:c * 512 + n], rb[:, 0:n], wsb[:, tb, 0:1], None, ALU.mult, None, [r_rb_, r_wsb], [r_I])
                                else:
                                    stt("dve", Ib[:, c * 512:c * 512 + n], rb[:, 0:n], wsb[:, tb, h:h + 1], Ib[:, c * 512:c * 512 + n], ALU.mult, ALU.add, [r_rb_, r_wsb], [r_I])
                            yield
                        tt("dve", Ib[:, tb * 128:L], Ib[:, tb * 128:L], cb, ALU.add, [r_cst], [r_I])
                        if L > TOPK:
                            MX, MN, RG, T, CNT, DD = [sm[:, cB + i:cB + i + 1] for i in range(6)]
                            s.op("dve", lambda: nc.vector.tensor_reduce(out=MX, in_=Ib[:, 0:L], axis=AX.X, op=ALU.max), [r_I], [r_sb])
                            s.op("dve", lambda: nc.vector.tensor_reduce(out=MN, in_=Ib[:, 0:L - 128], axis=AX.X, op=ALU.min), [r_I], [r_sb])
                            tt("dve", RG, MX, MN, ALU.subtract, [r_sb], [r_sb])
                            ts("dve", RG, RG, 1.0 + 1.0 / 1024, None, ALU.mult, None, [r_sb], [r_sb])
                            tt("dve", T, MX, RG, ALU.subtract, [r_sb], [r_sb])
                            ts("dve", stp[:, 0:NITER + 2], p2[:, 0:NITER + 2], RG, None, ALU.mult, None, [r_sb, r_cst], [r_sb])
                            stt("dve", T, stp[:, 1:2], 1.0, T, ALU.mult, ALU.add, [r_sb], [r_sb])
                            if p == 1:
                                ts("dve", nstp[:, 0:NITER + 2], stp[:, 0:NITER + 2], -0.5, None, ALU.mult, None, [r_sb], [r_sb])
                            junk = jk[:, 0:1].to_broadcast([128, L])
                            if p == 0:
                                for i in range(NITER):
                                    ts("dve", junk, Ib[:, 0:L], T, None, ALU.is_gt, ALU.add, [r_I, r_sb], [r_jk], accum=CNT)
                                    s.op("dve", lambda: nc.vector.tensor_scalar(out=DD, in0=CNT, scalar1=TOPK - 0.5, scalar2=-0.5, op0=ALU.is_gt, op1=ALU.add), [r_jk], [r_sb])
                                    stt("dve", T, DD, stp[:, i + 1:i + 2], T, ALU.mult, ALU.add, [r_sb], [r_sb])
                                    if i % 4 == 3:
                                        yield
                            else:
                                NT = sm[:, cB + 6:cB + 7]
                                DS = sm[:, cB + 7:cB + 8]
                                junk2 = jk[:, 1:2].to_broadcast([128, L])
                                ts("dve", NT, T, -1.0, None, ALU.mult, None, [r_sb], [r_sb])
                                for i in range(NITER):
                                    s.op("act", lambda: nc.scalar.activation(out=junk2, in_=Ib[:, 0:L], func=AF.Sign, bias=NT, scale=1.0, accum_out=CNT), [r_I, r_sb], [r_jk2])
                                    s.op("act", lambda: nc.scalar.activation(out=DS, in_=CNT, func=AF.Sign, bias=BIASC[L], scale=1.0), [r_jk2, r_smc], [r_sb2])
                                    s.op("act", lambda i=i: nc.scalar.activation(out=NT, in_=DS, func=AF.Identity, bias=NT, scale=nstp[:, i + 1:i + 2]), [r_sb2, r_sb], [r_sb])
                                    if i % 4 == 3:
                                        yield
                                ts("dve", T, NT, -1.0, None, ALU.mult, None, [r_sb], [r_sb])
                            tt("dve", T, T, stp[:, NITER + 1:NITER + 2], ALU.subtract, [r_sb], [r_sb])
                            TAU = T
                            r_tau = r_sb
                        else:
                            TAU = TAUALL
                            r_tau = r_smc
                        while not mb_free[p]:
                            yield
                        mb_free[p] = False
                        ts("dve", mb[p][:, 0:L], Ib[:, 0:L], TAU, MASK_NEG, ALU.is_le, ALU.mult, [r_I, r_tau], [r_mb[p]])

                    def backC(tb):
                        p = tb % 2
                        q3 = tb % 3
                        L = (tb + 1) * 128
                        nch = (L + 511) // 512
                        cA = 96 + p * 16
                        ebs = [ebuf, ebuf2]
                        r_ebs = [r_e, r_e2]

                        def hbank(h):
                            return 4 + 2 * (h % 2) if nch <= 2 else 4

                        def qk(h):
                            hb = hbank(h)
                            for c in range(nch):
                                n = min(512, L - 512 * c)
                                kres = r_KT[4 * c:4 * c + (n + 127) // 128]
                                mm(bankf(hb + c)[:, 0:n], [(qT[q3][:, h, :], KT[:, h, c * 512:c * 512 + n]), (ident[:], mb[p][:, c * 512:c * 512 + n])],
                                   [r_qT[q3], r_mb[p], r_ident] + kres, [r_bk[hb + c]])

                        def smax(h):
                            hb = hbank(h)
                            sc_ = ps[:, hb * 512:hb * 512 + L]
                            rsc = r_bk[hb:hb + nch]
                            MXa, NBa = [sm[:, cA + 3 * (h % 2) + i:cA + 3 * (h % 2) + i + 1] for i in range(2)]
                            r_sah = res("smatt%d_%d" % (p, h % 2))
                            s.op("dve", lambda: nc.vector.tensor_reduce(out=MXa, in_=sc_, axis=AX.X, op=ALU.max), rsc, [r_sah])
                            ts("dve", NBa, MXa, -ATT_SCALE, None, ALU.mult, None, [r_sah], [r_sah])

                        def sexp(h):
                            hb = hbank(h)
                            sc_ = ps[:, hb * 512:hb * 512 + L]
                            rsc = r_bk[hb:hb + nch]
                            NBa = sm[:, cA + 3 * (h % 2) + 1:cA + 3 * (h % 2) + 2]
                            r_sah = res("smatt%d_%d" % (p, h % 2))
                            act(ebs[h % 2][:, 0:L], sc_, AF.Exp, rsc + [r_sah], [r_ebs[h % 2]], bias=NBa, scale=ATT_SCALE)

                        def tpose(h):
                            pbb = bankb(3)
                            pts = PTs[0]
                            eb, r_eb = ebs[h % 2], r_ebs[h % 2]
                            for half in range((tb + 8) // 8):
                                nblk = min(8, tb + 1 - 8 * half)
                                for j in range(nblk):
                                    sbk = half * 8 + j
                                    tr(pbb[:, j * 128:(j + 1) * 128], eb[:, sbk * 128:(sbk + 1) * 128], [r_eb], [r_bk[3]])
                                copy("act", pts[:, half * 1024:half * 1024 + nblk * 128], pbb[:, 0:nblk * 128], [r_bk[3]], [r_PTs[0]])

                        def pv(h):
                            pts = PTs[0]
                            RDa = sm[:, cA + 6 + (h % 2):cA + 7 + (h % 2)]
                            r_sr = res("smrd%d_%d" % (p, h % 2))
                            po = bankf(0)[:, 0:129]
                            mm(po, [(pts[:, sbk * 128:(sbk + 1) * 128], Vg[:, sbk, h, 0:129]) for sbk in range(tb + 1)],
                               [r_PTs[0]] + r_V[0:tb + 1], [r_bk[0]])
                            s.op("dve", lambda: nc.vector.reciprocal(out=RDa, in_=bankf(0)[:, 128:129]), [r_bk[0]], [r_sr])
                            ts("dve", attb[p][:, h * 128:(h + 1) * 128], bankf(0)[:, 0:128], RDa, None, ALU.mult, None, [r_bk[0], r_sr], [r_attb[p]])

                        qk(0)
                        smax(0)
                        sexp(0)
                        yield
                        for h in range(4):
                            if h + 1 < 4:
                                qk(h + 1)
                            tpose(h)
                            if h + 1 < 4:
                                smax(h + 1)
                                sexp(h + 1)
                            pv(h)
                            yield
                        pbb = bankb(3)
                        for h in range(4):
                            tr(pbb[:, h * 128:(h + 1) * 128], attb[p][:, h * 128:(h + 1) * 128], [r_attb[p]], [r_bk[3]])
                        copy("act", attTall[:, :, tb * 128:(tb + 1) * 128], pbb[:, 0:512].rearrange("p (h t) -> p h t", h=4), [r_bk[3]], [r_attTall[tb]])

                    fronts = []
                    ready = []
                    back = None
                    done_back = set()
                    next_tb = 0
                    while next_tb < NB or fronts or ready or back is not None:
                        while (next_tb < NB and len(fronts) < 2
                               and all(t % 2 != next_tb % 2 for t, _ in fronts)
                               and (next_tb < 3 or (next_tb - 3) in done_back)):
                            fronts.append((next_tb, frontC(next_tb)))
                            next_tb += 1
                        if back is None and ready:
                            tbb = ready.pop(0)
                            back = (tbb, backC(tbb))
                        if back is not None:
                            try:
                                next(back[1])
                            except StopIteration:
                                mb_free[back[0] % 2] = True
                                done_back.add(back[0])
                                back = None
                        for item in list(fronts):
                            try:
                                next(item[1])
                            except StopIteration:
                                fronts.remove(item)
                                ready.append(item[0])
                    chk('C')
                    s.barrier(list(R.values()))
                    cv = Carver(PERSIST)
                    mixT = cv.take(4 * S).rearrange("p (k t) -> p k t", k=4)
                    PERSIST2 = cv.off
                    wR = cv.take(KC * 2048).rearrange("p (k c) -> p k c", k=KC)
                    xbf = [cv.take(1024) for _ in range(3)]
                    xT = [cv.take(1024) for _ in range(3)]
                    tA = cv.take(2048, F32)
                    tB = cv.take(2048, F32)
                    qk_r = cv.take(2048, F32)
                    qkb = [cv.take(1024) for _ in range(3)]
                    kdec = [cv.take(512) for _ in range(3)]
                    vbf = [cv.take(512) for _ in range(3)]
                    gsl = [cv.take(1024, F32) for _ in range(3)]
                    qkT = [cv.take(1024) for _ in range(3)]
                    stb = [cv.take(512) for _ in range(3)]
                    yn = cv.take(1024, F32)
                    y3 = [cv.take(512) for _ in range(3)]
                    state_f = cv.take(1024, F32)
                    state_bf = cv.take(512)
                    gbA = cv.take(1024, F32)
                    r_mixT = [res("mixT%d" % b) for b in range(NB)]
                    r_wR, r_gbA = res("wR"), res("gbA")
                    r_qk, r_yn = res("qk_r"), res("yn")
                    r_qkb = [res("qkb%d" % i) for i in range(3)]
                    r_kdec = [res("kdec%d" % i) for i in range(3)]
                    r_vbf = [res("vbf%d" % i) for i in range(3)]
                    r_gsl = [res("gsl%d" % i) for i in range(3)]
                    r_qkT = [res("qkT%d" % i) for i in range(3)]
                    r_stb = [res("stb%d" % i) for i in range(3)]
                    r_y3 = [res("y3%d" % i) for i in range(3)]
                    r_stf, r_stbf = res("state_f"), res("state_bf")
                    r_xbfA = [res("xbfA%d" % i) for i in range(3)]
                    r_xTA = [res("xTA%d" % i) for i in range(3)]
                    r_tA2, r_tB2 = res("tA2"), res("tB2")
                    s.dma("sp", gbA[:, 0:512], gret_d[l], writes=[r_gbA])
                    r_wRg = [res("wRg%d" % i) for i in range(4)]
                    for i in range(4):
                        s.dma("pool", wR[:, :, i * 512:(i + 1) * 512], wr_d[l][:, :, i * 512:(i + 1) * 512], writes=[r_wRg[i]])
                    s.op("dve", lambda: nc.vector.memset(state_f, 0.0), [], [r_stf])
                    s.op("pool", lambda: nc.gpsimd.memset(state_bf, 0.0), [], [r_stbf])
                    dct = cst[:, C_RT + 16:C_RT + 20].unsqueeze(2).to_broadcast([128, 4, 128])
                    kps = cst[:, C_RT + 4:C_RT + 8].unsqueeze(2).to_broadcast([128, 4, 128])
                    dks = cst[:, C_RT + 8:C_RT + 12].unsqueeze(2).to_broadcast([128, 4, 128])

                    def blkA(tb):
                        p = tb % 3
                        r_sg = res("smgn%d" % (tb % 2))
                        cG = 128 + (tb % 2) * 48
                        make_xT(tb, xbf[p], xT[p], r_xbfA[p], r_xTA[p], 0)
                        xt, r_xt = xT[p], r_xTA[p]
                        for j in range(4):
                            mm(bankf(1 + j), [(xt[:, k * 128:(k + 1) * 128], wR[:, k, j * 512:(j + 1) * 512]) for k in range(KC)], [r_xt, r_wRg[j]], [r_bk[1 + j]])
                        yield
                        rope(bankf(1, 2), [r_bk[1], r_bk[2]], 8, 64, cs_cos[:, tb, :], cs_sin[:, tb, :], qk_r, r_qk, tA, tB, r_tA2, r_tB2)
                        copy("act", vbf[p], bankf(3), [r_bk[3]], [r_vbf[p]])
                        act(gsl[p], bankf(4), AF.Silu, [r_bk[4]], [r_gsl[p]])
                        copy("pool", qkb[p][:, 0:512], qk_r[:, 0:512], [r_qk], [r_qkb[p]])
                        k3 = qk_r[:, 512:1024].rearrange("p (h d) -> p h d", h=4)
                        tt("dve", qkb[p][:, 512:1024].rearrange("p (h d) -> p h d", h=4), k3, kps, ALU.mult, [r_qk, r_cst], [r_qkb[p]])
                        tt("pool", kdec[p].rearrange("p (h d) -> p h d", h=4), k3, dks, ALU.mult, [r_qk, r_cst], [r_kdec[p]])
                        pb = bankb(0)
                        for j in range(8):
                            tr(pb[:, j * 128:(j + 1) * 128], qkb[p][:, j * 128:(j + 1) * 128], [r_qkb[p]], [r_bk[0]])
                        copy("act", qkT[p], pb[:, :], [r_bk[0]], [r_qkT[p]])
                        yield
                        for h in range(4):
                            mm(bankf(5)[:, h * 128:(h + 1) * 128], [(qkT[p][:, 512 + h * 128:512 + (h + 1) * 128], qkT[p][:, h * 128:(h + 1) * 128])], [r_qkT[p]], [r_bk[5]])
                        tt("dve", stb[p].rearrange("p (h d) -> p h d", h=4), bankf(5).rearrange("p (h d) -> p h d", h=4),
                           tri.unsqueeze(1).to_broadcast([128, 4, 128]), ALU.mult, [r_bk[5], r_cst], [r_stb[p]])
                        for h in range(4):
                            sl = slice(h * 128, (h + 1) * 128)
                            mm(bankf(6)[:, sl], [(stb[p][:, sl], vbf[p][:, sl]), (qkT[p][:, sl], state_bf[:, sl])], [r_stb[p], r_vbf[p], r_qkT[p], r_stbf], [r_bk[6]])
                        for h in range(4):
                            sl = slice(h * 128, (h + 1) * 128)
                            mm(bankf(7)[:, sl], [(kdec[p][:, sl], vbf[p][:, sl])], [r_kdec[p], r_vbf[p]], [r_bk[7]])
                        tt("pool", state_f.rearrange("p (h d) -> p h d", h=4), state_f.rearrange("p (h d) -> p h d", h=4), dct, ALU.mult, [r_cst], [r_stf])
                        tt("dve", state_f, state_f, bankf(7), ALU.add, [r_bk[7]], [r_stf])
                        copy("pool", state_bf, state_f, [r_stf], [r_stbf])
                        yield
                        st6 = sm[:, cG:cG + 24].rearrange("p (h a) -> p h a", h=4)
                        mv = sm[:, cG + 24:cG + 32].rearrange("p (h a) -> p h a", h=4)
                        for h in range(4):
                            s.op("dve", lambda h=h: nc.vector.bn_stats(out=st6[:, h, :], in_=bankf(6)[:, h * 128:(h + 1) * 128]), [r_bk[6]], [r_sg])
                        for h in range(4):
                            s.op("dve", lambda h=h: nc.vector.bn_aggr(out=mv[:, h, :], in_=st6[:, h, :]), [r_sg], [r_sg])
                        A4 = sm[:, cG + 32:cG + 36]
                        RS4 = sm[:, cG + 36:cG + 40]
                        SC4 = sm[:, cG + 40:cG + 44]
                        NB4 = sm[:, cG + 44:cG + 48]
                        tt("dve", A4, mv[:, :, 1], cst[:, C_RT + 12:C_RT + 16], ALU.mult, [r_sg, r_cst], [r_sg])
                        ts("dve", A4, A4, LN_EPS, None, ALU.add, None, [r_sg], [r_sg])
                        tt("pool", RS4, A4, NEGH4, ALU.pow, [r_sg, r_smc], [r_sg])
                        tt("dve", SC4, RS4, cst[:, C_RT:C_RT + 4], ALU.mult, [r_sg, r_cst], [r_sg])
                        stt("dve", NB4, mv[:, :, 0], -1.0, SC4, ALU.mult, ALU.mult, [r_sg], [r_sg])
                        for h in range(4):
                            sl = slice(h * 128, (h + 1) * 128)
                            act(yn[:, sl], bankf(6)[:, sl], AF.Identity, [r_bk[6], r_sg], [r_yn], bias=NB4[:, h:h + 1], scale=SC4[:, h:h + 1])
                        tt("dve", yn, yn, gbA[:, 0:512], ALU.mult, [r_yn, r_gbA], [r_yn])
                        tt("pool", y3[p], yn, gsl[p], ALU.mult, [r_yn, r_gsl[p]], [r_y3[p]])
                        pb = bankb(0)
                        for h in range(4):
                            tr(pb[:, h * 128:(h + 1) * 128], y3[p][:, h * 128:(h + 1) * 128], [r_y3[p]], [r_bk[0]])
                        copy("act", mixT[:, :, tb * 128:(tb + 1) * 128], pb[:, 0:512].rearrange("p (h t) -> p h t", h=4), [r_bk[0]], [r_mixT[tb]])

                    pipeline([blkA(tb) for tb in range(NB)], 3)
                    chk('A')
                    s.barrier(list(R.values()))
                    cv = Carver(PERSIST2)
                    wO = cv.take(KC * 1024).rearrange("p (k c) -> p k c", k=KC)
                    gbm = cv.take(4096, F32)
                    r_wO, r_gbm = res("wO"), res("gbm")
                    r_wOg = [res("wOg%d" % i) for i in range(2)]
                    for i in range(2):
                        s.dma("pool", wO[:, :, i * 512:(i + 1) * 512], wout_d[l][:, :, i * 512:(i + 1) * 512], writes=[r_wOg[i]])
                    s.dma("sp", gbm[:, :], gmix_d[l], writes=[r_gbm])
                    for tb in range(NB):
                        b0 = 2 * (tb % 2)
                        for hh in range(2):
                            pairs = [(mixT[:, k, tb * 128:(tb + 1) * 128], wO[:, k, hh * 512:(hh + 1) * 512]) for k in range(4)]
                            pairs += [(attTall[:, k, tb * 128:(tb + 1) * 128], wO[:, 4 + k, hh * 512:(hh + 1) * 512]) for k in range(4)]
                            mm(bankf(b0 + hh), pairs, [r_mixT[tb], r_attTall[tb], r_wOg[hh]], [r_bk[b0 + hh]])
                        layer_norm_block(tb, [bankf(b0), bankf(b0 + 1)], [r_bk[b0], r_bk[b0 + 1]], gbm, r_gbm)
                    chk('C2')
                    s.barrier(list(R.values()))
                    cv = Carver()
                    NG = S // GT
                    nblk = GT // 128
                    ncol = GT // 512
                    x1T = [cv.take(KC * GT).rearrange("p (k t) -> p k t", k=KC) for _ in range(2)]
                    hT = cv.take(HC * GT).rearrange("p (c t) -> p c t", c=HC)
                    wD = cv.take(HC * 1024).rearrange("p (c n) -> p c n", c=HC)
                    wG = [cv.take(KC * 512).rearrange("p (k c) -> p k c", k=KC) for _ in range(3)]
                    xbf2 = [cv.take(1024) for _ in range(2)]
                    sgb = [cv.take(1024, F32) for _ in range(2)]
                    gbf = cv.take(4096, F32)
                    r_x1T = [[res("x1T%d_%d" % (q, i)) for i in range(nblk)] for q in range(2)]
                    r_hT = [res("hT%d_%d" % (c, t)) for c in range(HC) for t in range(ncol)]
                    r_wD = [res("wD%d" % i) for i in range(4)]
                    r_wG = [res("wG%d" % i) for i in range(3)]
                    r_xbf2 = [res("xbf2_0"), res("xbf2_1")]
                    r_sgl = [res("sg0"), res("sg1")]
                    r_gbf = res("gbf")
                    wdq = [(0, 6), (6, 12), (12, 18), (18, 22)]
                    for i, (c0, c1) in enumerate(wdq):
                        s.dma("pool", wD[:, c0:c1, :], wd_d[l][:, c0:c1, :], writes=[r_wD[i]])
                    s.dma("sp", gbf[:, :], gffn_d[l], writes=[r_gbf])
                    NP = HC // 2
                    gu = [0]

                    def load_gu(pr):
                        slot = gu[0] % 3
                        gu[0] += 1
                        s.dma("sp", wG[slot].rearrange("p k c -> p (k c)"), wgu_bf[l, pr], reads=[r_conv[l]], writes=[r_wG[slot]])
                        return slot

                    def prep_x1T(g):
                        q = g % 2
                        for j in range(nblk):
                            tb = g * nblk + j
                            i2 = j % 2
                            copy("act" if j % 2 == 0 else "dve", xbf2[i2], x_sb[:, tb, :], [r_x[tb]], [r_xbf2[i2]])
                            pb = bankb(i2)
                            for k in range(KC):
                                tr(pb[:, k * 128:(k + 1) * 128], xbf2[i2][:, k * 128:(k + 1) * 128], [r_xbf2[i2]], [r_bk[i2]])
                            copy("act", x1T[q][:, :, j * 128:(j + 1) * 128], pb[:, :].rearrange("p (k t) -> p k t", k=KC), [r_bk[i2]], [r_x1T[q][j]])

                    prep_x1T(0)
                    for g in range(NG):
                        q = g % 2
                        slots = [load_gu(0), load_gu(1)]
                        for pr in range(NP):
                            if pr + 2 < NP:
                                slots.append(load_gu(pr + 2))
                            slot = slots[pr]
                            for ci in range(2):
                                c = 2 * pr + ci
                                for th in range(ncol):
                                    xr = r_x1T[q][4 * th:4 * th + 4]
                                    qq = (c * ncol + th) % 2
                                    bG, bU = 2 + 2 * qq, 3 + 2 * qq
                                    mm(bankf(bG), [(wG[slot][:, k, ci * 256:ci * 256 + 128], x1T[q][:, k, th * 512:(th + 1) * 512]) for k in range(KC)], xr + [r_wG[slot]], [r_bk[bG]])
                                    mm(bankf(bU), [(wG[slot][:, k, ci * 256 + 128:ci * 256 + 256], x1T[q][:, k, th * 512:(th + 1) * 512]) for k in range(KC)], xr + [r_wG[slot]], [r_bk[bU]])
                                    act(sgb[qq], bankf(bG), AF.Silu, [r_bk[bG]], [r_sgl[qq]])
                                    tt("dve", hT[:, c, th * 512:(th + 1) * 512], sgb[qq], bankf(bU), ALU.mult, [r_sgl[qq], r_bk[bU]], [r_hT[c * ncol + th]])
                        if g + 1 < NG:
                            prep_x1T(g + 1)
                        for j in range(nblk):
                            tb = g * nblk + j
                            th = j // 4
                            hres = [r_hT[c * ncol + th] for c in range(HC)]
                            for hh in range(2):
                                mm(bankf(6 + hh), [(hT[:, c, j * 128:(j + 1) * 128], wD[:, c, hh * 512:(hh + 1) * 512]) for c in range(HC)],
                                   hres + r_wD, [r_bk[6 + hh]])
                            layer_norm_block(tb, [bankf(6), bankf(7)], [r_bk[6], r_bk[7]], gbf, r_gbf)
                    s.barrier(list(R.values()))
              except _Stop:
                pass
              if True:
                yv = y_d[sq].rearrange("(b p) d -> p b d", p=128)
                for b in range(NB):
                    s.dma("sp", yv[:, b, :], x_sb[:, b, :], reads=[r_x[b]], semres=r_stb_[b])
            for b in range(NB):
                s._wait("sp", ("dma", r_stb_[b], r_stb_[b].cnt))

        s1 = Sched(nc, es, None)
        emit(s1)
        s2 = Sched(nc, es, s1.needed)
        emit(s2)
    return nc


_CACHE = {}


def _prep_inputs(cfg, x, positions, w_in, ret_gn_gain, w_out, ln_mix_gain, ln_mix_bias,
                 w_gate_up, w_down, ln_ffn_gain, ln_ffn_bias, ncores):
    DEPTH = cfg.DEPTH
    f = lambda a: np.ascontiguousarray(np.asarray(a, dtype=np.float32))
    g_ret = np.ascontiguousarray(np.broadcast_to(f(ret_gn_gain)[:, None, :], (DEPTH, 128, 512)))
    g_mix = np.ascontiguousarray(np.broadcast_to(
        np.concatenate([f(ln_mix_gain), f(ln_mix_bias)], axis=1)[:, None, :], (DEPTH, 128, 2048)))
    g_ffn = np.ascontiguousarray(np.broadcast_to(
        np.concatenate([f(ln_ffn_gain), f(ln_ffn_bias)], axis=1)[:, None, :], (DEPTH, 128, 2048)))
    cst = make_consts(cfg)
    w_in = f(w_in)

    def tile_k(w):
        return np.ascontiguousarray(w.reshape(DEPTH, KC, 128, w.shape[-1]).transpose(0, 2, 1, 3))

    w_kv = tile_k(np.concatenate([w_in[:, :, OFF_AK:OFF_AK + 1024], w_in[:, :, OFF_IK:OFF_IK + 72]], axis=2))
    w_q = tile_k(np.concatenate([w_in[:, :, OFF_AQ:OFF_AQ + 512], w_in[:, :, OFF_IQ:OFF_IQ + 512]], axis=2))
    w_r = tile_k(w_in[:, :, 0:2048])
    w_o = tile_k(f(w_out))
    wgu = f(w_gate_up)
    g4 = wgu[:, :, 0:HID].reshape(DEPTH, KC, 128, HC // 2, 2, 128)
    u4 = wgu[:, :, HID:2 * HID].reshape(DEPTH, KC, 128, HC // 2, 2, 128)
    gu = np.stack([g4, u4], axis=5)
    gu = gu.transpose(0, 3, 2, 1, 4, 5, 6)
    w_gu = np.ascontiguousarray(gu.reshape(DEPTH, HC // 2, 128, KC * 512))
    w_d = np.ascontiguousarray(f(w_down).reshape(DEPTH, HC, 128, 1024).transpose(0, 2, 1, 3))
    x = f(x)
    pos = np.asarray(positions).astype(np.int32)
    shared = {"w_kv": w_kv, "w_q": w_q, "w_r": w_r, "w_o": w_o, "w_gu": w_gu, "w_d": w_d,
              "g_ret": g_ret, "g_mix": g_mix, "g_ffn": g_ffn, "cst": cst}
    maps = []
    for c in range(ncores):
        xs = x[c * cfg.NSEQ:(c + 1) * cfg.NSEQ]
        ps_ = pos[c * cfg.NSEQ:(c + 1) * cfg.NSEQ]
        ps_ = np.ascontiguousarray(ps_.reshape(cfg.NSEQ, cfg.NB, 128).transpose(0, 2, 1))
        m = dict(shared)
        m["x"] = np.ascontiguousarray(xs)
        m["pos"] = ps_
        maps.append(m)
    return maps


def run(cfg, ncores, **inputs):
    key = (cfg.S, cfg.NSEQ, cfg.DEPTH, cfg.NITER, cfg.stop)
    if key not in _CACHE:
        _CACHE[key] = build(cfg)
    nc = _CACHE[key]
    maps = _prep_inputs(cfg, ncores=ncores, **inputs)
    res = run_bass_kernel_spmd(nc, maps, core_ids=list(range(ncores)))
    return np.concatenate([r["y"] for r in res.results], axis=0)


def kernel(x, positions, w_in, ret_gn_gain, w_out, ln_mix_gain, ln_mix_bias,
           w_gate_up, w_down, ln_ffn_gain, ln_ffn_bias):
    cfg = Cfg(S=2048, NSEQ=2, DEPTH=2, NITER=17)
    out = run(cfg, 8, x=x, positions=positions, w_in=w_in, ret_gn_gain=ret_gn_gain, w_out=w_out,
              ln_mix_gain=ln_mix_gain, ln_mix_bias=ln_mix_bias, w_gate_up=w_gate_up, w_down=w_down,
              ln_ffn_gain=ln_ffn_gain, ln_ffn_bias=ln_ffn_bias)
    return out.astype(np.float32)
```

```python
import math
from contextlib import ExitStack
import numpy as np
import concourse.bass as bass
import concourse.mybir as mybir
from concourse.bass_utils import run_bass_kernel_spmd

F32 = mybir.dt.float32
BF16 = mybir.dt.bfloat16
I32 = mybir.dt.int32
ALU = mybir.AluOpType
AF = mybir.ActivationFunctionType
AX = mybir.AxisListType

D = 1024
KC = 8
HID = 2816
HC = 22
IN_COLS = 4168
OFF_RQ, OFF_RK, OFF_RV, OFF_RG, OFF_AQ, OFF_AK, OFF_AV, OFF_IQ, OFF_IK, OFF_IW = (
    0, 512, 1024, 1536, 2048, 2560, 3072, 3584, 4096, 4160)
LN_EPS = 1e-5
MAGIC = 12582912.0
TWO_PI = 2.0 * math.pi
C1 = 6.28125
C2 = TWO_PI - C1
PI_LO = 3.1415925
NEG_BIG = -1.0e30
MASK_NEG = -30000.0

C_ID, C_TRI, C_CB, C_RT, C_INVF, C_P2, C_END = 0, 128, 256, 384, 404, 468, 500


class Res:
    __slots__ = ("name", "w", "r", "sem", "cnt")

    def __init__(self, name):
        self.name = name
        self.w = None
        self.r = {}
        self.sem = None
        self.cnt = 0


class Sched:
    ENG = ("pe", "act", "dve", "pool", "sp")

    def __init__(self, nc, es, needed):
        self.nc = nc
        self.es = es
        self.rec = needed is None
        self.needed = set() if self.rec else needed
        self.idx = {e: 0 for e in self.ENG}
        self.sig = {e: 0 for e in self.ENG}
        self.cntof = {}
        self.waited = {}
        self.eng = dict(pe=nc.tensor, act=nc.scalar, dve=nc.vector, pool=nc.gpsimd, sp=nc.sync)
        self.sem = {}
        self.nsem = 0
        if not self.rec:
            for e in self.ENG:
                self.sem[e] = es.enter_context(nc.semaphore("sem_" + e))

    def _wait(self, e, tok):
        if tok is None:
            return
        if tok[0] == "eng":
            _, pe_, i = tok
            if pe_ == e and e in ("pe", "sp"):
                return
            if self.rec:
                self.needed.add((pe_, i))
                return
            val = self.cntof[(pe_, i)]
            key = (e, pe_)
            semh = self.sem[pe_]
        else:
            _, res, val = tok
            if self.rec:
                return
            key = (e, "dma", res.name)
            semh = res.sem
        if self.waited.get(key, 0) >= val:
            return
        self.waited[key] = val
        self.eng[e].wait_ge(semh, val)

    def _deps(self, e, reads, writes):
        for r in reads:
            self._wait(e, r.w)
        for w in writes:
            self._wait(e, w.w)
            for t in list(w.r.values()):
                self._wait(e, t)

    def op(self, e, fn, reads=(), writes=()):
        self._deps(e, reads, writes)
        i = self.idx[e]
        self.idx[e] += 1
        tok = ("eng", e, i)
        if not self.rec:
            ins = fn()
            if (e, i) in self.needed:
                ins.then_inc(self.sem[e], 1)
                self.sig[e] += 1
                self.cntof[(e, i)] = self.sig[e]
        for r in reads:
            r.r[e] = tok
        for w in writes:
            w.w = tok
            w.r = {}
        return tok

    def _getsem(self, res):
        if res.sem is None and not self.rec:
            res.sem = self.es.enter_context(self.nc.semaphore("dsem_%d" % self.nsem))
            self.nsem += 1

    def dma(self, q, out_ap, in_ap, reads=(), writes=(), semres=None):
        self._deps(q, reads, writes)
        self.idx[q] += 1
        tr = writes[0] if writes else semres
        self._getsem(tr)
        tr.cnt += 16
        tok = ("dma", tr, tr.cnt)
        if not self.rec:
            self.eng[q].dma_start(out=out_ap, in_=in_ap).then_inc(tr.sem, 16)
        for w in writes:
            w.w = tok
            w.r = {}
        for r in reads:
            r.r["dma_" + tr.name] = tok
        return tok

    def barrier(self, allres):
        toks = []
        for r in allres:
            if r.w is not None:
                toks.append(r.w)
            toks.extend(r.r.values())
        for e in ("pe", "act", "dve", "pool", "sp"):
            for t in toks:
                self._wait(e, t)


class _Stop(Exception):
    pass


class Cfg:
    stop = None

    def __init__(self, S=2048, NSEQ=2, DEPTH=2, NITER=20):
        self.S = S
        self.NSEQ = NSEQ
        self.DEPTH = DEPTH
        self.NB = S // 128
        self.TOPK = min(256, S // 4)
        self.NITER = NITER
        self.GT = min(512, S)


def make_consts(cfg):
    c = np.zeros((128, C_END), np.float32)
    p = np.arange(128)
    c[:, C_ID:C_ID + 128] = np.eye(128, dtype=np.float32)
    c[:, C_TRI:C_TRI + 128] = (p[None, :] >= p[:, None]).astype(np.float32)
    c[:, C_CB:C_CB + 128] = np.where(p[None, :] <= p[:, None], 0.0, NEG_BIG)
    for h in range(4):
        lg = math.log(1.0 - 2.0 ** (-5.0 - h))
        dq = np.exp(lg * (p + 1.0))
        c[:, C_RT + h] = dq
        c[:, C_RT + 4 + h] = np.exp(-lg * (p + 1.0)) * 128 ** -0.5
        c[:, C_RT + 8 + h] = np.exp(lg * (127.0 - p)) * 128 ** -0.5
        c[:, C_RT + 12 + h] = dq * dq
        c[:, C_RT + 16 + h] = (1.0 - 2.0 ** (-5.0 - h)) ** 128
    invf = (10000.0 ** (-np.arange(0, 128, 2, dtype=np.float32) / 128)).astype(np.float32)
    c[:, C_INVF:C_INVF + 64] = invf[None, :]
    c[:, C_P2:C_P2 + 32] = (2.0 ** (-np.arange(32, dtype=np.float64)))[None, :]
    return c


def pipeline(gens, depth):
    gens = list(gens)
    active = []
    nxt = 0
    while nxt < len(gens) or active:
        if nxt < len(gens) and len(active) < depth:
            active.append(gens[nxt])
            nxt += 1
        for g in list(active):
            try:
                next(g)
            except StopIteration:
                active.remove(g)


def build(cfg):
    S, NSEQ, DEPTH, NB, TOPK, NITER, GT = cfg.S, cfg.NSEQ, cfg.DEPTH, cfg.NB, cfg.TOPK, cfg.NITER, cfg.GT
    nc = bass.Bass("TRN2", target_bir_lowering=False)
    dt = nc.dram_tensor
    x_d = dt("x", [NSEQ, S, D], F32, kind="ExternalInput").ap()
    pos_d = dt("pos", [NSEQ, 128, NB], I32, kind="ExternalInput").ap()
    wkv_d = dt("w_kv", [DEPTH, 128, KC, 1096], F32, kind="ExternalInput").ap()
    wq_d = dt("w_q", [DEPTH, 128, KC, 1024], F32, kind="ExternalInput").ap()
    wr_d = dt("w_r", [DEPTH, 128, KC, 2048], F32, kind="ExternalInput").ap()
    wout_d = dt("w_o", [DEPTH, 128, KC, 1024], F32, kind="ExternalInput").ap()
    wgu_d = dt("w_gu", [DEPTH, HC // 2, 128, KC * 512], F32, kind="ExternalInput").ap()
    wd_d = dt("w_d", [DEPTH, 128, HC, 1024], F32, kind="ExternalInput").ap()
    gret_d = dt("g_ret", [DEPTH, 128, 512], F32, kind="ExternalInput").ap()
    gmix_d = dt("g_mix", [DEPTH, 128, 2048], F32, kind="ExternalInput").ap()
    gffn_d = dt("g_ffn", [DEPTH, 128, 2048], F32, kind="ExternalInput").ap()
    cst_d = dt("cst", [128, C_END], F32, kind="ExternalInput").ap()
    y_d = dt("y", [NSEQ, S, D], F32, kind="ExternalOutput").ap()
    wgu_bf = dt("wgu_bf", [DEPTH, HC // 2, 128, KC * 512], BF16, kind="Internal").ap()

    ARENA = 63 * 1024
    with ExitStack() as es:
        sb = lambda name, shape, dtp: es.enter_context(nc.sbuf_tensor(name, shape, dtp))
        x_sb = sb("x_sb", [128, NB, D], F32)
        cs_cos = sb("cs_cos", [128, NB, 64], F32)
        cs_sin = sb("cs_sin", [128, NB, 64], F32)
        cst = sb("cst_sb", [128, C_END], F32)
        ident = sb("ident", [128, 128], BF16)
        posi = sb("posi", [128, NB], I32)
        posf = sb("posf", [128, NB], F32)
        wsb = sb("wsb", [128, NB, 8], F32)
        sm = sb("sm", [128, 256], F32)
        steps = sb("steps", [128, 2, 32], F32)
        nstp = sb("nstp", [128, 32], F32)
        sm2 = sb("sm2", [128, 160], F32)
        ones4 = sb("ones4", [4, 128], F32)
        biasc_t = sb("biasc", [128, 32], F32)
        sb_late = lambda name, shape, dtp: biasc_t
        ang = sb("ang", [128, 4, 64], F32)
        arena = sb("arena", [128, ARENA], BF16)
        ps = es.enter_context(nc.psum_tensor("ps", [128, 4096], F32))

        def bankf(i, n=1):
            return ps[:, i * 512:(i + n) * 512]

        def bankb(i):
            return ps[:, i * 512:(i + 1) * 512].bitcast(BF16)

        class Carver:
            def __init__(self, off=0):
                self.off = off

            def take(self, n_bf16, dtype=BF16):
                n = (n_bf16 + 15) // 16 * 16
                ap = arena[:, self.off:self.off + n_bf16]
                self.off += n
                assert self.off <= ARENA, "arena overflow %d" % self.off
                if dtype == F32:
                    ap = ap.bitcast(F32)
                return ap

        def emit(s):
            R = {}

            def res(name):
                if name not in R:
                    R[name] = Res(name)
                return R[name]

            E = s.eng
            r_cst, r_ident, r_cs, r_pos, r_wsb = res("cst"), res("ident"), res("cs"), res("pos"), res("wsb")
            r_x = [res("x%d" % b) for b in range(NB)]
            r_bk = [res("bank%d" % i) for i in range(8)]
            r_stb_ = [res("store%d" % b) for b in range(NB)]
            r_smc = res("smc")

            def tt(e, out, in0, in1, op, reads, writes):
                return s.op(e, lambda: E[e].tensor_tensor(out=out, in0=in0, in1=in1, op=op), reads, writes)

            def ts(e, out, in0, s1, s2, op0, op1, reads, writes, accum=None):
                if op1 is None:
                    return s.op(e, lambda: E[e].tensor_scalar(out=out, in0=in0, scalar1=s1, scalar2=None, op0=op0), reads, writes)
                if accum is not None:
                    return s.op(e, lambda: E[e].tensor_scalar(out=out, in0=in0, scalar1=s1, scalar2=s2, op0=op0, op1=op1, accum_out=accum), reads, writes)
                return s.op(e, lambda: E[e].tensor_scalar(out=out, in0=in0, scalar1=s1, scalar2=s2, op0=op0, op1=op1), reads, writes)

            def stt(e, out, in0, sc, in1, op0, op1, reads, writes):
                return s.op(e, lambda: E[e].scalar_tensor_tensor(out=out, in0=in0, scalar=sc, in1=in1, op0=op0, op1=op1), reads, writes)

            def act(out, in_, func, reads, writes, bias=None, scale=None):
                kw = {}
                if bias is not None:
                    kw["bias"] = bias
                if scale is not None:
                    kw["scale"] = scale
                return s.op("act", lambda: nc.scalar.activation(out=out, in_=in_, func=func, **kw), reads, writes)

            def copy(e, out, in_, reads, writes):
                if e == "act":
                    return s.op("act", lambda: nc.scalar.copy(out=out, in_=in_), reads, writes)
                return s.op(e, lambda: E[e].tensor_copy(out=out, in_=in_), reads, writes)

            def mm(out, pairs, reads, writes):
                def f():
                    ins = None
                    n = len(pairs)
                    for i, (l, r) in enumerate(pairs):
                        ins = nc.tensor.matmul(out, lhsT=l, rhs=r, start=(i == 0), stop=(i == n - 1))
                    return ins
                return s.op("pe", f, reads, writes)

            def tr(out, in_, reads, writes):
                return s.op("pe", lambda: nc.tensor.transpose(out=out, in_=in_, identity=ident[:]), list(reads) + [r_ident], writes)

            s.dma("sp", cst[:], cst_d, writes=[r_cst])
            copy("dve", ident[:], cst[:, C_ID:C_ID + 128], [r_cst], [r_ident])
            cb = cst[:, C_CB:C_CB + 128]
            tri = cst[:, C_TRI:C_TRI + 128]
            invf = cst[:, C_INVF:C_INVF + 64]
            p2 = cst[:, C_P2:C_P2 + 32]
            s.op("dve", lambda: nc.vector.memset(sm[:, 0:8], -0.5), [], [r_smc])
            s.op("dve", lambda: nc.vector.memset(sm[:, 8:9], -1.0e29), [], [r_smc])
            biasc = sb_late("biasc", [128, 32], F32)
            BIASC = {}
            for tbq in range(NB):
                Lq = (tbq + 1) * 128
                s.op("dve", lambda tbq=tbq, Lq=Lq: nc.vector.memset(biasc[:, tbq:tbq + 1], float(-(2 * TOPK - Lq - 1))), [], [r_smc])
                BIASC[Lq] = biasc[:, tbq:tbq + 1]
            r_ones4 = res("ones4")
            s.op("dve", lambda: nc.vector.memset(ones4[:], 1.0), [], [r_ones4])
            s.op("dve", lambda: nc.vector.memset(sm2[:, 0:4], 0.5), [], [r_smc])
            POSH4 = sm2[:, 0:4]
            K2 = sm2[:, 16:16 + 64]
            K2M = sm2[:, 84:88]
            identf = cst[:, C_ID:C_ID + 128]

            def allmax4(src, r_src, dst, r_dst, bk, tmpc):
                r_t = res("amx%d" % tmpc)
                mx4 = sm2[0:4, 100 + tmpc * 8:101 + tmpc * 8]
                dg4 = sm2[0:4, 101 + tmpc * 8:105 + tmpc * 8]
                pT = bankf(bk)[0:4, 0:128]
                s.op("pe", lambda: nc.tensor.transpose(out=pT, in_=src, identity=identf), [r_src, r_cst], [r_bk[bk]])
                s.op("dve", lambda: nc.vector.tensor_reduce(out=mx4, in_=pT, axis=AX.X, op=ALU.max), [r_bk[bk]], [r_t])
                ts("dve", dg4, cst[0:4, C_ID:C_ID + 4], mx4, None, ALU.mult, None, [r_t, r_cst], [r_t])
                pB = bankf(bk)[:, 128:132]
                s.op("pe", lambda: nc.tensor.matmul(pB, lhsT=ones4[0:4, :], rhs=dg4, start=True, stop=True), [r_t, r_ones4], [r_bk[bk]])
                copy("dve", dst, pB, [r_bk[bk]], [r_dst])

            NEGH4 = sm[:, 0:4]
            NEGH = sm[:, 0:1]
            TAUALL = sm[:, 8:9]
            r_conv = [res("conv%d" % l) for l in range(DEPTH)]
            for l in range(DEPTH):
                for pr in range(HC // 2):
                    s.dma("pool", wgu_bf[l, pr], wgu_d[l, pr], semres=r_conv[l])
                r_conv[l].w = ("dma", r_conv[l], r_conv[l].cnt)

            xrot = [0]

            def make_xT(tb, xbf, xT, r_xbf, r_xT, bk):
                e = "act" if xrot[0] % 2 == 0 else "dve"
                xrot[0] += 1
                copy(e, xbf, x_sb[:, tb, :], [r_x[tb]], [r_xbf])
                pb = bankb(bk)
                for k in range(KC):
                    tr(pb[:, k * 128:(k + 1) * 128], xbf[:, k * 128:(k + 1) * 128], [r_xbf], [r_bk[bk]])
                copy("act", xT, pb[:, :], [r_bk[bk]], [r_xT])

            def rope(src, r_src, G, half, cosap, sinap, out, r_out, tA, tB, r_tA, r_tB):
                n = G * 2 * half
                s3 = src.rearrange("p (g d) -> p g d", d=half)
                cbb = cosap.unsqueeze(1).to_broadcast([128, 2 * G, half])
                sbb = sinap.unsqueeze(1).to_broadcast([128, 2 * G, half])
                tA3 = tA[:, 0:n].rearrange("p (g d) -> p g d", d=half)
                tB3 = tB[:, 0:n].rearrange("p (g d) -> p g d", d=half)
                tt("dve", tA3, s3, cbb, ALU.mult, list(r_src) + [r_cs], [r_tA])
                tt("dve", tB3, s3, sbb, ALU.mult, list(r_src) + [r_cs], [r_tB])
                tA4 = tA[:, 0:n].rearrange("p (g t d) -> p g t d", t=2, d=half)
                tB4 = tB[:, 0:n].rearrange("p (g t d) -> p g t d", t=2, d=half)
                o4 = out.rearrange("p (g t d) -> p g t d", t=2, d=half)
                tt("pool", o4[:, :, 0, :], tA4[:, :, 0, :], tB4[:, :, 1, :], ALU.subtract, [r_tA, r_tB], [r_out])
                tt("pool", o4[:, :, 1, :], tA4[:, :, 1, :], tB4[:, :, 0, :], ALU.add, [r_tA, r_tB], [r_out])

            lncnt = [0]

            def layer_norm_block(tb, ypair, r_y, gbt, r_gbt):
                par = lncnt[0] % 2
                lncnt[0] += 1
                c0 = 16 + par * 24
                r_s = res("smln%d" % par)
                xb = x_sb[:, tb, :]
                for hh in range(2):
                    stt("dve", xb[:, hh * 512:(hh + 1) * 512], xb[:, hh * 512:(hh + 1) * 512], ALPHA,
                        ypair[hh], ALU.mult, ALU.add, [r_y[hh]], [r_x[tb]])
                st6 = sm[:, c0:c0 + 12].rearrange("p (a b) -> p a b", a=2)
                for hh in range(2):
                    s.op("dve", lambda hh=hh: nc.vector.bn_stats(out=st6[:, hh, :], in_=xb[:, hh * 512:(hh + 1) * 512]), [r_x[tb]], [r_s])
                s.op("dve", lambda: nc.vector.bn_aggr(out=sm[:, c0 + 12:c0 + 14], in_=st6), [r_s], [r_s])
                ts("dve", sm[:, c0 + 14:c0 + 15], sm[:, c0 + 13:c0 + 14], LN_EPS, None, ALU.add, None, [r_s], [r_s])
                tt("pool", sm[:, c0 + 15:c0 + 16], sm[:, c0 + 14:c0 + 15], NEGH, ALU.pow, [r_s, r_smc], [r_s])
                stt("dve", sm[:, c0 + 16:c0 + 17], sm[:, c0 + 12:c0 + 13], -1.0, sm[:, c0 + 15:c0 + 16], ALU.mult, ALU.mult, [r_s], [r_s])
                act(xb, xb, AF.Identity, [r_x[tb], r_s], [r_x[tb]], bias=sm[:, c0 + 16:c0 + 17], scale=sm[:, c0 + 15:c0 + 16])
                tt("dve", xb, xb, gbt[:, 0:1024], ALU.mult, [r_x[tb], r_gbt], [r_x[tb]])
                tt("pool", xb, xb, gbt[:, 1024:2048], ALU.add, [r_x[tb], r_gbt], [r_x[tb]])

            ALPHA = (2.0 * DEPTH) ** 0.25
            ATT_SCALE = 128 ** -0.5
            W_SCALE = (8 ** -0.5) * (64 ** -0.5)

            def chk(name):
                if cfg.stop == name:
                    raise _Stop()

            for sq in range(NSEQ):
              try:
                xv = x_d[sq].rearrange("(b p) d -> p b d", p=128)
                for b in range(NB):
                    s.dma("sp", x_sb[:, b, :], xv[:, b, :], writes=[r_x[b]])
                s.dma("sp", posi[:], pos_d[sq], writes=[r_pos])
                copy("dve", posf[:], posi[:], [r_pos], [r_pos])
                r_ang = res("ang")
                for b in range(NB):
                    A_, K_, Rs, Rc = ang[:, 0, :], ang[:, 1, :], ang[:, 2, :], ang[:, 3, :]
                    ts("dve", A_, invf, posf[:, b:b + 1], None, ALU.mult, None, [r_cst, r_pos], [r_ang])
                    for which, dst in ((0, Rs), (1, Rc)):
                        src = A_
                        if which == 1:
                            ts("dve", Rc, A_, math.pi / 2, None, ALU.add, None, [r_ang], [r_ang])
                            src = Rc
                        ts("dve", K_, src, 1.0 / TWO_PI, MAGIC, ALU.mult, ALU.add, [r_ang], [r_ang])
                        ts("dve", K_, K_, MAGIC, None, ALU.subtract, None, [r_ang], [r_ang])
                        stt("dve", dst, K_, -C1, src, ALU.mult, ALU.add, [r_ang], [r_ang])
                        stt("dve", dst, K_, -C2, dst, ALU.mult, ALU.add, [r_ang], [r_ang])
                        ts("dve", dst, dst, PI_LO, -PI_LO, ALU.min, ALU.max, [r_ang], [r_ang])
                    act(cs_sin[:, b, :], Rs, AF.Sin, [r_ang], [r_cs])
                    act(cs_cos[:, b, :], Rc, AF.Sin, [r_ang], [r_cs])
                chk('setup')
                for l in range(DEPTH):
                    cv = Carver()
                    attTall = cv.take(4 * S).rearrange("p (k t) -> p k t", k=4)
                    PERSIST = cv.off
                    KT = cv.take(4 * S).rearrange("p (h t) -> p h t", h=4)
                    Vg = cv.take(NB * 4 * 130).rearrange("p (b h e) -> p b h e", b=NB, h=4)
                    ikT = cv.take(S)
                    wA = cv.take(KC * 1096)
                    xbf_ = cv.take(1024)
                    xT_ = cv.take(1024)
                    xbf = [xbf_, xbf_]
                    xT = [xT_, xT_]
                    tA = cv.take(1024, F32)
                    tB = cv.take(1024, F32)
                    rbuf = [cv.take(1024, F32) for _ in range(2)]
                    qiq_ = cv.take(1024)
                    qiq = [qiq_, qiq_]
                    ebuf2 = cv.take(S)
                    attb_ = cv.take(512)
                    attb = [attb_, attb_]
                    qT = [cv.take(512).rearrange("p (h t) -> p h t", h=4) for _ in range(3)]
                    iqT = [cv.take(512).rearrange("p (h t) -> p h t", h=4) for _ in range(2)]
                    Ibuf2 = [cv.take(2 * S, F32) for _ in range(2)]
                    jk = cv.take(16)
                    mb_ = cv.take(S)
                    mb = [mb_, mb_]
                    PT2 = cv.take(S)
                    ebuf = cv.take(S)
                    PTs_ = cv.take(S)
                    PTs = [PTs_, PTs_]
                    r_attTall = [res("attTall%d" % b) for b in range(NB)]
                    r_KT = [res("KT%d" % b) for b in range(NB)]
                    r_V = [res("V%d" % b) for b in range(NB)]
                    r_ikT = [res("ikT%d" % b) for b in range(NB)]
                    r_xbf = [res("xbf0"), res("xbf0")]
                    r_xT = [res("xT0"), res("xT0")]
                    r_tA, r_tB = res("tA"), res("tB")
                    r_wA = res("wA")
                    r_I2, r_e = [res("I0"), res("I1")], res("e")
                    r_jk = res("jk")
                    r_mb = [res("mb0"), res("mb0")]
                    r_mbT = [res("mbT0"), res("mbT1")]
                    r_PT = [res("PTs0"), res("PT2")]
                    r_nbq = [res("nbq%d" % i) for i in range(3)]
                    r_k2, r_k2m = res("k2"), res("k2m")
                    r_PTs = [res("PTs0"), res("PTs0")]
                    r_rb = [res("rb0"), res("rb1")]
                    r_qbf = [res("qbf0"), res("qbf0")]
                    r_iqb = [res("iqb0"), res("iqb0")]
                    r_e2, r_jk2, r_sb2, r_jk3 = res("e2"), res("jk2"), res("sb2"), res("jk3")
                    r_attb = [res("attb0"), res("attb0")]
                    r_kb = r_qbf
                    r_ikb = r_iqb
                    r_qT = [res("qT0"), res("qT1"), res("qT2")]
                    r_iqT = [res("iqT0"), res("iqT1")]

                    wKV = wA[:, 0:KC * 1096].rearrange("p (k c) -> p k c", k=KC)
                    r_wKVg = [res("wKVg%d" % i) for i in range(3)]
                    r_wQg = [res("wQg%d" % i) for i in range(2)]
                    for i, (c0, c1) in enumerate(((0, 512), (512, 1024), (1024, 1096))):
                        s.dma("pool", wKV[:, :, c0:c1], wkv_d[l][:, :, c0:c1], writes=[r_wKVg[i]] + r_wQg)
                    for tb in range(NB):
                        s.op("pool", lambda tb=tb: nc.gpsimd.memset(Vg[:, tb, :, 128:130], 1.0), [], [r_V[tb]])

                    def blkB(tb):
                        p = tb % 2
                        b0 = 4 * p
                        make_xT(tb, xbf[p], xT[p], r_xbf[p], r_xT[p], b0)
                        xt, r_xt = xT[p], r_xT[p]
                        mm(bankf(b0 + 1), [(xt[:, k * 128:(k + 1) * 128], wKV[:, k, 0:512]) for k in range(KC)], [r_xt, r_wKVg[0]], [r_bk[b0 + 1]])
                        mm(bankf(b0 + 2), [(xt[:, k * 128:(k + 1) * 128], wKV[:, k, 512:1024]) for k in range(KC)], [r_xt, r_wKVg[1]], [r_bk[b0 + 2]])
                        mm(bankf(b0 + 3)[:, 0:72], [(xt[:, k * 128:(k + 1) * 128], wKV[:, k, 1024:1096]) for k in range(KC)], [r_xt, r_wKVg[2]], [r_bk[b0 + 3]])
                        yield
                        kbf = qiq[p][:, 0:512]
                        ikb = qiq[p][:, 512:640]
                        rope(bankf(b0 + 1), [r_bk[b0 + 1]], 4, 64, cs_cos[:, tb, :], cs_sin[:, tb, :], kbf, r_kb[p], tA, tB, r_tA, r_tB)
                        copy("act", Vg[:, tb, :, 0:128], bankf(b0 + 2).rearrange("p (h e) -> p h e", h=4), [r_bk[b0 + 2]], [r_V[tb]])
                        for h in range(4):
                            s.op("act", lambda h=h: nc.scalar.activation(out=jk[:, 2:3].to_broadcast([128, 128]), in_=kbf[:, h * 128:(h + 1) * 128], func=AF.Square, accum_out=K2[:, tb * 4 + h:tb * 4 + h + 1]), [r_kb[p]], [r_k2, r_jk3])
                        rope(bankf(b0 + 3)[:, 0:64], [r_bk[b0 + 3]], 1, 32, cs_cos[:, tb, 0:64:2], cs_sin[:, tb, 0:64:2], ikb[:, 0:64], r_ikb[p], tA, tB, r_tA, r_tB)
                        copy("pool", ikb[:, 64:128], ikb[:, 0:64], [r_ikb[p]], [r_ikb[p]])
                        ts("dve", wsb[:, tb, :], bankf(b0 + 3)[:, 64:72], W_SCALE, None, ALU.mult, None, [r_bk[b0 + 3]], [r_wsb])
                        yield
                        pb = bankb(b0)
                        for h in range(4):
                            tr(pb[:, h * 128:(h + 1) * 128], kbf[:, h * 128:(h + 1) * 128], [r_kb[p]], [r_bk[b0]])
                        tr(pb[:, 512:640], ikb, [r_ikb[p]], [r_bk[b0]])
                        copy("act", KT[:, :, tb * 128:(tb + 1) * 128], pb[:, 0:512].rearrange("p (h t) -> p h t", h=4), [r_bk[b0]], [r_KT[tb]])
                        copy("act", ikT[:, tb * 128:(tb + 1) * 128], pb[:, 512:640], [r_bk[b0]], [r_ikT[tb]])

                    pipeline([blkB(tb) for tb in range(NB)], 2)
                    K2P = sm2[:, 88:92]
                    s.op("dve", lambda: nc.vector.tensor_reduce(out=K2P, in_=K2[:, 0:NB * 4].rearrange("p (b h) -> p h b", h=4), axis=AX.X, op=ALU.max), [r_k2], [r_k2m])
                    allmax4(K2P, r_k2m, K2M, r_k2m, 0, 0)
                    chk('B')
                    wQ = wA[:, 0:KC * 1024].rearrange("p (k c) -> p k c", k=KC)
                    for i in range(2):
                        s.dma("pool", wQ[:, :, i * 512:(i + 1) * 512], wq_d[l][:, :, i * 512:(i + 1) * 512], writes=[r_wQg[i]] + r_wKVg)
                    mb_free = [True, True]

                    def frontC(tb):
                        p = tb % 2
                        q3 = tb % 3
                        Ib = Ibuf2[p]
                        r_I = r_I2[p]
                        L = (tb + 1) * 128
                        r_sb = res("smbis%d" % p)
                        r_sa = res("smatt%d" % p)
                        cB = 64 + p * 16
                        cA = 96 + p * 16
                        stp = steps[:, p, :]
                        make_xT(tb, xbf[p], xT[p], r_xbf[p], r_xT[p], 0)
                        xt, r_xt = xT[p], r_xT[p]
                        mm(bankf(1), [(xt[:, k * 128:(k + 1) * 128], wQ[:, k, 0:512]) for k in range(KC)], [r_xt, r_wQg[0]], [r_bk[1]])
                        mm(bankf(2), [(xt[:, k * 128:(k + 1) * 128], wQ[:, k, 512:1024]) for k in range(KC)], [r_xt, r_wQg[1]], [r_bk[2]])
                        qbf = qiq[p][:, 0:512]
                        iqb = qiq[p][:, 512:1024]
                        rope(bankf(1), [r_bk[1]], 4, 64, cs_cos[:, tb, :], cs_sin[:, tb, :], qbf, r_qbf[p], tA, tB, r_tA, r_tB)
                        rope(bankf(2), [r_bk[2]], 8, 32, cs_cos[:, tb, 0:64:2], cs_sin[:, tb, 0:64:2], iqb, r_iqb[p], tA, tB, r_tA, r_tB)
                        pb = bankb(0)
                        for h in range(4):
                            tr(pb[:, h * 128:(h + 1) * 128], qbf[:, h * 128:(h + 1) * 128], [r_qbf[p]], [r_bk[0]])
                        for h in range(4):
                            tr(pb[:, 512 + h * 128:512 + (h + 1) * 128], iqb[:, h * 128:(h + 1) * 128], [r_iqb[p]], [r_bk[0]])
                        copy("act", qT[q3], pb[:, 0:512].rearrange("p (h t) -> p h t", h=4), [r_bk[0]], [r_qT[q3]])
                        copy("act", iqT[p], pb[:, 512:1024].rearrange("p (h t) -> p h t", h=4), [r_bk[0]], [r_iqT[p]])
                        Q2 = sm2[:, 120 + p * 4:124 + p * 4]
                        Q2M = sm2[:, 128 + p * 4:132 + p * 4]
                        NBQ = sm2[:, 136 + q3 * 4:140 + q3 * 4]
                        r_q2 = res("q2_%d" % p)
                        for h in range(4):
                            s.op("act", lambda h=h: nc.scalar.activation(out=jk[:, 2:3].to_broadcast([128, 128]), in_=qbf[:, h * 128:(h + 1) * 128], func=AF.Square, accum_out=Q2[:, h:h + 1]), [r_qbf[p]], [r_q2, r_jk3])
                        allmax4(Q2, r_q2, Q2M, r_q2, 2, 1 + p)
                        tt("dve", Q2M, Q2M, K2M, ALU.mult, [r_q2, r_k2m], [r_q2])
                        tt("pool", Q2M, Q2M, POSH4, ALU.pow, [r_q2, r_smc], [r_q2])
                        ts("dve", NBQ, Q2M, -ATT_SCALE, None, ALU.mult, None, [r_q2], [r_nbq[q3]])
                        yield
                        nch = (L + 511) // 512
                        cnt = 0
                        for c in range(nch):
                            n = min(512, L - 512 * c)
                            kres = r_ikT[4 * c:4 * c + (n + 127) // 128]
                            for h in range(8):
                                m_, base = h // 2, 64 * (h % 2)
                                bk = 1 + cnt % 2
                                rb, r_rb_ = rbuf[cnt % 2], r_rb[cnt % 2]
                                cnt += 1
                                mm(bankf(bk)[:, 0:n], [(iqT[p][base:base + 64, m_, :], ikT[base:base + 64, c * 512:c * 512 + n])], [r_iqT[p]] + kres, [r_bk[bk]])
                                act(rb[:, 0:n], bankf(bk)[:, 0:n], AF.Relu, [r_bk[bk]], [r_rb_])
                                if h == 0:
                                    ts("dve", Ib[:, c * 512:c * 512 + n], rb[:, 0:n], wsb[:, tb, 0:1], None, ALU.mult, None, [r_rb_, r_wsb], [r_I])
                                else:
                                    stt("dve", Ib[:, c * 512:c * 512 + n], rb[:, 0:n], wsb[:, tb, h:h + 1], Ib[:, c * 512:c * 512 + n], ALU.mult, ALU.add, [r_rb_, r_wsb], [r_I])
                            yield
                        tt("dve", Ib[:, tb * 128:L], Ib[:, tb * 128:L], cb, ALU.add, [r_cst], [r_I])
                        if L > TOPK:
                            MX, MN, RG, T, CNT, DD = [sm[:, cB + i:cB + i + 1] for i in range(6)]
                            s.op("dve", lambda: nc.vector.tensor_reduce(out=MX, in_=Ib[:, 0:L], axis=AX.X, op=ALU.max), [r_I], [r_sb])
                            s.op("dve", lambda: nc.vector.tensor_reduce(out=MN, in_=Ib[:, 0:L - 128], axis=AX.X, op=ALU.min), [r_I], [r_sb])
                            tt("dve", RG, MX, MN, ALU.subtract, [r_sb], [r_sb])
                            ts("dve", RG, RG, 1.0 + 1.0 / 1024, None, ALU.mult, None, [r_sb], [r_sb])
                            tt("dve", T, MX, RG, ALU.subtract, [r_sb], [r_sb])
                            ts("dve", stp[:, 0:NITER + 2], p2[:, 0:NITER + 2], RG, None, ALU.mult, None, [r_sb, r_cst], [r_sb])
                            stt("dve", T, stp[:, 1:2], 1.0, T, ALU.mult, ALU.add, [r_sb], [r_sb])
                            if p == 1:
                                ts("dve", nstp[:, 0:NITER + 2], stp[:, 0:NITER + 2], -0.5, None, ALU.mult, None, [r_sb], [r_sb])
                            junk = jk[:, 0:1].to_broadcast([128, L])
                            if p == 0:
                                for i in range(NITER):
                                    ts("dve", junk, Ib[:, 0:L], T, None, ALU.is_gt, ALU.add, [r_I, r_sb], [r_jk], accum=CNT)
                                    s.op("dve", lambda: nc.vector.tensor_scalar(out=DD, in0=CNT, scalar1=TOPK - 0.5, scalar2=-0.5, op0=ALU.is_gt, op1=ALU.add), [r_jk], [r_sb])
                                    stt("dve", T, DD, stp[:, i + 1:i + 2], T, ALU.mult, ALU.add, [r_sb], [r_sb])
                                    if i % 4 == 3:
                                        yield
                            else:
                                NT = sm[:, cB + 6:cB + 7]
                                DS = sm[:, cB + 7:cB + 8]
                                junk2 = jk[:, 1:2].to_broadcast([128, L])
                                ts("dve", NT, T, -1.0, None, ALU.mult, None, [r_sb], [r_sb])
                                for i in range(NITER):
                                    s.op("act", lambda: nc.scalar.activation(out=junk2, in_=Ib[:, 0:L], func=AF.Sign, bias=NT, scale=1.0, accum_out=CNT), [r_I, r_sb], [r_jk2])
                                    s.op("act", lambda: nc.scalar.activation(out=DS, in_=CNT, func=AF.Sign, bias=BIASC[L], scale=1.0), [r_jk2, r_smc], [r_sb2])
                                    s.op("act", lambda i=i: nc.scalar.activation(out=NT, in_=DS, func=AF.Identity, bias=NT, scale=nstp[:, i + 1:i + 2]), [r_sb2, r_sb], [r_sb])
                                    if i % 4 == 3:
                                        yield
                                ts("dve", T, NT, -1.0, None, ALU.mult, None, [r_sb], [r_sb])
                            tt("dve", T, T, stp[:, NITER + 1:NITER + 2], ALU.subtract, [r_sb], [r_sb])
                            TAU = T
                            r_tau = r_sb
                        else:
                            TAU = TAUALL
                            r_tau = r_smc
                        while not mb_free[p]:
                            yield
                        mb_free[p] = False
                        ts("dve", mb[p][:, 0:L], Ib[:, 0:L], TAU, MASK_NEG, ALU.is_le, ALU.mult, [r_I, r_tau], [r_mb[p]])
                        mbT = [ebuf, ebuf2][p]
                        pb0 = bankb(0)
                        for half in range((tb + 8) // 8):
                            nblk = min(8, tb + 1 - 8 * half)
                            for j in range(nblk):
                                sbk = half * 8 + j
                                tr(pb0[:, j * 128:(j + 1) * 128], mb[p][:, sbk * 128:(sbk + 1) * 128], [r_mb[p]], [r_bk[0]])
                            copy("act", mbT[:, half * 1024:half * 1024 + nblk * 128], pb0[:, 0:nblk * 128], [r_bk[0]], [r_mbT[p]])

                    def backC(tb):
                        p = tb % 2
                        q3 = tb % 3
                        L = (tb + 1) * 128
                        nch = (L + 511) // 512
                        cA = 96 + p * 16
                        mbT = [ebuf, ebuf2][p]
                        PTb = [PTs[0], PT2]
                        NBQ = sm2[:, 136 + q3 * 4:140 + q3 * 4]

                        def hbank(h):
                            return 4 + 2 * (h % 2) if nch <= 2 else 4

                        def qk(h):
                            hb = hbank(h)
                            for sbk in range(tb + 1):
                                bkk = hb + sbk // 4
                                mm(ps[:, hb * 512 + sbk * 128:hb * 512 + (sbk + 1) * 128],
                                   [(KT[:, h, sbk * 128:(sbk + 1) * 128], qT[q3][:, h, :]), (ident[:], mbT[:, sbk * 128:(sbk + 1) * 128])],
                                   [r_qT[q3], r_mbT[p], r_ident, r_KT[sbk]], [r_bk[bkk]])

                        def sexp(h):
                            hb = hbank(h)
                            sc_ = ps[:, hb * 512:hb * 512 + L]
                            rsc = r_bk[hb:hb + nch]
                            act(PTb[h % 2][:, 0:L], sc_, AF.Exp, rsc + [r_nbq[q3]], [r_PT[h % 2]], bias=NBQ[:, h:h + 1], scale=ATT_SCALE)

                        def pv(h):
                            pts = PTb[h % 2]
                            RDa = sm[:, cA + 6 + (h % 2):cA + 7 + (h % 2)]
                            r_sr = res("smrd%d_%d" % (p, h % 2))
                            po = bankf(3)[:, 0:129]
                            mm(po, [(pts[:, sbk * 128:(sbk + 1) * 128], Vg[:, sbk, h, 0:129]) for sbk in range(tb + 1)],
                               [r_PT[h % 2]] + r_V[0:tb + 1], [r_bk[3]])
                            s.op("dve", lambda: nc.vector.reciprocal(out=RDa, in_=bankf(3)[:, 128:129]), [r_bk[3]], [r_sr])
                            ts("dve", attb[p][:, h * 128:(h + 1) * 128], bankf(3)[:, 0:128], RDa, None, ALU.mult, None, [r_bk[3], r_sr], [r_attb[p]])

                        qk(0)
                        sexp(0)
                        yield
                        for h in range(4):
                            if h + 1 < 4:
                                qk(h + 1)
                                sexp(h + 1)
                            pv(h)
                            yield
                        pbb = bankb(3)
                        for h in range(4):
                            tr(pbb[:, h * 128:(h + 1) * 128], attb[p][:, h * 128:(h + 1) * 128], [r_attb[p]], [r_bk[3]])
                        copy("act", attTall[:, :, tb * 128:(tb + 1) * 128], pbb[:, 0:512].rearrange("p (h t) -> p h t", h=4), [r_bk[3]], [r_attTall[tb]])

                    fronts = []
                    ready = []
                    back = None
                    done_back = set()
                    next_tb = 0
                    while next_tb < NB or fronts or ready or back is not None:
                        while (next_tb < NB and len(fronts) < 2
                               and all(t % 2 != next_tb % 2 for t, _ in fronts)
                               and (next_tb < 3 or (next_tb - 3) in done_back)):
                            fronts.append((next_tb, frontC(next_tb)))
                            next_tb += 1
                        if back is None and ready:
                            tbb = ready.pop(0)
                            back = (tbb, backC(tbb))
                        if back is not None:
                            try:
                                next(back[1])
                            except StopIteration:
                                mb_free[back[0] % 2] = True
                                done_back.add(back[0])
                                back = None
                        for item in list(fronts):
                            try:
                                next(item[1])
                            except StopIteration:
                                fronts.remove(item)
                                ready.append(item[0])
                    chk('C')
                    s.barrier(list(R.values()))
                    cv = Carver(PERSIST)
                    mixT = cv.take(4 * S).rearrange("p (k t) -> p k t", k=4)
                    PERSIST2 = cv.off
                    wR = cv.take(KC * 2048).rearrange("p (k c) -> p k c", k=KC)
                    xbf = [cv.take(1024) for _ in range(3)]
                    xT = [cv.take(1024) for _ in range(3)]
                    tA = cv.take(2048, F32)
                    tB = cv.take(2048, F32)
                    qk_r = cv.take(2048, F32)
                    qkb = [cv.take(1024) for _ in range(3)]
                    kdec = [cv.take(512) for _ in range(3)]
                    vbf = [cv.take(512) for _ in range(3)]
                    gsl = [cv.take(1024, F32) for _ in range(3)]
                    qkT = [cv.take(1024) for _ in range(3)]
                    stb = [cv.take(512) for _ in range(3)]
                    yn = cv.take(1024, F32)
                    y3 = [cv.take(512) for _ in range(3)]
                    state_f = cv.take(1024, F32)
                    state_bf = cv.take(512)
                    gbA = cv.take(1024, F32)
                    r_mixT = [res("mixT%d" % b) for b in range(NB)]
                    r_wR, r_gbA = res("wR"), res("gbA")
                    r_qk, r_yn = res("qk_r"), res("yn")
                    r_qkb = [res("qkb%d" % i) for i in range(3)]
                    r_kdec = [res("kdec%d" % i) for i in range(3)]
                    r_vbf = [res("vbf%d" % i) for i in range(3)]
                    r_gsl = [res("gsl%d" % i) for i in range(3)]
                    r_qkT = [res("qkT%d" % i) for i in range(3)]
                    r_stb = [res("stb%d" % i) for i in range(3)]
                    r_y3 = [res("y3%d" % i) for i in range(3)]
                    r_stf, r_stbf = res("state_f"), res("state_bf")
                    r_xbfA = [res("xbfA%d" % i) for i in range(3)]
                    r_xTA = [res("xTA%d" % i) for i in range(3)]
                    r_tA2, r_tB2 = res("tA2"), res("tB2")
                    s.dma("sp", gbA[:, 0:512], gret_d[l], writes=[r_gbA])
                    r_wRg = [res("wRg%d" % i) for i in range(4)]
                    for i in range(4):
                        s.dma("pool", wR[:, :, i * 512:(i + 1) * 512], wr_d[l][:, :, i * 512:(i + 1) * 512], writes=[r_wRg[i]])
                    s.op("dve", lambda: nc.vector.memset(state_f, 0.0), [], [r_stf])
                    s.op("pool", lambda: nc.gpsimd.memset(state_bf, 0.0), [], [r_stbf])
                    dct = cst[:, C_RT + 16:C_RT + 20].unsqueeze(2).to_broadcast([128, 4, 128])
                    kps = cst[:, C_RT + 4:C_RT + 8].unsqueeze(2).to_broadcast([128, 4, 128])
                    dks = cst[:, C_RT + 8:C_RT + 12].unsqueeze(2).to_broadcast([128, 4, 128])

                    def blkA(tb):
                        p = tb % 3
                        r_sg = res("smgn%d" % (tb % 2))
                        cG = 128 + (tb % 2) * 48
                        make_xT(tb, xbf[p], xT[p], r_xbfA[p], r_xTA[p], 0)
                        xt, r_xt = xT[p], r_xTA[p]
                        for j in range(4):
                            mm(bankf(1 + j), [(xt[:, k * 128:(k + 1) * 128], wR[:, k, j * 512:(j + 1) * 512]) for k in range(KC)], [r_xt, r_wRg[j]], [r_bk[1 + j]])
                        yield
                        rope(bankf(1, 2), [r_bk[1], r_bk[2]], 8, 64, cs_cos[:, tb, :], cs_sin[:, tb, :], qk_r, r_qk, tA, tB, r_tA2, r_tB2)
                        copy("act", vbf[p], bankf(3), [r_bk[3]], [r_vbf[p]])
                        act(gsl[p], bankf(4), AF.Silu, [r_bk[4]], [r_gsl[p]])
                        copy("pool", qkb[p][:, 0:512], qk_r[:, 0:512], [r_qk], [r_qkb[p]])
                        k3 = qk_r[:, 512:1024].rearrange("p (h d) -> p h d", h=4)
                        tt("dve", qkb[p][:, 512:1024].rearrange("p (h d) -> p h d", h=4), k3, kps, ALU.mult, [r_qk, r_cst], [r_qkb[p]])
                        tt("pool", kdec[p].rearrange("p (h d) -> p h d", h=4), k3, dks, ALU.mult, [r_qk, r_cst], [r_kdec[p]])
                        pb = bankb(0)
                        for j in range(8):
                            tr(pb[:, j * 128:(j + 1) * 128], qkb[p][:, j * 128:(j + 1) * 128], [r_qkb[p]], [r_bk[0]])
                        copy("act", qkT[p], pb[:, :], [r_bk[0]], [r_qkT[p]])
                        yield
                        for h in range(4):
                            mm(bankf(5)[:, h * 128:(h + 1) * 128], [(qkT[p][:, 512 + h * 128:512 + (h + 1) * 128], qkT[p][:, h * 128:(h + 1) * 128])], [r_qkT[p]], [r_bk[5]])
                        tt("dve", stb[p].rearrange("p (h d) -> p h d", h=4), bankf(5).rearrange("p (h d) -> p h d", h=4),
                           tri.unsqueeze(1).to_broadcast([128, 4, 128]), ALU.mult, [r_bk[5], r_cst], [r_stb[p]])
                        for h in range(4):
                            sl = slice(h * 128, (h + 1) * 128)
                            mm(bankf(6)[:, sl], [(stb[p][:, sl], vbf[p][:, sl]), (qkT[p][:, sl], state_bf[:, sl])], [r_stb[p], r_vbf[p], r_qkT[p], r_stbf], [r_bk[6]])
                        for h in range(4):
                            sl = slice(h * 128, (h + 1) * 128)
                            mm(bankf(7)[:, sl], [(kdec[p][:, sl], vbf[p][:, sl])], [r_kdec[p], r_vbf[p]], [r_bk[7]])
                        tt("pool", state_f.rearrange("p (h d) -> p h d", h=4), state_f.rearrange("p (h d) -> p h d", h=4), dct, ALU.mult, [r_cst], [r_stf])
                        tt("dve", state_f, state_f, bankf(7), ALU.add, [r_bk[7]], [r_stf])
                        copy("pool", state_bf, state_f, [r_stf], [r_stbf])
                        yield
                        st6 = sm[:, cG:cG + 24].rearrange("p (h a) -> p h a", h=4)
                        mv = sm[:, cG + 24:cG + 32].rearrange("p (h a) -> p h a", h=4)
                        for h in range(4):
                            s.op("dve", lambda h=h: nc.vector.bn_stats(out=st6[:, h, :], in_=bankf(6)[:, h * 128:(h + 1) * 128]), [r_bk[6]], [r_sg])
                        for h in range(4):
                            s.op("dve", lambda h=h: nc.vector.bn_aggr(out=mv[:, h, :], in_=st6[:, h, :]), [r_sg], [r_sg])
                        A4 = sm[:, cG + 32:cG + 36]
                        RS4 = sm[:, cG + 36:cG + 40]
                        SC4 = sm[:, cG + 40:cG + 44]
                        NB4 = sm[:, cG + 44:cG + 48]
                        tt("dve", A4, mv[:, :, 1], cst[:, C_RT + 12:C_RT + 16], ALU.mult, [r_sg, r_cst], [r_sg])
                        ts("dve", A4, A4, LN_EPS, None, ALU.add, None, [r_sg], [r_sg])
                        tt("pool", RS4, A4, NEGH4, ALU.pow, [r_sg, r_smc], [r_sg])
                        tt("dve", SC4, RS4, cst[:, C_RT:C_RT + 4], ALU.mult, [r_sg, r_cst], [r_sg])
                        stt("dve", NB4, mv[:, :, 0], -1.0, SC4, ALU.mult, ALU.mult, [r_sg], [r_sg])
                        for h in range(4):
                            sl = slice(h * 128, (h + 1) * 128)
                            act(yn[:, sl], bankf(6)[:, sl], AF.Identity, [r_bk[6], r_sg], [r_yn], bias=NB4[:, h:h + 1], scale=SC4[:, h:h + 1])
                        tt("dve", yn, yn, gbA[:, 0:512], ALU.mult, [r_yn, r_gbA], [r_yn])
                        tt("pool", y3[p], yn, gsl[p], ALU.mult, [r_yn, r_gsl[p]], [r_y3[p]])
                        pb = bankb(0)
                        for h in range(4):
                            tr(pb[:, h * 128:(h + 1) * 128], y3[p][:, h * 128:(h + 1) * 128], [r_y3[p]], [r_bk[0]])
                        copy("act", mixT[:, :, tb * 128:(tb + 1) * 128], pb[:, 0:512].rearrange("p (h t) -> p h t", h=4), [r_bk[0]], [r_mixT[tb]])

                    pipeline([blkA(tb) for tb in range(NB)], 3)
                    chk('A')
                    s.barrier(list(R.values()))
                    cv = Carver(PERSIST2)
                    wO = cv.take(KC * 1024).rearrange("p (k c) -> p k c", k=KC)
                    gbm = cv.take(4096, F32)
                    r_wO, r_gbm = res("wO"), res("gbm")
                    r_wOg = [res("wOg%d" % i) for i in range(2)]
                    for i in range(2):
                        s.dma("pool", wO[:, :, i * 512:(i + 1) * 512], wout_d[l][:, :, i * 512:(i + 1) * 512], writes=[r_wOg[i]])
                    s.dma("sp", gbm[:, :], gmix_d[l], writes=[r_gbm])
                    for tb in range(NB):
                        b0 = 2 * (tb % 2)
                        for hh in range(2):
                            pairs = [(mixT[:, k, tb * 128:(tb + 1) * 128], wO[:, k, hh * 512:(hh + 1) * 512]) for k in range(4)]
                            pairs += [(attTall[:, k, tb * 128:(tb + 1) * 128], wO[:, 4 + k, hh * 512:(hh + 1) * 512]) for k in range(4)]
                            mm(bankf(b0 + hh), pairs, [r_mixT[tb], r_attTall[tb], r_wOg[hh]], [r_bk[b0 + hh]])
                        layer_norm_block(tb, [bankf(b0), bankf(b0 + 1)], [r_bk[b0], r_bk[b0 + 1]], gbm, r_gbm)
                    chk('C2')
                    s.barrier(list(R.values()))
                    cv = Carver()
                    NG = S // GT
                    nblk = GT // 128
                    ncol = GT // 512
                    x1T = [cv.take(KC * GT).rearrange("p (k t) -> p k t", k=KC) for _ in range(2)]
                    hT = cv.take(HC * GT).rearrange("p (c t) -> p c t", c=HC)
                    wD = cv.take(HC * 1024).rearrange("p (c n) -> p c n", c=HC)
                    wG = [cv.take(KC * 512).rearrange("p (k c) -> p k c", k=KC) for _ in range(3)]
                    xbf2 = [cv.take(1024) for _ in range(2)]
                    sgb = [cv.take(1024, F32) for _ in range(2)]
                    gbf = cv.take(4096, F32)
                    r_x1T = [[res("x1T%d_%d" % (q, i)) for i in range(nblk)] for q in range(2)]
                    r_hT = [res("hT%d_%d" % (c, t)) for c in range(HC) for t in range(ncol)]
                    r_wD = [res("wD%d" % i) for i in range(4)]
                    r_wG = [res("wG%d" % i) for i in range(3)]
                    r_xbf2 = [res("xbf2_0"), res("xbf2_1")]
                    r_sgl = [res("sg0"), res("sg1")]
                    r_gbf = res("gbf")
                    wdq = [(0, 6), (6, 12), (12, 18), (18, 22)]
                    for i, (c0, c1) in enumerate(wdq):
                        s.dma("pool", wD[:, c0:c1, :], wd_d[l][:, c0:c1, :], writes=[r_wD[i]])
                    s.dma("sp", gbf[:, :], gffn_d[l], writes=[r_gbf])
                    NP = HC // 2
                    gu = [0]

                    def load_gu(pr):
                        slot = gu[0] % 3
                        gu[0] += 1
                        s.dma("sp", wG[slot].rearrange("p k c -> p (k c)"), wgu_bf[l, pr], reads=[r_conv[l]], writes=[r_wG[slot]])
                        return slot

                    def prep_x1T(g):
                        q = g % 2
                        for j in range(nblk):
                            tb = g * nblk + j
                            i2 = j % 2
                            copy("act" if j % 2 == 0 else "dve", xbf2[i2], x_sb[:, tb, :], [r_x[tb]], [r_xbf2[i2]])
                            pb = bankb(i2)
                            for k in range(KC):
                                tr(pb[:, k * 128:(k + 1) * 128], xbf2[i2][:, k * 128:(k + 1) * 128], [r_xbf2[i2]], [r_bk[i2]])
                            copy("act", x1T[q][:, :, j * 128:(j + 1) * 128], pb[:, :].rearrange("p (k t) -> p k t", k=KC), [r_bk[i2]], [r_x1T[q][j]])

                    prep_x1T(0)
                    for g in range(NG):
                        q = g % 2
                        slots = [load_gu(0), load_gu(1)]
                        for pr in range(NP):
                            if pr + 2 < NP:
                                slots.append(load_gu(pr + 2))
                            slot = slots[pr]
                            for ci in range(2):
                                c = 2 * pr + ci
                                for th in range(ncol):
                                    xr = r_x1T[q][4 * th:4 * th + 4]
                                    qq = (c * ncol + th) % 2
                                    bG, bU = 2 + 2 * qq, 3 + 2 * qq
                                    mm(bankf(bG), [(wG[slot][:, k, ci * 256:ci * 256 + 128], x1T[q][:, k, th * 512:(th + 1) * 512]) for k in range(KC)], xr + [r_wG[slot]], [r_bk[bG]])
                                    mm(bankf(bU), [(wG[slot][:, k, ci * 256 + 128:ci * 256 + 256], x1T[q][:, k, th * 512:(th + 1) * 512]) for k in range(KC)], xr + [r_wG[slot]], [r_bk[bU]])
                                    act(sgb[qq], bankf(bG), AF.Silu, [r_bk[bG]], [r_sgl[qq]])
                                    tt("dve", hT[:, c, th * 512:(th + 1) * 512], sgb[qq], bankf(bU), ALU.mult, [r_sgl[qq], r_bk[bU]], [r_hT[c * ncol + th]])
                        if g + 1 < NG:
                            prep_x1T(g + 1)
                        for j in range(nblk):
                            tb = g * nblk + j
                            th = j // 4
                            hres = [r_hT[c * ncol + th] for c in range(HC)]
                            for hh in range(2):
                                mm(bankf(6 + hh), [(hT[:, c, j * 128:(j + 1) * 128], wD[:, c, hh * 512:(hh + 1) * 512]) for c in range(HC)],
                                   hres + r_wD, [r_bk[6 + hh]])
                            layer_norm_block(tb, [bankf(6), bankf(7)], [r_bk[6], r_bk[7]], gbf, r_gbf)
                    s.barrier(list(R.values()))
              except _Stop:
                pass
              if True:
                yv = y_d[sq].rearrange("(b p) d -> p b d", p=128)
                for b in range(NB):
                    s.dma("sp", yv[:, b, :], x_sb[:, b, :], reads=[r_x[b]], semres=r_stb_[b])
            for b in range(NB):
                s._wait("sp", ("dma", r_stb_[b], r_stb_[b].cnt))

        s1 = Sched(nc, es, None)
        emit(s1)
        s2 = Sched(nc, es, s1.needed)
        emit(s2)
    return nc


_CACHE = {}


def _prep_inputs(cfg, x, positions, w_in, ret_gn_gain, w_out, ln_mix_gain, ln_mix_bias,
                 w_gate_up, w_down, ln_ffn_gain, ln_ffn_bias, ncores):
    DEPTH = cfg.DEPTH
    f = lambda a: np.ascontiguousarray(np.asarray(a, dtype=np.float32))
    g_ret = np.ascontiguousarray(np.broadcast_to(f(ret_gn_gain)[:, None, :], (DEPTH, 128, 512)))
    g_mix = np.ascontiguousarray(np.broadcast_to(
        np.concatenate([f(ln_mix_gain), f(ln_mix_bias)], axis=1)[:, None, :], (DEPTH, 128, 2048)))
    g_ffn = np.ascontiguousarray(np.broadcast_to(
        np.concatenate([f(ln_ffn_gain), f(ln_ffn_bias)], axis=1)[:, None, :], (DEPTH, 128, 2048)))
    cst = make_consts(cfg)
    w_in = f(w_in)

    def tile_k(w):
        return np.ascontiguousarray(w.reshape(DEPTH, KC, 128, w.shape[-1]).transpose(0, 2, 1, 3))

    w_kv = tile_k(np.concatenate([w_in[:, :, OFF_AK:OFF_AK + 1024], w_in[:, :, OFF_IK:OFF_IK + 72]], axis=2))
    w_q = tile_k(np.concatenate([w_in[:, :, OFF_AQ:OFF_AQ + 512], w_in[:, :, OFF_IQ:OFF_IQ + 512]], axis=2))
    w_r = tile_k(w_in[:, :, 0:2048])
    w_o = tile_k(f(w_out))
    wgu = f(w_gate_up)
    g4 = wgu[:, :, 0:HID].reshape(DEPTH, KC, 128, HC // 2, 2, 128)
    u4 = wgu[:, :, HID:2 * HID].reshape(DEPTH, KC, 128, HC // 2, 2, 128)
    gu = np.stack([g4, u4], axis=5)
    gu = gu.transpose(0, 3, 2, 1, 4, 5, 6)
    w_gu = np.ascontiguousarray(gu.reshape(DEPTH, HC // 2, 128, KC * 512))
    w_d = np.ascontiguousarray(f(w_down).reshape(DEPTH, HC, 128, 1024).transpose(0, 2, 1, 3))
    x = f(x)
    pos = np.asarray(positions).astype(np.int32)
    shared = {"w_kv": w_kv, "w_q": w_q, "w_r": w_r, "w_o": w_o, "w_gu": w_gu, "w_d": w_d,
              "g_ret": g_ret, "g_mix": g_mix, "g_ffn": g_ffn, "cst": cst}
    maps = []
    for c in range(ncores):
        xs = x[c * cfg.NSEQ:(c + 1) * cfg.NSEQ]
        ps_ = pos[c * cfg.NSEQ:(c + 1) * cfg.NSEQ]
        ps_ = np.ascontiguousarray(ps_.reshape(cfg.NSEQ, cfg.NB, 128).transpose(0, 2, 1))
        m = dict(shared)
        m["x"] = np.ascontiguousarray(xs)
        m["pos"] = ps_
        maps.append(m)
    return maps


def run(cfg, ncores, **inputs):
    key = (cfg.S, cfg.NSEQ, cfg.DEPTH, cfg.NITER, cfg.stop)
    if key not in _CACHE:
        _CACHE[key] = build(cfg)
    nc = _CACHE[key]
    maps = _prep_inputs(cfg, ncores=ncores, **inputs)
    res = run_bass_kernel_spmd(nc, maps, core_ids=list(range(ncores)))
    return np.concatenate([r["y"] for r in res.results], axis=0)


def kernel(x, positions, w_in, ret_gn_gain, w_out, ln_mix_gain, ln_mix_bias,
           w_gate_up, w_down, ln_ffn_gain, ln_ffn_bias):
    cfg = Cfg(S=2048, NSEQ=2, DEPTH=2, NITER=17)
    out = run(cfg, 8, x=x, positions=positions, w_in=w_in, ret_gn_gain=ret_gn_gain, w_out=w_out,
              ln_mix_gain=ln_mix_gain, ln_mix_bias=ln_mix_bias, w_gate_up=w_gate_up, w_down=w_down,
              ln_ffn_gain=ln_ffn_gain, ln_ffn_bias=ln_ffn_bias)
    return out.astype(np.float32)
```

```python
import math
from contextlib import ExitStack
import numpy as np
import concourse.bass as bass
import concourse.mybir as mybir
from concourse.bass_utils import run_bass_kernel_spmd

F32 = mybir.dt.float32
BF16 = mybir.dt.bfloat16
I32 = mybir.dt.int32
ALU = mybir.AluOpType
AF = mybir.ActivationFunctionType
AX = mybir.AxisListType

D = 1024
KC = 8
HID = 2816
HC = 22
IN_COLS = 4168
OFF_RQ, OFF_RK, OFF_RV, OFF_RG, OFF_AQ, OFF_AK, OFF_AV, OFF_IQ, OFF_IK, OFF_IW = (
    0, 512, 1024, 1536, 2048, 2560, 3072, 3584, 4096, 4160)
LN_EPS = 1e-5
MAGIC = 12582912.0
TWO_PI = 2.0 * math.pi
C1 = 6.28125
C2 = TWO_PI - C1
PI_LO = 3.1415925
NEG_BIG = -1.0e30
MASK_NEG = -30000.0

C_ID, C_TRI, C_CB, C_RT, C_INVF, C_P2, C_END = 0, 128, 256, 384, 404, 468, 500


class Res:
    __slots__ = ("name", "w", "r", "sem", "cnt")

    def __init__(self, name):
        self.name = name
        self.w = None
        self.r = {}
        self.sem = None
        self.cnt = 0


class Sched:
    ENG = ("pe", "act", "dve", "pool", "sp")

    def __init__(self, nc, es, needed):
        self.nc = nc
        self.es = es
        self.rec = needed is None
        self.needed = set() if self.rec else needed
        self.idx = {e: 0 for e in self.ENG}
        self.sig = {e: 0 for e in self.ENG}
        self.cntof = {}
        self.waited = {}
        self.eng = dict(pe=nc.tensor, act=nc.scalar, dve=nc.vector, pool=nc.gpsimd, sp=nc.sync)
        self.sem = {}
        self.nsem = 0
        if not self.rec:
            for e in self.ENG:
                self.sem[e] = es.enter_context(nc.semaphore("sem_" + e))

    def _wait(self, e, tok):
        if tok is None:
            return
        if tok[0] == "eng":
            _, pe_, i = tok
            if pe_ == e and e in ("pe", "sp"):
                return
            if self.rec:
                self.needed.add((pe_, i))
                return
            val = self.cntof[(pe_, i)]
            key = (e, pe_)
            semh = self.sem[pe_]
        else:
            _, res, val = tok
            if self.rec:
                return
            key = (e, "dma", res.name)
            semh = res.sem
        if self.waited.get(key, 0) >= val:
            return
        self.waited[key] = val
        self.eng[e].wait_ge(semh, val)

    def _deps(self, e, reads, writes):
        for r in reads:
            self._wait(e, r.w)
        for w in writes:
            self._wait(e, w.w)
            for t in list(w.r.values()):
                self._wait(e, t)

    def op(self, e, fn, reads=(), writes=()):
        self._deps(e, reads, writes)
        i = self.idx[e]
        self.idx[e] += 1
        tok = ("eng", e, i)
        if not self.rec:
            ins = fn()
            if (e, i) in self.needed:
                ins.then_inc(self.sem[e], 1)
                self.sig[e] += 1
                self.cntof[(e, i)] = self.sig[e]
        for r in reads:
            r.r[e] = tok
        for w in writes:
            w.w = tok
            w.r = {}
        return tok

    def _getsem(self, res):
        if res.sem is None and not self.rec:
            res.sem = self.es.enter_context(self.nc.semaphore("dsem_%d" % self.nsem))
            self.nsem += 1

    def dma(self, q, out_ap, in_ap, reads=(), writes=(), semres=None):
        self._deps(q, reads, writes)
        self.idx[q] += 1
        tr = writes[0] if writes else semres
        self._getsem(tr)
        tr.cnt += 16
        tok = ("dma", tr, tr.cnt)
        if not self.rec:
            self.eng[q].dma_start(out=out_ap, in_=in_ap).then_inc(tr.sem, 16)
        for w in writes:
            w.w = tok
            w.r = {}
        for r in reads:
            r.r["dma_" + tr.name] = tok
        return tok

    def barrier(self, allres):
        toks = []
        for r in allres:
            if r.w is not None:
                toks.append(r.w)
            toks.extend(r.r.values())
        for e in ("pe", "act", "dve", "pool", "sp"):
            for t in toks:
                self._wait(e, t)


class _Stop(Exception):
    pass


class Cfg:
    stop = None

    def __init__(self, S=2048, NSEQ=2, DEPTH=2, NITER=20):
        self.S = S
        self.NSEQ = NSEQ
        self.DEPTH = DEPTH
        self.NB = S // 128
        self.TOPK = min(256, S // 4)
        self.NITER = NITER
        self.GT = min(512, S)


def make_consts(cfg):
    c = np.zeros((128, C_END), np.float32)
    p = np.arange(128)
    c[:, C_ID:C_ID + 128] = np.eye(128, dtype=np.float32)
    c[:, C_TRI:C_TRI + 128] = (p[None, :] >= p[:, None]).astype(np.float32)
    c[:, C_CB:C_CB + 128] = np.where(p[None, :] <= p[:, None], 0.0, NEG_BIG)
    for h in range(4):
        lg = math.log(1.0 - 2.0 ** (-5.0 - h))
        dq = np.exp(lg * (p + 1.0))
        c[:, C_RT + h] = dq
        c[:, C_RT + 4 + h] = np.exp(-lg * (p + 1.0)) * 128 ** -0.5
        c[:, C_RT + 8 + h] = np.exp(lg * (127.0 - p)) * 128 ** -0.5
        c[:, C_RT + 12 + h] = dq * dq
        c[:, C_RT + 16 + h] = (1.0 - 2.0 ** (-5.0 - h)) ** 128
    invf = (10000.0 ** (-np.arange(0, 128, 2, dtype=np.float32) / 128)).astype(np.float32)
    c[:, C_INVF:C_INVF + 64] = invf[None, :]
    c[:, C_P2:C_P2 + 32] = (2.0 ** (-np.arange(32, dtype=np.float64)))[None, :]
    return c


def pipeline(gens, depth):
    gens = list(gens)
    active = []
    nxt = 0
    while nxt < len(gens) or active:
        if nxt < len(gens) and len(active) < depth:
            active.append(gens[nxt])
            nxt += 1
        for g in list(active):
            try:
                next(g)
            except StopIteration:
                active.remove(g)


def build(cfg):
    S, NSEQ, DEPTH, NB, TOPK, NITER, GT = cfg.S, cfg.NSEQ, cfg.DEPTH, cfg.NB, cfg.TOPK, cfg.NITER, cfg.GT
    nc = bass.Bass("TRN2", target_bir_lowering=False)
    dt = nc.dram_tensor
    x_d = dt("x", [NSEQ, S, D], F32, kind="ExternalInput").ap()
    pos_d = dt("pos", [NSEQ, 128, NB], I32, kind="ExternalInput").ap()
    wkv_d = dt("w_kv", [DEPTH, 128, KC, 1096], F32, kind="ExternalInput").ap()
    wq_d = dt("w_q", [DEPTH, 128, KC, 1024], F32, kind="ExternalInput").ap()
    wr_d = dt("w_r", [DEPTH, 128, KC, 2048], F32, kind="ExternalInput").ap()
    wout_d = dt("w_o", [DEPTH, 128, KC, 1024], F32, kind="ExternalInput").ap()
    wgu_d = dt("w_gu", [DEPTH, HC // 2, 128, KC * 512], F32, kind="ExternalInput").ap()
    wd_d = dt("w_d", [DEPTH, 128, HC, 1024], F32, kind="ExternalInput").ap()
    gret_d = dt("g_ret", [DEPTH, 128, 512], F32, kind="ExternalInput").ap()
    gmix_d = dt("g_mix", [DEPTH, 128, 2048], F32, kind="ExternalInput").ap()
    gffn_d = dt("g_ffn", [DEPTH, 128, 2048], F32, kind="ExternalInput").ap()
    cst_d = dt("cst", [128, C_END], F32, kind="ExternalInput").ap()
    y_d = dt("y", [NSEQ, S, D], F32, kind="ExternalOutput").ap()
    wgu_bf = dt("wgu_bf", [DEPTH, HC // 2, 128, KC * 512], BF16, kind="Internal").ap()

    ARENA = 63 * 1024
    with ExitStack() as es:
        sb = lambda name, shape, dtp: es.enter_context(nc.sbuf_tensor(name, shape, dtp))
        x_sb = sb("x_sb", [128, NB, D], F32)
        cs_cos = sb("cs_cos", [128, NB, 64], F32)
        cs_sin = sb("cs_sin", [128, NB, 64], F32)
        cst = sb("cst_sb", [128, C_END], F32)
        ident = sb("ident", [128, 128], BF16)
        posi = sb("posi", [128, NB], I32)
        posf = sb("posf", [128, NB], F32)
        wsb = sb("wsb", [128, NB, 8], F32)
        sm = sb("sm", [128, 256], F32)
        steps = sb("steps", [128, 2, 32], F32)
        nstp = sb("nstp", [128, 32], F32)
        sm2 = sb("sm2", [128, 160], F32)
        ones4 = sb("ones4", [4, 128], F32)
        biasc_t = sb("biasc", [128, 32], F32)
        sb_late = lambda name, shape, dtp: biasc_t
        ang = sb("ang", [128, 4, 64], F32)
        arena = sb("arena", [128, ARENA], BF16)
        ps = es.enter_context(nc.psum_tensor("ps", [128, 4096], F32))

        def bankf(i, n=1):
            return ps[:, i * 512:(i + n) * 512]

        def bankb(i):
            return ps[:, i * 512:(i + 1) * 512].bitcast(BF16)

        class Carver:
            def __init__(self, off=0):
                self.off = off

            def take(self, n_bf16, dtype=BF16):
                n = (n_bf16 + 15) // 16 * 16
                ap = arena[:, self.off:self.off + n_bf16]
                self.off += n
                assert self.off <= ARENA, "arena overflow %d" % self.off
                if dtype == F32:
                    ap = ap.bitcast(F32)
                return ap

        def emit(s):
            R = {}

            def res(name):
                if name not in R:
                    R[name] = Res(name)
                return R[name]

            E = s.eng
            r_cst, r_ident, r_cs, r_pos, r_wsb = res("cst"), res("ident"), res("cs"), res("pos"), res("wsb")
            r_x = [res("x%d" % b) for b in range(NB)]
            r_bk = [res("bank%d" % i) for i in range(8)]
            r_stb_ = [res("store%d" % b) for b in range(NB)]
            r_smc = res("smc")

            def tt(e, out, in0, in1, op, reads, writes):
                return s.op(e, lambda: E[e].tensor_tensor(out=out, in0=in0, in1=in1, op=op), reads, writes)

            def ts(e, out, in0, s1, s2, op0, op1, reads, writes, accum=None):
                if op1 is None:
                    return s.op(e, lambda: E[e].tensor_scalar(out=out, in0=in0, scalar1=s1, scalar2=None, op0=op0), reads, writes)
                if accum is not None:
                    return s.op(e, lambda: E[e].tensor_scalar(out=out, in0=in0, scalar1=s1, scalar2=s2, op0=op0, op1=op1, accum_out=accum), reads, writes)
                return s.op(e, lambda: E[e].tensor_scalar(out=out, in0=in0, scalar1=s1, scalar2=s2, op0=op0, op1=op1), reads, writes)

            def stt(e, out, in0, sc, in1, op0, op1, reads, writes):
                return s.op(e, lambda: E[e].scalar_tensor_tensor(out=out, in0=in0, scalar=sc, in1=in1, op0=op0, op1=op1), reads, writes)

            def act(out, in_, func, reads, writes, bias=None, scale=None):
                kw = {}
                if bias is not None:
                    kw["bias"] = bias
                if scale is not None:
                    kw["scale"] = scale
                return s.op("act", lambda: nc.scalar.activation(out=out, in_=in_, func=func, **kw), reads, writes)

            def copy(e, out, in_, reads, writes):
                if e == "act":
                    return s.op("act", lambda: nc.scalar.copy(out=out, in_=in_), reads, writes)
                return s.op(e, lambda: E[e].tensor_copy(out=out, in_=in_), reads, writes)

            def mm(out, pairs, reads, writes):
                def f():
                    ins = None
                    n = len(pairs)
                    for i, (l, r) in enumerate(pairs):
                        ins = nc.tensor.matmul(out, lhsT=l, rhs=r, start=(i == 0), stop=(i == n - 1))
                    return ins
                return s.op("pe", f, reads, writes)

            def tr(out, in_, reads, writes):
                return s.op("pe", lambda: nc.tensor.transpose(out=out, in_=in_, identity=ident[:]), list(reads) + [r_ident], writes)

            s.dma("sp", cst[:], cst_d, writes=[r_cst])
            copy("dve", ident[:], cst[:, C_ID:C_ID + 128], [r_cst], [r_ident])
            cb = cst[:, C_CB:C_CB + 128]
            tri = cst[:, C_TRI:C_TRI + 128]
            invf = cst[:, C_INVF:C_INVF + 64]
            p2 = cst[:, C_P2:C_P2 + 32]
            s.op("dve", lambda: nc.vector.memset(sm[:, 0:8], -0.5), [], [r_smc])
            s.op("dve", lambda: nc.vector.memset(sm[:, 8:9], -1.0e29), [], [r_smc])
            biasc = sb_late("biasc", [128, 32], F32)
            BIASC = {}
            for tbq in range(NB):
                Lq = (tbq + 1) * 128
                s.op("dve", lambda tbq=tbq, Lq=Lq: nc.vector.memset(biasc[:, tbq:tbq + 1], float(-(2 * TOPK - Lq - 1))), [], [r_smc])
                BIASC[Lq] = biasc[:, tbq:tbq + 1]
            r_ones4 = res("ones4")
            s.op("dve", lambda: nc.vector.memset(ones4[:], 1.0), [], [r_ones4])
            s.op("dve", lambda: nc.vector.memset(sm2[:, 0:4], 0.5), [], [r_smc])
            POSH4 = sm2[:, 0:4]
            K2 = sm2[:, 16:16 + 64]
            K2M = sm2[:, 84:88]
            identf = cst[:, C_ID:C_ID + 128]

            def allmax4(src, r_src, dst, r_dst, bk, tmpc):
                r_t = res("amx%d" % tmpc)
                mx4 = sm2[0:4, 100 + tmpc * 8:101 + tmpc * 8]
                dg4 = sm2[0:4, 101 + tmpc * 8:105 + tmpc * 8]
                pT = bankf(bk)[0:4, 0:128]
                s.op("pe", lambda: nc.tensor.transpose(out=pT, in_=src, identity=identf), [r_src, r_cst], [r_bk[bk]])
                s.op("dve", lambda: nc.vector.tensor_reduce(out=mx4, in_=pT, axis=AX.X, op=ALU.max), [r_bk[bk]], [r_t])
                ts("dve", dg4, cst[0:4, C_ID:C_ID + 4], mx4, None, ALU.mult, None, [r_t, r_cst], [r_t])
                pB = bankf(bk)[:, 128:132]
                s.op("pe", lambda: nc.tensor.matmul(pB, lhsT=ones4[0:4, :], rhs=dg4, start=True, stop=True), [r_t, r_ones4], [r_bk[bk]])
                copy("dve", dst, pB, [r_bk[bk]], [r_dst])

            NEGH4 = sm[:, 0:4]
            NEGH = sm[:, 0:1]
            TAUALL = sm[:, 8:9]
            r_conv = [res("conv%d" % l) for l in range(DEPTH)]
            for l in range(DEPTH):
                for pr in range(HC // 2):
                    s.dma("pool", wgu_bf[l, pr], wgu_d[l, pr], semres=r_conv[l])
                r_conv[l].w = ("dma", r_conv[l], r_conv[l].cnt)

            xrot = [0]

            def make_xT(tb, xbf, xT, r_xbf, r_xT, bk):
                e = "act" if xrot[0] % 2 == 0 else "dve"
                xrot[0] += 1
                copy(e, xbf, x_sb[:, tb, :], [r_x[tb]], [r_xbf])
                pb = bankb(bk)
                for k in range(KC):
                    tr(pb[:, k * 128:(k + 1) * 128], xbf[:, k * 128:(k + 1) * 128], [r_xbf], [r_bk[bk]])
                copy("act", xT, pb[:, :], [r_bk[bk]], [r_xT])

            def rope(src, r_src, G, half, cosap, sinap, out, r_out, tA, tB, r_tA, r_tB):
                n = G * 2 * half
                s3 = src.rearrange("p (g d) -> p g d", d=half)
                cbb = cosap.unsqueeze(1).to_broadcast([128, 2 * G, half])
                sbb = sinap.unsqueeze(1).to_broadcast([128, 2 * G, half])
                tA3 = tA[:, 0:n].rearrange("p (g d) -> p g d", d=half)
                tB3 = tB[:, 0:n].rearrange("p (g d) -> p g d", d=half)
                tt("dve", tA3, s3, cbb, ALU.mult, list(r_src) + [r_cs], [r_tA])
                tt("dve", tB3, s3, sbb, ALU.mult, list(r_src) + [r_cs], [r_tB])
                tA4 = tA[:, 0:n].rearrange("p (g t d) -> p g t d", t=2, d=half)
                tB4 = tB[:, 0:n].rearrange("p (g t d) -> p g t d", t=2, d=half)
                o4 = out.rearrange("p (g t d) -> p g t d", t=2, d=half)
                tt("pool", o4[:, :, 0, :], tA4[:, :, 0, :], tB4[:, :, 1, :], ALU.subtract, [r_tA, r_tB], [r_out])
                tt("pool", o4[:, :, 1, :], tA4[:, :, 1, :], tB4[:, :, 0, :], ALU.add, [r_tA, r_tB], [r_out])

            lncnt = [0]

            def layer_norm_block(tb, ypair, r_y, gbt, r_gbt):
                par = lncnt[0] % 2
                lncnt[0] += 1
                c0 = 16 + par * 24
                r_s = res("smln%d" % par)
                xb = x_sb[:, tb, :]
                for hh in range(2):
                    stt("dve", xb[:, hh * 512:(hh + 1) * 512], xb[:, hh * 512:(hh + 1) * 512], ALPHA,
                        ypair[hh], ALU.mult, ALU.add, [r_y[hh]], [r_x[tb]])
                st6 = sm[:, c0:c0 + 12].rearrange("p (a b) -> p a b", a=2)
                for hh in range(2):
                    s.op("dve", lambda hh=hh: nc.vector.bn_stats(out=st6[:, hh, :], in_=xb[:, hh * 512:(hh + 1) * 512]), [r_x[tb]], [r_s])
                s.op("dve", lambda: nc.vector.bn_aggr(out=sm[:, c0 + 12:c0 + 14], in_=st6), [r_s], [r_s])
                ts("dve", sm[:, c0 + 14:c0 + 15], sm[:, c0 + 13:c0 + 14], LN_EPS, None, ALU.add, None, [r_s], [r_s])
                tt("pool", sm[:, c0 + 15:c0 + 16], sm[:, c0 + 14:c0 + 15], NEGH, ALU.pow, [r_s, r_smc], [r_s])
                stt("dve", sm[:, c0 + 16:c0 + 17], sm[:, c0 + 12:c0 + 13], -1.0, sm[:, c0 + 15:c0 + 16], ALU.mult, ALU.mult, [r_s], [r_s])
                act(xb, xb, AF.Identity, [r_x[tb], r_s], [r_x[tb]], bias=sm[:, c0 + 16:c0 + 17], scale=sm[:, c0 + 15:c0 + 16])
                tt("dve", xb, xb, gbt[:, 0:1024], ALU.mult, [r_x[tb], r_gbt], [r_x[tb]])
                tt("pool", xb, xb, gbt[:, 1024:2048], ALU.add, [r_x[tb], r_gbt], [r_x[tb]])

            ALPHA = (2.0 * DEPTH) ** 0.25
            ATT_SCALE = 128 ** -0.5
            W_SCALE = (8 ** -0.5) * (64 ** -0.5)

            def chk(name):
                if cfg.stop == name:
                    raise _Stop()

            for sq in range(NSEQ):
              try:
                xv = x_d[sq].rearrange("(b p) d -> p b d", p=128)
                for b in range(NB):
                    s.dma("sp", x_sb[:, b, :], xv[:, b, :], writes=[r_x[b]])
                s.dma("sp", posi[:], pos_d[sq], writes=[r_pos])
                copy("dve", posf[:], posi[:], [r_pos], [r_pos])
                r_ang = res("ang")
                for b in range(NB):
                    A_, K_, Rs, Rc = ang[:, 0, :], ang[:, 1, :], ang[:, 2, :], ang[:, 3, :]
                    ts("dve", A_, invf, posf[:, b:b + 1], None, ALU.mult, None, [r_cst, r_pos], [r_ang])
                    for which, dst in ((0, Rs), (1, Rc)):
                        src = A_
                        if which == 1:
                            ts("dve", Rc, A_, math.pi / 2, None, ALU.add, None, [r_ang], [r_ang])
                            src = Rc
                        ts("dve", K_, src, 1.0 / TWO_PI, MAGIC, ALU.mult, ALU.add, [r_ang], [r_ang])
                        ts("dve", K_, K_, MAGIC, None, ALU.subtract, None, [r_ang], [r_ang])
                        stt("dve", dst, K_, -C1, src, ALU.mult, ALU.add, [r_ang], [r_ang])
                        stt("dve", dst, K_, -C2, dst, ALU.mult, ALU.add, [r_ang], [r_ang])
                        ts("dve", dst, dst, PI_LO, -PI_LO, ALU.min, ALU.max, [r_ang], [r_ang])
                    act(cs_sin[:, b, :], Rs, AF.Sin, [r_ang], [r_cs])
                    act(cs_cos[:, b, :], Rc, AF.Sin, [r_ang], [r_cs])
                chk('setup')
                for l in range(DEPTH):
                    cv = Carver()
                    attTall = cv.take(4 * S).rearrange("p (k t) -> p k t", k=4)
                    PERSIST = cv.off
                    KT = cv.take(4 * S).rearrange("p (h t) -> p h t", h=4)
                    Vg = cv.take(NB * 4 * 130).rearrange("p (b h e) -> p b h e", b=NB, h=4)
                    ikT = cv.take(S)
                    wA = cv.take(KC * 1096)
                    xbf_ = cv.take(1024)
                    xT_ = cv.take(1024)
                    xbf = [xbf_, xbf_]
                    xT = [xT_, xT_]
                    tA = cv.take(1024, F32)
                    tB = cv.take(1024, F32)
                    rbuf = [cv.take(1024, F32) for _ in range(2)]
                    qiq_ = cv.take(1024)
                    qiq = [qiq_, qiq_]
                    ebuf2 = cv.take(S)
                    attb_ = cv.take(512)
                    attb = [attb_, attb_]
                    qT = [cv.take(512).rearrange("p (h t) -> p h t", h=4) for _ in range(3)]
                    iqT = [cv.take(512).rearrange("p (h t) -> p h t", h=4) for _ in range(2)]
                    Ibuf2 = [cv.take(2 * S, F32) for _ in range(2)]
                    jk = cv.take(16)
                    mb_ = cv.take(S)
                    mb = [mb_, mb_]
                    PT2 = cv.take(S)
                    ebuf = cv.take(S)
                    PTs_ = cv.take(S)
                    PTs = [PTs_, PTs_]
                    r_attTall = [res("attTall%d" % b) for b in range(NB)]
                    r_KT = [res("KT%d" % b) for b in range(NB)]
                    r_V = [res("V%d" % b) for b in range(NB)]
                    r_ikT = [res("ikT%d" % b) for b in range(NB)]
                    r_xbf = [res("xbf0"), res("xbf0")]
                    r_xT = [res("xT0"), res("xT0")]
                    r_tA, r_tB = res("tA"), res("tB")
                    r_wA = res("wA")
                    r_I2, r_e = [res("I0"), res("I1")], res("e")
                    r_jk = res("jk")
                    r_mb = [res("mb0"), res("mb0")]
                    r_mbT = [res("mbT0"), res("mbT1")]
                    r_PT = [res("PTs0"), res("PT2")]
                    r_nbq = [res("nbq%d" % i) for i in range(3)]
                    r_k2, r_k2m = res("k2"), res("k2m")
                    r_PTs = [res("PTs0"), res("PTs0")]
                    r_rb = [res("rb0"), res("rb1")]
                    r_qbf = [res("qbf0"), res("qbf0")]
                    r_iqb = [res("iqb0"), res("iqb0")]
                    r_e2, r_jk2, r_sb2, r_jk3 = res("e2"), res("jk2"), res("sb2"), res("jk3")
                    r_attb = [res("attb0"), res("attb0")]
                    r_kb = r_qbf
                    r_ikb = r_iqb
                    r_qT = [res("qT0"), res("qT1"), res("qT2")]
                    r_iqT = [res("iqT0"), res("iqT1")]

                    wKV = wA[:, 0:KC * 1096].rearrange("p (k c) -> p k c", k=KC)
                    r_wKVg = [res("wKVg%d" % i) for i in range(3)]
                    r_wQg = [res("wQg%d" % i) for i in range(2)]
                    for i, (c0, c1) in enumerate(((0, 512), (512, 1024), (1024, 1096))):
                        s.dma("pool", wKV[:, :, c0:c1], wkv_d[l][:, :, c0:c1], writes=[r_wKVg[i]] + r_wQg)
                    for tb in range(NB):
                        s.op("pool", lambda tb=tb: nc.gpsimd.memset(Vg[:, tb, :, 128:130], 1.0), [], [r_V[tb]])

                    def blkB(tb):
                        p = tb % 2
                        b0 = 4 * p
                        make_xT(tb, xbf[p], xT[p], r_xbf[p], r_xT[p], b0)
                        xt, r_xt = xT[p], r_xT[p]
                        mm(bankf(b0 + 1), [(xt[:, k * 128:(k + 1) * 128], wKV[:, k, 0:512]) for k in range(KC)], [r_xt, r_wKVg[0]], [r_bk[b0 + 1]])
                        mm(bankf(b0 + 2), [(xt[:, k * 128:(k + 1) * 128], wKV[:, k, 512:1024]) for k in range(KC)], [r_xt, r_wKVg[1]], [r_bk[b0 + 2]])
                        mm(bankf(b0 + 3)[:, 0:72], [(xt[:, k * 128:(k + 1) * 128], wKV[:, k, 1024:1096]) for k in range(KC)], [r_xt, r_wKVg[2]], [r_bk[b0 + 3]])
                        yield
                        kbf = qiq[p][:, 0:512]
                        ikb = qiq[p][:, 512:640]
                        rope(bankf(b0 + 1), [r_bk[b0 + 1]], 4, 64, cs_cos[:, tb, :], cs_sin[:, tb, :], kbf, r_kb[p], tA, tB, r_tA, r_tB)
                        copy("act", Vg[:, tb, :, 0:128], bankf(b0 + 2).rearrange("p (h e) -> p h e", h=4), [r_bk[b0 + 2]], [r_V[tb]])
                        for h in range(4):
                            s.op("act", lambda h=h: nc.scalar.activation(out=jk[:, 2:3].to_broadcast([128, 128]), in_=kbf[:, h * 128:(h + 1) * 128], func=AF.Square, accum_out=K2[:, tb * 4 + h:tb * 4 + h + 1]), [r_kb[p]], [r_k2, r_jk3])
                        rope(bankf(b0 + 3)[:, 0:64], [r_bk[b0 + 3]], 1, 32, cs_cos[:, tb, 0:64:2], cs_sin[:, tb, 0:64:2], ikb[:, 0:64], r_ikb[p], tA, tB, r_tA, r_tB)
                        copy("pool", ikb[:, 64:128], ikb[:, 0:64], [r_ikb[p]], [r_ikb[p]])
                        ts("dve", wsb[:, tb, :], bankf(b0 + 3)[:, 64:72], W_SCALE, None, ALU.mult, None, [r_bk[b0 + 3]], [r_wsb])
                        yield
                        pb = bankb(b0)
                        for h in range(4):
                            tr(pb[:, h * 128:(h + 1) * 128], kbf[:, h * 128:(h + 1) * 128], [r_kb[p]], [r_bk[b0]])
                        tr(pb[:, 512:640], ikb, [r_ikb[p]], [r_bk[b0]])
                        copy("act", KT[:, :, tb * 128:(tb + 1) * 128], pb[:, 0:512].rearrange("p (h t) -> p h t", h=4), [r_bk[b0]], [r_KT[tb]])
                        copy("act", ikT[:, tb * 128:(tb + 1) * 128], pb[:, 512:640], [r_bk[b0]], [r_ikT[tb]])

                    pipeline([blkB(tb) for tb in range(NB)], 2)
                    K2P = sm2[:, 88:92]
                    s.op("dve", lambda: nc.vector.tensor_reduce(out=K2P, in_=K2[:, 0:NB * 4].rearrange("p (b h) -> p h b", h=4), axis=AX.X, op=ALU.max), [r_k2], [r_k2m])
                    allmax4(K2P, r_k2m, K2M, r_k2m, 0, 0)
                    chk('B')
                    wQ = wA[:, 0:KC * 1024].rearrange("p (k c) -> p k c", k=KC)
                    for i in range(2):
                        s.dma("pool", wQ[:, :, i * 512:(i + 1) * 512], wq_d[l][:, :, i * 512:(i + 1) * 512], writes=[r_wQg[i]] + r_wKVg)
                    mb_free = [True, True]

                    def frontC(tb):
                        p = tb % 2
                        q3 = tb % 3
                        Ib = Ibuf2[p]
                        r_I = r_I2[p]
                        L = (tb + 1) * 128
                        r_sb = res("smbis%d" % p)
                        r_sa = res("smatt%d" % p)
                        cB = 64 + p * 16
                        cA = 96 + p * 16
                        stp = steps[:, p, :]
                        make_xT(tb, xbf[p], xT[p], r_xbf[p], r_xT[p], 0)
                        xt, r_xt = xT[p], r_xT[p]
                        mm(bankf(1), [(xt[:, k * 128:(k + 1) * 128], wQ[:, k, 0:512]) for k in range(KC)], [r_xt, r_wQg[0]], [r_bk[1]])
                        mm(bankf(2), [(xt[:, k * 128:(k + 1) * 128], wQ[:, k, 512:1024]) for k in range(KC)], [r_xt, r_wQg[1]], [r_bk[2]])
                        qbf = qiq[p][:, 0:512]
                        iqb = qiq[p][:, 512:1024]
                        rope(bankf(1), [r_bk[1]], 4, 64, cs_cos[:, tb, :], cs_sin[:, tb, :], qbf, r_qbf[p], tA, tB, r_tA, r_tB)
                        rope(bankf(2), [r_bk[2]], 8, 32, cs_cos[:, tb, 0:64:2], cs_sin[:, tb, 0:64:2], iqb, r_iqb[p], tA, tB, r_tA, r_tB)
                        pb = bankb(0)
                        for h in range(4):
                            tr(pb[:, h * 128:(h + 1) * 128], qbf[:, h * 128:(h + 1) * 128], [r_qbf[p]], [r_bk[0]])
                        for h in range(4):
                            tr(pb[:, 512 + h * 128:512 + (h + 1) * 128], iqb[:, h * 128:(h + 1) * 128], [r_iqb[p]], [r_bk[0]])
                        copy("act", qT[q3], pb[:, 0:512].rearrange("p (h t) -> p h t", h=4), [r_bk[0]], [r_qT[q3]])
                        copy("act", iqT[p], pb[:, 512:1024].rearrange("p (h t) -> p h t", h=4), [r_bk[0]], [r_iqT[p]])
                        Q2 = sm2[:, 120 + p * 4:124 + p * 4]
                        Q2M = sm2[:, 128 + p * 4:132 + p * 4]
                        NBQ = sm2[:, 136 + q3 * 4:140 + q3 * 4]
                        r_q2 = res("q2_%d" % p)
                        for h in range(4):
                            s.op("act", lambda h=h: nc.scalar.activation(out=jk[:, 2:3].to_broadcast([128, 128]), in_=qbf[:, h * 128:(h + 1) * 128], func=AF.Square, accum_out=Q2[:, h:h + 1]), [r_qbf[p]], [r_q2, r_jk3])
                        allmax4(Q2, r_q2, Q2M, r_q2, 2, 1 + p)
                        tt("dve", Q2M, Q2M, K2M, ALU.mult, [r_q2, r_k2m], [r_q2])
                        tt("pool", Q2M, Q2M, POSH4, ALU.pow, [r_q2, r_smc], [r_q2])
                        ts("dve", NBQ, Q2M, -ATT_SCALE, None, ALU.mult, None, [r_q2], [r_nbq[q3]])
                        yield
                        nch = (L + 511) // 512
                        cnt = 0
                        for c in range(nch):
                            n = min(512, L - 512 * c)
                            kres = r_ikT[4 * c:4 * c + (n + 127) // 128]
                            for h in range(8):
                                m_, base = h // 2, 64 * (h % 2)
                                bk = 1 + cnt % 2
                                rb, r_rb_ = rbuf[cnt % 2], r_rb[cnt % 2]
                                cnt += 1
                                mm(bankf(bk)[:, 0:n], [(iqT[p][base:base + 64, m_, :], ikT[base:base + 64, c * 512:c * 512 + n])], [r_iqT[p]] + kres, [r_bk[bk]])
                                act(rb[:, 0:n], bankf(bk)[:, 0:n], AF.Relu, [r_bk[bk]], [r_rb_])
                                if h == 0:
                                    ts("dve", Ib[:, c * 512:c * 512 + n], rb[:, 0:n], wsb[:, tb, 0:1], None, ALU.mult, None, [r_rb_, r_wsb], [r_I])
                                else:
                                    stt("dve", Ib[:, c * 512:c * 512 + n], rb[:, 0:n], wsb[:, tb, h:h + 1], Ib[:, c * 512:c * 512 + n], ALU.mult, ALU.add, [r_rb_, r_wsb], [r_I])
                                if h == 3:
                                    yield
                            yield
                        tt("dve", Ib[:, tb * 128:L], Ib[:, tb * 128:L], cb, ALU.add, [r_cst], [r_I])
                        if L > TOPK:
                            MX, MN, RG, T, CNT, DD = [sm[:, cB + i:cB + i + 1] for i in range(6)]
                            s.op("dve", lambda: nc.vector.tensor_reduce(out=MX, in_=Ib[:, 0:L], axis=AX.X, op=ALU.max), [r_I], [r_sb])
                            s.op("dve", lambda: nc.vector.tensor_reduce(out=MN, in_=Ib[:, 0:L - 128], axis=AX.X, op=ALU.min), [r_I], [r_sb])
                            tt("dve", RG, MX, MN, ALU.subtract, [r_sb], [r_sb])
                            ts("dve", RG, RG, 1.0 + 1.0 / 1024, None, ALU.mult, None, [r_sb], [r_sb])
                            tt("dve", T, MX, RG, ALU.subtract, [r_sb], [r_sb])
                            ts("dve", stp[:, 0:NITER + 2], p2[:, 0:NITER + 2], RG, None, ALU.mult, None, [r_sb, r_cst], [r_sb])
                            stt("dve", T, stp[:, 1:2], 1.0, T, ALU.mult, ALU.add, [r_sb], [r_sb])
                            if p == 1:
                                ts("dve", nstp[:, 0:NITER + 2], stp[:, 0:NITER + 2], -0.5, None, ALU.mult, None, [r_sb], [r_sb])
                            junk = jk[:, 0:1].to_broadcast([128, L])
                            if p == 0:
                                for i in range(NITER):
                                    ts("dve", junk, Ib[:, 0:L], T, None, ALU.is_gt, ALU.add, [r_I, r_sb], [r_jk], accum=CNT)
                                    s.op("dve", lambda: nc.vector.tensor_scalar(out=DD, in0=CNT, scalar1=TOPK - 0.5, scalar2=-0.5, op0=ALU.is_gt, op1=ALU.add), [r_jk], [r_sb])
                                    stt("dve", T, DD, stp[:, i + 1:i + 2], T, ALU.mult, ALU.add, [r_sb], [r_sb])
                                    if i % 2 == 1:
                                        yield
                            else:
                                NT = sm[:, cB + 6:cB + 7]
                                DS = sm[:, cB + 7:cB + 8]
                                junk2 = jk[:, 1:2].to_broadcast([128, L])
                                ts("dve", NT, T, -1.0, None, ALU.mult, None, [r_sb], [r_sb])
                                for i in range(NITER):
                                    s.op("act", lambda: nc.scalar.activation(out=junk2, in_=Ib[:, 0:L], func=AF.Sign, bias=NT, scale=1.0, accum_out=CNT), [r_I, r_sb], [r_jk2])
                                    s.op("act", lambda: nc.scalar.activation(out=DS, in_=CNT, func=AF.Sign, bias=BIASC[L], scale=1.0), [r_jk2, r_smc], [r_sb2])
                                    s.op("act", lambda i=i: nc.scalar.activation(out=NT, in_=DS, func=AF.Identity, bias=NT, scale=nstp[:, i + 1:i + 2]), [r_sb2, r_sb], [r_sb])
                                    if i % 2 == 1:
                                        yield
                                ts("dve", T, NT, -1.0, None, ALU.mult, None, [r_sb], [r_sb])
                            tt("dve", T, T, stp[:, NITER + 1:NITER + 2], ALU.subtract, [r_sb], [r_sb])
                            TAU = T
                            r_tau = r_sb
                        else:
                            TAU = TAUALL
                            r_tau = r_smc
                        while not mb_free[p]:
                            yield
                        mb_free[p] = False
                        ts("dve", mb[p][:, 0:L], Ib[:, 0:L], TAU, MASK_NEG, ALU.is_le, ALU.mult, [r_I, r_tau], [r_mb[p]])
                        mbT = [ebuf, ebuf2][p]
                        pb0 = bankb(0)
                        for half in range((tb + 8) // 8):
                            nblk = min(8, tb + 1 - 8 * half)
                            for j in range(nblk):
                                sbk = half * 8 + j
                                tr(pb0[:, j * 128:(j + 1) * 128], mb[p][:, sbk * 128:(sbk + 1) * 128], [r_mb[p]], [r_bk[0]])
                            copy("act", mbT[:, half * 1024:half * 1024 + nblk * 128], pb0[:, 0:nblk * 128], [r_bk[0]], [r_mbT[p]])

                    def backC(tb):
                        p = tb % 2
                        q3 = tb % 3
                        L = (tb + 1) * 128
                        nch = (L + 511) // 512
                        cA = 96 + p * 16
                        mbT = [ebuf, ebuf2][p]
                        PTb = [PTs[0], PT2]
                        NBQ = sm2[:, 136 + q3 * 4:140 + q3 * 4]

                        def hbank(h):
                            return 4 + 2 * (h % 2) if nch <= 2 else 4

                        def qk(h):
                            hb = hbank(h)
                            for sbk in range(tb + 1):
                                bkk = hb + sbk // 4
                                mm(ps[:, hb * 512 + sbk * 128:hb * 512 + (sbk + 1) * 128],
                                   [(KT[:, h, sbk * 128:(sbk + 1) * 128], qT[q3][:, h, :]), (ident[:], mbT[:, sbk * 128:(sbk + 1) * 128])],
                                   [r_qT[q3], r_mbT[p], r_ident, r_KT[sbk]], [r_bk[bkk]])

                        def sexp(h):
                            hb = hbank(h)
                            sc_ = ps[:, hb * 512:hb * 512 + L]
                            rsc = r_bk[hb:hb + nch]
                            act(PTb[h % 2][:, 0:L], sc_, AF.Exp, rsc + [r_nbq[q3]], [r_PT[h % 2]], bias=NBQ[:, h:h + 1], scale=ATT_SCALE)

                        def pv(h):
                            pts = PTb[h % 2]
                            RDa = sm[:, cA + 6 + (h % 2):cA + 7 + (h % 2)]
                            r_sr = res("smrd%d_%d" % (p, h % 2))
                            po = bankf(3)[:, 0:129]
                            mm(po, [(pts[:, sbk * 128:(sbk + 1) * 128], Vg[:, sbk, h, 0:129]) for sbk in range(tb + 1)],
                               [r_PT[h % 2]] + r_V[0:tb + 1], [r_bk[3]])
                            s.op("dve", lambda: nc.vector.reciprocal(out=RDa, in_=bankf(3)[:, 128:129]), [r_bk[3]], [r_sr])
                            ts("dve", attb[p][:, h * 128:(h + 1) * 128], bankf(3)[:, 0:128], RDa, None, ALU.mult, None, [r_bk[3], r_sr], [r_attb[p]])

                        qk(0)
                        sexp(0)
                        yield
                        for h in range(4):
                            if h + 1 < 4:
                                qk(h + 1)
                                sexp(h + 1)
                            pv(h)
                            yield
                        pbb = bankb(3)
                        for h in range(4):
                            tr(pbb[:, h * 128:(h + 1) * 128], attb[p][:, h * 128:(h + 1) * 128], [r_attb[p]], [r_bk[3]])
                        copy("act", attTall[:, :, tb * 128:(tb + 1) * 128], pbb[:, 0:512].rearrange("p (h t) -> p h t", h=4), [r_bk[3]], [r_attTall[tb]])

                    fronts = []
                    ready = []
                    back = None
                    done_back = set()
                    next_tb = 0
                    while next_tb < NB or fronts or ready or back is not None:
                        while (next_tb < NB and len(fronts) < 2
                               and all(t % 2 != next_tb % 2 for t, _ in fronts)
                               and (next_tb < 3 or (next_tb - 3) in done_back)):
                            fronts.append((next_tb, frontC(next_tb)))
                            next_tb += 1
                        if back is None and ready:
                            tbb = ready.pop(0)
                            back = (tbb, backC(tbb))
                        if back is not None:
                            try:
                                next(back[1])
                            except StopIteration:
                                mb_free[back[0] % 2] = True
                                done_back.add(back[0])
                                back = None
                        for item in list(fronts):
                            try:
                                next(item[1])
                            except StopIteration:
                                fronts.remove(item)
                                ready.append(item[0])
                    chk('C')
                    s.barrier(list(R.values()))
                    cv = Carver(PERSIST)
                    mixT = cv.take(4 * S).rearrange("p (k t) -> p k t", k=4)
                    PERSIST2 = cv.off
                    wR = cv.take(KC * 2048).rearrange("p (k c) -> p k c", k=KC)
                    xbf = [cv.take(1024) for _ in range(3)]
                    xT = [cv.take(1024) for _ in range(3)]
                    tA = cv.take(2048, F32)
                    tB = cv.take(2048, F32)
                    qk_r = cv.take(2048, F32)
                    qkb = [cv.take(1024) for _ in range(3)]
                    kdec = [cv.take(512) for _ in range(3)]
                    vbf = [cv.take(512) for _ in range(3)]
                    gsl = [cv.take(1024, F32) for _ in range(3)]
                    qkT = [cv.take(1024) for _ in range(3)]
                    stb = [cv.take(512) for _ in range(3)]
                    yn = cv.take(1024, F32)
                    y3 = [cv.take(512) for _ in range(3)]
                    state_f = cv.take(1024, F32)
                    state_bf = cv.take(512)
                    gbA = cv.take(1024, F32)
                    r_mixT = [res("mixT%d" % b) for b in range(NB)]
                    r_wR, r_gbA = res("wR"), res("gbA")
                    r_qk, r_yn = res("qk_r"), res("yn")
                    r_qkb = [res("qkb%d" % i) for i in range(3)]
                    r_kdec = [res("kdec%d" % i) for i in range(3)]
                    r_vbf = [res("vbf%d" % i) for i in range(3)]
                    r_gsl = [res("gsl%d" % i) for i in range(3)]
                    r_qkT = [res("qkT%d" % i) for i in range(3)]
                    r_stb = [res("stb%d" % i) for i in range(3)]
                    r_y3 = [res("y3%d" % i) for i in range(3)]
                    r_stf, r_stbf = res("state_f"), res("state_bf")
                    r_xbfA = [res("xbfA%d" % i) for i in range(3)]
                    r_xTA = [res("xTA%d" % i) for i in range(3)]
                    r_tA2, r_tB2 = res("tA2"), res("tB2")
                    s.dma("sp", gbA[:, 0:512], gret_d[l], writes=[r_gbA])
                    r_wRg = [res("wRg%d" % i) for i in range(4)]
                    for i in range(4):
                        s.dma("pool", wR[:, :, i * 512:(i + 1) * 512], wr_d[l][:, :, i * 512:(i + 1) * 512], writes=[r_wRg[i]])
                    s.op("dve", lambda: nc.vector.memset(state_f, 0.0), [], [r_stf])
                    s.op("pool", lambda: nc.gpsimd.memset(state_bf, 0.0), [], [r_stbf])
                    dct = cst[:, C_RT + 16:C_RT + 20].unsqueeze(2).to_broadcast([128, 4, 128])
                    kps = cst[:, C_RT + 4:C_RT + 8].unsqueeze(2).to_broadcast([128, 4, 128])
                    dks = cst[:, C_RT + 8:C_RT + 12].unsqueeze(2).to_broadcast([128, 4, 128])

                    def blkA(tb):
                        p = tb % 3
                        r_sg = res("smgn%d" % (tb % 2))
                        cG = 128 + (tb % 2) * 48
                        make_xT(tb, xbf[p], xT[p], r_xbfA[p], r_xTA[p], 0)
                        xt, r_xt = xT[p], r_xTA[p]
                        for j in range(4):
                            mm(bankf(1 + j), [(xt[:, k * 128:(k + 1) * 128], wR[:, k, j * 512:(j + 1) * 512]) for k in range(KC)], [r_xt, r_wRg[j]], [r_bk[1 + j]])
                        yield
                        rope(bankf(1, 2), [r_bk[1], r_bk[2]], 8, 64, cs_cos[:, tb, :], cs_sin[:, tb, :], qk_r, r_qk, tA, tB, r_tA2, r_tB2)
                        copy("act", vbf[p], bankf(3), [r_bk[3]], [r_vbf[p]])
                        act(gsl[p], bankf(4), AF.Silu, [r_bk[4]], [r_gsl[p]])
                        copy("pool", qkb[p][:, 0:512], qk_r[:, 0:512], [r_qk], [r_qkb[p]])
                        k3 = qk_r[:, 512:1024].rearrange("p (h d) -> p h d", h=4)
                        tt("dve", qkb[p][:, 512:1024].rearrange("p (h d) -> p h d", h=4), k3, kps, ALU.mult, [r_qk, r_cst], [r_qkb[p]])
                        tt("pool", kdec[p].rearrange("p (h d) -> p h d", h=4), k3, dks, ALU.mult, [r_qk, r_cst], [r_kdec[p]])
                        pb = bankb(0)
                        for j in range(8):
                            tr(pb[:, j * 128:(j + 1) * 128], qkb[p][:, j * 128:(j + 1) * 128], [r_qkb[p]], [r_bk[0]])
                        copy("act", qkT[p], pb[:, :], [r_bk[0]], [r_qkT[p]])
                        yield
                        for h in range(4):
                            mm(bankf(5)[:, h * 128:(h + 1) * 128], [(qkT[p][:, 512 + h * 128:512 + (h + 1) * 128], qkT[p][:, h * 128:(h + 1) * 128])], [r_qkT[p]], [r_bk[5]])
                        tt("dve", stb[p].rearrange("p (h d) -> p h d", h=4), bankf(5).rearrange("p (h d) -> p h d", h=4),
                           tri.unsqueeze(1).to_broadcast([128, 4, 128]), ALU.mult, [r_bk[5], r_cst], [r_stb[p]])
                        for h in range(4):
                            sl = slice(h * 128, (h + 1) * 128)
                            mm(bankf(6)[:, sl], [(stb[p][:, sl], vbf[p][:, sl]), (qkT[p][:, sl], state_bf[:, sl])], [r_stb[p], r_vbf[p], r_qkT[p], r_stbf], [r_bk[6]])
                        for h in range(4):
                            sl = slice(h * 128, (h + 1) * 128)
                            mm(bankf(7)[:, sl], [(kdec[p][:, sl], vbf[p][:, sl])], [r_kdec[p], r_vbf[p]], [r_bk[7]])
                        tt("pool", state_f.rearrange("p (h d) -> p h d", h=4), state_f.rearrange("p (h d) -> p h d", h=4), dct, ALU.mult, [r_cst], [r_stf])
                        tt("dve", state_f, state_f, bankf(7), ALU.add, [r_bk[7]], [r_stf])
                        copy("pool", state_bf, state_f, [r_stf], [r_stbf])
                        yield
                        st6 = sm[:, cG:cG + 24].rearrange("p (h a) -> p h a", h=4)
                        mv = sm[:, cG + 24:cG + 32].rearrange("p (h a) -> p h a", h=4)
                        for h in range(4):
                            s.op("dve", lambda h=h: nc.vector.bn_stats(out=st6[:, h, :], in_=bankf(6)[:, h * 128:(h + 1) * 128]), [r_bk[6]], [r_sg])
                        for h in range(4):
                            s.op("dve", lambda h=h: nc.vector.bn_aggr(out=mv[:, h, :], in_=st6[:, h, :]), [r_sg], [r_sg])
                        A4 = sm[:, cG + 32:cG + 36]
                        RS4 = sm[:, cG + 36:cG + 40]
                        SC4 = sm[:, cG + 40:cG + 44]
                        NB4 = sm[:, cG + 44:cG + 48]
                        tt("dve", A4, mv[:, :, 1], cst[:, C_RT + 12:C_RT + 16], ALU.mult, [r_sg, r_cst], [r_sg])
                        ts("dve", A4, A4, LN_EPS, None, ALU.add, None, [r_sg], [r_sg])
                        tt("pool", RS4, A4, NEGH4, ALU.pow, [r_sg, r_smc], [r_sg])
                        tt("dve", SC4, RS4, cst[:, C_RT:C_RT + 4], ALU.mult, [r_sg, r_cst], [r_sg])
                        stt("dve", NB4, mv[:, :, 0], -1.0, SC4, ALU.mult, ALU.mult, [r_sg], [r_sg])
                        for h in range(4):
                            sl = slice(h * 128, (h + 1) * 128)
                            act(yn[:, sl], bankf(6)[:, sl], AF.Identity, [r_bk[6], r_sg], [r_yn], bias=NB4[:, h:h + 1], scale=SC4[:, h:h + 1])
                        tt("dve", yn, yn, gbA[:, 0:512], ALU.mult, [r_yn, r_gbA], [r_yn])
                        tt("pool", y3[p], yn, gsl[p], ALU.mult, [r_yn, r_gsl[p]], [r_y3[p]])
                        pb = bankb(0)
                        for h in range(4):
                            tr(pb[:, h * 128:(h + 1) * 128], y3[p][:, h * 128:(h + 1) * 128], [r_y3[p]], [r_bk[0]])
                        copy("act", mixT[:, :, tb * 128:(tb + 1) * 128], pb[:, 0:512].rearrange("p (h t) -> p h t", h=4), [r_bk[0]], [r_mixT[tb]])

                    pipeline([blkA(tb) for tb in range(NB)], 3)
                    chk('A')
                    s.barrier(list(R.values()))
                    cv = Carver(PERSIST2)
                    wO = cv.take(KC * 1024).rearrange("p (k c) -> p k c", k=KC)
                    gbm = cv.take(4096, F32)
                    r_wO, r_gbm = res("wO"), res("gbm")
                    r_wOg = [res("wOg%d" % i) for i in range(2)]
                    for i in range(2):
                        s.dma("pool", wO[:, :, i * 512:(i + 1) * 512], wout_d[l][:, :, i * 512:(i + 1) * 512], writes=[r_wOg[i]])
                    s.dma("sp", gbm[:, :], gmix_d[l], writes=[r_gbm])
                    for tb in range(NB):
                        b0 = 2 * (tb % 2)
                        for hh in range(2):
                            pairs = [(mixT[:, k, tb * 128:(tb + 1) * 128], wO[:, k, hh * 512:(hh + 1) * 512]) for k in range(4)]
                            pairs += [(attTall[:, k, tb * 128:(tb + 1) * 128], wO[:, 4 + k, hh * 512:(hh + 1) * 512]) for k in range(4)]
                            mm(bankf(b0 + hh), pairs, [r_mixT[tb], r_attTall[tb], r_wOg[hh]], [r_bk[b0 + hh]])
                        layer_norm_block(tb, [bankf(b0), bankf(b0 + 1)], [r_bk[b0], r_bk[b0 + 1]], gbm, r_gbm)
                    chk('C2')
                    s.barrier(list(R.values()))
                    cv = Carver()
                    NG = S // GT
                    nblk = GT // 128
                    ncol = GT // 512
                    x1T = [cv.take(KC * GT).rearrange("p (k t) -> p k t", k=KC) for _ in range(2)]
                    hT = cv.take(HC * GT).rearrange("p (c t) -> p c t", c=HC)
                    wD = cv.take(HC * 1024).rearrange("p (c n) -> p c n", c=HC)
                    wG = [cv.take(KC * 512).rearrange("p (k c) -> p k c", k=KC) for _ in range(3)]
                    xbf2 = [cv.take(1024) for _ in range(2)]
                    sgb = [cv.take(1024, F32) for _ in range(2)]
                    gbf = cv.take(4096, F32)
                    r_x1T = [[res("x1T%d_%d" % (q, i)) for i in range(nblk)] for q in range(2)]
                    r_hT = [res("hT%d_%d" % (c, t)) for c in range(HC) for t in range(ncol)]
                    r_wD = [res("wD%d" % i) for i in range(4)]
                    r_wG = [res("wG%d" % i) for i in range(3)]
                    r_xbf2 = [res("xbf2_0"), res("xbf2_1")]
                    r_sgl = [res("sg0"), res("sg1")]
                    r_gbf = res("gbf")
                    wdq = [(0, 6), (6, 12), (12, 18), (18, 22)]
                    for i, (c0, c1) in enumerate(wdq):
                        s.dma("pool", wD[:, c0:c1, :], wd_d[l][:, c0:c1, :], writes=[r_wD[i]])
                    s.dma("sp", gbf[:, :], gffn_d[l], writes=[r_gbf])
                    NP = HC // 2
                    gu = [0]

                    def load_gu(pr):
                        slot = gu[0] % 3
                        gu[0] += 1
                        s.dma("sp", wG[slot].rearrange("p k c -> p (k c)"), wgu_bf[l, pr], reads=[r_conv[l]], writes=[r_wG[slot]])
                        return slot

                    def prep_x1T(g):
                        q = g % 2
                        for j in range(nblk):
                            tb = g * nblk + j
                            i2 = j % 2
                            copy("act" if j % 2 == 0 else "dve", xbf2[i2], x_sb[:, tb, :], [r_x[tb]], [r_xbf2[i2]])
                            pb = bankb(i2)
                            for k in range(KC):
                                tr(pb[:, k * 128:(k + 1) * 128], xbf2[i2][:, k * 128:(k + 1) * 128], [r_xbf2[i2]], [r_bk[i2]])
                            copy("act", x1T[q][:, :, j * 128:(j + 1) * 128], pb[:, :].rearrange("p (k t) -> p k t", k=KC), [r_bk[i2]], [r_x1T[q][j]])

                    prep_x1T(0)
                    for g in range(NG):
                        q = g % 2
                        slots = [load_gu(0), load_gu(1)]
                        for pr in range(NP):
                            if pr + 2 < NP:
                                slots.append(load_gu(pr + 2))
                            slot = slots[pr]
                            for ci in range(2):
                                c = 2 * pr + ci
                                for th in range(ncol):
                                    xr = r_x1T[q][4 * th:4 * th + 4]
                                    qq = (c * ncol + th) % 2
                                    bG, bU = 2 + 2 * qq, 3 + 2 * qq
                                    mm(bankf(bG), [(wG[slot][:, k, ci * 256:ci * 256 + 128], x1T[q][:, k, th * 512:(th + 1) * 512]) for k in range(KC)], xr + [r_wG[slot]], [r_bk[bG]])
                                    mm(bankf(bU), [(wG[slot][:, k, ci * 256 + 128:ci * 256 + 256], x1T[q][:, k, th * 512:(th + 1) * 512]) for k in range(KC)], xr + [r_wG[slot]], [r_bk[bU]])
                                    act(sgb[qq], bankf(bG), AF.Silu, [r_bk[bG]], [r_sgl[qq]])
                                    tt("dve", hT[:, c, th * 512:(th + 1) * 512], sgb[qq], bankf(bU), ALU.mult, [r_sgl[qq], r_bk[bU]], [r_hT[c * ncol + th]])
                        if g + 1 < NG:
                            prep_x1T(g + 1)
                        for j in range(nblk):
                            tb = g * nblk + j
                            th = j // 4
                            hres = [r_hT[c * ncol + th] for c in range(HC)]
                            for hh in range(2):
                                mm(bankf(6 + hh), [(hT[:, c, j * 128:(j + 1) * 128], wD[:, c, hh * 512:(hh + 1) * 512]) for c in range(HC)],
                                   hres + r_wD, [r_bk[6 + hh]])
                            layer_norm_block(tb, [bankf(6), bankf(7)], [r_bk[6], r_bk[7]], gbf, r_gbf)
                    s.barrier(list(R.values()))
              except _Stop:
                pass
              if True:
                yv = y_d[sq].rearrange("(b p) d -> p b d", p=128)
                for b in range(NB):
                    s.dma("sp", yv[:, b, :], x_sb[:, b, :], reads=[r_x[b]], semres=r_stb_[b])
            for b in range(NB):
                s._wait("sp", ("dma", r_stb_[b], r_stb_[b].cnt))

        s1 = Sched(nc, es, None)
        emit(s1)
        s2 = Sched(nc, es, s1.needed)
        emit(s2)
    return nc


_CACHE = {}


def _prep_inputs(cfg, x, positions, w_in, ret_gn_gain, w_out, ln_mix_gain, ln_mix_bias,
                 w_gate_up, w_down, ln_ffn_gain, ln_ffn_bias, ncores):
    DEPTH = cfg.DEPTH
    f = lambda a: np.ascontiguousarray(np.asarray(a, dtype=np.float32))
    g_ret = np.ascontiguousarray(np.broadcast_to(f(ret_gn_gain)[:, None, :], (DEPTH, 128, 512)))
    g_mix = np.ascontiguousarray(np.broadcast_to(
        np.concatenate([f(ln_mix_gain), f(ln_mix_bias)], axis=1)[:, None, :], (DEPTH, 128, 2048)))
    g_ffn = np.ascontiguousarray(np.broadcast_to(
        np.concatenate([f(ln_ffn_gain), f(ln_ffn_bias)], axis=1)[:, None, :], (DEPTH, 128, 2048)))
    cst = make_consts(cfg)
    w_in = f(w_in)

    def tile_k(w):
        return np.ascontiguousarray(w.reshape(DEPTH, KC, 128, w.shape[-1]).transpose(0, 2, 1, 3))

    w_kv = tile_k(np.concatenate([w_in[:, :, OFF_AK:OFF_AK + 1024], w_in[:, :, OFF_IK:OFF_IK + 72]], axis=2))
    w_q = tile_k(np.concatenate([w_in[:, :, OFF_AQ:OFF_AQ + 512], w_in[:, :, OFF_IQ:OFF_IQ + 512]], axis=2))
    w_r = tile_k(w_in[:, :, 0:2048])
    w_o = tile_k(f(w_out))
    wgu = f(w_gate_up)
    g4 = wgu[:, :, 0:HID].reshape(DEPTH, KC, 128, HC // 2, 2, 128)
    u4 = wgu[:, :, HID:2 * HID].reshape(DEPTH, KC, 128, HC // 2, 2, 128)
    gu = np.stack([g4, u4], axis=5)
    gu = gu.transpose(0, 3, 2, 1, 4, 5, 6)
    w_gu = np.ascontiguousarray(gu.reshape(DEPTH, HC // 2, 128, KC * 512))
    w_d = np.ascontiguousarray(f(w_down).reshape(DEPTH, HC, 128, 1024).transpose(0, 2, 1, 3))
    x = f(x)
    pos = np.asarray(positions).astype(np.int32)
    shared = {"w_kv": w_kv, "w_q": w_q, "w_r": w_r, "w_o": w_o, "w_gu": w_gu, "w_d": w_d,
              "g_ret": g_ret, "g_mix": g_mix, "g_ffn": g_ffn, "cst": cst}
    maps = []
    for c in range(ncores):
        xs = x[c * cfg.NSEQ:(c + 1) * cfg.NSEQ]
        ps_ = pos[c * cfg.NSEQ:(c + 1) * cfg.NSEQ]
        ps_ = np.ascontiguousarray(ps_.reshape(cfg.NSEQ, cfg.NB, 128).transpose(0, 2, 1))
        m = dict(shared)
        m["x"] = np.ascontiguousarray(xs)
        m["pos"] = ps_
        maps.append(m)
    return maps


def run(cfg, ncores, **inputs):
    key = (cfg.S, cfg.NSEQ, cfg.DEPTH, cfg.NITER, cfg.stop)
    if key not in _CACHE:
        _CACHE[key] = build(cfg)
    nc = _CACHE[key]
    maps = _prep_inputs(cfg, ncores=ncores, **inputs)
    res = run_bass_kernel_spmd(nc, maps, core_ids=list(range(ncores)))
    return np.concatenate([r["y"] for r in res.results], axis=0)


def kernel(x, positions, w_in, ret_gn_gain, w_out, ln_mix_gain, ln_mix_bias,
           w_gate_up, w_down, ln_ffn_gain, ln_ffn_bias):
    cfg = Cfg(S=2048, NSEQ=2, DEPTH=2, NITER=17)
    out = run(cfg, 8, x=x, positions=positions, w_in=w_in, ret_gn_gain=ret_gn_gain, w_out=w_out,
              ln_mix_gain=ln_mix_gain, ln_mix_bias=ln_mix_bias, w_gate_up=w_gate_up, w_down=w_down,
              ln_ffn_gain=ln_ffn_gain, ln_ffn_bias=ln_ffn_bias)
    return out.astype(np.float32)
```

```python
import math
from contextlib import ExitStack
import numpy as np
import concourse.bass as bass
import concourse.mybir as mybir
from concourse.bass_utils import run_bass_kernel_spmd

F32 = mybir.dt.float32
BF16 = mybir.dt.bfloat16
I32 = mybir.dt.int32
ALU = mybir.AluOpType
AF = mybir.ActivationFunctionType
AX = mybir.AxisListType

D = 1024
KC = 8
HID = 2816
HC = 22
IN_COLS = 4168
OFF_RQ, OFF_RK, OFF_RV, OFF_RG, OFF_AQ, OFF_AK, OFF_AV, OFF_IQ, OFF_IK, OFF_IW = (
    0, 512, 1024, 1536, 2048, 2560, 3072, 3584, 4096, 4160)
LN_EPS = 1e-5
MAGIC = 12582912.0
TWO_PI = 2.0 * math.pi
C1 = 6.28125
C2 = TWO_PI - C1
PI_LO = 3.1415925
NEG_BIG = -1.0e30
MASK_NEG = -30000.0

C_ID, C_TRI, C_CB, C_RT, C_INVF, C_P2, C_END = 0, 128, 256, 384, 404, 468, 500


class Res:
    __slots__ = ("name", "w", "r", "sem", "cnt")

    def __init__(self, name):
        self.name = name
        self.w = None
        self.r = {}
        self.sem = None
        self.cnt = 0


class Sched:
    ENG = ("pe", "act", "dve", "pool", "sp")

    def __init__(self, nc, es, needed):
        self.nc = nc
        self.es = es
        self.rec = needed is None
        self.needed = set() if self.rec else needed
        self.idx = {e: 0 for e in self.ENG}
        self.sig = {e: 0 for e in self.ENG}
        self.cntof = {}
        self.waited = {}
        self.eng = dict(pe=nc.tensor, act=nc.scalar, dve=nc.vector, pool=nc.gpsimd, sp=nc.sync)
        self.sem = {}
        self.nsem = 0
        if not self.rec:
            for e in self.ENG:
                self.sem[e] = es.enter_context(nc.semaphore("sem_" + e))

    def _wait(self, e, tok):
        if tok is None:
            return
        if tok[0] == "eng":
            _, pe_, i = tok
            if pe_ == e and e in ("pe", "sp"):
                return
            if self.rec:
                self.needed.add((pe_, i))
                return
            val = self.cntof[(pe_, i)]
            key = (e, pe_)
            semh = self.sem[pe_]
        else:
            _, res, val = tok
            if self.rec:
                return
            key = (e, "dma", res.name)
            semh = res.sem
        if self.waited.get(key, 0) >= val:
            return
        self.waited[key] = val
        self.eng[e].wait_ge(semh, val)

    def _deps(self, e, reads, writes):
        for r in reads:
            self._wait(e, r.w)
        for w in writes:
            self._wait(e, w.w)
            for t in list(w.r.values()):
                self._wait(e, t)

    def op(self, e, fn, reads=(), writes=()):
        self._deps(e, reads, writes)
        i = self.idx[e]
        self.idx[e] += 1
        tok = ("eng", e, i)
        if not self.rec:
            ins = fn()
            if (e, i) in self.needed:
                ins.then_inc(self.sem[e], 1)
                self.sig[e] += 1
                self.cntof[(e, i)] = self.sig[e]
        for r in reads:
            r.r[e] = tok
        for w in writes:
            w.w = tok
            w.r = {}
        return tok

    def _getsem(self, res):
        if res.sem is None and not self.rec:
            res.sem = self.es.enter_context(self.nc.semaphore("dsem_%d" % self.nsem))
            self.nsem += 1

    def dma(self, q, out_ap, in_ap, reads=(), writes=(), semres=None):
        self._deps(q, reads, writes)
        self.idx[q] += 1
        tr = writes[0] if writes else semres
        self._getsem(tr)
        tr.cnt += 16
        tok = ("dma", tr, tr.cnt)
        if not self.rec:
            self.eng[q].dma_start(out=out_ap, in_=in_ap).then_inc(tr.sem, 16)
        for w in writes:
            w.w = tok
            w.r = {}
        for r in reads:
            r.r["dma_" + tr.name] = tok
        return tok

    def barrier(self, allres):
        toks = []
        for r in allres:
            if r.w is not None:
                toks.append(r.w)
            toks.extend(r.r.values())
        for e in ("pe", "act", "dve", "pool", "sp"):
            for t in toks:
                self._wait(e, t)


class _Stop(Exception):
    pass


class Cfg:
    stop = None

    def __init__(self, S=2048, NSEQ=2, DEPTH=2, NITER=20):
        self.S = S
        self.NSEQ = NSEQ
        self.DEPTH = DEPTH
        self.NB = S // 128
        self.TOPK = min(256, S // 4)
        self.NITER = NITER
        self.GT = min(512, S)


def make_consts(cfg):
    c = np.zeros((128, C_END), np.float32)
    p = np.arange(128)
    c[:, C_ID:C_ID + 128] = np.eye(128, dtype=np.float32)
    c[:, C_TRI:C_TRI + 128] = (p[None, :] >= p[:, None]).astype(np.float32)
    c[:, C_CB:C_CB + 128] = np.where(p[None, :] <= p[:, None], 0.0, NEG_BIG)
    for h in range(4):
        lg = math.log(1.0 - 2.0 ** (-5.0 - h))
        dq = np.exp(lg * (p + 1.0))
        c[:, C_RT + h] = dq
        c[:, C_RT + 4 + h] = np.exp(-lg * (p + 1.0)) * 128 ** -0.5
        c[:, C_RT + 8 + h] = np.exp(lg * (127.0 - p)) * 128 ** -0.5
        c[:, C_RT + 12 + h] = dq * dq
        c[:, C_RT + 16 + h] = (1.0 - 2.0 ** (-5.0 - h)) ** 128
    invf = (10000.0 ** (-np.arange(0, 128, 2, dtype=np.float32) / 128)).astype(np.float32)
    c[:, C_INVF:C_INVF + 64] = invf[None, :]
    c[:, C_P2:C_P2 + 32] = (2.0 ** (-np.arange(32, dtype=np.float64)))[None, :]
    return c


def pipeline(gens, depth):
    gens = list(gens)
    active = []
    nxt = 0
    while nxt < len(gens) or active:
        if nxt < len(gens) and len(active) < depth:
            active.append(gens[nxt])
            nxt += 1
        for g in list(active):
            try:
                next(g)
            except StopIteration:
                active.remove(g)


def build(cfg):
    S, NSEQ, DEPTH, NB, TOPK, NITER, GT = cfg.S, cfg.NSEQ, cfg.DEPTH, cfg.NB, cfg.TOPK, cfg.NITER, cfg.GT
    nc = bass.Bass("TRN2", target_bir_lowering=False)
    dt = nc.dram_tensor
    x_d = dt("x", [NSEQ, S, D], F32, kind="ExternalInput").ap()
    pos_d = dt("pos", [NSEQ, 128, NB], I32, kind="ExternalInput").ap()
    wkv_d = dt("w_kv", [DEPTH, 128, KC, 1096], F32, kind="ExternalInput").ap()
    wq_d = dt("w_q", [DEPTH, 128, KC, 1024], F32, kind="ExternalInput").ap()
    wr_d = dt("w_r", [DEPTH, 128, KC, 2048], F32, kind="ExternalInput").ap()
    wout_d = dt("w_o", [DEPTH, 128, KC, 1024], F32, kind="ExternalInput").ap()
    wgu_d = dt("w_gu", [DEPTH, HC // 2, 128, KC * 512], F32, kind="ExternalInput").ap()
    wd_d = dt("w_d", [DEPTH, 128, HC, 1024], F32, kind="ExternalInput").ap()
    gret_d = dt("g_ret", [DEPTH, 128, 512], F32, kind="ExternalInput").ap()
    gmix_d = dt("g_mix", [DEPTH, 128, 2048], F32, kind="ExternalInput").ap()
    gffn_d = dt("g_ffn", [DEPTH, 128, 2048], F32, kind="ExternalInput").ap()
    cst_d = dt("cst", [128, C_END], F32, kind="ExternalInput").ap()
    y_d = dt("y", [NSEQ, S, D], F32, kind="ExternalOutput").ap()
    wgu_bf = dt("wgu_bf", [DEPTH, HC // 2, 128, KC * 512], BF16, kind="Internal").ap()

    ARENA = 63 * 1024
    with ExitStack() as es:
        sb = lambda name, shape, dtp: es.enter_context(nc.sbuf_tensor(name, shape, dtp))
        x_sb = sb("x_sb", [128, NB, D], F32)
        cs_cos = sb("cs_cos", [128, NB, 64], F32)
        cs_sin = sb("cs_sin", [128, NB, 64], F32)
        cst = sb("cst_sb", [128, C_END], F32)
        ident = sb("ident", [128, 128], BF16)
        posi = sb("posi", [128, NB], I32)
        posf = sb("posf", [128, NB], F32)
        wsb = sb("wsb", [128, NB, 8], F32)
        sm = sb("sm", [128, 256], F32)
        steps = sb("steps", [128, 2, 32], F32)
        nstp = sb("nstp", [128, 32], F32)
        sm2 = sb("sm2", [128, 160], F32)
        ones4 = sb("ones4", [4, 128], F32)
        biasc_t = sb("biasc", [128, 32], F32)
        sb_late = lambda name, shape, dtp: biasc_t
        ang = sb("ang", [128, 4, 64], F32)
        arena = sb("arena", [128, ARENA], BF16)
        ps = es.enter_context(nc.psum_tensor("ps", [128, 4096], F32))

        def bankf(i, n=1):
            return ps[:, i * 512:(i + n) * 512]

        def bankb(i):
            return ps[:, i * 512:(i + 1) * 512].bitcast(BF16)

        class Carver:
            def __init__(self, off=0):
                self.off = off

            def take(self, n_bf16, dtype=BF16):
                n = (n_bf16 + 15) // 16 * 16
                ap = arena[:, self.off:self.off + n_bf16]
                self.off += n
                assert self.off <= ARENA, "arena overflow %d" % self.off
                if dtype == F32:
                    ap = ap.bitcast(F32)
                return ap

        def emit(s):
            R = {}

            def res(name):
                if name not in R:
                    R[name] = Res(name)
                return R[name]

            E = s.eng
            r_cst, r_ident, r_cs, r_pos, r_wsb = res("cst"), res("ident"), res("cs"), res("pos"), res("wsb")
            r_x = [res("x%d" % b) for b in range(NB)]
            r_bk = [res("bank%d" % i) for i in range(8)]
            r_stb_ = [res("store%d" % b) for b in range(NB)]
            r_smc = res("smc")

            def tt(e, out, in0, in1, op, reads, writes):
                return s.op(e, lambda: E[e].tensor_tensor(out=out, in0=in0, in1=in1, op=op), reads, writes)

            def ts(e, out, in0, s1, s2, op0, op1, reads, writes, accum=None):
                if op1 is None:
                    return s.op(e, lambda: E[e].tensor_scalar(out=out, in0=in0, scalar1=s1, scalar2=None, op0=op0), reads, writes)
                if accum is not None:
                    return s.op(e, lambda: E[e].tensor_scalar(out=out, in0=in0, scalar1=s1, scalar2=s2, op0=op0, op1=op1, accum_out=accum), reads, writes)
                return s.op(e, lambda: E[e].tensor_scalar(out=out, in0=in0, scalar1=s1, scalar2=s2, op0=op0, op1=op1), reads, writes)

            def stt(e, out, in0, sc, in1, op0, op1, reads, writes):
                return s.op(e, lambda: E[e].scalar_tensor_tensor(out=out, in0=in0, scalar=sc, in1=in1, op0=op0, op1=op1), reads, writes)

            def act(out, in_, func, reads, writes, bias=None, scale=None):
                kw = {}
                if bias is not None:
                    kw["bias"] = bias
                if scale is not None:
                    kw["scale"] = scale
                return s.op("act", lambda: nc.scalar.activation(out=out, in_=in_, func=func, **kw), reads, writes)

            def copy(e, out, in_, reads, writes):
                if e == "act":
                    return s.op("act", lambda: nc.scalar.copy(out=out, in_=in_), reads, writes)
                return s.op(e, lambda: E[e].tensor_copy(out=out, in_=in_), reads, writes)

            def mm(out, pairs, reads, writes):
                def f():
                    ins = None
                    n = len(pairs)
                    for i, (l, r) in enumerate(pairs):
                        ins = nc.tensor.matmul(out, lhsT=l, rhs=r, start=(i == 0), stop=(i == n - 1))
                    return ins
                return s.op("pe", f, reads, writes)

            def tr(out, in_, reads, writes):
                return s.op("pe", lambda: nc.tensor.transpose(out=out, in_=in_, identity=ident[:]), list(reads) + [r_ident], writes)

            s.dma("sp", cst[:], cst_d, writes=[r_cst])
            copy("dve", ident[:], cst[:, C_ID:C_ID + 128], [r_cst], [r_ident])
            cb = cst[:, C_CB:C_CB + 128]
            tri = cst[:, C_TRI:C_TRI + 128]
            invf = cst[:, C_INVF:C_INVF + 64]
            p2 = cst[:, C_P2:C_P2 + 32]
            s.op("dve", lambda: nc.vector.memset(sm[:, 0:8], -0.5), [], [r_smc])
            s.op("dve", lambda: nc.vector.memset(sm[:, 8:9], -1.0e29), [], [r_smc])
            biasc = sb_late("biasc", [128, 32], F32)
            BIASC = {}
            for tbq in range(NB):
                Lq = (tbq + 1) * 128
                s.op("dve", lambda tbq=tbq, Lq=Lq: nc.vector.memset(biasc[:, tbq:tbq + 1], float(-(2 * TOPK - Lq - 1))), [], [r_smc])
                BIASC[Lq] = biasc[:, tbq:tbq + 1]
            r_ones4 = res("ones4")
            s.op("dve", lambda: nc.vector.memset(ones4[:], 1.0), [], [r_ones4])
            s.op("dve", lambda: nc.vector.memset(sm2[:, 0:4], 0.5), [], [r_smc])
            POSH4 = sm2[:, 0:4]
            K2 = sm2[:, 16:16 + 64]
            K2M = sm2[:, 84:88]
            identf = cst[:, C_ID:C_ID + 128]

            def allmax4(src, r_src, dst, r_dst, bk, tmpc):
                r_t = res("amx%d" % tmpc)
                mx4 = sm2[0:4, 100 + tmpc * 8:101 + tmpc * 8]
                dg4 = sm2[0:4, 101 + tmpc * 8:105 + tmpc * 8]
                pT = bankf(bk)[0:4, 0:128]
                s.op("pe", lambda: nc.tensor.transpose(out=pT, in_=src, identity=identf), [r_src, r_cst], [r_bk[bk]])
                s.op("dve", lambda: nc.vector.tensor_reduce(out=mx4, in_=pT, axis=AX.X, op=ALU.max), [r_bk[bk]], [r_t])
                ts("dve", dg4, cst[0:4, C_ID:C_ID + 4], mx4, None, ALU.mult, None, [r_t, r_cst], [r_t])
                pB = bankf(bk)[:, 128:132]
                s.op("pe", lambda: nc.tensor.matmul(pB, lhsT=ones4[0:4, :], rhs=dg4, start=True, stop=True), [r_t, r_ones4], [r_bk[bk]])
                copy("dve", dst, pB, [r_bk[bk]], [r_dst])

            NEGH4 = sm[:, 0:4]
            NEGH = sm[:, 0:1]
            TAUALL = sm[:, 8:9]
            r_conv = [res("conv%d" % l) for l in range(DEPTH)]
            for l in range(DEPTH):
                for pr in range(HC // 2):
                    s.dma("pool", wgu_bf[l, pr], wgu_d[l, pr], semres=r_conv[l])
                r_conv[l].w = ("dma", r_conv[l], r_conv[l].cnt)

            xrot = [0]

            def make_xT(tb, xbf, xT, r_xbf, r_xT, bk):
                e = "act" if xrot[0] % 2 == 0 else "dve"
                xrot[0] += 1
                copy(e, xbf, x_sb[:, tb, :], [r_x[tb]], [r_xbf])
                pb = bankb(bk)
                for k in range(KC):
                    tr(pb[:, k * 128:(k + 1) * 128], xbf[:, k * 128:(k + 1) * 128], [r_xbf], [r_bk[bk]])
                copy("act", xT, pb[:, :], [r_bk[bk]], [r_xT])

            def rope(src, r_src, G, half, cosap, sinap, out, r_out, tA, tB, r_tA, r_tB):
                n = G * 2 * half
                s3 = src.rearrange("p (g d) -> p g d", d=half)
                cbb = cosap.unsqueeze(1).to_broadcast([128, 2 * G, half])
                sbb = sinap.unsqueeze(1).to_broadcast([128, 2 * G, half])
                tA3 = tA[:, 0:n].rearrange("p (g d) -> p g d", d=half)
                tB3 = tB[:, 0:n].rearrange("p (g d) -> p g d", d=half)
                tt("dve", tA3, s3, cbb, ALU.mult, list(r_src) + [r_cs], [r_tA])
                tt("dve", tB3, s3, sbb, ALU.mult, list(r_src) + [r_cs], [r_tB])
                tA4 = tA[:, 0:n].rearrange("p (g t d) -> p g t d", t=2, d=half)
                tB4 = tB[:, 0:n].rearrange("p (g t d) -> p g t d", t=2, d=half)
                o4 = out.rearrange("p (g t d) -> p g t d", t=2, d=half)
                tt("pool", o4[:, :, 0, :], tA4[:, :, 0, :], tB4[:, :, 1, :], ALU.subtract, [r_tA, r_tB], [r_out])
                tt("pool", o4[:, :, 1, :], tA4[:, :, 1, :], tB4[:, :, 0, :], ALU.add, [r_tA, r_tB], [r_out])

            lncnt = [0]

            def layer_norm_block(tb, ypair, r_y, gbt, r_gbt):
                par = lncnt[0] % 2
                lncnt[0] += 1
                c0 = 16 + par * 24
                r_s = res("smln%d" % par)
                xb = x_sb[:, tb, :]
                for hh in range(2):
                    stt("dve", xb[:, hh * 512:(hh + 1) * 512], xb[:, hh * 512:(hh + 1) * 512], ALPHA,
                        ypair[hh], ALU.mult, ALU.add, [r_y[hh]], [r_x[tb]])
                st6 = sm[:, c0:c0 + 12].rearrange("p (a b) -> p a b", a=2)
                for hh in range(2):
                    s.op("dve", lambda hh=hh: nc.vector.bn_stats(out=st6[:, hh, :], in_=xb[:, hh * 512:(hh + 1) * 512]), [r_x[tb]], [r_s])
                s.op("dve", lambda: nc.vector.bn_aggr(out=sm[:, c0 + 12:c0 + 14], in_=st6), [r_s], [r_s])
                ts("dve", sm[:, c0 + 14:c0 + 15], sm[:, c0 + 13:c0 + 14], LN_EPS, None, ALU.add, None, [r_s], [r_s])
                tt("pool", sm[:, c0 + 15:c0 + 16], sm[:, c0 + 14:c0 + 15], NEGH, ALU.pow, [r_s, r_smc], [r_s])
                stt("dve", sm[:, c0 + 16:c0 + 17], sm[:, c0 + 12:c0 + 13], -1.0, sm[:, c0 + 15:c0 + 16], ALU.mult, ALU.mult, [r_s], [r_s])
                act(xb, xb, AF.Identity, [r_x[tb], r_s], [r_x[tb]], bias=sm[:, c0 + 16:c0 + 17], scale=sm[:, c0 + 15:c0 + 16])
                tt("dve", xb, xb, gbt[:, 0:1024], ALU.mult, [r_x[tb], r_gbt], [r_x[tb]])
                tt("pool", xb, xb, gbt[:, 1024:2048], ALU.add, [r_x[tb], r_gbt], [r_x[tb]])

            ALPHA = (2.0 * DEPTH) ** 0.25
            ATT_SCALE = 128 ** -0.5
            W_SCALE = (8 ** -0.5) * (64 ** -0.5)

            def chk(name):
                if cfg.stop == name:
                    raise _Stop()

            for sq in range(NSEQ):
              try:
                xv = x_d[sq].rearrange("(b p) d -> p b d", p=128)
                for b in range(NB):
                    s.dma("sp", x_sb[:, b, :], xv[:, b, :], writes=[r_x[b]])
                s.dma("sp", posi[:], pos_d[sq], writes=[r_pos])
                copy("dve", posf[:], posi[:], [r_pos], [r_pos])
                r_ang = res("ang")
                for b in range(NB):
                    A_, K_, Rs, Rc = ang[:, 0, :], ang[:, 1, :], ang[:, 2, :], ang[:, 3, :]
                    ts("dve", A_, invf, posf[:, b:b + 1], None, ALU.mult, None, [r_cst, r_pos], [r_ang])
                    for which, dst in ((0, Rs), (1, Rc)):
                        src = A_
                        if which == 1:
                            ts("dve", Rc, A_, math.pi / 2, None, ALU.add, None, [r_ang], [r_ang])
                            src = Rc
                        ts("dve", K_, src, 1.0 / TWO_PI, MAGIC, ALU.mult, ALU.add, [r_ang], [r_ang])
                        ts("dve", K_, K_, MAGIC, None, ALU.subtract, None, [r_ang], [r_ang])
                        stt("dve", dst, K_, -C1, src, ALU.mult, ALU.add, [r_ang], [r_ang])
                        stt("dve", dst, K_, -C2, dst, ALU.mult, ALU.add, [r_ang], [r_ang])
                        ts("dve", dst, dst, PI_LO, -PI_LO, ALU.min, ALU.max, [r_ang], [r_ang])
                    act(cs_sin[:, b, :], Rs, AF.Sin, [r_ang], [r_cs])
                    act(cs_cos[:, b, :], Rc, AF.Sin, [r_ang], [r_cs])
                chk('setup')
                for l in range(DEPTH):
                    cv = Carver()
                    attTall = cv.take(4 * S).rearrange("p (k t) -> p k t", k=4)
                    PERSIST = cv.off
                    KT = cv.take(4 * S).rearrange("p (h t) -> p h t", h=4)
                    Vg = cv.take(NB * 4 * 130).rearrange("p (b h e) -> p b h e", b=NB, h=4)
                    ikT = cv.take(S)
                    wA = cv.take(KC * 1096)
                    xbf_ = cv.take(1024)
                    xT_ = cv.take(1024)
                    xbf = [xbf_, xbf_]
                    xT = [xT_, xT_]
                    tA = cv.take(1024, F32)
                    tB = cv.take(1024, F32)
                    rbuf = [cv.take(1024, F32) for _ in range(2)]
                    qiq_ = cv.take(1024)
                    qiq = [qiq_, qiq_]
                    ebuf2 = cv.take(S)
                    attb_ = cv.take(512)
                    attb = [attb_, attb_]
                    qT = [cv.take(512).rearrange("p (h t) -> p h t", h=4) for _ in range(3)]
                    iqT = [cv.take(512).rearrange("p (h t) -> p h t", h=4) for _ in range(2)]
                    Ibuf2 = [cv.take(2 * S, F32) for _ in range(2)]
                    jk = cv.take(16)
                    mb_ = cv.take(S)
                    mb = [mb_, mb_]
                    PT2 = cv.take(S)
                    ebuf = cv.take(S)
                    PTs_ = cv.take(S)
                    PTs = [PTs_, PTs_]
                    r_attTall = [res("attTall%d" % b) for b in range(NB)]
                    r_KT = [res("KT%d" % b) for b in range(NB)]
                    r_V = [res("V%d" % b) for b in range(NB)]
                    r_ikT = [res("ikT%d" % b) for b in range(NB)]
                    r_xbf = [res("xbf0"), res("xbf0")]
                    r_xT = [res("xT0"), res("xT0")]
                    r_tA, r_tB = res("tA"), res("tB")
                    r_wA = res("wA")
                    r_I2, r_e = [res("I0"), res("I1")], res("e")
                    r_jk = res("jk")
                    r_mb = [res("mb0"), res("mb0")]
                    r_mbT = [res("mbT0"), res("mbT1")]
                    r_PT = [res("PTs0"), res("PT2")]
                    r_nbq = [res("nbq%d" % i) for i in range(3)]
                    r_k2, r_k2m = res("k2"), res("k2m")
                    r_PTs = [res("PTs0"), res("PTs0")]
                    r_rb = [res("rb0"), res("rb1")]
                    r_qbf = [res("qbf0"), res("qbf0")]
                    r_iqb = [res("iqb0"), res("iqb0")]
                    r_e2, r_jk2, r_sb2, r_jk3 = res("e2"), res("jk2"), res("sb2"), res("jk3")
                    r_attb = [res("attb0"), res("attb0")]
                    r_kb = r_qbf
                    r_ikb = r_iqb
                    r_qT = [res("qT0"), res("qT1"), res("qT2")]
                    r_iqT = [res("iqT0"), res("iqT1")]

                    wKV = wA[:, 0:KC * 1096].rearrange("p (k c) -> p k c", k=KC)
                    r_wKVg = [res("wKVg%d" % i) for i in range(3)]
                    r_wQg = [res("wQg%d" % i) for i in range(2)]
                    for i, (c0, c1) in enumerate(((0, 512), (512, 1024), (1024, 1096))):
                        s.dma("pool", wKV[:, :, c0:c1], wkv_d[l][:, :, c0:c1], writes=[r_wKVg[i]] + r_wQg)
                    for tb in range(NB):
                        s.op("pool", lambda tb=tb: nc.gpsimd.memset(Vg[:, tb, :, 128:130], 1.0), [], [r_V[tb]])

                    def blkB(tb):
                        p = tb % 2
                        b0 = 4 * p
                        make_xT(tb, xbf[p], xT[p], r_xbf[p], r_xT[p], b0)
                        xt, r_xt = xT[p], r_xT[p]
                        mm(bankf(b0 + 1), [(xt[:, k * 128:(k + 1) * 128], wKV[:, k, 0:512]) for k in range(KC)], [r_xt, r_wKVg[0]], [r_bk[b0 + 1]])
                        mm(bankf(b0 + 2), [(xt[:, k * 128:(k + 1) * 128], wKV[:, k, 512:1024]) for k in range(KC)], [r_xt, r_wKVg[1]], [r_bk[b0 + 2]])
                        mm(bankf(b0 + 3)[:, 0:72], [(xt[:, k * 128:(k + 1) * 128], wKV[:, k, 1024:1096]) for k in range(KC)], [r_xt, r_wKVg[2]], [r_bk[b0 + 3]])
                        yield
                        kbf = qiq[p][:, 0:512]
                        ikb = qiq[p][:, 512:640]
                        rope(bankf(b0 + 1), [r_bk[b0 + 1]], 4, 64, cs_cos[:, tb, :], cs_sin[:, tb, :], kbf, r_kb[p], tA, tB, r_tA, r_tB)
                        copy("act", Vg[:, tb, :, 0:128], bankf(b0 + 2).rearrange("p (h e) -> p h e", h=4), [r_bk[b0 + 2]], [r_V[tb]])
                        for h in range(4):
                            s.op("act", lambda h=h: nc.scalar.activation(out=jk[:, 2:3].to_broadcast([128, 128]), in_=kbf[:, h * 128:(h + 1) * 128], func=AF.Square, accum_out=K2[:, tb * 4 + h:tb * 4 + h + 1]), [r_kb[p]], [r_k2, r_jk3])
                        rope(bankf(b0 + 3)[:, 0:64], [r_bk[b0 + 3]], 1, 32, cs_cos[:, tb, 0:64:2], cs_sin[:, tb, 0:64:2], ikb[:, 0:64], r_ikb[p], tA, tB, r_tA, r_tB)
                        copy("pool", ikb[:, 64:128], ikb[:, 0:64], [r_ikb[p]], [r_ikb[p]])
                        ts("dve", wsb[:, tb, :], bankf(b0 + 3)[:, 64:72], W_SCALE, None, ALU.mult, None, [r_bk[b0 + 3]], [r_wsb])
                        yield
                        pb = bankb(b0)
                        for h in range(4):
                            tr(pb[:, h * 128:(h + 1) * 128], kbf[:, h * 128:(h + 1) * 128], [r_kb[p]], [r_bk[b0]])
                        tr(pb[:, 512:640], ikb, [r_ikb[p]], [r_bk[b0]])
                        copy("act", KT[:, :, tb * 128:(tb + 1) * 128], pb[:, 0:512].rearrange("p (h t) -> p h t", h=4), [r_bk[b0]], [r_KT[tb]])
                        copy("act", ikT[:, tb * 128:(tb + 1) * 128], pb[:, 512:640], [r_bk[b0]], [r_ikT[tb]])

                    pipeline([blkB(tb) for tb in range(NB)], 2)
                    K2P = sm2[:, 88:92]
                    s.op("dve", lambda: nc.vector.tensor_reduce(out=K2P, in_=K2[:, 0:NB * 4].rearrange("p (b h) -> p h b", h=4), axis=AX.X, op=ALU.max), [r_k2], [r_k2m])
                    allmax4(K2P, r_k2m, K2M, r_k2m, 0, 0)
                    chk('B')
                    wQ = wA[:, 0:KC * 1024].rearrange("p (k c) -> p k c", k=KC)
                    for i in range(2):
                        s.dma("pool", wQ[:, :, i * 512:(i + 1) * 512], wq_d[l][:, :, i * 512:(i + 1) * 512], writes=[r_wQg[i]] + r_wKVg)
                    mb_free = [True, True]

                    def frontC(tb):
                        p = tb % 2
                        q3 = tb % 3
                        Ib = Ibuf2[p]
                        r_I = r_I2[p]
                        L = (tb + 1) * 128
                        r_sb = res("smbis%d" % p)
                        r_sa = res("smatt%d" % p)
                        cB = 64 + p * 16
                        cA = 96 + p * 16
                        stp = steps[:, p, :]
                        make_xT(tb, xbf[p], xT[p], r_xbf[p], r_xT[p], 0)
                        xt, r_xt = xT[p], r_xT[p]
                        mm(bankf(1), [(xt[:, k * 128:(k + 1) * 128], wQ[:, k, 0:512]) for k in range(KC)], [r_xt, r_wQg[0]], [r_bk[1]])
                        mm(bankf(2), [(xt[:, k * 128:(k + 1) * 128], wQ[:, k, 512:1024]) for k in range(KC)], [r_xt, r_wQg[1]], [r_bk[2]])
                        qbf = qiq[p][:, 0:512]
                        iqb = qiq[p][:, 512:1024]
                        rope(bankf(1), [r_bk[1]], 4, 64, cs_cos[:, tb, :], cs_sin[:, tb, :], qbf, r_qbf[p], tA, tB, r_tA, r_tB)
                        rope(bankf(2), [r_bk[2]], 8, 32, cs_cos[:, tb, 0:64:2], cs_sin[:, tb, 0:64:2], iqb, r_iqb[p], tA, tB, r_tA, r_tB)
                        pb = bankb(0)
                        for h in range(4):
                            tr(pb[:, h * 128:(h + 1) * 128], qbf[:, h * 128:(h + 1) * 128], [r_qbf[p]], [r_bk[0]])
                        for h in range(4):
                            tr(pb[:, 512 + h * 128:512 + (h + 1) * 128], iqb[:, h * 128:(h + 1) * 128], [r_iqb[p]], [r_bk[0]])
                        copy("act", qT[q3], pb[:, 0:512].rearrange("p (h t) -> p h t", h=4), [r_bk[0]], [r_qT[q3]])
                        copy("act", iqT[p], pb[:, 512:1024].rearrange("p (h t) -> p h t", h=4), [r_bk[0]], [r_iqT[p]])
                        Q2 = sm2[:, 120 + p * 4:124 + p * 4]
                        Q2M = sm2[:, 128 + p * 4:132 + p * 4]
                        NBQ = sm2[:, 136 + q3 * 4:140 + q3 * 4]
                        r_q2 = res("q2_%d" % p)
                        for h in range(4):
                            s.op("act", lambda h=h: nc.scalar.activation(out=jk[:, 2:3].to_broadcast([128, 128]), in_=qbf[:, h * 128:(h + 1) * 128], func=AF.Square, accum_out=Q2[:, h:h + 1]), [r_qbf[p]], [r_q2, r_jk3])
                        allmax4(Q2, r_q2, Q2M, r_q2, 2, 1 + p)
                        tt("dve", Q2M, Q2M, K2M, ALU.mult, [r_q2, r_k2m], [r_q2])
                        tt("pool", Q2M, Q2M, POSH4, ALU.pow, [r_q2, r_smc], [r_q2])
                        ts("dve", NBQ, Q2M, -ATT_SCALE, None, ALU.mult, None, [r_q2], [r_nbq[q3]])
                        yield
                        nch = (L + 511) // 512
                        cnt = 0
                        for c in range(nch):
                            n = min(512, L - 512 * c)
                            kres = r_ikT[4 * c:4 * c + (n + 127) // 128]
                            for h in range(8):
                                m_, base = h // 2, 64 * (h % 2)
                                bk = 1 + cnt % 2
                                rb, r_rb_ = rbuf[cnt % 2], r_rb[cnt % 2]
                                cnt += 1
                                mm(bankf(bk)[:, 0:n], [(iqT[p][base:base + 64, m_, :], ikT[base:base + 64, c * 512:c * 512 + n])], [r_iqT[p]] + kres, [r_bk[bk]])
                                act(rb[:, 0:n], bankf(bk)[:, 0:n], AF.Relu, [r_bk[bk]], [r_rb_])
                                if h == 0:
                                    ts("dve", Ib[:, c * 512:c * 512 + n], rb[:, 0:n], wsb[:, tb, 0:1], None, ALU.mult, None, [r_rb_, r_wsb], [r_I])
                                else:
                                    stt("dve", Ib[:, c * 512:c * 512 + n], rb[:, 0:n], wsb[:, tb, h:h + 1], Ib[:, c * 512:c * 512 + n], ALU.mult, ALU.add, [r_rb_, r_wsb], [r_I])
                                if h % 2 == 1 and h < 7:
                                    yield
                            yield
                        tt("dve", Ib[:, tb * 128:L], Ib[:, tb * 128:L], cb, ALU.add, [r_cst], [r_I])
                        if L > TOPK:
                            MX, MN, RG, T, CNT, DD = [sm[:, cB + i:cB + i + 1] for i in range(6)]
                            s.op("dve", lambda: nc.vector.tensor_reduce(out=MX, in_=Ib[:, 0:L], axis=AX.X, op=ALU.max), [r_I], [r_sb])
                            s.op("dve", lambda: nc.vector.tensor_reduce(out=MN, in_=Ib[:, 0:L - 128], axis=AX.X, op=ALU.min), [r_I], [r_sb])
                            tt("dve", RG, MX, MN, ALU.subtract, [r_sb], [r_sb])
                            ts("dve", RG, RG, 1.0 + 1.0 / 1024, None, ALU.mult, None, [r_sb], [r_sb])
                            tt("dve", T, MX, RG, ALU.subtract, [r_sb], [r_sb])
                            ts("dve", stp[:, 0:NITER + 2], p2[:, 0:NITER + 2], RG, None, ALU.mult, None, [r_sb, r_cst], [r_sb])
                            stt("dve", T, stp[:, 1:2], 1.0, T, ALU.mult, ALU.add, [r_sb], [r_sb])
                            if p == 1:
                                ts("dve", nstp[:, 0:NITER + 2], stp[:, 0:NITER + 2], -0.5, None, ALU.mult, None, [r_sb], [r_sb])
                            junk = jk[:, 0:1].to_broadcast([128, L])
                            if p == 0:
                                for i in range(NITER):
                                    ts("dve", junk, Ib[:, 0:L], T, None, ALU.is_gt, ALU.add, [r_I, r_sb], [r_jk], accum=CNT)
                                    s.op("dve", lambda: nc.vector.tensor_scalar(out=DD, in0=CNT, scalar1=TOPK - 0.5, scalar2=-0.5, op0=ALU.is_gt, op1=ALU.add), [r_jk], [r_sb])
                                    stt("dve", T, DD, stp[:, i + 1:i + 2], T, ALU.mult, ALU.add, [r_sb], [r_sb])
                                    yield
                            else:
                                NT = sm[:, cB + 6:cB + 7]
                                DS = sm[:, cB + 7:cB + 8]
                                junk2 = jk[:, 1:2].to_broadcast([128, L])
                                ts("dve", NT, T, -1.0, None, ALU.mult, None, [r_sb], [r_sb])
                                for i in range(NITER):
                                    s.op("act", lambda: nc.scalar.activation(out=junk2, in_=Ib[:, 0:L], func=AF.Sign, bias=NT, scale=1.0, accum_out=CNT), [r_I, r_sb], [r_jk2])
                                    s.op("act", lambda: nc.scalar.activation(out=DS, in_=CNT, func=AF.Sign, bias=BIASC[L], scale=1.0), [r_jk2, r_smc], [r_sb2])
                                    s.op("act", lambda i=i: nc.scalar.activation(out=NT, in_=DS, func=AF.Identity, bias=NT, scale=nstp[:, i + 1:i + 2]), [r_sb2, r_sb], [r_sb])
                                    yield
                                ts("dve", T, NT, -1.0, None, ALU.mult, None, [r_sb], [r_sb])
                            tt("dve", T, T, stp[:, NITER + 1:NITER + 2], ALU.subtract, [r_sb], [r_sb])
                            TAU = T
                            r_tau = r_sb
                        else:
                            TAU = TAUALL
                            r_tau = r_smc
                        while not mb_free[p]:
                            yield
                        mb_free[p] = False
                        ts("dve", mb[p][:, 0:L], Ib[:, 0:L], TAU, MASK_NEG, ALU.is_le, ALU.mult, [r_I, r_tau], [r_mb[p]])
                        mbT = [ebuf, ebuf2][p]
                        pb0 = bankb(0)
                        for half in range((tb + 8) // 8):
                            nblk = min(8, tb + 1 - 8 * half)
                            for j in range(nblk):
                                sbk = half * 8 + j
                                tr(pb0[:, j * 128:(j + 1) * 128], mb[p][:, sbk * 128:(sbk + 1) * 128], [r_mb[p]], [r_bk[0]])
                            copy("act", mbT[:, half * 1024:half * 1024 + nblk * 128], pb0[:, 0:nblk * 128], [r_bk[0]], [r_mbT[p]])

                    def backC(tb):
                        p = tb % 2
                        q3 = tb % 3
                        L = (tb + 1) * 128
                        nch = (L + 511) // 512
                        cA = 96 + p * 16
                        mbT = [ebuf, ebuf2][p]
                        PTb = [PTs[0], PT2]
                        NBQ = sm2[:, 136 + q3 * 4:140 + q3 * 4]

                        def hbank(h):
                            return 4 + 2 * (h % 2) if nch <= 2 else 4

                        def qk(h):
                            hb = hbank(h)
                            for sbk in range(tb + 1):
                                bkk = hb + sbk // 4
                                mm(ps[:, hb * 512 + sbk * 128:hb * 512 + (sbk + 1) * 128],
                                   [(KT[:, h, sbk * 128:(sbk + 1) * 128], qT[q3][:, h, :]), (ident[:], mbT[:, sbk * 128:(sbk + 1) * 128])],
                                   [r_qT[q3], r_mbT[p], r_ident, r_KT[sbk]], [r_bk[bkk]])

                        def sexp(h):
                            hb = hbank(h)
                            sc_ = ps[:, hb * 512:hb * 512 + L]
                            rsc = r_bk[hb:hb + nch]
                            act(PTb[h % 2][:, 0:L], sc_, AF.Exp, rsc + [r_nbq[q3]], [r_PT[h % 2]], bias=NBQ[:, h:h + 1], scale=ATT_SCALE)

                        def pv(h):
                            pts = PTb[h % 2]
                            RDa = sm[:, cA + 6 + (h % 2):cA + 7 + (h % 2)]
                            r_sr = res("smrd%d_%d" % (p, h % 2))
                            po = bankf(3)[:, 0:129]
                            mm(po, [(pts[:, sbk * 128:(sbk + 1) * 128], Vg[:, sbk, h, 0:129]) for sbk in range(tb + 1)],
                               [r_PT[h % 2]] + r_V[0:tb + 1], [r_bk[3]])
                            s.op("dve", lambda: nc.vector.reciprocal(out=RDa, in_=bankf(3)[:, 128:129]), [r_bk[3]], [r_sr])
                            ts("dve", attb[p][:, h * 128:(h + 1) * 128], bankf(3)[:, 0:128], RDa, None, ALU.mult, None, [r_bk[3], r_sr], [r_attb[p]])

                        qk(0)
                        sexp(0)
                        yield
                        for h in range(4):
                            if h + 1 < 4:
                                qk(h + 1)
                                sexp(h + 1)
                                yield
                            pv(h)
                            yield
                        pbb = bankb(3)
                        for h in range(4):
                            tr(pbb[:, h * 128:(h + 1) * 128], attb[p][:, h * 128:(h + 1) * 128], [r_attb[p]], [r_bk[3]])
                        copy("act", attTall[:, :, tb * 128:(tb + 1) * 128], pbb[:, 0:512].rearrange("p (h t) -> p h t", h=4), [r_bk[3]], [r_attTall[tb]])

                    fronts = []
                    ready = []
                    back = None
                    done_back = set()
                    next_tb = 0
                    while next_tb < NB or fronts or ready or back is not None:
                        while (next_tb < NB and len(fronts) < 2
                               and all(t % 2 != next_tb % 2 for t, _ in fronts)
                               and (next_tb < 3 or (next_tb - 3) in done_back)):
                            fronts.append((next_tb, frontC(next_tb)))
                            next_tb += 1
                        if back is None and ready:
                            tbb = ready.pop(0)
                            back = (tbb, backC(tbb))
                        if back is not None:
                            try:
                                next(back[1])
                            except StopIteration:
                                mb_free[back[0] % 2] = True
                                done_back.add(back[0])
                                back = None
                        for item in list(fronts):
                            try:
                                next(item[1])
                            except StopIteration:
                                fronts.remove(item)
                                ready.append(item[0])
                    chk('C')
                    s.barrier(list(R.values()))
                    cv = Carver(PERSIST)
                    mixT = cv.take(4 * S).rearrange("p (k t) -> p k t", k=4)
                    PERSIST2 = cv.off
                    wR = cv.take(KC * 2048).rearrange("p (k c) -> p k c", k=KC)
                    xbf = [cv.take(1024) for _ in range(3)]
                    xT = [cv.take(1024) for _ in range(3)]
                    tA = cv.take(2048, F32)
                    tB = cv.take(2048, F32)
                    qk_r = cv.take(2048, F32)
                    qkb = [cv.take(1024) for _ in range(3)]
                    kdec = [cv.take(512) for _ in range(3)]
                    vbf = [cv.take(512) for _ in range(3)]
                    gsl = [cv.take(1024, F32) for _ in range(3)]
                    qkT = [cv.take(1024) for _ in range(3)]
                    stb = [cv.take(512) for _ in range(3)]
                    yn = cv.take(1024, F32)
                    y3 = [cv.take(512) for _ in range(3)]
                    state_f = cv.take(1024, F32)
                    state_bf = cv.take(512)
                    gbA = cv.take(1024, F32)
                    r_mixT = [res("mixT%d" % b) for b in range(NB)]
                    r_wR, r_gbA = res("wR"), res("gbA")
                    r_qk, r_yn = res("qk_r"), res("yn")
                    r_qkb = [res("qkb%d" % i) for i in range(3)]
                    r_kdec = [res("kdec%d" % i) for i in range(3)]
                    r_vbf = [res("vbf%d" % i) for i in range(3)]
                    r_gsl = [res("gsl%d" % i) for i in range(3)]
                    r_qkT = [res("qkT%d" % i) for i in range(3)]
                    r_stb = [res("stb%d" % i) for i in range(3)]
                    r_y3 = [res("y3%d" % i) for i in range(3)]
                    r_stf, r_stbf = res("state_f"), res("state_bf")
                    r_xbfA = [res("xbfA%d" % i) for i in range(3)]
                    r_xTA = [res("xTA%d" % i) for i in range(3)]
                    r_tA2, r_tB2 = res("tA2"), res("tB2")
                    s.dma("sp", gbA[:, 0:512], gret_d[l], writes=[r_gbA])
                    r_wRg = [res("wRg%d" % i) for i in range(4)]
                    for i in range(4):
                        s.dma("pool", wR[:, :, i * 512:(i + 1) * 512], wr_d[l][:, :, i * 512:(i + 1) * 512], writes=[r_wRg[i]])
                    s.op("dve", lambda: nc.vector.memset(state_f, 0.0), [], [r_stf])
                    s.op("pool", lambda: nc.gpsimd.memset(state_bf, 0.0), [], [r_stbf])
                    dct = cst[:, C_RT + 16:C_RT + 20].unsqueeze(2).to_broadcast([128, 4, 128])
                    kps = cst[:, C_RT + 4:C_RT + 8].unsqueeze(2).to_broadcast([128, 4, 128])
                    dks = cst[:, C_RT + 8:C_RT + 12].unsqueeze(2).to_broadcast([128, 4, 128])

                    def blkA(tb):
                        p = tb % 3
                        r_sg = res("smgn%d" % (tb % 2))
                        cG = 128 + (tb % 2) * 48
                        make_xT(tb, xbf[p], xT[p], r_xbfA[p], r_xTA[p], 0)
                        xt, r_xt = xT[p], r_xTA[p]
                        for j in range(4):
                            mm(bankf(1 + j), [(xt[:, k * 128:(k + 1) * 128], wR[:, k, j * 512:(j + 1) * 512]) for k in range(KC)], [r_xt, r_wRg[j]], [r_bk[1 + j]])
                        yield
                        rope(bankf(1, 2), [r_bk[1], r_bk[2]], 8, 64, cs_cos[:, tb, :], cs_sin[:, tb, :], qk_r, r_qk, tA, tB, r_tA2, r_tB2)
                        copy("act", vbf[p], bankf(3), [r_bk[3]], [r_vbf[p]])
                        act(gsl[p], bankf(4), AF.Silu, [r_bk[4]], [r_gsl[p]])
                        copy("pool", qkb[p][:, 0:512], qk_r[:, 0:512], [r_qk], [r_qkb[p]])
                        k3 = qk_r[:, 512:1024].rearrange("p (h d) -> p h d", h=4)
                        tt("dve", qkb[p][:, 512:1024].rearrange("p (h d) -> p h d", h=4), k3, kps, ALU.mult, [r_qk, r_cst], [r_qkb[p]])
                        tt("pool", kdec[p].rearrange("p (h d) -> p h d", h=4), k3, dks, ALU.mult, [r_qk, r_cst], [r_kdec[p]])
                        pb = bankb(0)
                        for j in range(8):
                            tr(pb[:, j * 128:(j + 1) * 128], qkb[p][:, j * 128:(j + 1) * 128], [r_qkb[p]], [r_bk[0]])
                        copy("act", qkT[p], pb[:, :], [r_bk[0]], [r_qkT[p]])
                        yield
                        for h in range(4):
                            mm(bankf(5)[:, h * 128:(h + 1) * 128], [(qkT[p][:, 512 + h * 128:512 + (h + 1) * 128], qkT[p][:, h * 128:(h + 1) * 128])], [r_qkT[p]], [r_bk[5]])
                        tt("dve", stb[p].rearrange("p (h d) -> p h d", h=4), bankf(5).rearrange("p (h d) -> p h d", h=4),
                           tri.unsqueeze(1).to_broadcast([128, 4, 128]), ALU.mult, [r_bk[5], r_cst], [r_stb[p]])
                        for h in range(4):
                            sl = slice(h * 128, (h + 1) * 128)
                            mm(bankf(6)[:, sl], [(stb[p][:, sl], vbf[p][:, sl]), (qkT[p][:, sl], state_bf[:, sl])], [r_stb[p], r_vbf[p], r_qkT[p], r_stbf], [r_bk[6]])
                        for h in range(4):
                            sl = slice(h * 128, (h + 1) * 128)
                            mm(bankf(7)[:, sl], [(kdec[p][:, sl], vbf[p][:, sl])], [r_kdec[p], r_vbf[p]], [r_bk[7]])
                        tt("pool", state_f.rearrange("p (h d) -> p h d", h=4), state_f.rearrange("p (h d) -> p h d", h=4), dct, ALU.mult, [r_cst], [r_stf])
                        tt("dve", state_f, state_f, bankf(7), ALU.add, [r_bk[7]], [r_stf])
                        copy("pool", state_bf, state_f, [r_stf], [r_stbf])
                        yield
                        st6 = sm[:, cG:cG + 24].rearrange("p (h a) -> p h a", h=4)
                        mv = sm[:, cG + 24:cG + 32].rearrange("p (h a) -> p h a", h=4)
                        for h in range(4):
                            s.op("dve", lambda h=h: nc.vector.bn_stats(out=st6[:, h, :], in_=bankf(6)[:, h * 128:(h + 1) * 128]), [r_bk[6]], [r_sg])
                        for h in range(4):
                            s.op("dve", lambda h=h: nc.vector.bn_aggr(out=mv[:, h, :], in_=st6[:, h, :]), [r_sg], [r_sg])
                        A4 = sm[:, cG + 32:cG + 36]
                        RS4 = sm[:, cG + 36:cG + 40]
                        SC4 = sm[:, cG + 40:cG + 44]
                        NB4 = sm[:, cG + 44:cG + 48]
                        tt("dve", A4, mv[:, :, 1], cst[:, C_RT + 12:C_RT + 16], ALU.mult, [r_sg, r_cst], [r_sg])
                        ts("dve", A4, A4, LN_EPS, None, ALU.add, None, [r_sg], [r_sg])
                        tt("pool", RS4, A4, NEGH4, ALU.pow, [r_sg, r_smc], [r_sg])
                        tt("dve", SC4, RS4, cst[:, C_RT:C_RT + 4], ALU.mult, [r_sg, r_cst], [r_sg])
                        stt("dve", NB4, mv[:, :, 0], -1.0, SC4, ALU.mult, ALU.mult, [r_sg], [r_sg])
                        for h in range(4):
                            sl = slice(h * 128, (h + 1) * 128)
                            act(yn[:, sl], bankf(6)[:, sl], AF.Identity, [r_bk[6], r_sg], [r_yn], bias=NB4[:, h:h + 1], scale=SC4[:, h:h + 1])
                        tt("dve", yn, yn, gbA[:, 0:512], ALU.mult, [r_yn, r_gbA], [r_yn])
                        tt("pool", y3[p], yn, gsl[p], ALU.mult, [r_yn, r_gsl[p]], [r_y3[p]])
                        pb = bankb(0)
                        for h in range(4):
                            tr(pb[:, h * 128:(h + 1) * 128], y3[p][:, h * 128:(h + 1) * 128], [r_y3[p]], [r_bk[0]])
                        copy("act", mixT[:, :, tb * 128:(tb + 1) * 128], pb[:, 0:512].rearrange("p (h t) -> p h t", h=4), [r_bk[0]], [r_mixT[tb]])

                    pipeline([blkA(tb) for tb in range(NB)], 3)
                    chk('A')
                    s.barrier(list(R.values()))
                    cv = Carver(PERSIST2)
                    wO = cv.take(KC * 1024).rearrange("p (k c) -> p k c", k=KC)
                    gbm = cv.take(4096, F32)
                    r_wO, r_gbm = res("wO"), res("gbm")
                    r_wOg = [res("wOg%d" % i) for i in range(2)]
                    for i in range(2):
                        s.dma("pool", wO[:, :, i * 512:(i + 1) * 512], wout_d[l][:, :, i * 512:(i + 1) * 512], writes=[r_wOg[i]])
                    s.dma("sp", gbm[:, :], gmix_d[l], writes=[r_gbm])
                    for tb in range(NB):
                        b0 = 2 * (tb % 2)
                        for hh in range(2):
                            pairs = [(mixT[:, k, tb * 128:(tb + 1) * 128], wO[:, k, hh * 512:(hh + 1) * 512]) for k in range(4)]
                            pairs += [(attTall[:, k, tb * 128:(tb + 1) * 128], wO[:, 4 + k, hh * 512:(hh + 1) * 512]) for k in range(4)]
                            mm(bankf(b0 + hh), pairs, [r_mixT[tb], r_attTall[tb], r_wOg[hh]], [r_bk[b0 + hh]])
                        layer_norm_block(tb, [bankf(b0), bankf(b0 + 1)], [r_bk[b0], r_bk[b0 + 1]], gbm, r_gbm)
                    chk('C2')
                    s.barrier(list(R.values()))
                    cv = Carver()
                    NG = S // GT
                    nblk = GT // 128
                    ncol = GT // 512
                    x1T = [cv.take(KC * GT).rearrange("p (k t) -> p k t", k=KC) for _ in range(2)]
                    hT = cv.take(HC * GT).rearrange("p (c t) -> p c t", c=HC)
                    wD = cv.take(HC * 1024).rearrange("p (c n) -> p c n", c=HC)
                    wG = [cv.take(KC * 512).rearrange("p (k c) -> p k c", k=KC) for _ in range(3)]
                    xbf2 = [cv.take(1024) for _ in range(2)]
                    sgb = [cv.take(1024, F32) for _ in range(2)]
                    gbf = cv.take(4096, F32)
                    r_x1T = [[res("x1T%d_%d" % (q, i)) for i in range(nblk)] for q in range(2)]
                    r_hT = [res("hT%d_%d" % (c, t)) for c in range(HC) for t in range(ncol)]
                    r_wD = [res("wD%d" % i) for i in range(4)]
                    r_wG = [res("wG%d" % i) for i in range(3)]
                    r_xbf2 = [res("xbf2_0"), res("xbf2_1")]
                    r_sgl = [res("sg0"), res("sg1")]
                    r_gbf = res("gbf")
                    wdq = [(0, 6), (6, 12), (12, 18), (18, 22)]
                    for i, (c0, c1) in enumerate(wdq):
                        s.dma("pool", wD[:, c0:c1, :], wd_d[l][:, c0:c1, :], writes=[r_wD[i]])
                    s.dma("sp", gbf[:, :], gffn_d[l], writes=[r_gbf])
                    NP = HC // 2
                    gu = [0]

                    def load_gu(pr):
                        slot = gu[0] % 3
                        gu[0] += 1
                        s.dma("sp", wG[slot].rearrange("p k c -> p (k c)"), wgu_bf[l, pr], reads=[r_conv[l]], writes=[r_wG[slot]])
                        return slot

                    def prep_x1T(g):
                        q = g % 2
                        for j in range(nblk):
                            tb = g * nblk + j
                            i2 = j % 2
                            copy("act" if j % 2 == 0 else "dve", xbf2[i2], x_sb[:, tb, :], [r_x[tb]], [r_xbf2[i2]])
                            pb = bankb(i2)
                            for k in range(KC):
                                tr(pb[:, k * 128:(k + 1) * 128], xbf2[i2][:, k * 128:(k + 1) * 128], [r_xbf2[i2]], [r_bk[i2]])
                            copy("act", x1T[q][:, :, j * 128:(j + 1) * 128], pb[:, :].rearrange("p (k t) -> p k t", k=KC), [r_bk[i2]], [r_x1T[q][j]])

                    prep_x1T(0)
                    for g in range(NG):
                        q = g % 2
                        slots = [load_gu(0), load_gu(1)]
                        for pr in range(NP):
                            if pr + 2 < NP:
                                slots.append(load_gu(pr + 2))
                            slot = slots[pr]
                            for ci in range(2):
                                c = 2 * pr + ci
                                for th in range(ncol):
                                    xr = r_x1T[q][4 * th:4 * th + 4]
                                    qq = (c * ncol + th) % 2
                                    bG, bU = 2 + 2 * qq, 3 + 2 * qq
                                    mm(bankf(bG), [(wG[slot][:, k, ci * 256:ci * 256 + 128], x1T[q][:, k, th * 512:(th + 1) * 512]) for k in range(KC)], xr + [r_wG[slot]], [r_bk[bG]])
                                    mm(bankf(bU), [(wG[slot][:, k, ci * 256 + 128:ci * 256 + 256], x1T[q][:, k, th * 512:(th + 1) * 512]) for k in range(KC)], xr + [r_wG[slot]], [r_bk[bU]])
                                    act(sgb[qq], bankf(bG), AF.Silu, [r_bk[bG]], [r_sgl[qq]])
                                    tt("dve", hT[:, c, th * 512:(th + 1) * 512], sgb[qq], bankf(bU), ALU.mult, [r_sgl[qq], r_bk[bU]], [r_hT[c * ncol + th]])
                        if g + 1 < NG:
                            prep_x1T(g + 1)
                        for j in range(nblk):
                            tb = g * nblk + j
                            th = j // 4
                            hres = [r_hT[c * ncol + th] for c in range(HC)]
                            for hh in range(2):
                                mm(bankf(6 + hh), [(hT[:, c, j * 128:(j + 1) * 128], wD[:, c, hh * 512:(hh + 1) * 512]) for c in range(HC)],
                                   hres + r_wD, [r_bk[6 + hh]])
                            layer_norm_block(tb, [bankf(6), bankf(7)], [r_bk[6], r_bk[7]], gbf, r_gbf)
                    s.barrier(list(R.values()))
              except _Stop:
                pass
              if True:
                yv = y_d[sq].rearrange("(b p) d -> p b d", p=128)
                for b in range(NB):
                    s.dma("sp", yv[:, b, :], x_sb[:, b, :], reads=[r_x[b]], semres=r_stb_[b])
            for b in range(NB):
                s._wait("sp", ("dma", r_stb_[b], r_stb_[b].cnt))

        s1 = Sched(nc, es, None)
        emit(s1)
        s2 = Sched(nc, es, s1.needed)
        emit(s2)
    return nc


_CACHE = {}


def _prep_inputs(cfg, x, positions, w_in, ret_gn_gain, w_out, ln_mix_gain, ln_mix_bias,
                 w_gate_up, w_down, ln_ffn_gain, ln_ffn_bias, ncores):
    DEPTH = cfg.DEPTH
    f = lambda a: np.ascontiguousarray(np.asarray(a, dtype=np.float32))
    g_ret = np.ascontiguousarray(np.broadcast_to(f(ret_gn_gain)[:, None, :], (DEPTH, 128, 512)))
    g_mix = np.ascontiguousarray(np.broadcast_to(
        np.concatenate([f(ln_mix_gain), f(ln_mix_bias)], axis=1)[:, None, :], (DEPTH, 128, 2048)))
    g_ffn = np.ascontiguousarray(np.broadcast_to(
        np.concatenate([f(ln_ffn_gain), f(ln_ffn_bias)], axis=1)[:, None, :], (DEPTH, 128, 2048)))
    cst = make_consts(cfg)
    w_in = f(w_in)

    def tile_k(w):
        return np.ascontiguousarray(w.reshape(DEPTH, KC, 128, w.shape[-1]).transpose(0, 2, 1, 3))

    w_kv = tile_k(np.concatenate([w_in[:, :, OFF_AK:OFF_AK + 1024], w_in[:, :, OFF_IK:OFF_IK + 72]], axis=2))
    w_q = tile_k(np.concatenate([w_in[:, :, OFF_AQ:OFF_AQ + 512], w_in[:, :, OFF_IQ:OFF_IQ + 512]], axis=2))
    w_r = tile_k(w_in[:, :, 0:2048])
    w_o = tile_k(f(w_out))
    wgu = f(w_gate_up)
    g4 = wgu[:, :, 0:HID].reshape(DEPTH, KC, 128, HC // 2, 2, 128)
    u4 = wgu[:, :, HID:2 * HID].reshape(DEPTH, KC, 128, HC // 2, 2, 128)
    gu = np.stack([g4, u4], axis=5)
    gu = gu.transpose(0, 3, 2, 1, 4, 5, 6)
    w_gu = np.ascontiguousarray(gu.reshape(DEPTH, HC // 2, 128, KC * 512))
    w_d = np.ascontiguousarray(f(w_down).reshape(DEPTH, HC, 128, 1024).transpose(0, 2, 1, 3))
    x = f(x)
    pos = np.asarray(positions).astype(np.int32)
    shared = {"w_kv": w_kv, "w_q": w_q, "w_r": w_r, "w_o": w_o, "w_gu": w_gu, "w_d": w_d,
              "g_ret": g_ret, "g_mix": g_mix, "g_ffn": g_ffn, "cst": cst}
    maps = []
    for c in range(ncores):
        xs = x[c * cfg.NSEQ:(c + 1) * cfg.NSEQ]
        ps_ = pos[c * cfg.NSEQ:(c + 1) * cfg.NSEQ]
        ps_ = np.ascontiguousarray(ps_.reshape(cfg.NSEQ, cfg.NB, 128).transpose(0, 2, 1))
        m = dict(shared)
        m["x"] = np.ascontiguousarray(xs)
        m["pos"] = ps_
        maps.append(m)
    return maps


def run(cfg, ncores, **inputs):
    key = (cfg.S, cfg.NSEQ, cfg.DEPTH, cfg.NITER, cfg.stop)
    if key not in _CACHE:
        _CACHE[key] = build(cfg)
    nc = _CACHE[key]
    maps = _prep_inputs(cfg, ncores=ncores, **inputs)
    res = run_bass_kernel_spmd(nc, maps, core_ids=list(range(ncores)))
    return np.concatenate([r["y"] for r in res.results], axis=0)


def kernel(x, positions, w_in, ret_gn_gain, w_out, ln_mix_gain, ln_mix_bias,
           w_gate_up, w_down, ln_ffn_gain, ln_ffn_bias):
    cfg = Cfg(S=2048, NSEQ=2, DEPTH=2, NITER=17)
    out = run(cfg, 8, x=x, positions=positions, w_in=w_in, ret_gn_gain=ret_gn_gain, w_out=w_out,
              ln_mix_gain=ln_mix_gain, ln_mix_bias=ln_mix_bias, w_gate_up=w_gate_up, w_down=w_down,
              ln_ffn_gain=ln_ffn_gain, ln_ffn_bias=ln_ffn_bias)
    return out.astype(np.float32)
```

```python
import math
from contextlib import ExitStack
import numpy as np
import concourse.bass as bass
import concourse.mybir as mybir
from concourse.bass_utils import run_bass_kernel_spmd

F32 = mybir.dt.float32
BF16 = mybir.dt.bfloat16
I32 = mybir.dt.int32
ALU = mybir.AluOpType
AF = mybir.ActivationFunctionType
AX = mybir.AxisListType

D = 1024
KC = 8
HID = 2816
HC = 22
IN_COLS = 4168
OFF_RQ, OFF_RK, OFF_RV, OFF_RG, OFF_AQ, OFF_AK, OFF_AV, OFF_IQ, OFF_IK, OFF_IW = (
    0, 512, 1024, 1536, 2048, 2560, 3072, 3584, 4096, 4160)
LN_EPS = 1e-5
MAGIC = 12582912.0
TWO_PI = 2.0 * math.pi
C1 = 6.28125
C2 = TWO_PI - C1
PI_LO = 3.1415925
NEG_BIG = -1.0e30
MASK_NEG = -30000.0

C_ID, C_TRI, C_CB, C_RT, C_INVF, C_P2, C_END = 0, 128, 256, 384, 404, 468, 500


class Res:
    __slots__ = ("name", "w", "r", "sem", "cnt")

    def __init__(self, name):
        self.name = name
        self.w = None
        self.r = {}
        self.sem = None
        self.cnt = 0


class Sched:
    ENG = ("pe", "act", "dve", "pool", "sp")

    def __init__(self, nc, es, needed):
        self.nc = nc
        self.es = es
        self.rec = needed is None
        self.needed = set() if self.rec else needed
        self.idx = {e: 0 for e in self.ENG}
        self.sig = {e: 0 for e in self.ENG}
        self.cntof = {}
        self.waited = {}
        self.eng = dict(pe=nc.tensor, act=nc.scalar, dve=nc.vector, pool=nc.gpsimd, sp=nc.sync)
        self.sem = {}
        self.nsem = 0
        if not self.rec:
            for e in self.ENG:
                self.sem[e] = es.enter_context(nc.semaphore("sem_" + e))

    def _wait(self, e, tok):
        if tok is None:
            return
        if tok[0] == "eng":
            _, pe_, i = tok
            if pe_ == e and e in ("pe", "sp"):
                return
            if self.rec:
                self.needed.add((pe_, i))
                return
            val = self.cntof[(pe_, i)]
            key = (e, pe_)
            semh = self.sem[pe_]
        else:
            _, res, val = tok
            if self.rec:
                return
            key = (e, "dma", res.name)
            semh = res.sem
        if self.waited.get(key, 0) >= val:
            return
        self.waited[key] = val
        self.eng[e].wait_ge(semh, val)

    def _deps(self, e, reads, writes):
        for r in reads:
            self._wait(e, r.w)
        for w in writes:
            self._wait(e, w.w)
            for t in list(w.r.values()):
                self._wait(e, t)

    def op(self, e, fn, reads=(), writes=()):
        self._deps(e, reads, writes)
        i = self.idx[e]
        self.idx[e] += 1
        tok = ("eng", e, i)
        if not self.rec:
            ins = fn()
            if (e, i) in self.needed:
                ins.then_inc(self.sem[e], 1)
                self.sig[e] += 1
                self.cntof[(e, i)] = self.sig[e]
        for r in reads:
            r.r[e] = tok
        for w in writes:
            w.w = tok
            w.r = {}
        return tok

    def _getsem(self, res):
        if res.sem is None and not self.rec:
            res.sem = self.es.enter_context(self.nc.semaphore("dsem_%d" % self.nsem))
            self.nsem += 1

    def dma(self, q, out_ap, in_ap, reads=(), writes=(), semres=None):
        self._deps(q, reads, writes)
        self.idx[q] += 1
        tr = writes[0] if writes else semres
        self._getsem(tr)
        tr.cnt += 16
        tok = ("dma", tr, tr.cnt)
        if not self.rec:
            self.eng[q].dma_start(out=out_ap, in_=in_ap).then_inc(tr.sem, 16)
        for w in writes:
            w.w = tok
            w.r = {}
        for r in reads:
            r.r["dma_" + tr.name] = tok
        return tok

    def barrier(self, allres):
        toks = []
        for r in allres:
            if r.w is not None:
                toks.append(r.w)
            toks.extend(r.r.values())
        for e in ("pe", "act", "dve", "pool", "sp"):
            for t in toks:
                self._wait(e, t)


class _Stop(Exception):
    pass


class Cfg:
    stop = None

    def __init__(self, S=2048, NSEQ=2, DEPTH=2, NITER=20):
        self.S = S
        self.NSEQ = NSEQ
        self.DEPTH = DEPTH
        self.NB = S // 128
        self.TOPK = min(256, S // 4)
        self.NITER = NITER
        self.GT = min(512, S)


def make_consts(cfg):
    c = np.zeros((128, C_END), np.float32)
    p = np.arange(128)
    c[:, C_ID:C_ID + 128] = np.eye(128, dtype=np.float32)
    c[:, C_TRI:C_TRI + 128] = (p[None, :] >= p[:, None]).astype(np.float32)
    c[:, C_CB:C_CB + 128] = np.where(p[None, :] <= p[:, None], 0.0, NEG_BIG)
    for h in range(4):
        lg = math.log(1.0 - 2.0 ** (-5.0 - h))
        dq = np.exp(lg * (p + 1.0))
        c[:, C_RT + h] = dq
        c[:, C_RT + 4 + h] = np.exp(-lg * (p + 1.0)) * 128 ** -0.5
        c[:, C_RT + 8 + h] = np.exp(lg * (127.0 - p)) * 128 ** -0.5
        c[:, C_RT + 12 + h] = dq * dq
        c[:, C_RT + 16 + h] = (1.0 - 2.0 ** (-5.0 - h)) ** 128
    invf = (10000.0 ** (-np.arange(0, 128, 2, dtype=np.float32) / 128)).astype(np.float32)
    c[:, C_INVF:C_INVF + 64] = invf[None, :]
    c[:, C_P2:C_P2 + 32] = (2.0 ** (-np.arange(32, dtype=np.float64)))[None, :]
    return c


def pipeline(gens, depth):
    gens = list(gens)
    active = []
    nxt = 0
    while nxt < len(gens) or active:
        if nxt < len(gens) and len(active) < depth:
            active.append(gens[nxt])
            nxt += 1
        for g in list(active):
            try:
                next(g)
            except StopIteration:
                active.remove(g)


def build(cfg):
    S, NSEQ, DEPTH, NB, TOPK, NITER, GT = cfg.S, cfg.NSEQ, cfg.DEPTH, cfg.NB, cfg.TOPK, cfg.NITER, cfg.GT
    nc = bass.Bass("TRN2", target_bir_lowering=False)
    dt = nc.dram_tensor
    x_d = dt("x", [NSEQ, S, D], F32, kind="ExternalInput").ap()
    pos_d = dt("pos", [NSEQ, 128, NB], I32, kind="ExternalInput").ap()
    wkv_d = dt("w_kv", [DEPTH, 128, KC, 1096], F32, kind="ExternalInput").ap()
    wq_d = dt("w_q", [DEPTH, 128, KC, 1024], F32, kind="ExternalInput").ap()
    wr_d = dt("w_r", [DEPTH, 128, KC, 2048], F32, kind="ExternalInput").ap()
    wout_d = dt("w_o", [DEPTH, 128, KC, 1024], F32, kind="ExternalInput").ap()
    wgu_d = dt("w_gu", [DEPTH, HC // 2, 128, KC * 512], F32, kind="ExternalInput").ap()
    wd_d = dt("w_d", [DEPTH, 128, HC, 1024], F32, kind="ExternalInput").ap()
    gret_d = dt("g_ret", [DEPTH, 128, 512], F32, kind="ExternalInput").ap()
    gmix_d = dt("g_mix", [DEPTH, 128, 2048], F32, kind="ExternalInput").ap()
    gffn_d = dt("g_ffn", [DEPTH, 128, 2048], F32, kind="ExternalInput").ap()
    cst_d = dt("cst", [128, C_END], F32, kind="ExternalInput").ap()
    y_d = dt("y", [NSEQ, S, D], F32, kind="ExternalOutput").ap()
    wgu_bf = dt("wgu_bf", [DEPTH, HC // 2, 128, KC * 512], BF16, kind="Internal").ap()

    ARENA = 63 * 1024
    with ExitStack() as es:
        sb = lambda name, shape, dtp: es.enter_context(nc.sbuf_tensor(name, shape, dtp))
        x_sb = sb("x_sb", [128, NB, D], F32)
        cs_cos = sb("cs_cos", [128, NB, 64], F32)
        cs_sin = sb("cs_sin", [128, NB, 64], F32)
        cst = sb("cst_sb", [128, C_END], F32)
        ident = sb("ident", [128, 128], BF16)
        posi = sb("posi", [128, NB], I32)
        posf = sb("posf", [128, NB], F32)
        wsb = sb("wsb", [128, NB, 8], F32)
        sm = sb("sm", [128, 256], F32)
        steps = sb("steps", [128, 2, 32], F32)
        nstp = sb("nstp", [128, 32], F32)
        sm2 = sb("sm2", [128, 160], F32)
        ones4 = sb("ones4", [4, 128], F32)
        biasc_t = sb("biasc", [128, 32], F32)
        sb_late = lambda name, shape, dtp: biasc_t
        ang = sb("ang", [128, 4, 64], F32)
        arena = sb("arena", [128, ARENA], BF16)
        ps = es.enter_context(nc.psum_tensor("ps", [128, 4096], F32))

        def bankf(i, n=1):
            return ps[:, i * 512:(i + n) * 512]

        def bankb(i):
            return ps[:, i * 512:(i + 1) * 512].bitcast(BF16)

        class Carver:
            def __init__(self, off=0):
                self.off = off

            def take(self, n_bf16, dtype=BF16):
                n = (n_bf16 + 15) // 16 * 16
                ap = arena[:, self.off:self.off + n_bf16]
                self.off += n
                assert self.off <= ARENA, "arena overflow %d" % self.off
                if dtype == F32:
                    ap = ap.bitcast(F32)
                return ap

        def emit(s):
            R = {}

            def res(name):
                if name not in R:
                    R[name] = Res(name)
                return R[name]

            E = s.eng
            r_cst, r_ident, r_cs, r_pos, r_wsb = res("cst"), res("ident"), res("cs"), res("pos"), res("wsb")
            r_x = [res("x%d" % b) for b in range(NB)]
            r_bk = [res("bank%d" % i) for i in range(8)]
            r_stb_ = [res("store%d" % b) for b in range(NB)]
            r_smc = res("smc")

            def tt(e, out, in0, in1, op, reads, writes):
                return s.op(e, lambda: E[e].tensor_tensor(out=out, in0=in0, in1=in1, op=op), reads, writes)

            def ts(e, out, in0, s1, s2, op0, op1, reads, writes, accum=None):
                if op1 is None:
                    return s.op(e, lambda: E[e].tensor_scalar(out=out, in0=in0, scalar1=s1, scalar2=None, op0=op0), reads, writes)
                if accum is not None:
                    return s.op(e, lambda: E[e].tensor_scalar(out=out, in0=in0, scalar1=s1, scalar2=s2, op0=op0, op1=op1, accum_out=accum), reads, writes)
                return s.op(e, lambda: E[e].tensor_scalar(out=out, in0=in0, scalar1=s1, scalar2=s2, op0=op0, op1=op1), reads, writes)

            def stt(e, out, in0, sc, in1, op0, op1, reads, writes):
                return s.op(e, lambda: E[e].scalar_tensor_tensor(out=out, in0=in0, scalar=sc, in1=in1, op0=op0, op1=op1), reads, writes)

            def act(out, in_, func, reads, writes, bias=None, scale=None):
                kw = {}
                if bias is not None:
                    kw["bias"] = bias
                if scale is not None:
                    kw["scale"] = scale
                return s.op("act", lambda: nc.scalar.activation(out=out, in_=in_, func=func, **kw), reads, writes)

            def copy(e, out, in_, reads, writes):
                if e == "act":
                    return s.op("act", lambda: nc.scalar.copy(out=out, in_=in_), reads, writes)
                return s.op(e, lambda: E[e].tensor_copy(out=out, in_=in_), reads, writes)

            def mm(out, pairs, reads, writes):
                def f():
                    ins = None
                    n = len(pairs)
                    for i, (l, r) in enumerate(pairs):
                        ins = nc.tensor.matmul(out, lhsT=l, rhs=r, start=(i == 0), stop=(i == n - 1))
                    return ins
                return s.op("pe", f, reads, writes)

            def tr(out, in_, reads, writes):
                return s.op("pe", lambda: nc.tensor.transpose(out=out, in_=in_, identity=ident[:]), list(reads) + [r_ident], writes)

            s.dma("sp", cst[:], cst_d, writes=[r_cst])
            copy("dve", ident[:], cst[:, C_ID:C_ID + 128], [r_cst], [r_ident])
            cb = cst[:, C_CB:C_CB + 128]
            tri = cst[:, C_TRI:C_TRI + 128]
            invf = cst[:, C_INVF:C_INVF + 64]
            p2 = cst[:, C_P2:C_P2 + 32]
            s.op("dve", lambda: nc.vector.memset(sm[:, 0:8], -0.5), [], [r_smc])
            s.op("dve", lambda: nc.vector.memset(sm[:, 8:9], -1.0e29), [], [r_smc])
            biasc = sb_late("biasc", [128, 32], F32)
            BIASC = {}
            for tbq in range(NB):
                Lq = (tbq + 1) * 128
                s.op("dve", lambda tbq=tbq, Lq=Lq: nc.vector.memset(biasc[:, tbq:tbq + 1], float(-(2 * TOPK - Lq - 1))), [], [r_smc])
                BIASC[Lq] = biasc[:, tbq:tbq + 1]
            r_ones4 = res("ones4")
            s.op("dve", lambda: nc.vector.memset(ones4[:], 1.0), [], [r_ones4])
            s.op("dve", lambda: nc.vector.memset(sm2[:, 0:4], 0.5), [], [r_smc])
            POSH4 = sm2[:, 0:4]
            K2 = sm2[:, 16:16 + 64]
            K2M = sm2[:, 84:88]
            identf = cst[:, C_ID:C_ID + 128]

            def allmax4(src, r_src, dst, r_dst, bk, tmpc):
                r_t = res("amx%d" % tmpc)
                mx4 = sm2[0:4, 100 + tmpc * 8:101 + tmpc * 8]
                dg4 = sm2[0:4, 101 + tmpc * 8:105 + tmpc * 8]
                pT = bankf(bk)[0:4, 0:128]
                s.op("pe", lambda: nc.tensor.transpose(out=pT, in_=src, identity=identf), [r_src, r_cst], [r_bk[bk]])
                s.op("dve", lambda: nc.vector.tensor_reduce(out=mx4, in_=pT, axis=AX.X, op=ALU.max), [r_bk[bk]], [r_t])
                ts("dve", dg4, cst[0:4, C_ID:C_ID + 4], mx4, None, ALU.mult, None, [r_t, r_cst], [r_t])
                pB = bankf(bk)[:, 128:132]
                s.op("pe", lambda: nc.tensor.matmul(pB, lhsT=ones4[0:4, :], rhs=dg4, start=True, stop=True), [r_t, r_ones4], [r_bk[bk]])
                copy("dve", dst, pB, [r_bk[bk]], [r_dst])

            NEGH4 = sm[:, 0:4]
            NEGH = sm[:, 0:1]
            TAUALL = sm[:, 8:9]
            r_conv = [res("conv%d" % l) for l in range(DEPTH)]
            for l in range(DEPTH):
                for pr in range(HC // 2):
                    s.dma("pool", wgu_bf[l, pr], wgu_d[l, pr], semres=r_conv[l])
                r_conv[l].w = ("dma", r_conv[l], r_conv[l].cnt)

            xrot = [0]

            def make_xT(tb, xbf, xT, r_xbf, r_xT, bk):
                e = "act" if xrot[0] % 2 == 0 else "dve"
                xrot[0] += 1
                copy(e, xbf, x_sb[:, tb, :], [r_x[tb]], [r_xbf])
                pb = bankb(bk)
                for k in range(KC):
                    tr(pb[:, k * 128:(k + 1) * 128], xbf[:, k * 128:(k + 1) * 128], [r_xbf], [r_bk[bk]])
                copy("act", xT, pb[:, :], [r_bk[bk]], [r_xT])

            def rope(src, r_src, G, half, cosap, sinap, out, r_out, tA, tB, r_tA, r_tB, e2="pool"):
                n = G * 2 * half
                s3 = src.rearrange("p (g d) -> p g d", d=half)
                cbb = cosap.unsqueeze(1).to_broadcast([128, 2 * G, half])
                sbb = sinap.unsqueeze(1).to_broadcast([128, 2 * G, half])
                tA3 = tA[:, 0:n].rearrange("p (g d) -> p g d", d=half)
                tB3 = tB[:, 0:n].rearrange("p (g d) -> p g d", d=half)
                tt("dve", tA3, s3, cbb, ALU.mult, list(r_src) + [r_cs], [r_tA])
                tt("dve", tB3, s3, sbb, ALU.mult, list(r_src) + [r_cs], [r_tB])
                tA4 = tA[:, 0:n].rearrange("p (g t d) -> p g t d", t=2, d=half)
                tB4 = tB[:, 0:n].rearrange("p (g t d) -> p g t d", t=2, d=half)
                o4 = out.rearrange("p (g t d) -> p g t d", t=2, d=half)
                tt(e2, o4[:, :, 0, :], tA4[:, :, 0, :], tB4[:, :, 1, :], ALU.subtract, [r_tA, r_tB], [r_out])
                tt(e2, o4[:, :, 1, :], tA4[:, :, 1, :], tB4[:, :, 0, :], ALU.add, [r_tA, r_tB], [r_out])

            lncnt = [0]

            def layer_norm_block(tb, ypair, r_y, gbt, r_gbt):
                par = lncnt[0] % 2
                lncnt[0] += 1
                c0 = 16 + par * 24
                r_s = res("smln%d" % par)
                xb = x_sb[:, tb, :]
                for hh in range(2):
                    stt("dve", xb[:, hh * 512:(hh + 1) * 512], xb[:, hh * 512:(hh + 1) * 512], ALPHA,
                        ypair[hh], ALU.mult, ALU.add, [r_y[hh]], [r_x[tb]])
                st6 = sm[:, c0:c0 + 12].rearrange("p (a b) -> p a b", a=2)
                for hh in range(2):
                    s.op("dve", lambda hh=hh: nc.vector.bn_stats(out=st6[:, hh, :], in_=xb[:, hh * 512:(hh + 1) * 512]), [r_x[tb]], [r_s])
                s.op("dve", lambda: nc.vector.bn_aggr(out=sm[:, c0 + 12:c0 + 14], in_=st6), [r_s], [r_s])
                ts("dve", sm[:, c0 + 14:c0 + 15], sm[:, c0 + 13:c0 + 14], LN_EPS, None, ALU.add, None, [r_s], [r_s])
                tt("pool", sm[:, c0 + 15:c0 + 16], sm[:, c0 + 14:c0 + 15], NEGH, ALU.pow, [r_s, r_smc], [r_s])
                stt("dve", sm[:, c0 + 16:c0 + 17], sm[:, c0 + 12:c0 + 13], -1.0, sm[:, c0 + 15:c0 + 16], ALU.mult, ALU.mult, [r_s], [r_s])
                act(xb, xb, AF.Identity, [r_x[tb], r_s], [r_x[tb]], bias=sm[:, c0 + 16:c0 + 17], scale=sm[:, c0 + 15:c0 + 16])
                tt("dve", xb, xb, gbt[:, 0:1024], ALU.mult, [r_x[tb], r_gbt], [r_x[tb]])
                tt("pool", xb, xb, gbt[:, 1024:2048], ALU.add, [r_x[tb], r_gbt], [r_x[tb]])

            ALPHA = (2.0 * DEPTH) ** 0.25
            ATT_SCALE = 128 ** -0.5
            W_SCALE = (8 ** -0.5) * (64 ** -0.5)

            def chk(name):
                if cfg.stop == name:
                    raise _Stop()

            for sq in range(NSEQ):
              try:
                xv = x_d[sq].rearrange("(b p) d -> p b d", p=128)
                for b in range(NB):
                    s.dma("sp", x_sb[:, b, :], xv[:, b, :], writes=[r_x[b]])
                s.dma("sp", posi[:], pos_d[sq], writes=[r_pos])
                copy("dve", posf[:], posi[:], [r_pos], [r_pos])
                r_ang = res("ang")
                for b in range(NB):
                    A_, K_, Rs, Rc = ang[:, 0, :], ang[:, 1, :], ang[:, 2, :], ang[:, 3, :]
                    ts("dve", A_, invf, posf[:, b:b + 1], None, ALU.mult, None, [r_cst, r_pos], [r_ang])
                    for which, dst in ((0, Rs), (1, Rc)):
                        src = A_
                        if which == 1:
                            ts("dve", Rc, A_, math.pi / 2, None, ALU.add, None, [r_ang], [r_ang])
                            src = Rc
                        ts("dve", K_, src, 1.0 / TWO_PI, MAGIC, ALU.mult, ALU.add, [r_ang], [r_ang])
                        ts("dve", K_, K_, MAGIC, None, ALU.subtract, None, [r_ang], [r_ang])
                        stt("dve", dst, K_, -C1, src, ALU.mult, ALU.add, [r_ang], [r_ang])
                        stt("dve", dst, K_, -C2, dst, ALU.mult, ALU.add, [r_ang], [r_ang])
                        ts("dve", dst, dst, PI_LO, -PI_LO, ALU.min, ALU.max, [r_ang], [r_ang])
                    act(cs_sin[:, b, :], Rs, AF.Sin, [r_ang], [r_cs])
                    act(cs_cos[:, b, :], Rc, AF.Sin, [r_ang], [r_cs])
                chk('setup')
                for l in range(DEPTH):
                    cv = Carver()
                    attTall = cv.take(4 * S).rearrange("p (k t) -> p k t", k=4)
                    PERSIST = cv.off
                    KT = cv.take(4 * S).rearrange("p (h t) -> p h t", h=4)
                    Vg = cv.take(NB * 4 * 130).rearrange("p (b h e) -> p b h e", b=NB, h=4)
                    ikT = cv.take(S)
                    wA = cv.take(KC * 1096)
                    xbf_ = cv.take(1024)
                    xT_ = cv.take(1024)
                    xbf = [xbf_, xbf_]
                    xT = [xT_, xT_]
                    tA = cv.take(1024, F32)
                    tB = cv.take(1024, F32)
                    rbuf = [cv.take(1024, F32) for _ in range(2)]
                    qiq_ = cv.take(1024)
                    qiq = [qiq_, qiq_]
                    ebuf2 = cv.take(S)
                    attb_ = cv.take(512)
                    attb = [attb_, attb_]
                    qT = [cv.take(512).rearrange("p (h t) -> p h t", h=4) for _ in range(3)]
                    iqT = [cv.take(512).rearrange("p (h t) -> p h t", h=4) for _ in range(2)]
                    Ibuf2 = [cv.take(2 * S, F32) for _ in range(2)]
                    jk = cv.take(16)
                    mb_ = cv.take(S)
                    mb = [mb_, mb_]
                    PT2 = cv.take(S)
                    ebuf = cv.take(S)
                    PTs_ = cv.take(S)
                    PTs = [PTs_, PTs_]
                    r_attTall = [res("attTall%d" % b) for b in range(NB)]
                    r_KT = [res("KT%d" % b) for b in range(NB)]
                    r_V = [res("V%d" % b) for b in range(NB)]
                    r_ikT = [res("ikT%d" % b) for b in range(NB)]
                    r_xbf = [res("xbf0"), res("xbf0")]
                    r_xT = [res("xT0"), res("xT0")]
                    if S >= 1024:
                        xbfB = [xbf_, PT2[:, 0:1024]]
                        xTB = [xT_, PTs_[:, 0:1024]]
                        r_xbfB = [r_xbf[0], res("PT2")]
                        r_xTB = [r_xT[0], res("PTs0")]
                    else:
                        xbfB, xTB, r_xbfB, r_xTB = xbf, xT, r_xbf, r_xT
                    r_tA, r_tB = res("tA"), res("tB")
                    r_wA = res("wA")
                    r_I2, r_e = [res("I0"), res("I1")], res("e")
                    r_jk = res("jk")
                    r_mb = [res("mb0"), res("mb0")]
                    r_mbT = [res("mbT0"), res("mbT1")]
                    r_PT = [res("PTs0"), res("PT2")]
                    r_nbq = [res("nbq%d" % i) for i in range(3)]
                    r_k2, r_k2m = res("k2"), res("k2m")
                    r_PTs = [res("PTs0"), res("PTs0")]
                    r_rb = [res("rb0"), res("rb1")]
                    r_qbf = [res("qbf0"), res("qbf0")]
                    r_iqb = [res("iqb0"), res("iqb0")]
                    r_e2, r_jk2, r_sb2, r_jk3 = res("e2"), res("jk2"), res("sb2"), res("jk3")
                    r_attb = [res("attb0"), res("attb0")]
                    r_kb = r_qbf
                    r_ikb = r_iqb
                    r_qT = [res("qT0"), res("qT1"), res("qT2")]
                    r_iqT = [res("iqT0"), res("iqT1")]

                    wKV = wA[:, 0:KC * 1096].rearrange("p (k c) -> p k c", k=KC)
                    r_wKVg = [res("wKVg%d" % i) for i in range(3)]
                    r_wQg = [res("wQg%d" % i) for i in range(2)]
                    for i, (c0, c1) in enumerate(((0, 512), (512, 1024), (1024, 1096))):
                        s.dma("pool", wKV[:, :, c0:c1], wkv_d[l][:, :, c0:c1], writes=[r_wKVg[i]] + r_wQg)
                    for tb in range(NB):
                        s.op("pool", lambda tb=tb: nc.gpsimd.memset(Vg[:, tb, :, 128:130], 1.0), [], [r_V[tb]])

                    def blkB(tb):
                        p = tb % 2
                        b0 = 4 * p
                        make_xT(tb, xbfB[p], xTB[p], r_xbfB[p], r_xTB[p], b0)
                        xt, r_xt = xTB[p], r_xTB[p]
                        mm(bankf(b0 + 1), [(xt[:, k * 128:(k + 1) * 128], wKV[:, k, 0:512]) for k in range(KC)], [r_xt, r_wKVg[0]], [r_bk[b0 + 1]])
                        mm(bankf(b0 + 2), [(xt[:, k * 128:(k + 1) * 128], wKV[:, k, 512:1024]) for k in range(KC)], [r_xt, r_wKVg[1]], [r_bk[b0 + 2]])
                        mm(bankf(b0 + 3)[:, 0:72], [(xt[:, k * 128:(k + 1) * 128], wKV[:, k, 1024:1096]) for k in range(KC)], [r_xt, r_wKVg[2]], [r_bk[b0 + 3]])
                        yield
                        kbf = qiq[p][:, 0:512]
                        ikb = qiq[p][:, 512:640]
                        rope(bankf(b0 + 1), [r_bk[b0 + 1]], 4, 64, cs_cos[:, tb, :], cs_sin[:, tb, :], kbf, r_kb[p], tA, tB, r_tA, r_tB)
                        copy("act", Vg[:, tb, :, 0:128], bankf(b0 + 2).rearrange("p (h e) -> p h e", h=4), [r_bk[b0 + 2]], [r_V[tb]])
                        for h in range(4):
                            s.op("act", lambda h=h: nc.scalar.activation(out=jk[:, 2:3].to_broadcast([128, 128]), in_=kbf[:, h * 128:(h + 1) * 128], func=AF.Square, accum_out=K2[:, tb * 4 + h:tb * 4 + h + 1]), [r_kb[p]], [r_k2, r_jk3])
                        rope(bankf(b0 + 3)[:, 0:64], [r_bk[b0 + 3]], 1, 32, cs_cos[:, tb, 0:64:2], cs_sin[:, tb, 0:64:2], ikb[:, 0:64], r_ikb[p], tA, tB, r_tA, r_tB)
                        copy("pool", ikb[:, 64:128], ikb[:, 0:64], [r_ikb[p]], [r_ikb[p]])
                        ts("dve", wsb[:, tb, :], bankf(b0 + 3)[:, 64:72], W_SCALE, None, ALU.mult, None, [r_bk[b0 + 3]], [r_wsb])
                        yield
                        pb = bankb(b0)
                        for h in range(4):
                            tr(pb[:, h * 128:(h + 1) * 128], kbf[:, h * 128:(h + 1) * 128], [r_kb[p]], [r_bk[b0]])
                        tr(pb[:, 512:640], ikb, [r_ikb[p]], [r_bk[b0]])
                        copy("act", KT[:, :, tb * 128:(tb + 1) * 128], pb[:, 0:512].rearrange("p (h t) -> p h t", h=4), [r_bk[b0]], [r_KT[tb]])
                        copy("act", ikT[:, tb * 128:(tb + 1) * 128], pb[:, 512:640], [r_bk[b0]], [r_ikT[tb]])

                    pipeline([blkB(tb) for tb in range(NB)], 2)
                    K2P = sm2[:, 88:92]
                    s.op("dve", lambda: nc.vector.tensor_reduce(out=K2P, in_=K2[:, 0:NB * 4].rearrange("p (b h) -> p h b", h=4), axis=AX.X, op=ALU.max), [r_k2], [r_k2m])
                    allmax4(K2P, r_k2m, K2M, r_k2m, 0, 0)
                    chk('B')
                    wQ = wA[:, 0:KC * 1024].rearrange("p (k c) -> p k c", k=KC)
                    for i in range(2):
                        s.dma("pool", wQ[:, :, i * 512:(i + 1) * 512], wq_d[l][:, :, i * 512:(i + 1) * 512], writes=[r_wQg[i]] + r_wKVg)
                    mb_free = [True, True]

                    def frontC(tb):
                        p = tb % 2
                        q3 = tb % 3
                        Ib = Ibuf2[p]
                        r_I = r_I2[p]
                        L = (tb + 1) * 128
                        r_sb = res("smbis%d" % p)
                        r_sa = res("smatt%d" % p)
                        cB = 64 + p * 16
                        cA = 96 + p * 16
                        stp = steps[:, p, :]
                        make_xT(tb, xbf[p], xT[p], r_xbf[p], r_xT[p], 0)
                        xt, r_xt = xT[p], r_xT[p]
                        mm(bankf(1), [(xt[:, k * 128:(k + 1) * 128], wQ[:, k, 0:512]) for k in range(KC)], [r_xt, r_wQg[0]], [r_bk[1]])
                        mm(bankf(2), [(xt[:, k * 128:(k + 1) * 128], wQ[:, k, 512:1024]) for k in range(KC)], [r_xt, r_wQg[1]], [r_bk[2]])
                        qbf = qiq[p][:, 0:512]
                        iqb = qiq[p][:, 512:1024]
                        rope(bankf(1), [r_bk[1]], 4, 64, cs_cos[:, tb, :], cs_sin[:, tb, :], qbf, r_qbf[p], tA, tB, r_tA, r_tB)
                        rope(bankf(2), [r_bk[2]], 8, 32, cs_cos[:, tb, 0:64:2], cs_sin[:, tb, 0:64:2], iqb, r_iqb[p], tA, tB, r_tA, r_tB)
                        pb = bankb(0)
                        for h in range(4):
                            tr(pb[:, h * 128:(h + 1) * 128], qbf[:, h * 128:(h + 1) * 128], [r_qbf[p]], [r_bk[0]])
                        for h in range(4):
                            tr(pb[:, 512 + h * 128:512 + (h + 1) * 128], iqb[:, h * 128:(h + 1) * 128], [r_iqb[p]], [r_bk[0]])
                        copy("act", qT[q3], pb[:, 0:512].rearrange("p (h t) -> p h t", h=4), [r_bk[0]], [r_qT[q3]])
                        copy("act", iqT[p], pb[:, 512:1024].rearrange("p (h t) -> p h t", h=4), [r_bk[0]], [r_iqT[p]])
                        Q2 = sm2[:, 120 + p * 4:124 + p * 4]
                        Q2M = sm2[:, 128 + p * 4:132 + p * 4]
                        NBQ = sm2[:, 136 + q3 * 4:140 + q3 * 4]
                        r_q2 = res("q2_%d" % p)
                        for h in range(4):
                            s.op("act", lambda h=h: nc.scalar.activation(out=jk[:, 2:3].to_broadcast([128, 128]), in_=qbf[:, h * 128:(h + 1) * 128], func=AF.Square, accum_out=Q2[:, h:h + 1]), [r_qbf[p]], [r_q2, r_jk3])
                        allmax4(Q2, r_q2, Q2M, r_q2, 2, 1 + p)
                        tt("dve", Q2M, Q2M, K2M, ALU.mult, [r_q2, r_k2m], [r_q2])
                        tt("pool", Q2M, Q2M, POSH4, ALU.pow, [r_q2, r_smc], [r_q2])
                        ts("dve", NBQ, Q2M, -ATT_SCALE, None, ALU.mult, None, [r_q2], [r_nbq[q3]])
                        yield
                        nch = (L + 511) // 512
                        cnt = 0
                        for c in range(nch):
                            n = min(512, L - 512 * c)
                            kres = r_ikT[4 * c:4 * c + (n + 127) // 128]
                            for h in range(8):
                                m_, base = h // 2, 64 * (h % 2)
                                bk = 1 + cnt % 2
                                rb, r_rb_ = rbuf[cnt % 2], r_rb[cnt % 2]
                                cnt += 1
                                mm(bankf(bk)[:, 0:n], [(iqT[p][base:base + 64, m_, :], ikT[base:base + 64, c * 512:c * 512 + n])], [r_iqT[p]] + kres, [r_bk[bk]])
                                act(rb[:, 0:n], bankf(bk)[:, 0:n], AF.Relu, [r_bk[bk]], [r_rb_])
                                if h == 0:
                                    ts("dve", Ib[:, c * 512:c * 512 + n], rb[:, 0:n], wsb[:, tb, 0:1], None, ALU.mult, None, [r_rb_, r_wsb], [r_I])
                                else:
                                    stt("dve", Ib[:, c * 512:c * 512 + n], rb[:, 0:n], wsb[:, tb, h:h + 1], Ib[:, c * 512:c * 512 + n], ALU.mult, ALU.add, [r_rb_, r_wsb], [r_I])
                                if h % 2 == 1 and h < 7:
                                    yield
                            yield
                        tt("dve", Ib[:, tb * 128:L], Ib[:, tb * 128:L], cb, ALU.add, [r_cst], [r_I])
                        if L > TOPK:
                            MX, MN, RG, T, CNT, DD = [sm[:, cB + i:cB + i + 1] for i in range(6)]
                            s.op("dve", lambda: nc.vector.tensor_reduce(out=MX, in_=Ib[:, 0:L], axis=AX.X, op=ALU.max), [r_I], [r_sb])
                            s.op("dve", lambda: nc.vector.tensor_reduce(out=MN, in_=Ib[:, 0:L - 128], axis=AX.X, op=ALU.min), [r_I], [r_sb])
                            tt("dve", RG, MX, MN, ALU.subtract, [r_sb], [r_sb])
                            ts("dve", RG, RG, 1.0 + 1.0 / 1024, None, ALU.mult, None, [r_sb], [r_sb])
                            tt("dve", T, MX, RG, ALU.subtract, [r_sb], [r_sb])
                            ts("dve", stp[:, 0:NITER + 2], p2[:, 0:NITER + 2], RG, None, ALU.mult, None, [r_sb, r_cst], [r_sb])
                            stt("dve", T, stp[:, 1:2], 1.0, T, ALU.mult, ALU.add, [r_sb], [r_sb])
                            if p == 1:
                                ts("dve", nstp[:, 0:NITER + 2], stp[:, 0:NITER + 2], -0.5, None, ALU.mult, None, [r_sb], [r_sb])
                            junk = jk[:, 0:1].to_broadcast([128, L])
                            if p == 0:
                                for i in range(NITER):
                                    ts("dve", junk, Ib[:, 0:L], T, None, ALU.is_gt, ALU.add, [r_I, r_sb], [r_jk], accum=CNT)
                                    s.op("dve", lambda: nc.vector.tensor_scalar(out=DD, in0=CNT, scalar1=TOPK - 0.5, scalar2=-0.5, op0=ALU.is_gt, op1=ALU.add), [r_jk], [r_sb])
                                    stt("dve", T, DD, stp[:, i + 1:i + 2], T, ALU.mult, ALU.add, [r_sb], [r_sb])
                                    yield
                            else:
                                NT = sm[:, cB + 6:cB + 7]
                                DS = sm[:, cB + 7:cB + 8]
                                junk2 = jk[:, 1:2].to_broadcast([128, L])
                                ts("dve", NT, T, -1.0, None, ALU.mult, None, [r_sb], [r_sb])
                                for i in range(NITER):
                                    s.op("act", lambda: nc.scalar.activation(out=junk2, in_=Ib[:, 0:L], func=AF.Sign, bias=NT, scale=1.0, accum_out=CNT), [r_I, r_sb], [r_jk2])
                                    s.op("act", lambda: nc.scalar.activation(out=DS, in_=CNT, func=AF.Sign, bias=BIASC[L], scale=1.0), [r_jk2, r_smc], [r_sb2])
                                    s.op("act", lambda i=i: nc.scalar.activation(out=NT, in_=DS, func=AF.Identity, bias=NT, scale=nstp[:, i + 1:i + 2]), [r_sb2, r_sb], [r_sb])
                                    yield
                                ts("dve", T, NT, -1.0, None, ALU.mult, None, [r_sb], [r_sb])
                            tt("dve", T, T, stp[:, NITER + 1:NITER + 2], ALU.subtract, [r_sb], [r_sb])
                            TAU = T
                            r_tau = r_sb
                        else:
                            TAU = TAUALL
                            r_tau = r_smc
                        while not mb_free[p]:
                            yield
                        mb_free[p] = False
                        ts("dve", mb[p][:, 0:L], Ib[:, 0:L], TAU, MASK_NEG, ALU.is_le, ALU.mult, [r_I, r_tau], [r_mb[p]])
                        mbT = [ebuf, ebuf2][p]
                        pb0 = bankb(0)
                        for half in range((tb + 8) // 8):
                            nblk = min(8, tb + 1 - 8 * half)
                            for j in range(nblk):
                                sbk = half * 8 + j
                                tr(pb0[:, j * 128:(j + 1) * 128], mb[p][:, sbk * 128:(sbk + 1) * 128], [r_mb[p]], [r_bk[0]])
                            copy("act", mbT[:, half * 1024:half * 1024 + nblk * 128], pb0[:, 0:nblk * 128], [r_bk[0]], [r_mbT[p]])

                    def backC(tb):
                        p = tb % 2
                        q3 = tb % 3
                        L = (tb + 1) * 128
                        nch = (L + 511) // 512
                        cA = 96 + p * 16
                        mbT = [ebuf, ebuf2][p]
                        PTb = [PTs[0], PT2]
                        NBQ = sm2[:, 136 + q3 * 4:140 + q3 * 4]

                        def hbank(h):
                            return 4 + 2 * (h % 2) if nch <= 2 else 4

                        def qk(h):
                            hb = hbank(h)
                            for sbk in range(tb + 1):
                                bkk = hb + sbk // 4
                                mm(ps[:, hb * 512 + sbk * 128:hb * 512 + (sbk + 1) * 128],
                                   [(KT[:, h, sbk * 128:(sbk + 1) * 128], qT[q3][:, h, :]), (ident[:], mbT[:, sbk * 128:(sbk + 1) * 128])],
                                   [r_qT[q3], r_mbT[p], r_ident, r_KT[sbk]], [r_bk[bkk]])

                        def sexp(h):
                            hb = hbank(h)
                            sc_ = ps[:, hb * 512:hb * 512 + L]
                            rsc = r_bk[hb:hb + nch]
                            act(PTb[h % 2][:, 0:L], sc_, AF.Exp, rsc + [r_nbq[q3]], [r_PT[h % 2]], bias=NBQ[:, h:h + 1], scale=ATT_SCALE)

                        def pv(h):
                            pts = PTb[h % 2]
                            RDa = sm[:, cA + 6 + (h % 2):cA + 7 + (h % 2)]
                            r_sr = res("smrd%d_%d" % (p, h % 2))
                            po = bankf(3)[:, 0:129]
                            mm(po, [(pts[:, sbk * 128:(sbk + 1) * 128], Vg[:, sbk, h, 0:129]) for sbk in range(tb + 1)],
                               [r_PT[h % 2]] + r_V[0:tb + 1], [r_bk[3]])
                            s.op("dve", lambda: nc.vector.reciprocal(out=RDa, in_=bankf(3)[:, 128:129]), [r_bk[3]], [r_sr])
                            ts("dve", attb[p][:, h * 128:(h + 1) * 128], bankf(3)[:, 0:128], RDa, None, ALU.mult, None, [r_bk[3], r_sr], [r_attb[p]])

                        qk(0)
                        sexp(0)
                        yield
                        for h in range(4):
                            if h + 1 < 4:
                                qk(h + 1)
                                sexp(h + 1)
                                yield
                            pv(h)
                            yield
                        pbb = bankb(3)
                        for h in range(4):
                            tr(pbb[:, h * 128:(h + 1) * 128], attb[p][:, h * 128:(h + 1) * 128], [r_attb[p]], [r_bk[3]])
                        copy("act", attTall[:, :, tb * 128:(tb + 1) * 128], pbb[:, 0:512].rearrange("p (h t) -> p h t", h=4), [r_bk[3]], [r_attTall[tb]])

                    fronts = []
                    ready = []
                    back = None
                    done_back = set()
                    next_tb = 0
                    while next_tb < NB or fronts or ready or back is not None:
                        while (next_tb < NB and len(fronts) < 2
                               and all(t % 2 != next_tb % 2 for t, _ in fronts)
                               and (next_tb < 3 or (next_tb - 3) in done_back)):
                            fronts.append((next_tb, frontC(next_tb)))
                            next_tb += 1
                        if back is None and ready:
                            tbb = ready.pop(0)
                            back = (tbb, backC(tbb))
                        if back is not None:
                            try:
                                next(back[1])
                            except StopIteration:
                                mb_free[back[0] % 2] = True
                                done_back.add(back[0])
                                back = None
                        for item in list(fronts):
                            try:
                                next(item[1])
                            except StopIteration:
                                fronts.remove(item)
                                ready.append(item[0])
                    chk('C')
                    s.barrier(list(R.values()))
                    cv = Carver(PERSIST)
                    mixT = cv.take(4 * S).rearrange("p (k t) -> p k t", k=4)
                    PERSIST2 = cv.off
                    wR = cv.take(KC * 2048).rearrange("p (k c) -> p k c", k=KC)
                    xbf = [cv.take(1024) for _ in range(3)]
                    xT = [cv.take(1024) for _ in range(3)]
                    tA = cv.take(2048, F32)
                    tB = cv.take(2048, F32)
                    qk_r = cv.take(2048, F32)
                    qkb = [cv.take(1024) for _ in range(3)]
                    kdec = [cv.take(512) for _ in range(3)]
                    vbf = [cv.take(512) for _ in range(3)]
                    gsl = [cv.take(1024, F32) for _ in range(3)]
                    qkT = [cv.take(1024) for _ in range(3)]
                    stb = [cv.take(512) for _ in range(3)]
                    yn = cv.take(1024, F32)
                    y3 = [cv.take(512) for _ in range(3)]
                    state_f = cv.take(1024, F32)
                    state_bf = cv.take(512)
                    gbA = cv.take(1024, F32)
                    r_mixT = [res("mixT%d" % b) for b in range(NB)]
                    r_wR, r_gbA = res("wR"), res("gbA")
                    r_qk, r_yn = res("qk_r"), res("yn")
                    r_qkb = [res("qkb%d" % i) for i in range(3)]
                    r_kdec = [res("kdec%d" % i) for i in range(3)]
                    r_vbf = [res("vbf%d" % i) for i in range(3)]
                    r_gsl = [res("gsl%d" % i) for i in range(3)]
                    r_qkT = [res("qkT%d" % i) for i in range(3)]
                    r_stb = [res("stb%d" % i) for i in range(3)]
                    r_y3 = [res("y3%d" % i) for i in range(3)]
                    r_stf, r_stbf = res("state_f"), res("state_bf")
                    r_xbfA = [res("xbfA%d" % i) for i in range(3)]
                    r_xTA = [res("xTA%d" % i) for i in range(3)]
                    r_tA2, r_tB2 = res("tA2"), res("tB2")
                    s.dma("sp", gbA[:, 0:512], gret_d[l], writes=[r_gbA])
                    r_wRg = [res("wRg%d" % i) for i in range(4)]
                    for i in range(4):
                        s.dma("pool", wR[:, :, i * 512:(i + 1) * 512], wr_d[l][:, :, i * 512:(i + 1) * 512], writes=[r_wRg[i]])
                    s.op("dve", lambda: nc.vector.memset(state_f, 0.0), [], [r_stf])
                    s.op("pool", lambda: nc.gpsimd.memset(state_bf, 0.0), [], [r_stbf])
                    dct = cst[:, C_RT + 16:C_RT + 20].unsqueeze(2).to_broadcast([128, 4, 128])
                    kps = cst[:, C_RT + 4:C_RT + 8].unsqueeze(2).to_broadcast([128, 4, 128])
                    dks = cst[:, C_RT + 8:C_RT + 12].unsqueeze(2).to_broadcast([128, 4, 128])

                    def blkA(tb):
                        p = tb % 3
                        r_sg = res("smgn%d" % (tb % 2))
                        cG = 128 + (tb % 2) * 48
                        make_xT(tb, xbf[p], xT[p], r_xbfA[p], r_xTA[p], 0)
                        xt, r_xt = xT[p], r_xTA[p]
                        for j in range(4):
                            mm(bankf(1 + j), [(xt[:, k * 128:(k + 1) * 128], wR[:, k, j * 512:(j + 1) * 512]) for k in range(KC)], [r_xt, r_wRg[j]], [r_bk[1 + j]])
                        yield
                        rope(bankf(1, 2), [r_bk[1], r_bk[2]], 8, 64, cs_cos[:, tb, :], cs_sin[:, tb, :], qk_r, r_qk, tA, tB, r_tA2, r_tB2, e2="dve")
                        copy("act", vbf[p], bankf(3), [r_bk[3]], [r_vbf[p]])
                        act(gsl[p], bankf(4), AF.Silu, [r_bk[4]], [r_gsl[p]])
                        yield
                        copy("pool", qkb[p][:, 0:512], qk_r[:, 0:512], [r_qk], [r_qkb[p]])
                        k3 = qk_r[:, 512:1024].rearrange("p (h d) -> p h d", h=4)
                        tt("dve", qkb[p][:, 512:1024].rearrange("p (h d) -> p h d", h=4), k3, kps, ALU.mult, [r_qk, r_cst], [r_qkb[p]])
                        tt("pool", kdec[p].rearrange("p (h d) -> p h d", h=4), k3, dks, ALU.mult, [r_qk, r_cst], [r_kdec[p]])
                        pb = bankb(0)
                        for j in range(8):
                            tr(pb[:, j * 128:(j + 1) * 128], qkb[p][:, j * 128:(j + 1) * 128], [r_qkb[p]], [r_bk[0]])
                        copy("act", qkT[p], pb[:, :], [r_bk[0]], [r_qkT[p]])
                        yield
                        for h in range(4):
                            mm(bankf(5)[:, h * 128:(h + 1) * 128], [(qkT[p][:, 512 + h * 128:512 + (h + 1) * 128], qkT[p][:, h * 128:(h + 1) * 128])], [r_qkT[p]], [r_bk[5]])
                        tt("dve", stb[p].rearrange("p (h d) -> p h d", h=4), bankf(5).rearrange("p (h d) -> p h d", h=4),
                           tri.unsqueeze(1).to_broadcast([128, 4, 128]), ALU.mult, [r_bk[5], r_cst], [r_stb[p]])
                        yield
                        for h in range(4):
                            sl = slice(h * 128, (h + 1) * 128)
                            mm(bankf(6)[:, sl], [(stb[p][:, sl], vbf[p][:, sl]), (qkT[p][:, sl], state_bf[:, sl])], [r_stb[p], r_vbf[p], r_qkT[p], r_stbf], [r_bk[6]])
                        for h in range(4):
                            sl = slice(h * 128, (h + 1) * 128)
                            mm(bankf(7)[:, sl], [(kdec[p][:, sl], vbf[p][:, sl])], [r_kdec[p], r_vbf[p]], [r_bk[7]])
                        tt("pool", state_f.rearrange("p (h d) -> p h d", h=4), state_f.rearrange("p (h d) -> p h d", h=4), dct, ALU.mult, [r_cst], [r_stf])
                        tt("dve", state_f, state_f, bankf(7), ALU.add, [r_bk[7]], [r_stf])
                        copy("pool", state_bf, state_f, [r_stf], [r_stbf])
                        yield
                        st6 = sm[:, cG:cG + 24].rearrange("p (h a) -> p h a", h=4)
                        mv = sm[:, cG + 24:cG + 32].rearrange("p (h a) -> p h a", h=4)
                        for h in range(4):
                            s.op("dve", lambda h=h: nc.vector.bn_stats(out=st6[:, h, :], in_=bankf(6)[:, h * 128:(h + 1) * 128]), [r_bk[6]], [r_sg])
                        for h in range(4):
                            s.op("dve", lambda h=h: nc.vector.bn_aggr(out=mv[:, h, :], in_=st6[:, h, :]), [r_sg], [r_sg])
                        A4 = sm[:, cG + 32:cG + 36]
                        RS4 = sm[:, cG + 36:cG + 40]
                        SC4 = sm[:, cG + 40:cG + 44]
                        NB4 = sm[:, cG + 44:cG + 48]
                        tt("dve", A4, mv[:, :, 1], cst[:, C_RT + 12:C_RT + 16], ALU.mult, [r_sg, r_cst], [r_sg])
                        ts("dve", A4, A4, LN_EPS, None, ALU.add, None, [r_sg], [r_sg])
                        tt("pool", RS4, A4, NEGH4, ALU.pow, [r_sg, r_smc], [r_sg])
                        tt("dve", SC4, RS4, cst[:, C_RT:C_RT + 4], ALU.mult, [r_sg, r_cst], [r_sg])
                        stt("dve", NB4, mv[:, :, 0], -1.0, SC4, ALU.mult, ALU.mult, [r_sg], [r_sg])
                        for h in range(4):
                            sl = slice(h * 128, (h + 1) * 128)
                            act(yn[:, sl], bankf(6)[:, sl], AF.Identity, [r_bk[6], r_sg], [r_yn], bias=NB4[:, h:h + 1], scale=SC4[:, h:h + 1])
                        yield
                        tt("dve", yn, yn, gbA[:, 0:512], ALU.mult, [r_yn, r_gbA], [r_yn])
                        tt("dve", y3[p], yn, gsl[p], ALU.mult, [r_yn, r_gsl[p]], [r_y3[p]])
                        pb = bankb(0)
                        for h in range(4):
                            tr(pb[:, h * 128:(h + 1) * 128], y3[p][:, h * 128:(h + 1) * 128], [r_y3[p]], [r_bk[0]])
                        copy("act", mixT[:, :, tb * 128:(tb + 1) * 128], pb[:, 0:512].rearrange("p (h t) -> p h t", h=4), [r_bk[0]], [r_mixT[tb]])

                    pipeline([blkA(tb) for tb in range(NB)], 3)
                    chk('A')
                    s.barrier(list(R.values()))
                    cv = Carver(PERSIST2)
                    wO = cv.take(KC * 1024).rearrange("p (k c) -> p k c", k=KC)
                    gbm = cv.take(4096, F32)
                    r_wO, r_gbm = res("wO"), res("gbm")
                    r_wOg = [res("wOg%d" % i) for i in range(2)]
                    for i in range(2):
                        s.dma("pool", wO[:, :, i * 512:(i + 1) * 512], wout_d[l][:, :, i * 512:(i + 1) * 512], writes=[r_wOg[i]])
                    s.dma("sp", gbm[:, :], gmix_d[l], writes=[r_gbm])
                    for tb in range(NB):
                        b0 = 2 * (tb % 2)
                        for hh in range(2):
                            pairs = [(mixT[:, k, tb * 128:(tb + 1) * 128], wO[:, k, hh * 512:(hh + 1) * 512]) for k in range(4)]
                            pairs += [(attTall[:, k, tb * 128:(tb + 1) * 128], wO[:, 4 + k, hh * 512:(hh + 1) * 512]) for k in range(4)]
                            mm(bankf(b0 + hh), pairs, [r_mixT[tb], r_attTall[tb], r_wOg[hh]], [r_bk[b0 + hh]])
                        layer_norm_block(tb, [bankf(b0), bankf(b0 + 1)], [r_bk[b0], r_bk[b0 + 1]], gbm, r_gbm)
                    chk('C2')
                    s.barrier(list(R.values()))
                    cv = Carver()
                    NG = S // GT
                    nblk = GT // 128
                    ncol = GT // 512
                    x1T = [cv.take(KC * GT).rearrange("p (k t) -> p k t", k=KC) for _ in range(2)]
                    hT = cv.take(HC * GT).rearrange("p (c t) -> p c t", c=HC)
                    wD = cv.take(HC * 1024).rearrange("p (c n) -> p c n", c=HC)
                    wG = [cv.take(KC * 512).rearrange("p (k c) -> p k c", k=KC) for _ in range(3)]
                    xbf2 = [cv.take(1024) for _ in range(2)]
                    sgb = [cv.take(1024, F32) for _ in range(2)]
                    gbf = cv.take(4096, F32)
                    r_x1T = [[res("x1T%d_%d" % (q, i)) for i in range(nblk)] for q in range(2)]
                    r_hT = [res("hT%d_%d" % (c, t)) for c in range(HC) for t in range(ncol)]
                    r_wD = [res("wD%d" % i) for i in range(4)]
                    r_wG = [res("wG%d" % i) for i in range(3)]
                    r_xbf2 = [res("xbf2_0"), res("xbf2_1")]
                    r_sgl = [res("sg0"), res("sg1")]
                    r_gbf = res("gbf")
                    wdq = [(0, 6), (6, 12), (12, 18), (18, 22)]
                    for i, (c0, c1) in enumerate(wdq):
                        s.dma("pool", wD[:, c0:c1, :], wd_d[l][:, c0:c1, :], writes=[r_wD[i]])
                    s.dma("sp", gbf[:, :], gffn_d[l], writes=[r_gbf])
                    NP = HC // 2
                    gu = [0]

                    def load_gu(pr):
                        slot = gu[0] % 3
                        gu[0] += 1
                        s.dma("sp", wG[slot].rearrange("p k c -> p (k c)"), wgu_bf[l, pr], reads=[r_conv[l]], writes=[r_wG[slot]])
                        return slot

                    def prep_x1T(g):
                        q = g % 2
                        for j in range(nblk):
                            tb = g * nblk + j
                            i2 = j % 2
                            copy("act" if j % 2 == 0 else "dve", xbf2[i2], x_sb[:, tb, :], [r_x[tb]], [r_xbf2[i2]])
                            pb = bankb(i2)
                            for k in range(KC):
                                tr(pb[:, k * 128:(k + 1) * 128], xbf2[i2][:, k * 128:(k + 1) * 128], [r_xbf2[i2]], [r_bk[i2]])
                            copy("act", x1T[q][:, :, j * 128:(j + 1) * 128], pb[:, :].rearrange("p (k t) -> p k t", k=KC), [r_bk[i2]], [r_x1T[q][j]])

                    prep_x1T(0)
                    for g in range(NG):
                        q = g % 2
                        slots = [load_gu(0), load_gu(1)]
                        for pr in range(NP):
                            if pr + 2 < NP:
                                slots.append(load_gu(pr + 2))
                            slot = slots[pr]
                            for ci in range(2):
                                c = 2 * pr + ci
                                for th in range(ncol):
                                    xr = r_x1T[q][4 * th:4 * th + 4]
                                    qq = (c * ncol + th) % 2
                                    bG, bU = 2 + 2 * qq, 3 + 2 * qq
                                    mm(bankf(bG), [(wG[slot][:, k, ci * 256:ci * 256 + 128], x1T[q][:, k, th * 512:(th + 1) * 512]) for k in range(KC)], xr + [r_wG[slot]], [r_bk[bG]])
                                    mm(bankf(bU), [(wG[slot][:, k, ci * 256 + 128:ci * 256 + 256], x1T[q][:, k, th * 512:(th + 1) * 512]) for k in range(KC)], xr + [r_wG[slot]], [r_bk[bU]])
                                    act(sgb[qq], bankf(bG), AF.Silu, [r_bk[bG]], [r_sgl[qq]])
                                    tt("dve", hT[:, c, th * 512:(th + 1) * 512], sgb[qq], bankf(bU), ALU.mult, [r_sgl[qq], r_bk[bU]], [r_hT[c * ncol + th]])
                        if g + 1 < NG:
                            prep_x1T(g + 1)
                        for j in range(nblk):
                            tb = g * nblk + j
                            th = j // 4
                            hres = [r_hT[c * ncol + th] for c in range(HC)]
                            for hh in range(2):
                                mm(bankf(6 + hh), [(hT[:, c, j * 128:(j + 1) * 128], wD[:, c, hh * 512:(hh + 1) * 512]) for c in range(HC)],
                                   hres + r_wD, [r_bk[6 + hh]])
                            layer_norm_block(tb, [bankf(6), bankf(7)], [r_bk[6], r_bk[7]], gbf, r_gbf)
                    s.barrier(list(R.values()))
              except _Stop:
                pass
              if True:
                yv = y_d[sq].rearrange("(b p) d -> p b d", p=128)
                for b in range(NB):
                    s.dma("sp", yv[:, b, :], x_sb[:, b, :], reads=[r_x[b]], semres=r_stb_[b])
            for b in range(NB):
                s._wait("sp", ("dma", r_stb_[b], r_stb_[b].cnt))

        s1 = Sched(nc, es, None)
        emit(s1)
        s2 = Sched(nc, es, s1.needed)
        emit(s2)
    return nc


_CACHE = {}


def _prep_inputs(cfg, x, positions, w_in, ret_gn_gain, w_out, ln_mix_gain, ln_mix_bias,
                 w_gate_up, w_down, ln_ffn_gain, ln_ffn_bias, ncores):
    DEPTH = cfg.DEPTH
    f = lambda a: np.ascontiguousarray(np.asarray(a, dtype=np.float32))
    g_ret = np.ascontiguousarray(np.broadcast_to(f(ret_gn_gain)[:, None, :], (DEPTH, 128, 512)))
    g_mix = np.ascontiguousarray(np.broadcast_to(
        np.concatenate([f(ln_mix_gain), f(ln_mix_bias)], axis=1)[:, None, :], (DEPTH, 128, 2048)))
    g_ffn = np.ascontiguousarray(np.broadcast_to(
        np.concatenate([f(ln_ffn_gain), f(ln_ffn_bias)], axis=1)[:, None, :], (DEPTH, 128, 2048)))
    cst = make_consts(cfg)
    w_in = f(w_in)

    def tile_k(w):
        return np.ascontiguousarray(w.reshape(DEPTH, KC, 128, w.shape[-1]).transpose(0, 2, 1, 3))

    w_kv = tile_k(np.concatenate([w_in[:, :, OFF_AK:OFF_AK + 1024], w_in[:, :, OFF_IK:OFF_IK + 72]], axis=2))
    w_q = tile_k(np.concatenate([w_in[:, :, OFF_AQ:OFF_AQ + 512], w_in[:, :, OFF_IQ:OFF_IQ + 512]], axis=2))
    w_r = tile_k(w_in[:, :, 0:2048])
    w_o = tile_k(f(w_out))
    wgu = f(w_gate_up)
    g4 = wgu[:, :, 0:HID].reshape(DEPTH, KC, 128, HC // 2, 2, 128)
    u4 = wgu[:, :, HID:2 * HID].reshape(DEPTH, KC, 128, HC // 2, 2, 128)
    gu = np.stack([g4, u4], axis=5)
    gu = gu.transpose(0, 3, 2, 1, 4, 5, 6)
    w_gu = np.ascontiguousarray(gu.reshape(DEPTH, HC // 2, 128, KC * 512))
    w_d = np.ascontiguousarray(f(w_down).reshape(DEPTH, HC, 128, 1024).transpose(0, 2, 1, 3))
    x = f(x)
    pos = np.asarray(positions).astype(np.int32)
    shared = {"w_kv": w_kv, "w_q": w_q, "w_r": w_r, "w_o": w_o, "w_gu": w_gu, "w_d": w_d,
              "g_ret": g_ret, "g_mix": g_mix, "g_ffn": g_ffn, "cst": cst}
    maps = []
    for c in range(ncores):
        xs = x[c * cfg.NSEQ:(c + 1) * cfg.NSEQ]
        ps_ = pos[c * cfg.NSEQ:(c + 1) * cfg.NSEQ]
        ps_ = np.ascontiguousarray(ps_.reshape(cfg.NSEQ, cfg.NB, 128).transpose(0, 2, 1))
        m = dict(shared)
        m["x"] = np.ascontiguousarray(xs)
        m["pos"] = ps_
        maps.append(m)
    return maps


def run(cfg, ncores, **inputs):
    key = (cfg.S, cfg.NSEQ, cfg.DEPTH, cfg.NITER, cfg.stop)
    if key not in _CACHE:
        _CACHE[key] = build(cfg)
    nc = _CACHE[key]
    maps = _prep_inputs(cfg, ncores=ncores, **inputs)
    res = run_bass_kernel_spmd(nc, maps, core_ids=list(range(ncores)))
    return np.concatenate([r["y"] for r in res.results], axis=0)


def kernel(x, positions, w_in, ret_gn_gain, w_out, ln_mix_gain, ln_mix_bias,
           w_gate_up, w_down, ln_ffn_gain, ln_ffn_bias):
    cfg = Cfg(S=2048, NSEQ=2, DEPTH=2, NITER=17)
    out = run(cfg, 8, x=x, positions=positions, w_in=w_in, ret_gn_gain=ret_gn_gain, w_out=w_out,
              ln_mix_gain=ln_mix_gain, ln_mix_bias=ln_mix_bias, w_gate_up=w_gate_up, w_down=w_down,
              ln_ffn_gain=ln_ffn_gain, ln_ffn_bias=ln_ffn_bias)
    return out.astype(np.float32)
```

```python
import math
from contextlib import ExitStack
import numpy as np
import concourse.bass as bass
import concourse.mybir as mybir
from concourse.bass_utils import run_bass_kernel_spmd

F32 = mybir.dt.float32
BF16 = mybir.dt.bfloat16
I32 = mybir.dt.int32
ALU = mybir.AluOpType
AF = mybir.ActivationFunctionType
AX = mybir.AxisListType

D = 1024
KC = 8
HID = 2816
HC = 22
IN_COLS = 4168
OFF_RQ, OFF_RK, OFF_RV, OFF_RG, OFF_AQ, OFF_AK, OFF_AV, OFF_IQ, OFF_IK, OFF_IW = (
    0, 512, 1024, 1536, 2048, 2560, 3072, 3584, 4096, 4160)
LN_EPS = 1e-5
MAGIC = 12582912.0
TWO_PI = 2.0 * math.pi
C1 = 6.28125
C2 = TWO_PI - C1
PI_LO = 3.1415925
NEG_BIG = -1.0e30
MASK_NEG = -30000.0

C_ID, C_TRI, C_CB, C_RT, C_INVF, C_P2, C_END = 0, 128, 256, 384, 404, 468, 500


class Res:
    __slots__ = ("name", "w", "r", "sem", "cnt")

    def __init__(self, name):
        self.name = name
        self.w = None
        self.r = {}
        self.sem = None
        self.cnt = 0


class Sched:
    ENG = ("pe", "act", "dve", "pool", "sp")

    def __init__(self, nc, es, needed):
        self.nc = nc
        self.es = es
        self.rec = needed is None
        self.needed = set() if self.rec else needed
        self.idx = {e: 0 for e in self.ENG}
        self.sig = {e: 0 for e in self.ENG}
        self.cntof = {}
        self.waited = {}
        self.eng = dict(pe=nc.tensor, act=nc.scalar, dve=nc.vector, pool=nc.gpsimd, sp=nc.sync)
        self.sem = {}
        self.nsem = 0
        if not self.rec:
            for e in self.ENG:
                self.sem[e] = es.enter_context(nc.semaphore("sem_" + e))

    def _wait(self, e, tok):
        if tok is None:
            return
        if tok[0] == "eng":
            _, pe_, i = tok
            if pe_ == e and e in ("pe", "sp"):
                return
            if self.rec:
                self.needed.add((pe_, i))
                return
            val = self.cntof[(pe_, i)]
            key = (e, pe_)
            semh = self.sem[pe_]
        else:
            _, res, val = tok
            if self.rec:
                return
            key = (e, "dma", res.name)
            semh = res.sem
        if self.waited.get(key, 0) >= val:
            return
        self.waited[key] = val
        self.eng[e].wait_ge(semh, val)

    def _deps(self, e, reads, writes):
        for r in reads:
            self._wait(e, r.w)
        for w in writes:
            self._wait(e, w.w)
            for t in list(w.r.values()):
                self._wait(e, t)

    def op(self, e, fn, reads=(), writes=()):
        self._deps(e, reads, writes)
        i = self.idx[e]
        self.idx[e] += 1
        tok = ("eng", e, i)
        if not self.rec:
            ins = fn()
            if (e, i) in self.needed:
                ins.then_inc(self.sem[e], 1)
                self.sig[e] += 1
                self.cntof[(e, i)] = self.sig[e]
        for r in reads:
            r.r[e] = tok
        for w in writes:
            w.w = tok
            w.r = {}
        return tok

    def _getsem(self, res):
        if res.sem is None and not self.rec:
            res.sem = self.es.enter_context(self.nc.semaphore("dsem_%d" % self.nsem))
            self.nsem += 1

    def dma(self, q, out_ap, in_ap, reads=(), writes=(), semres=None):
        self._deps(q, reads, writes)
        self.idx[q] += 1
        tr = writes[0] if writes else semres
        self._getsem(tr)
        tr.cnt += 16
        tok = ("dma", tr, tr.cnt)
        if not self.rec:
            self.eng[q].dma_start(out=out_ap, in_=in_ap).then_inc(tr.sem, 16)
        for w in writes:
            w.w = tok
            w.r = {}
        for r in reads:
            r.r["dma_" + tr.name] = tok
        return tok

    def barrier(self, allres):
        toks = []
        for r in allres:
            if r.w is not None:
                toks.append(r.w)
            toks.extend(r.r.values())
        for e in ("pe", "act", "dve", "pool", "sp"):
            for t in toks:
                self._wait(e, t)


class _Stop(Exception):
    pass


class Cfg:
    stop = None

    def __init__(self, S=2048, NSEQ=2, DEPTH=2, NITER=20):
        self.S = S
        self.NSEQ = NSEQ
        self.DEPTH = DEPTH
        self.NB = S // 128
        self.TOPK = min(256, S // 4)
        self.NITER = NITER
        self.GT = min(512, S)


def make_consts(cfg):
    c = np.zeros((128, C_END), np.float32)
    p = np.arange(128)
    c[:, C_ID:C_ID + 128] = np.eye(128, dtype=np.float32)
    c[:, C_TRI:C_TRI + 128] = (p[None, :] >= p[:, None]).astype(np.float32)
    c[:, C_CB:C_CB + 128] = np.where(p[None, :] <= p[:, None], 0.0, NEG_BIG)
    for h in range(4):
        lg = math.log(1.0 - 2.0 ** (-5.0 - h))
        dq = np.exp(lg * (p + 1.0))
        c[:, C_RT + h] = dq
        c[:, C_RT + 4 + h] = np.exp(-lg * (p + 1.0)) * 128 ** -0.5
        c[:, C_RT + 8 + h] = np.exp(lg * (127.0 - p)) * 128 ** -0.5
        c[:, C_RT + 12 + h] = dq * dq
        c[:, C_RT + 16 + h] = (1.0 - 2.0 ** (-5.0 - h)) ** 128
    invf = (10000.0 ** (-np.arange(0, 128, 2, dtype=np.float32) / 128)).astype(np.float32)
    c[:, C_INVF:C_INVF + 64] = invf[None, :]
    c[:, C_P2:C_P2 + 32] = (2.0 ** (-np.arange(32, dtype=np.float64)))[None, :]
    return c


def pipeline(gens, depth):
    gens = list(gens)
    active = []
    nxt = 0
    while nxt < len(gens) or active:
        if nxt < len(gens) and len(active) < depth:
            active.append(gens[nxt])
            nxt += 1
        for g in list(active):
            try:
                next(g)
            except StopIteration:
                active.remove(g)


def build(cfg):
    S, NSEQ, DEPTH, NB, TOPK, NITER, GT = cfg.S, cfg.NSEQ, cfg.DEPTH, cfg.NB, cfg.TOPK, cfg.NITER, cfg.GT
    nc = bass.Bass("TRN2", target_bir_lowering=False)
    dt = nc.dram_tensor
    x_d = dt("x", [NSEQ, S, D], F32, kind="ExternalInput").ap()
    pos_d = dt("pos", [NSEQ, 128, NB], I32, kind="ExternalInput").ap()
    wkv_d = dt("w_kv", [DEPTH, 128, KC, 1096], F32, kind="ExternalInput").ap()
    wq_d = dt("w_q", [DEPTH, 128, KC, 1024], F32, kind="ExternalInput").ap()
    wr_d = dt("w_r", [DEPTH, 128, KC, 2048], F32, kind="ExternalInput").ap()
    wout_d = dt("w_o", [DEPTH, 128, KC, 1024], F32, kind="ExternalInput").ap()
    wgu_d = dt("w_gu", [DEPTH, HC // 2, 128, KC * 512], F32, kind="ExternalInput").ap()
    wd_d = dt("w_d", [DEPTH, 128, HC, 1024], F32, kind="ExternalInput").ap()
    gret_d = dt("g_ret", [DEPTH, 128, 512], F32, kind="ExternalInput").ap()
    gmix_d = dt("g_mix", [DEPTH, 128, 2048], F32, kind="ExternalInput").ap()
    gffn_d = dt("g_ffn", [DEPTH, 128, 2048], F32, kind="ExternalInput").ap()
    cst_d = dt("cst", [128, C_END], F32, kind="ExternalInput").ap()
    y_d = dt("y", [NSEQ, S, D], F32, kind="ExternalOutput").ap()
    wgu_bf = dt("wgu_bf", [DEPTH, HC // 2, 128, KC * 512], BF16, kind="Internal").ap()

    ARENA = 63 * 1024
    with ExitStack() as es:
        sb = lambda name, shape, dtp: es.enter_context(nc.sbuf_tensor(name, shape, dtp))
        x_sb = sb("x_sb", [128, NB, D], F32)
        cs_cos = sb("cs_cos", [128, NB, 64], F32)
        cs_sin = sb("cs_sin", [128, NB, 64], F32)
        cst = sb("cst_sb", [128, C_END], F32)
        ident = sb("ident", [128, 128], BF16)
        posi = sb("posi", [128, NB], I32)
        posf = sb("posf", [128, NB], F32)
        wsb = sb("wsb", [128, NB, 8], F32)
        sm = sb("sm", [128, 256], F32)
        steps = sb("steps", [128, 2, 32], F32)
        nstp = sb("nstp", [128, 32], F32)
        sm2 = sb("sm2", [128, 160], F32)
        ones4 = sb("ones4", [4, 128], F32)
        biasc_t = sb("biasc", [128, 32], F32)
        sb_late = lambda name, shape, dtp: biasc_t
        ang = sb("ang", [128, 4, 64], F32)
        arena = sb("arena", [128, ARENA], BF16)
        ps = es.enter_context(nc.psum_tensor("ps", [128, 4096], F32))

        def bankf(i, n=1):
            return ps[:, i * 512:(i + n) * 512]

        def bankb(i):
            return ps[:, i * 512:(i + 1) * 512].bitcast(BF16)

        class Carver:
            def __init__(self, off=0):
                self.off = off

            def take(self, n_bf16, dtype=BF16):
                n = (n_bf16 + 15) // 16 * 16
                ap = arena[:, self.off:self.off + n_bf16]
                self.off += n
                assert self.off <= ARENA, "arena overflow %d" % self.off
                if dtype == F32:
                    ap = ap.bitcast(F32)
                return ap

        def emit(s):
            R = {}

            def res(name):
                if name not in R:
                    R[name] = Res(name)
                return R[name]

            E = s.eng
            r_cst, r_ident, r_cs, r_pos, r_wsb = res("cst"), res("ident"), res("cs"), res("pos"), res("wsb")
            r_x = [res("x%d" % b) for b in range(NB)]
            r_bk = [res("bank%d" % i) for i in range(8)]
            r_stb_ = [res("store%d" % b) for b in range(NB)]
            r_smc = res("smc")

            def tt(e, out, in0, in1, op, reads, writes):
                return s.op(e, lambda: E[e].tensor_tensor(out=out, in0=in0, in1=in1, op=op), reads, writes)

            def ts(e, out, in0, s1, s2, op0, op1, reads, writes, accum=None):
                if op1 is None:
                    return s.op(e, lambda: E[e].tensor_scalar(out=out, in0=in0, scalar1=s1, scalar2=None, op0=op0), reads, writes)
                if accum is not None:
                    return s.op(e, lambda: E[e].tensor_scalar(out=out, in0=in0, scalar1=s1, scalar2=s2, op0=op0, op1=op1, accum_out=accum), reads, writes)
                return s.op(e, lambda: E[e].tensor_scalar(out=out, in0=in0, scalar1=s1, scalar2=s2, op0=op0, op1=op1), reads, writes)

            def stt(e, out, in0, sc, in1, op0, op1, reads, writes):
                return s.op(e, lambda: E[e].scalar_tensor_tensor(out=out, in0=in0, scalar=sc, in1=in1, op0=op0, op1=op1), reads, writes)

            def act(out, in_, func, reads, writes, bias=None, scale=None):
                kw = {}
                if bias is not None:
                    kw["bias"] = bias
                if scale is not None:
                    kw["scale"] = scale
                return s.op("act", lambda: nc.scalar.activation(out=out, in_=in_, func=func, **kw), reads, writes)

            def copy(e, out, in_, reads, writes):
                if e == "act":
                    return s.op("act", lambda: nc.scalar.copy(out=out, in_=in_), reads, writes)
                return s.op(e, lambda: E[e].tensor_copy(out=out, in_=in_), reads, writes)

            def mm(out, pairs, reads, writes):
                def f():
                    ins = None
                    n = len(pairs)
                    for i, (l, r) in enumerate(pairs):
                        ins = nc.tensor.matmul(out, lhsT=l, rhs=r, start=(i == 0), stop=(i == n - 1))
                    return ins
                return s.op("pe", f, reads, writes)

            def tr(out, in_, reads, writes):
                return s.op("pe", lambda: nc.tensor.transpose(out=out, in_=in_, identity=ident[:]), list(reads) + [r_ident], writes)

            s.dma("sp", cst[:], cst_d, writes=[r_cst])
            copy("dve", ident[:], cst[:, C_ID:C_ID + 128], [r_cst], [r_ident])
            cb = cst[:, C_CB:C_CB + 128]
            tri = cst[:, C_TRI:C_TRI + 128]
            invf = cst[:, C_INVF:C_INVF + 64]
            p2 = cst[:, C_P2:C_P2 + 32]
            s.op("dve", lambda: nc.vector.memset(sm[:, 0:8], -0.5), [], [r_smc])
            s.op("dve", lambda: nc.vector.memset(sm[:, 8:9], -1.0e29), [], [r_smc])
            biasc = sb_late("biasc", [128, 32], F32)
            BIASC = {}
            for tbq in range(NB):
                Lq = (tbq + 1) * 128
                s.op("dve", lambda tbq=tbq, Lq=Lq: nc.vector.memset(biasc[:, tbq:tbq + 1], float(-(2 * TOPK - Lq - 1))), [], [r_smc])
                BIASC[Lq] = biasc[:, tbq:tbq + 1]
            r_ones4 = res("ones4")
            s.op("dve", lambda: nc.vector.memset(ones4[:], 1.0), [], [r_ones4])
            s.op("dve", lambda: nc.vector.memset(sm2[:, 0:4], 0.5), [], [r_smc])
            POSH4 = sm2[:, 0:4]
            K2 = sm2[:, 16:16 + 64]
            K2M = sm2[:, 84:88]
            identf = cst[:, C_ID:C_ID + 128]

            def allmax4(src, r_src, dst, r_dst, bk, tmpc):
                r_t = res("amx%d" % tmpc)
                mx4 = sm2[0:4, 100 + tmpc * 8:101 + tmpc * 8]
                dg4 = sm2[0:4, 101 + tmpc * 8:105 + tmpc * 8]
                pT = bankf(bk)[0:4, 0:128]
                s.op("pe", lambda: nc.tensor.transpose(out=pT, in_=src, identity=identf), [r_src, r_cst], [r_bk[bk]])
                s.op("dve", lambda: nc.vector.tensor_reduce(out=mx4, in_=pT, axis=AX.X, op=ALU.max), [r_bk[bk]], [r_t])
                ts("dve", dg4, cst[0:4, C_ID:C_ID + 4], mx4, None, ALU.mult, None, [r_t, r_cst], [r_t])
                pB = bankf(bk)[:, 128:132]
                s.op("pe", lambda: nc.tensor.matmul(pB, lhsT=ones4[0:4, :], rhs=dg4, start=True, stop=True), [r_t, r_ones4], [r_bk[bk]])
                copy("dve", dst, pB, [r_bk[bk]], [r_dst])

            NEGH4 = sm[:, 0:4]
            NEGH = sm[:, 0:1]
            TAUALL = sm[:, 8:9]
            r_conv = [res("conv%d" % l) for l in range(DEPTH)]
            for l in range(DEPTH):
                for pr in range(HC // 2):
                    s.dma("pool", wgu_bf[l, pr], wgu_d[l, pr], semres=r_conv[l])
                r_conv[l].w = ("dma", r_conv[l], r_conv[l].cnt)

            xrot = [0]

            def make_xT(tb, xbf, xT, r_xbf, r_xT, bk):
                e = "act" if xrot[0] % 2 == 0 else "dve"
                xrot[0] += 1
                copy(e, xbf, x_sb[:, tb, :], [r_x[tb]], [r_xbf])
                pb = bankb(bk)
                for k in range(KC):
                    tr(pb[:, k * 128:(k + 1) * 128], xbf[:, k * 128:(k + 1) * 128], [r_xbf], [r_bk[bk]])
                copy("act", xT, pb[:, :], [r_bk[bk]], [r_xT])

            def rope(src, r_src, G, half, cosap, sinap, out, r_out, tA, tB, r_tA, r_tB, e2="pool"):
                n = G * 2 * half
                s3 = src.rearrange("p (g d) -> p g d", d=half)
                cbb = cosap.unsqueeze(1).to_broadcast([128, 2 * G, half])
                sbb = sinap.unsqueeze(1).to_broadcast([128, 2 * G, half])
                tA3 = tA[:, 0:n].rearrange("p (g d) -> p g d", d=half)
                tB3 = tB[:, 0:n].rearrange("p (g d) -> p g d", d=half)
                tt("dve", tA3, s3, cbb, ALU.mult, list(r_src) + [r_cs], [r_tA])
                tt("dve", tB3, s3, sbb, ALU.mult, list(r_src) + [r_cs], [r_tB])
                tA4 = tA[:, 0:n].rearrange("p (g t d) -> p g t d", t=2, d=half)
                tB4 = tB[:, 0:n].rearrange("p (g t d) -> p g t d", t=2, d=half)
                o4 = out.rearrange("p (g t d) -> p g t d", t=2, d=half)
                tt(e2, o4[:, :, 0, :], tA4[:, :, 0, :], tB4[:, :, 1, :], ALU.subtract, [r_tA, r_tB], [r_out])
                tt(e2, o4[:, :, 1, :], tA4[:, :, 1, :], tB4[:, :, 0, :], ALU.add, [r_tA, r_tB], [r_out])

            lncnt = [0]

            def layer_norm_block(tb, ypair, r_y, gbt, r_gbt):
                par = lncnt[0] % 2
                lncnt[0] += 1
                c0 = 16 + par * 24
                r_s = res("smln%d" % par)
                xb = x_sb[:, tb, :]
                for hh in range(2):
                    stt("dve", xb[:, hh * 512:(hh + 1) * 512], xb[:, hh * 512:(hh + 1) * 512], ALPHA,
                        ypair[hh], ALU.mult, ALU.add, [r_y[hh]], [r_x[tb]])
                st6 = sm[:, c0:c0 + 12].rearrange("p (a b) -> p a b", a=2)
                for hh in range(2):
                    s.op("dve", lambda hh=hh: nc.vector.bn_stats(out=st6[:, hh, :], in_=xb[:, hh * 512:(hh + 1) * 512]), [r_x[tb]], [r_s])
                s.op("dve", lambda: nc.vector.bn_aggr(out=sm[:, c0 + 12:c0 + 14], in_=st6), [r_s], [r_s])
                ts("dve", sm[:, c0 + 14:c0 + 15], sm[:, c0 + 13:c0 + 14], LN_EPS, None, ALU.add, None, [r_s], [r_s])
                tt("pool", sm[:, c0 + 15:c0 + 16], sm[:, c0 + 14:c0 + 15], NEGH, ALU.pow, [r_s, r_smc], [r_s])
                stt("dve", sm[:, c0 + 16:c0 + 17], sm[:, c0 + 12:c0 + 13], -1.0, sm[:, c0 + 15:c0 + 16], ALU.mult, ALU.mult, [r_s], [r_s])
                act(xb, xb, AF.Identity, [r_x[tb], r_s], [r_x[tb]], bias=sm[:, c0 + 16:c0 + 17], scale=sm[:, c0 + 15:c0 + 16])
                tt("dve", xb, xb, gbt[:, 0:1024], ALU.mult, [r_x[tb], r_gbt], [r_x[tb]])
                tt("pool", xb, xb, gbt[:, 1024:2048], ALU.add, [r_x[tb], r_gbt], [r_x[tb]])

            ALPHA = (2.0 * DEPTH) ** 0.25
            ATT_SCALE = 128 ** -0.5
            W_SCALE = (8 ** -0.5) * (64 ** -0.5)

            def chk(name):
                if cfg.stop == name:
                    raise _Stop()

            for sq in range(NSEQ):
              try:
                xv = x_d[sq].rearrange("(b p) d -> p b d", p=128)
                for b in range(NB):
                    s.dma("sp", x_sb[:, b, :], xv[:, b, :], writes=[r_x[b]])
                s.dma("sp", posi[:], pos_d[sq], writes=[r_pos])
                copy("dve", posf[:], posi[:], [r_pos], [r_pos])
                r_ang = res("ang")
                for b in range(NB):
                    A_, K_, Rs, Rc = ang[:, 0, :], ang[:, 1, :], ang[:, 2, :], ang[:, 3, :]
                    ts("dve", A_, invf, posf[:, b:b + 1], None, ALU.mult, None, [r_cst, r_pos], [r_ang])
                    for which, dst in ((0, Rs), (1, Rc)):
                        src = A_
                        if which == 1:
                            ts("dve", Rc, A_, math.pi / 2, None, ALU.add, None, [r_ang], [r_ang])
                            src = Rc
                        ts("dve", K_, src, 1.0 / TWO_PI, MAGIC, ALU.mult, ALU.add, [r_ang], [r_ang])
                        ts("dve", K_, K_, MAGIC, None, ALU.subtract, None, [r_ang], [r_ang])
                        stt("dve", dst, K_, -C1, src, ALU.mult, ALU.add, [r_ang], [r_ang])
                        stt("dve", dst, K_, -C2, dst, ALU.mult, ALU.add, [r_ang], [r_ang])
                        ts("dve", dst, dst, PI_LO, -PI_LO, ALU.min, ALU.max, [r_ang], [r_ang])
                    act(cs_sin[:, b, :], Rs, AF.Sin, [r_ang], [r_cs])
                    act(cs_cos[:, b, :], Rc, AF.Sin, [r_ang], [r_cs])
                chk('setup')
                for l in range(DEPTH):
                    cv = Carver()
                    attTall = cv.take(4 * S).rearrange("p (k t) -> p k t", k=4)
                    PERSIST = cv.off
                    KT = cv.take(4 * S).rearrange("p (h t) -> p h t", h=4)
                    Vg = cv.take(NB * 4 * 130).rearrange("p (b h e) -> p b h e", b=NB, h=4)
                    ikT = cv.take(S)
                    wA = cv.take(KC * 1096)
                    xbf_ = cv.take(1024)
                    xT_ = cv.take(1024)
                    xbf = [xbf_, xbf_]
                    xT = [xT_, xT_]
                    tA = cv.take(1024, F32)
                    tB = cv.take(1024, F32)
                    rbuf = [cv.take(1024, F32) for _ in range(2)]
                    qiq_ = cv.take(1024)
                    qiq = [qiq_, qiq_]
                    ebuf2 = cv.take(S)
                    attb_ = cv.take(512)
                    attb = [attb_, attb_]
                    qT = [cv.take(512).rearrange("p (h t) -> p h t", h=4) for _ in range(3)]
                    iqT = [cv.take(512).rearrange("p (h t) -> p h t", h=4) for _ in range(2)]
                    Ibuf2 = [cv.take(2 * S, F32) for _ in range(2)]
                    jk = cv.take(16)
                    mb_ = cv.take(S)
                    mb = [mb_, mb_]
                    PT2 = cv.take(S)
                    ebuf = cv.take(S)
                    PTs_ = cv.take(S)
                    PTs = [PTs_, PTs_]
                    r_attTall = [res("attTall%d" % b) for b in range(NB)]
                    r_KT = [res("KT%d" % b) for b in range(NB)]
                    r_V = [res("V%d" % b) for b in range(NB)]
                    r_ikT = [res("ikT%d" % b) for b in range(NB)]
                    r_xbf = [res("xbf0"), res("xbf0")]
                    r_xT = [res("xT0"), res("xT0")]
                    if S >= 1024:
                        xbfB = [xbf_, PT2[:, 0:1024]]
                        xTB = [xT_, PTs_[:, 0:1024]]
                        r_xbfB = [r_xbf[0], res("PT2")]
                        r_xTB = [r_xT[0], res("PTs0")]
                    else:
                        xbfB, xTB, r_xbfB, r_xTB = xbf, xT, r_xbf, r_xT
                    r_tA, r_tB = res("tA"), res("tB")
                    r_wA = res("wA")
                    r_I2, r_e = [res("I0"), res("I1")], res("e")
                    r_jk = res("jk")
                    r_mb = [res("mb0"), res("mb0")]
                    r_mbT = [res("mbT0"), res("mbT1")]
                    r_PT = [res("PTs0"), res("PT2")]
                    r_nbq = [res("nbq%d" % i) for i in range(3)]
                    r_k2, r_k2m = res("k2"), res("k2m")
                    r_PTs = [res("PTs0"), res("PTs0")]
                    r_rb = [res("rb0"), res("rb1")]
                    r_qbf = [res("qbf0"), res("qbf0")]
                    r_iqb = [res("iqb0"), res("iqb0")]
                    r_e2, r_jk2, r_sb2, r_jk3 = res("e2"), res("jk2"), res("sb2"), res("jk3")
                    r_attb = [res("attb0"), res("attb0")]
                    r_kb = r_qbf
                    r_ikb = r_iqb
                    r_qT = [res("qT0"), res("qT1"), res("qT2")]
                    r_iqT = [res("iqT0"), res("iqT1")]

                    wKV = wA[:, 0:KC * 1096].rearrange("p (k c) -> p k c", k=KC)
                    r_wKVg = [res("wKVg%d" % i) for i in range(3)]
                    r_wQg = [res("wQg%d" % i) for i in range(2)]
                    for i, (c0, c1) in enumerate(((0, 512), (512, 1024), (1024, 1096))):
                        s.dma("pool", wKV[:, :, c0:c1], wkv_d[l][:, :, c0:c1], writes=[r_wKVg[i]] + r_wQg)
                    for tb in range(NB):
                        s.op("pool", lambda tb=tb: nc.gpsimd.memset(Vg[:, tb, :, 128:130], 1.0), [], [r_V[tb]])

                    def blkB(tb):
                        p = tb % 2
                        b0 = 4 * p
                        make_xT(tb, xbfB[p], xTB[p], r_xbfB[p], r_xTB[p], b0)
                        xt, r_xt = xTB[p], r_xTB[p]
                        mm(bankf(b0 + 1), [(xt[:, k * 128:(k + 1) * 128], wKV[:, k, 0:512]) for k in range(KC)], [r_xt, r_wKVg[0]], [r_bk[b0 + 1]])
                        mm(bankf(b0 + 2), [(xt[:, k * 128:(k + 1) * 128], wKV[:, k, 512:1024]) for k in range(KC)], [r_xt, r_wKVg[1]], [r_bk[b0 + 2]])
                        mm(bankf(b0 + 3)[:, 0:72], [(xt[:, k * 128:(k + 1) * 128], wKV[:, k, 1024:1096]) for k in range(KC)], [r_xt, r_wKVg[2]], [r_bk[b0 + 3]])
                        yield
                        kbf = qiq[p][:, 0:512]
                        ikb = qiq[p][:, 512:640]
                        rope(bankf(b0 + 1), [r_bk[b0 + 1]], 4, 64, cs_cos[:, tb, :], cs_sin[:, tb, :], kbf, r_kb[p], tA, tB, r_tA, r_tB)
                        copy("act", Vg[:, tb, :, 0:128], bankf(b0 + 2).rearrange("p (h e) -> p h e", h=4), [r_bk[b0 + 2]], [r_V[tb]])
                        for h in range(4):
                            s.op("act", lambda h=h: nc.scalar.activation(out=jk[:, 2:3].to_broadcast([128, 128]), in_=kbf[:, h * 128:(h + 1) * 128], func=AF.Square, accum_out=K2[:, tb * 4 + h:tb * 4 + h + 1]), [r_kb[p]], [r_k2, r_jk3])
                        rope(bankf(b0 + 3)[:, 0:64], [r_bk[b0 + 3]], 1, 32, cs_cos[:, tb, 0:64:2], cs_sin[:, tb, 0:64:2], ikb[:, 0:64], r_ikb[p], tA, tB, r_tA, r_tB)
                        copy("pool", ikb[:, 64:128], ikb[:, 0:64], [r_ikb[p]], [r_ikb[p]])
                        ts("dve", wsb[:, tb, :], bankf(b0 + 3)[:, 64:72], W_SCALE, None, ALU.mult, None, [r_bk[b0 + 3]], [r_wsb])
                        yield
                        pb = bankb(b0)
                        for h in range(4):
                            tr(pb[:, h * 128:(h + 1) * 128], kbf[:, h * 128:(h + 1) * 128], [r_kb[p]], [r_bk[b0]])
                        tr(pb[:, 512:640], ikb, [r_ikb[p]], [r_bk[b0]])
                        copy("act", KT[:, :, tb * 128:(tb + 1) * 128], pb[:, 0:512].rearrange("p (h t) -> p h t", h=4), [r_bk[b0]], [r_KT[tb]])
                        copy("act", ikT[:, tb * 128:(tb + 1) * 128], pb[:, 512:640], [r_bk[b0]], [r_ikT[tb]])

                    pipeline([blkB(tb) for tb in range(NB)], 2)
                    K2P = sm2[:, 88:92]
                    s.op("dve", lambda: nc.vector.tensor_reduce(out=K2P, in_=K2[:, 0:NB * 4].rearrange("p (b h) -> p h b", h=4), axis=AX.X, op=ALU.max), [r_k2], [r_k2m])
                    allmax4(K2P, r_k2m, K2M, r_k2m, 0, 0)
                    chk('B')
                    wQ = wA[:, 0:KC * 1024].rearrange("p (k c) -> p k c", k=KC)
                    for i in range(2):
                        s.dma("pool", wQ[:, :, i * 512:(i + 1) * 512], wq_d[l][:, :, i * 512:(i + 1) * 512], writes=[r_wQg[i]] + r_wKVg)
                    mb_free = [True, True]

                    def frontC(tb):
                        p = tb % 2
                        q3 = tb % 3
                        Ib = Ibuf2[p]
                        r_I = r_I2[p]
                        L = (tb + 1) * 128
                        r_sb = res("smbis%d" % p)
                        r_sa = res("smatt%d" % p)
                        cB = 64 + p * 16
                        cA = 96 + p * 16
                        stp = steps[:, p, :]
                        make_xT(tb, xbf[p], xT[p], r_xbf[p], r_xT[p], 0)
                        xt, r_xt = xT[p], r_xT[p]
                        mm(bankf(1), [(xt[:, k * 128:(k + 1) * 128], wQ[:, k, 0:512]) for k in range(KC)], [r_xt, r_wQg[0]], [r_bk[1]])
                        mm(bankf(2), [(xt[:, k * 128:(k + 1) * 128], wQ[:, k, 512:1024]) for k in range(KC)], [r_xt, r_wQg[1]], [r_bk[2]])
                        qbf = qiq[p][:, 0:512]
                        iqb = qiq[p][:, 512:1024]
                        rope(bankf(1), [r_bk[1]], 4, 64, cs_cos[:, tb, :], cs_sin[:, tb, :], qbf, r_qbf[p], tA, tB, r_tA, r_tB)
                        rope(bankf(2), [r_bk[2]], 8, 32, cs_cos[:, tb, 0:64:2], cs_sin[:, tb, 0:64:2], iqb, r_iqb[p], tA, tB, r_tA, r_tB)
                        pb = bankb(0)
                        for h in range(4):
                            tr(pb[:, h * 128:(h + 1) * 128], qbf[:, h * 128:(h + 1) * 128], [r_qbf[p]], [r_bk[0]])
                        for h in range(4):
                            tr(pb[:, 512 + h * 128:512 + (h + 1) * 128], iqb[:, h * 128:(h + 1) * 128], [r_iqb[p]], [r_bk[0]])
                        copy("act", qT[q3], pb[:, 0:512].rearrange("p (h t) -> p h t", h=4), [r_bk[0]], [r_qT[q3]])
                        copy("act", iqT[p], pb[:, 512:1024].rearrange("p (h t) -> p h t", h=4), [r_bk[0]], [r_iqT[p]])
                        Q2 = sm2[:, 120 + p * 4:124 + p * 4]
                        Q2M = sm2[:, 128 + p * 4:132 + p * 4]
                        NBQ = sm2[:, 136 + q3 * 4:140 + q3 * 4]
                        r_q2 = res("q2_%d" % p)
                        for h in range(4):
                            s.op("act", lambda h=h: nc.scalar.activation(out=jk[:, 2:3].to_broadcast([128, 128]), in_=qbf[:, h * 128:(h + 1) * 128], func=AF.Square, accum_out=Q2[:, h:h + 1]), [r_qbf[p]], [r_q2, r_jk3])
                        allmax4(Q2, r_q2, Q2M, r_q2, 2, 1 + p)
                        tt("dve", Q2M, Q2M, K2M, ALU.mult, [r_q2, r_k2m], [r_q2])
                        tt("pool", Q2M, Q2M, POSH4, ALU.pow, [r_q2, r_smc], [r_q2])
                        ts("dve", NBQ, Q2M, -ATT_SCALE, None, ALU.mult, None, [r_q2], [r_nbq[q3]])
                        yield
                        nch = (L + 511) // 512
                        cnt = 0
                        for c in range(nch):
                            n = min(512, L - 512 * c)
                            kres = r_ikT[4 * c:4 * c + (n + 127) // 128]
                            for h in range(8):
                                m_, base = h // 2, 64 * (h % 2)
                                bk = 1 + cnt % 2
                                rb, r_rb_ = rbuf[cnt % 2], r_rb[cnt % 2]
                                cnt += 1
                                mm(bankf(bk)[:, 0:n], [(iqT[p][base:base + 64, m_, :], ikT[base:base + 64, c * 512:c * 512 + n])], [r_iqT[p]] + kres, [r_bk[bk]])
                                act(rb[:, 0:n], bankf(bk)[:, 0:n], AF.Relu, [r_bk[bk]], [r_rb_])
                                if h == 0:
                                    ts("dve", Ib[:, c * 512:c * 512 + n], rb[:, 0:n], wsb[:, tb, 0:1], None, ALU.mult, None, [r_rb_, r_wsb], [r_I])
                                else:
                                    stt("dve", Ib[:, c * 512:c * 512 + n], rb[:, 0:n], wsb[:, tb, h:h + 1], Ib[:, c * 512:c * 512 + n], ALU.mult, ALU.add, [r_rb_, r_wsb], [r_I])
                                if h == 3:
                                    yield
                            yield
                        tt("dve", Ib[:, tb * 128:L], Ib[:, tb * 128:L], cb, ALU.add, [r_cst], [r_I])
                        if L > TOPK:
                            MX, MN, RG, T, CNT, DD = [sm[:, cB + i:cB + i + 1] for i in range(6)]
                            s.op("dve", lambda: nc.vector.tensor_reduce(out=MX, in_=Ib[:, 0:L], axis=AX.X, op=ALU.max), [r_I], [r_sb])
                            s.op("dve", lambda: nc.vector.tensor_reduce(out=MN, in_=Ib[:, 0:L - 128], axis=AX.X, op=ALU.min), [r_I], [r_sb])
                            tt("dve", RG, MX, MN, ALU.subtract, [r_sb], [r_sb])
                            ts("dve", RG, RG, 1.0 + 1.0 / 1024, None, ALU.mult, None, [r_sb], [r_sb])
                            tt("dve", T, MX, RG, ALU.subtract, [r_sb], [r_sb])
                            ts("dve", stp[:, 0:NITER + 2], p2[:, 0:NITER + 2], RG, None, ALU.mult, None, [r_sb, r_cst], [r_sb])
                            stt("dve", T, stp[:, 1:2], 1.0, T, ALU.mult, ALU.add, [r_sb], [r_sb])
                            if p == 1:
                                ts("dve", nstp[:, 0:NITER + 2], stp[:, 0:NITER + 2], -0.5, None, ALU.mult, None, [r_sb], [r_sb])
                            junk = jk[:, 0:1].to_broadcast([128, L])
                            if p == 0:
                                for i in range(NITER):
                                    ts("dve", junk, Ib[:, 0:L], T, None, ALU.is_gt, ALU.add, [r_I, r_sb], [r_jk], accum=CNT)
                                    s.op("dve", lambda: nc.vector.tensor_scalar(out=DD, in0=CNT, scalar1=TOPK - 0.5, scalar2=-0.5, op0=ALU.is_gt, op1=ALU.add), [r_jk], [r_sb])
                                    stt("dve", T, DD, stp[:, i + 1:i + 2], T, ALU.mult, ALU.add, [r_sb], [r_sb])
                                    yield
                            else:
                                NT = sm[:, cB + 6:cB + 7]
                                DS = sm[:, cB + 7:cB + 8]
                                junk2 = jk[:, 1:2].to_broadcast([128, L])
                                ts("dve", NT, T, -1.0, None, ALU.mult, None, [r_sb], [r_sb])
                                for i in range(NITER):
                                    s.op("act", lambda: nc.scalar.activation(out=junk2, in_=Ib[:, 0:L], func=AF.Sign, bias=NT, scale=1.0, accum_out=CNT), [r_I, r_sb], [r_jk2])
                                    s.op("act", lambda: nc.scalar.activation(out=DS, in_=CNT, func=AF.Sign, bias=BIASC[L], scale=1.0), [r_jk2, r_smc], [r_sb2])
                                    s.op("act", lambda i=i: nc.scalar.activation(out=NT, in_=DS, func=AF.Identity, bias=NT, scale=nstp[:, i + 1:i + 2]), [r_sb2, r_sb], [r_sb])
                                    yield
                                ts("dve", T, NT, -1.0, None, ALU.mult, None, [r_sb], [r_sb])
                            tt("dve", T, T, stp[:, NITER + 1:NITER + 2], ALU.subtract, [r_sb], [r_sb])
                            TAU = T
                            r_tau = r_sb
                        else:
                            TAU = TAUALL
                            r_tau = r_smc
                        while not mb_free[p]:
                            yield
                        mb_free[p] = False
                        ts("dve", mb[p][:, 0:L], Ib[:, 0:L], TAU, MASK_NEG, ALU.is_le, ALU.mult, [r_I, r_tau], [r_mb[p]])
                        mbT = [ebuf, ebuf2][p]
                        pb0 = bankb(0)
                        for half in range((tb + 8) // 8):
                            nblk = min(8, tb + 1 - 8 * half)
                            for j in range(nblk):
                                sbk = half * 8 + j
                                tr(pb0[:, j * 128:(j + 1) * 128], mb[p][:, sbk * 128:(sbk + 1) * 128], [r_mb[p]], [r_bk[0]])
                            copy("act", mbT[:, half * 1024:half * 1024 + nblk * 128], pb0[:, 0:nblk * 128], [r_bk[0]], [r_mbT[p]])

                    def backC(tb):
                        p = tb % 2
                        q3 = tb % 3
                        L = (tb + 1) * 128
                        nch = (L + 511) // 512
                        cA = 96 + p * 16
                        mbT = [ebuf, ebuf2][p]
                        PTb = [PTs[0], PT2]
                        NBQ = sm2[:, 136 + q3 * 4:140 + q3 * 4]

                        def hbank(h):
                            return 4 + 2 * (h % 2) if nch <= 2 else 4

                        def qk(h):
                            hb = hbank(h)
                            for sbk in range(tb + 1):
                                bkk = hb + sbk // 4
                                mm(ps[:, hb * 512 + sbk * 128:hb * 512 + (sbk + 1) * 128],
                                   [(KT[:, h, sbk * 128:(sbk + 1) * 128], qT[q3][:, h, :]), (ident[:], mbT[:, sbk * 128:(sbk + 1) * 128])],
                                   [r_qT[q3], r_mbT[p], r_ident, r_KT[sbk]], [r_bk[bkk]])

                        def sexp(h):
                            hb = hbank(h)
                            sc_ = ps[:, hb * 512:hb * 512 + L]
                            rsc = r_bk[hb:hb + nch]
                            act(PTb[h % 2][:, 0:L], sc_, AF.Exp, rsc + [r_nbq[q3]], [r_PT[h % 2]], bias=NBQ[:, h:h + 1], scale=ATT_SCALE)

                        def pv(h):
                            pts = PTb[h % 2]
                            RDa = sm[:, cA + 6 + (h % 2):cA + 7 + (h % 2)]
                            r_sr = res("smrd%d_%d" % (p, h % 2))
                            po = bankf(3)[:, 0:129]
                            mm(po, [(pts[:, sbk * 128:(sbk + 1) * 128], Vg[:, sbk, h, 0:129]) for sbk in range(tb + 1)],
                               [r_PT[h % 2]] + r_V[0:tb + 1], [r_bk[3]])
                            s.op("dve", lambda: nc.vector.reciprocal(out=RDa, in_=bankf(3)[:, 128:129]), [r_bk[3]], [r_sr])
                            ts("dve", attb[p][:, h * 128:(h + 1) * 128], bankf(3)[:, 0:128], RDa, None, ALU.mult, None, [r_bk[3], r_sr], [r_attb[p]])

                        qk(0)
                        sexp(0)
                        yield
                        for h in range(4):
                            if h + 1 < 4:
                                qk(h + 1)
                                sexp(h + 1)
                                yield
                            pv(h)
                            yield
                        pbb = bankb(3)
                        for h in range(4):
                            tr(pbb[:, h * 128:(h + 1) * 128], attb[p][:, h * 128:(h + 1) * 128], [r_attb[p]], [r_bk[3]])
                        copy("act", attTall[:, :, tb * 128:(tb + 1) * 128], pbb[:, 0:512].rearrange("p (h t) -> p h t", h=4), [r_bk[3]], [r_attTall[tb]])

                    fronts = []
                    ready = []
                    back = None
                    done_back = set()
                    next_tb = 0
                    while next_tb < NB or fronts or ready or back is not None:
                        while (next_tb < NB and len(fronts) < 2
                               and all(t % 2 != next_tb % 2 for t, _ in fronts)
                               and (next_tb < 3 or (next_tb - 3) in done_back)):
                            fronts.append((next_tb, frontC(next_tb)))
                            next_tb += 1
                        if back is None and ready:
                            tbb = ready.pop(0)
                            back = (tbb, backC(tbb))
                        if back is not None:
                            try:
                                next(back[1])
                            except StopIteration:
                                mb_free[back[0] % 2] = True
                                done_back.add(back[0])
                                back = None
                        for item in list(fronts):
                            try:
                                next(item[1])
                            except StopIteration:
                                fronts.remove(item)
                                ready.append(item[0])
                    chk('C')
                    s.barrier(list(R.values()))
                    cv = Carver(PERSIST)
                    mixT = cv.take(4 * S).rearrange("p (k t) -> p k t", k=4)
                    PERSIST2 = cv.off
                    wR = cv.take(KC * 2048).rearrange("p (k c) -> p k c", k=KC)
                    xbf = [cv.take(1024) for _ in range(3)]
                    xT = [cv.take(1024) for _ in range(3)]
                    tA = cv.take(2048, F32)
                    tB = cv.take(2048, F32)
                    qk_r = cv.take(2048, F32)
                    qkb = [cv.take(1024) for _ in range(3)]
                    kdec = [cv.take(512) for _ in range(3)]
                    vbf = [cv.take(512) for _ in range(3)]
                    gsl = [cv.take(1024, F32) for _ in range(3)]
                    qkT = [cv.take(1024) for _ in range(3)]
                    stb = [cv.take(512) for _ in range(3)]
                    yn = cv.take(1024, F32)
                    y3 = [cv.take(512) for _ in range(3)]
                    state_f = cv.take(1024, F32)
                    state_bf = cv.take(512)
                    gbA = cv.take(1024, F32)
                    r_mixT = [res("mixT%d" % b) for b in range(NB)]
                    r_wR, r_gbA = res("wR"), res("gbA")
                    r_qk, r_yn = res("qk_r"), res("yn")
                    r_qkb = [res("qkb%d" % i) for i in range(3)]
                    r_kdec = [res("kdec%d" % i) for i in range(3)]
                    r_vbf = [res("vbf%d" % i) for i in range(3)]
                    r_gsl = [res("gsl%d" % i) for i in range(3)]
                    r_qkT = [res("qkT%d" % i) for i in range(3)]
                    r_stb = [res("stb%d" % i) for i in range(3)]
                    r_y3 = [res("y3%d" % i) for i in range(3)]
                    r_stf, r_stbf = res("state_f"), res("state_bf")
                    r_xbfA = [res("xbfA%d" % i) for i in range(3)]
                    r_xTA = [res("xTA%d" % i) for i in range(3)]
                    r_tA2, r_tB2 = res("tA2"), res("tB2")
                    s.dma("sp", gbA[:, 0:512], gret_d[l], writes=[r_gbA])
                    r_wRg = [res("wRg%d" % i) for i in range(4)]
                    for i in range(4):
                        s.dma("pool", wR[:, :, i * 512:(i + 1) * 512], wr_d[l][:, :, i * 512:(i + 1) * 512], writes=[r_wRg[i]])
                    s.op("dve", lambda: nc.vector.memset(state_f, 0.0), [], [r_stf])
                    s.op("pool", lambda: nc.gpsimd.memset(state_bf, 0.0), [], [r_stbf])
                    dct = cst[:, C_RT + 16:C_RT + 20].unsqueeze(2).to_broadcast([128, 4, 128])
                    kps = cst[:, C_RT + 4:C_RT + 8].unsqueeze(2).to_broadcast([128, 4, 128])
                    dks = cst[:, C_RT + 8:C_RT + 12].unsqueeze(2).to_broadcast([128, 4, 128])

                    def blkA(tb):
                        p = tb % 3
                        r_sg = res("smgn%d" % (tb % 2))
                        cG = 128 + (tb % 2) * 48
                        make_xT(tb, xbf[p], xT[p], r_xbfA[p], r_xTA[p], 0)
                        xt, r_xt = xT[p], r_xTA[p]
                        for j in range(4):
                            mm(bankf(1 + j), [(xt[:, k * 128:(k + 1) * 128], wR[:, k, j * 512:(j + 1) * 512]) for k in range(KC)], [r_xt, r_wRg[j]], [r_bk[1 + j]])
                        yield
                        rope(bankf(1, 2), [r_bk[1], r_bk[2]], 8, 64, cs_cos[:, tb, :], cs_sin[:, tb, :], qk_r, r_qk, tA, tB, r_tA2, r_tB2, e2="dve")
                        copy("act", vbf[p], bankf(3), [r_bk[3]], [r_vbf[p]])
                        act(gsl[p], bankf(4), AF.Silu, [r_bk[4]], [r_gsl[p]])
                        yield
                        copy("pool", qkb[p][:, 0:512], qk_r[:, 0:512], [r_qk], [r_qkb[p]])
                        k3 = qk_r[:, 512:1024].rearrange("p (h d) -> p h d", h=4)
                        tt("dve", qkb[p][:, 512:1024].rearrange("p (h d) -> p h d", h=4), k3, kps, ALU.mult, [r_qk, r_cst], [r_qkb[p]])
                        tt("pool", kdec[p].rearrange("p (h d) -> p h d", h=4), k3, dks, ALU.mult, [r_qk, r_cst], [r_kdec[p]])
                        pb = bankb(0)
                        for j in range(8):
                            tr(pb[:, j * 128:(j + 1) * 128], qkb[p][:, j * 128:(j + 1) * 128], [r_qkb[p]], [r_bk[0]])
                        copy("act", qkT[p], pb[:, :], [r_bk[0]], [r_qkT[p]])
                        yield
                        for h in range(4):
                            mm(bankf(5)[:, h * 128:(h + 1) * 128], [(qkT[p][:, 512 + h * 128:512 + (h + 1) * 128], qkT[p][:, h * 128:(h + 1) * 128])], [r_qkT[p]], [r_bk[5]])
                        tt("dve", stb[p].rearrange("p (h d) -> p h d", h=4), bankf(5).rearrange("p (h d) -> p h d", h=4),
                           tri.unsqueeze(1).to_broadcast([128, 4, 128]), ALU.mult, [r_bk[5], r_cst], [r_stb[p]])
                        yield
                        for h in range(4):
                            sl = slice(h * 128, (h + 1) * 128)
                            mm(bankf(6)[:, sl], [(stb[p][:, sl], vbf[p][:, sl]), (qkT[p][:, sl], state_bf[:, sl])], [r_stb[p], r_vbf[p], r_qkT[p], r_stbf], [r_bk[6]])
                        for h in range(4):
                            sl = slice(h * 128, (h + 1) * 128)
                            mm(bankf(7)[:, sl], [(kdec[p][:, sl], vbf[p][:, sl])], [r_kdec[p], r_vbf[p]], [r_bk[7]])
                        tt("pool", state_f.rearrange("p (h d) -> p h d", h=4), state_f.rearrange("p (h d) -> p h d", h=4), dct, ALU.mult, [r_cst], [r_stf])
                        tt("dve", state_f, state_f, bankf(7), ALU.add, [r_bk[7]], [r_stf])
                        copy("pool", state_bf, state_f, [r_stf], [r_stbf])
                        yield
                        st6 = sm[:, cG:cG + 24].rearrange("p (h a) -> p h a", h=4)
                        mv = sm[:, cG + 24:cG + 32].rearrange("p (h a) -> p h a", h=4)
                        for h in range(4):
                            s.op("dve", lambda h=h: nc.vector.bn_stats(out=st6[:, h, :], in_=bankf(6)[:, h * 128:(h + 1) * 128]), [r_bk[6]], [r_sg])
                        for h in range(4):
                            s.op("dve", lambda h=h: nc.vector.bn_aggr(out=mv[:, h, :], in_=st6[:, h, :]), [r_sg], [r_sg])
                        A4 = sm[:, cG + 32:cG + 36]
                        RS4 = sm[:, cG + 36:cG + 40]
                        SC4 = sm[:, cG + 40:cG + 44]
                        NB4 = sm[:, cG + 44:cG + 48]
                        tt("dve", A4, mv[:, :, 1], cst[:, C_RT + 12:C_RT + 16], ALU.mult, [r_sg, r_cst], [r_sg])
                        ts("dve", A4, A4, LN_EPS, None, ALU.add, None, [r_sg], [r_sg])
                        tt("pool", RS4, A4, NEGH4, ALU.pow, [r_sg, r_smc], [r_sg])
                        tt("dve", SC4, RS4, cst[:, C_RT:C_RT + 4], ALU.mult, [r_sg, r_cst], [r_sg])
                        stt("dve", NB4, mv[:, :, 0], -1.0, SC4, ALU.mult, ALU.mult, [r_sg], [r_sg])
                        for h in range(4):
                            sl = slice(h * 128, (h + 1) * 128)
                            act(yn[:, sl], bankf(6)[:, sl], AF.Identity, [r_bk[6], r_sg], [r_yn], bias=NB4[:, h:h + 1], scale=SC4[:, h:h + 1])
                        yield
                        tt("dve", yn, yn, gbA[:, 0:512], ALU.mult, [r_yn, r_gbA], [r_yn])
                        tt("dve", y3[p], yn, gsl[p], ALU.mult, [r_yn, r_gsl[p]], [r_y3[p]])
                        pb = bankb(0)
                        for h in range(4):
                            tr(pb[:, h * 128:(h + 1) * 128], y3[p][:, h * 128:(h + 1) * 128], [r_y3[p]], [r_bk[0]])
                        copy("act", mixT[:, :, tb * 128:(tb + 1) * 128], pb[:, 0:512].rearrange("p (h t) -> p h t", h=4), [r_bk[0]], [r_mixT[tb]])

                    pipeline([blkA(tb) for tb in range(NB)], 3)
                    chk('A')
                    s.barrier(list(R.values()))
                    cv = Carver(PERSIST2)
                    wO = cv.take(KC * 1024).rearrange("p (k c) -> p k c", k=KC)
                    gbm = cv.take(4096, F32)
                    r_wO, r_gbm = res("wO"), res("gbm")
                    r_wOg = [res("wOg%d" % i) for i in range(2)]
                    for i in range(2):
                        s.dma("pool", wO[:, :, i * 512:(i + 1) * 512], wout_d[l][:, :, i * 512:(i + 1) * 512], writes=[r_wOg[i]])
                    s.dma("sp", gbm[:, :], gmix_d[l], writes=[r_gbm])
                    for tb in range(NB):
                        b0 = 2 * (tb % 2)
                        for hh in range(2):
                            pairs = [(mixT[:, k, tb * 128:(tb + 1) * 128], wO[:, k, hh * 512:(hh + 1) * 512]) for k in range(4)]
                            pairs += [(attTall[:, k, tb * 128:(tb + 1) * 128], wO[:, 4 + k, hh * 512:(hh + 1) * 512]) for k in range(4)]
                            mm(bankf(b0 + hh), pairs, [r_mixT[tb], r_attTall[tb], r_wOg[hh]], [r_bk[b0 + hh]])
                        layer_norm_block(tb, [bankf(b0), bankf(b0 + 1)], [r_bk[b0], r_bk[b0 + 1]], gbm, r_gbm)
                    chk('C2')
                    s.barrier(list(R.values()))
                    cv = Carver()
                    NG = S // GT
                    nblk = GT // 128
                    ncol = GT // 512
                    x1T = [cv.take(KC * GT).rearrange("p (k t) -> p k t", k=KC) for _ in range(2)]
                    hT = cv.take(HC * GT).rearrange("p (c t) -> p c t", c=HC)
                    wD = cv.take(HC * 1024).rearrange("p (c n) -> p c n", c=HC)
                    wG = [cv.take(KC * 512).rearrange("p (k c) -> p k c", k=KC) for _ in range(3)]
                    xbf2 = [cv.take(1024) for _ in range(2)]
                    sgb = [cv.take(1024, F32) for _ in range(2)]
                    gbf = cv.take(4096, F32)
                    r_x1T = [[res("x1T%d_%d" % (q, i)) for i in range(nblk)] for q in range(2)]
                    r_hT = [res("hT%d_%d" % (c, t)) for c in range(HC) for t in range(ncol)]
                    r_wD = [res("wD%d" % i) for i in range(4)]
                    r_wG = [res("wG%d" % i) for i in range(3)]
                    r_xbf2 = [res("xbf2_0"), res("xbf2_1")]
                    r_sgl = [res("sg0"), res("sg1")]
                    r_gbf = res("gbf")
                    wdq = [(0, 6), (6, 12), (12, 18), (18, 22)]
                    for i, (c0, c1) in enumerate(wdq):
                        s.dma("pool", wD[:, c0:c1, :], wd_d[l][:, c0:c1, :], writes=[r_wD[i]])
                    s.dma("sp", gbf[:, :], gffn_d[l], writes=[r_gbf])
                    NP = HC // 2
                    gu = [0]

                    def load_gu(pr):
                        slot = gu[0] % 3
                        gu[0] += 1
                        s.dma("sp", wG[slot].rearrange("p k c -> p (k c)"), wgu_bf[l, pr], reads=[r_conv[l]], writes=[r_wG[slot]])
                        return slot

                    def prep_x1T(g):
                        q = g % 2
                        for j in range(nblk):
                            tb = g * nblk + j
                            i2 = j % 2
                            copy("act" if j % 2 == 0 else "dve", xbf2[i2], x_sb[:, tb, :], [r_x[tb]], [r_xbf2[i2]])
                            pb = bankb(i2)
                            for k in range(KC):
                                tr(pb[:, k * 128:(k + 1) * 128], xbf2[i2][:, k * 128:(k + 1) * 128], [r_xbf2[i2]], [r_bk[i2]])
                            copy("act", x1T[q][:, :, j * 128:(j + 1) * 128], pb[:, :].rearrange("p (k t) -> p k t", k=KC), [r_bk[i2]], [r_x1T[q][j]])

                    prep_x1T(0)
                    for g in range(NG):
                        q = g % 2
                        slots = [load_gu(0), load_gu(1)]
                        for pr in range(NP):
                            if pr + 2 < NP:
                                slots.append(load_gu(pr + 2))
                            slot = slots[pr]
                            for ci in range(2):
                                c = 2 * pr + ci
                                for th in range(ncol):
                                    xr = r_x1T[q][4 * th:4 * th + 4]
                                    qq = (c * ncol + th) % 2
                                    bG, bU = 2 + 2 * qq, 3 + 2 * qq
                                    mm(bankf(bG), [(wG[slot][:, k, ci * 256:ci * 256 + 128], x1T[q][:, k, th * 512:(th + 1) * 512]) for k in range(KC)], xr + [r_wG[slot]], [r_bk[bG]])
                                    mm(bankf(bU), [(wG[slot][:, k, ci * 256 + 128:ci * 256 + 256], x1T[q][:, k, th * 512:(th + 1) * 512]) for k in range(KC)], xr + [r_wG[slot]], [r_bk[bU]])
                                    act(sgb[qq], bankf(bG), AF.Silu, [r_bk[bG]], [r_sgl[qq]])
                                    tt("dve", hT[:, c, th * 512:(th + 1) * 512], sgb[qq], bankf(bU), ALU.mult, [r_sgl[qq], r_bk[bU]], [r_hT[c * ncol + th]])
                        if g + 1 < NG:
                            prep_x1T(g + 1)
                        for j in range(nblk):
                            tb = g * nblk + j
                            th = j // 4
                            hres = [r_hT[c * ncol + th] for c in range(HC)]
                            for hh in range(2):
                                mm(bankf(6 + hh), [(hT[:, c, j * 128:(j + 1) * 128], wD[:, c, hh * 512:(hh + 1) * 512]) for c in range(HC)],
                                   hres + r_wD, [r_bk[6 + hh]])
                            layer_norm_block(tb, [bankf(6), bankf(7)], [r_bk[6], r_bk[7]], gbf, r_gbf)
                    s.barrier(list(R.values()))
              except _Stop:
                pass
              if True:
                yv = y_d[sq].rearrange("(b p) d -> p b d", p=128)
                for b in range(NB):
                    s.dma("sp", yv[:, b, :], x_sb[:, b, :], reads=[r_x[b]], semres=r_stb_[b])
            for b in range(NB):
                s._wait("sp", ("dma", r_stb_[b], r_stb_[b].cnt))

        s1 = Sched(nc, es, None)
        emit(s1)
        s2 = Sched(nc, es, s1.needed)
        emit(s2)
    return nc


_CACHE = {}


def _prep_inputs(cfg, x, positions, w_in, ret_gn_gain, w_out, ln_mix_gain, ln_mix_bias,
                 w_gate_up, w_down, ln_ffn_gain, ln_ffn_bias, ncores):
    DEPTH = cfg.DEPTH
    f = lambda a: np.ascontiguousarray(np.asarray(a, dtype=np.float32))
    g_ret = np.ascontiguousarray(np.broadcast_to(f(ret_gn_gain)[:, None, :], (DEPTH, 128, 512)))
    g_mix = np.ascontiguousarray(np.broadcast_to(
        np.concatenate([f(ln_mix_gain), f(ln_mix_bias)], axis=1)[:, None, :], (DEPTH, 128, 2048)))
    g_ffn = np.ascontiguousarray(np.broadcast_to(
        np.concatenate([f(ln_ffn_gain), f(ln_ffn_bias)], axis=1)[:, None, :], (DEPTH, 128, 2048)))
    cst = make_consts(cfg)
    w_in = f(w_in)

    def tile_k(w):
        return np.ascontiguousarray(w.reshape(DEPTH, KC, 128, w.shape[-1]).transpose(0, 2, 1, 3))

    w_kv = tile_k(np.concatenate([w_in[:, :, OFF_AK:OFF_AK + 1024], w_in[:, :, OFF_IK:OFF_IK + 72]], axis=2))
    w_q = tile_k(np.concatenate([w_in[:, :, OFF_AQ:OFF_AQ + 512], w_in[:, :, OFF_IQ:OFF_IQ + 512]], axis=2))
    w_r = tile_k(w_in[:, :, 0:2048])
    w_o = tile_k(f(w_out))
    wgu = f(w_gate_up)
    g4 = wgu[:, :, 0:HID].reshape(DEPTH, KC, 128, HC // 2, 2, 128)
    u4 = wgu[:, :, HID:2 * HID].reshape(DEPTH, KC, 128, HC // 2, 2, 128)
    gu = np.stack([g4, u4], axis=5)
    gu = gu.transpose(0, 3, 2, 1, 4, 5, 6)
    w_gu = np.ascontiguousarray(gu.reshape(DEPTH, HC // 2, 128, KC * 512))
    w_d = np.ascontiguousarray(f(w_down).reshape(DEPTH, HC, 128, 1024).transpose(0, 2, 1, 3))
    x = f(x)
    pos = np.asarray(positions).astype(np.int32)
    shared = {"w_kv": w_kv, "w_q": w_q, "w_r": w_r, "w_o": w_o, "w_gu": w_gu, "w_d": w_d,
              "g_ret": g_ret, "g_mix": g_mix, "g_ffn": g_ffn, "cst": cst}
    maps = []
    for c in range(ncores):
        xs = x[c * cfg.NSEQ:(c + 1) * cfg.NSEQ]
        ps_ = pos[c * cfg.NSEQ:(c + 1) * cfg.NSEQ]
        ps_ = np.ascontiguousarray(ps_.reshape(cfg.NSEQ, cfg.NB, 128).transpose(0, 2, 1))
        m = dict(shared)
        m["x"] = np.ascontiguousarray(xs)
        m["pos"] = ps_
        maps.append(m)
    return maps


def run(cfg, ncores, **inputs):
    key = (cfg.S, cfg.NSEQ, cfg.DEPTH, cfg.NITER, cfg.stop)
    if key not in _CACHE:
        _CACHE[key] = build(cfg)
    nc = _CACHE[key]
    maps = _prep_inputs(cfg, ncores=ncores, **inputs)
    res = run_bass_kernel_spmd(nc, maps, core_ids=list(range(ncores)))
    return np.concatenate([r["y"] for r in res.results], axis=0)


def kernel(x, positions, w_in, ret_gn_gain, w_out, ln_mix_gain, ln_mix_bias,
           w_gate_up, w_down, ln_ffn_gain, ln_ffn_bias):
    cfg = Cfg(S=2048, NSEQ=2, DEPTH=2, NITER=14)
    out = run(cfg, 8, x=x, positions=positions, w_in=w_in, ret_gn_gain=ret_gn_gain, w_out=w_out,
              ln_mix_gain=ln_mix_gain, ln_mix_bias=ln_mix_bias, w_gate_up=w_gate_up, w_down=w_down,
              ln_ffn_gain=ln_ffn_gain, ln_ffn_bias=ln_ffn_bias)
    return out.astype(np.float32)
```

```python
import math
from contextlib import ExitStack
import numpy as np
import concourse.bass as bass
import concourse.mybir as mybir
from concourse.bass_utils import run_bass_kernel_spmd

F32 = mybir.dt.float32
BF16 = mybir.dt.bfloat16
I32 = mybir.dt.int32
ALU = mybir.AluOpType
AF = mybir.ActivationFunctionType
AX = mybir.AxisListType

D = 1024
KC = 8
HID = 2816
HC = 22
IN_COLS = 4168
OFF_RQ, OFF_RK, OFF_RV, OFF_RG, OFF_AQ, OFF_AK, OFF_AV, OFF_IQ, OFF_IK, OFF_IW = (
    0, 512, 1024, 1536, 2048, 2560, 3072, 3584, 4096, 4160)
LN_EPS = 1e-5
MAGIC = 12582912.0
TWO_PI = 2.0 * math.pi
C1 = 6.28125
C2 = TWO_PI - C1
PI_LO = 3.1415925
NEG_BIG = -1.0e30
MASK_NEG = -30000.0

C_ID, C_TRI, C_CB, C_RT, C_INVF, C_P2, C_END = 0, 128, 256, 384, 404, 468, 500


class Res:
    __slots__ = ("name", "w", "r", "sem", "cnt")

    def __init__(self, name):
        self.name = name
        self.w = None
        self.r = {}
        self.sem = None
        self.cnt = 0


class Sched:
    ENG = ("pe", "act", "dve", "pool", "sp")

    def __init__(self, nc, es, needed):
        self.nc = nc
        self.es = es
        self.rec = needed is None
        self.needed = set() if self.rec else needed
        self.idx = {e: 0 for e in self.ENG}
        self.sig = {e: 0 for e in self.ENG}
        self.cntof = {}
        self.waited = {}
        self.eng = dict(pe=nc.tensor, act=nc.scalar, dve=nc.vector, pool=nc.gpsimd, sp=nc.sync)
        self.sem = {}
        self.nsem = 0
        if not self.rec:
            for e in self.ENG:
                self.sem[e] = es.enter_context(nc.semaphore("sem_" + e))

    def _wait(self, e, tok):
        if tok is None:
            return
        if tok[0] == "eng":
            _, pe_, i = tok
            if pe_ == e and e in ("pe", "sp"):
                return
            if self.rec:
                self.needed.add((pe_, i))
                return
            val = self.cntof[(pe_, i)]
            key = (e, pe_)
            semh = self.sem[pe_]
        else:
            _, res, val = tok
            if self.rec:
                return
            key = (e, "dma", res.name)
            semh = res.sem
        if self.waited.get(key, 0) >= val:
            return
        self.waited[key] = val
        self.eng[e].wait_ge(semh, val)

    def _deps(self, e, reads, writes):
        for r in reads:
            self._wait(e, r.w)
        for w in writes:
            self._wait(e, w.w)
            for t in list(w.r.values()):
                self._wait(e, t)

    def op(self, e, fn, reads=(), writes=()):
        self._deps(e, reads, writes)
        i = self.idx[e]
        self.idx[e] += 1
        tok = ("eng", e, i)
        if not self.rec:
            ins = fn()
            if (e, i) in self.needed:
                ins.then_inc(self.sem[e], 1)
                self.sig[e] += 1
                self.cntof[(e, i)] = self.sig[e]
        for r in reads:
            r.r[e] = tok
        for w in writes:
            w.w = tok
            w.r = {}
        return tok

    def _getsem(self, res):
        if res.sem is None and not self.rec:
            res.sem = self.es.enter_context(self.nc.semaphore("dsem_%d" % self.nsem))
            self.nsem += 1

    def dma(self, q, out_ap, in_ap, reads=(), writes=(), semres=None):
        self._deps(q, reads, writes)
        self.idx[q] += 1
        tr = writes[0] if writes else semres
        self._getsem(tr)
        tr.cnt += 16
        tok = ("dma", tr, tr.cnt)
        if not self.rec:
            self.eng[q].dma_start(out=out_ap, in_=in_ap).then_inc(tr.sem, 16)
        for w in writes:
            w.w = tok
            w.r = {}
        for r in reads:
            r.r["dma_" + tr.name] = tok
        return tok

    def barrier(self, allres):
        toks = []
        for r in allres:
            if r.w is not None:
                toks.append(r.w)
            toks.extend(r.r.values())
        for e in ("pe", "act", "dve", "pool", "sp"):
            for t in toks:
                self._wait(e, t)


class _Stop(Exception):
    pass


class Cfg:
    stop = None

    def __init__(self, S=2048, NSEQ=2, DEPTH=2, NITER=20):
        self.S = S
        self.NSEQ = NSEQ
        self.DEPTH = DEPTH
        self.NB = S // 128
        self.TOPK = min(256, S // 4)
        self.NITER = NITER
        self.GT = min(512, S)


def make_consts(cfg):
    c = np.zeros((128, C_END), np.float32)
    p = np.arange(128)
    c[:, C_ID:C_ID + 128] = np.eye(128, dtype=np.float32)
    c[:, C_TRI:C_TRI + 128] = (p[None, :] >= p[:, None]).astype(np.float32)
    c[:, C_CB:C_CB + 128] = np.where(p[None, :] <= p[:, None], 0.0, NEG_BIG)
    for h in range(4):
        lg = math.log(1.0 - 2.0 ** (-5.0 - h))
        dq = np.exp(lg * (p + 1.0))
        c[:, C_RT + h] = dq
        c[:, C_RT + 4 + h] = np.exp(-lg * (p + 1.0)) * 128 ** -0.5
        c[:, C_RT + 8 + h] = np.exp(lg * (127.0 - p)) * 128 ** -0.5
        c[:, C_RT + 12 + h] = dq * dq
        c[:, C_RT + 16 + h] = (1.0 - 2.0 ** (-5.0 - h)) ** 128
    invf = (10000.0 ** (-np.arange(0, 128, 2, dtype=np.float32) / 128)).astype(np.float32)
    c[:, C_INVF:C_INVF + 64] = invf[None, :]
    c[:, C_P2:C_P2 + 32] = (2.0 ** (-np.arange(32, dtype=np.float64)))[None, :]
    return c


def pipeline(gens, depth):
    gens = list(gens)
    active = []
    nxt = 0
    while nxt < len(gens) or active:
        if nxt < len(gens) and len(active) < depth:
            active.append(gens[nxt])
            nxt += 1
        for g in list(active):
            try:
                next(g)
            except StopIteration:
                active.remove(g)


def build(cfg):
    S, NSEQ, DEPTH, NB, TOPK, NITER, GT = cfg.S, cfg.NSEQ, cfg.DEPTH, cfg.NB, cfg.TOPK, cfg.NITER, cfg.GT
    nc = bass.Bass("TRN2", target_bir_lowering=False)
    dt = nc.dram_tensor
    x_d = dt("x", [NSEQ, S, D], F32, kind="ExternalInput").ap()
    pos_d = dt("pos", [NSEQ, 128, NB], I32, kind="ExternalInput").ap()
    wkv_d = dt("w_kv", [DEPTH, 128, KC, 1096], F32, kind="ExternalInput").ap()
    wq_d = dt("w_q", [DEPTH, 128, KC, 1024], F32, kind="ExternalInput").ap()
    wr_d = dt("w_r", [DEPTH, 128, KC, 2048], F32, kind="ExternalInput").ap()
    wout_d = dt("w_o", [DEPTH, 128, KC, 1024], F32, kind="ExternalInput").ap()
    wgu_d = dt("w_gu", [DEPTH, HC // 2, 128, KC * 512], F32, kind="ExternalInput").ap()
    wd_d = dt("w_d", [DEPTH, 128, HC, 1024], F32, kind="ExternalInput").ap()
    gret_d = dt("g_ret", [DEPTH, 128, 512], F32, kind="ExternalInput").ap()
    gmix_d = dt("g_mix", [DEPTH, 128, 2048], F32, kind="ExternalInput").ap()
    gffn_d = dt("g_ffn", [DEPTH, 128, 2048], F32, kind="ExternalInput").ap()
    cst_d = dt("cst", [128, C_END], F32, kind="ExternalInput").ap()
    y_d = dt("y", [NSEQ, S, D], F32, kind="ExternalOutput").ap()
    wgu_bf = dt("wgu_bf", [DEPTH, HC // 2, 128, KC * 512], BF16, kind="Internal").ap()

    ARENA = 63 * 1024
    with ExitStack() as es:
        sb = lambda name, shape, dtp: es.enter_context(nc.sbuf_tensor(name, shape, dtp))
        x_sb = sb("x_sb", [128, NB, D], F32)
        cs_cos = sb("cs_cos", [128, NB, 64], F32)
        cs_sin = sb("cs_sin", [128, NB, 64], F32)
        cst = sb("cst_sb", [128, C_END], F32)
        ident = sb("ident", [128, 128], BF16)
        posi = sb("posi", [128, NB], I32)
        posf = sb("posf", [128, NB], F32)
        wsb = sb("wsb", [128, NB, 8], F32)
        sm = sb("sm", [128, 256], F32)
        steps = sb("steps", [128, 2, 32], F32)
        nstp = sb("nstp", [128, 32], F32)
        sm2 = sb("sm2", [128, 160], F32)
        ones4 = sb("ones4", [4, 128], F32)
        biasc_t = sb("biasc", [128, 32], F32)
        sb_late = lambda name, shape, dtp: biasc_t
        ang = sb("ang", [128, 4, 64], F32)
        arena = sb("arena", [128, ARENA], BF16)
        ps = es.enter_context(nc.psum_tensor("ps", [128, 4096], F32))

        def bankf(i, n=1):
            return ps[:, i * 512:(i + n) * 512]

        def bankb(i):
            return ps[:, i * 512:(i + 1) * 512].bitcast(BF16)

        class Carver:
            def __init__(self, off=0):
                self.off = off

            def take(self, n_bf16, dtype=BF16):
                n = (n_bf16 + 15) // 16 * 16
                ap = arena[:, self.off:self.off + n_bf16]
                self.off += n
                assert self.off <= ARENA, "arena overflow %d" % self.off
                if dtype == F32:
                    ap = ap.bitcast(F32)
                return ap

        def emit(s):
            R = {}

            def res(name):
                if name not in R:
                    R[name] = Res(name)
                return R[name]

            E = s.eng
            r_cst, r_ident, r_cs, r_pos, r_wsb = res("cst"), res("ident"), res("cs"), res("pos"), res("wsb")
            r_x = [res("x%d" % b) for b in range(NB)]
            r_bk = [res("bank%d" % i) for i in range(8)]
            r_stb_ = [res("store%d" % b) for b in range(NB)]
            r_smc = res("smc")

            def tt(e, out, in0, in1, op, reads, writes):
                return s.op(e, lambda: E[e].tensor_tensor(out=out, in0=in0, in1=in1, op=op), reads, writes)

            def ts(e, out, in0, s1, s2, op0, op1, reads, writes, accum=None):
                if op1 is None:
                    return s.op(e, lambda: E[e].tensor_scalar(out=out, in0=in0, scalar1=s1, scalar2=None, op0=op0), reads, writes)
                if accum is not None:
                    return s.op(e, lambda: E[e].tensor_scalar(out=out, in0=in0, scalar1=s1, scalar2=s2, op0=op0, op1=op1, accum_out=accum), reads, writes)
                return s.op(e, lambda: E[e].tensor_scalar(out=out, in0=in0, scalar1=s1, scalar2=s2, op0=op0, op1=op1), reads, writes)

            def stt(e, out, in0, sc, in1, op0, op1, reads, writes):
                return s.op(e, lambda: E[e].scalar_tensor_tensor(out=out, in0=in0, scalar=sc, in1=in1, op0=op0, op1=op1), reads, writes)

            def act(out, in_, func, reads, writes, bias=None, scale=None):
                kw = {}
                if bias is not None:
                    kw["bias"] = bias
                if scale is not None:
                    kw["scale"] = scale
                return s.op("act", lambda: nc.scalar.activation(out=out, in_=in_, func=func, **kw), reads, writes)

            def copy(e, out, in_, reads, writes):
                if e == "act":
                    return s.op("act", lambda: nc.scalar.copy(out=out, in_=in_), reads, writes)
                return s.op(e, lambda: E[e].tensor_copy(out=out, in_=in_), reads, writes)

            def mm(out, pairs, reads, writes):
                def f():
                    ins = None
                    n = len(pairs)
                    for i, (l, r) in enumerate(pairs):
                        ins = nc.tensor.matmul(out, lhsT=l, rhs=r, start=(i == 0), stop=(i == n - 1))
                    return ins
                return s.op("pe", f, reads, writes)

            def tr(out, in_, reads, writes):
                return s.op("pe", lambda: nc.tensor.transpose(out=out, in_=in_, identity=ident[:]), list(reads) + [r_ident], writes)

            s.dma("sp", cst[:], cst_d, writes=[r_cst])
            copy("dve", ident[:], cst[:, C_ID:C_ID + 128], [r_cst], [r_ident])
            cb = cst[:, C_CB:C_CB + 128]
            tri = cst[:, C_TRI:C_TRI + 128]
            invf = cst[:, C_INVF:C_INVF + 64]
            p2 = cst[:, C_P2:C_P2 + 32]
            s.op("dve", lambda: nc.vector.memset(sm[:, 0:8], -0.5), [], [r_smc])
            s.op("dve", lambda: nc.vector.memset(sm[:, 8:9], -1.0e29), [], [r_smc])
            biasc = sb_late("biasc", [128, 32], F32)
            BIASC = {}
            for tbq in range(NB):
                Lq = (tbq + 1) * 128
                s.op("dve", lambda tbq=tbq, Lq=Lq: nc.vector.memset(biasc[:, tbq:tbq + 1], float(-(2 * TOPK - Lq - 1))), [], [r_smc])
                BIASC[Lq] = biasc[:, tbq:tbq + 1]
            r_ones4 = res("ones4")
            s.op("dve", lambda: nc.vector.memset(ones4[:], 1.0), [], [r_ones4])
            s.op("dve", lambda: nc.vector.memset(sm2[:, 0:4], 0.5), [], [r_smc])
            POSH4 = sm2[:, 0:4]
            K2 = sm2[:, 16:16 + 64]
            K2M = sm2[:, 84:88]
            identf = cst[:, C_ID:C_ID + 128]

            def allmax4(src, r_src, dst, r_dst, bk, tmpc):
                r_t = res("amx%d" % tmpc)
                mx4 = sm2[0:4, 100 + tmpc * 8:101 + tmpc * 8]
                dg4 = sm2[0:4, 101 + tmpc * 8:105 + tmpc * 8]
                pT = bankf(bk)[0:4, 0:128]
                s.op("pe", lambda: nc.tensor.transpose(out=pT, in_=src, identity=identf), [r_src, r_cst], [r_bk[bk]])
                s.op("dve", lambda: nc.vector.tensor_reduce(out=mx4, in_=pT, axis=AX.X, op=ALU.max), [r_bk[bk]], [r_t])
                ts("dve", dg4, cst[0:4, C_ID:C_ID + 4], mx4, None, ALU.mult, None, [r_t, r_cst], [r_t])
                pB = bankf(bk)[:, 128:132]
                s.op("pe", lambda: nc.tensor.matmul(pB, lhsT=ones4[0:4, :], rhs=dg4, start=True, stop=True), [r_t, r_ones4], [r_bk[bk]])
                copy("dve", dst, pB, [r_bk[bk]], [r_dst])

            NEGH4 = sm[:, 0:4]
            NEGH = sm[:, 0:1]
            TAUALL = sm[:, 8:9]
            r_conv = [res("conv%d" % l) for l in range(DEPTH)]
            for l in range(DEPTH):
                for pr in range(HC // 2):
                    s.dma("pool", wgu_bf[l, pr], wgu_d[l, pr], semres=r_conv[l])
                r_conv[l].w = ("dma", r_conv[l], r_conv[l].cnt)

            xrot = [0]

            def make_xT(tb, xbf, xT, r_xbf, r_xT, bk):
                e = "act" if xrot[0] % 2 == 0 else "dve"
                xrot[0] += 1
                copy(e, xbf, x_sb[:, tb, :], [r_x[tb]], [r_xbf])
                pb = bankb(bk)
                for k in range(KC):
                    tr(pb[:, k * 128:(k + 1) * 128], xbf[:, k * 128:(k + 1) * 128], [r_xbf], [r_bk[bk]])
                copy("act", xT, pb[:, :], [r_bk[bk]], [r_xT])

            def rope(src, r_src, G, half, cosap, sinap, out, r_out, tA, tB, r_tA, r_tB, e2="pool"):
                n = G * 2 * half
                s3 = src.rearrange("p (g d) -> p g d", d=half)
                cbb = cosap.unsqueeze(1).to_broadcast([128, 2 * G, half])
                sbb = sinap.unsqueeze(1).to_broadcast([128, 2 * G, half])
                tA3 = tA[:, 0:n].rearrange("p (g d) -> p g d", d=half)
                tB3 = tB[:, 0:n].rearrange("p (g d) -> p g d", d=half)
                tt("dve", tA3, s3, cbb, ALU.mult, list(r_src) + [r_cs], [r_tA])
                tt("dve", tB3, s3, sbb, ALU.mult, list(r_src) + [r_cs], [r_tB])
                tA4 = tA[:, 0:n].rearrange("p (g t d) -> p g t d", t=2, d=half)
                tB4 = tB[:, 0:n].rearrange("p (g t d) -> p g t d", t=2, d=half)
                o4 = out.rearrange("p (g t d) -> p g t d", t=2, d=half)
                tt(e2, o4[:, :, 0, :], tA4[:, :, 0, :], tB4[:, :, 1, :], ALU.subtract, [r_tA, r_tB], [r_out])
                tt(e2, o4[:, :, 1, :], tA4[:, :, 1, :], tB4[:, :, 0, :], ALU.add, [r_tA, r_tB], [r_out])

            lncnt = [0]

            def layer_norm_block(tb, ypair, r_y, gbt, r_gbt):
                par = lncnt[0] % 2
                lncnt[0] += 1
                c0 = 16 + par * 24
                r_s = res("smln%d" % par)
                xb = x_sb[:, tb, :]
                for hh in range(2):
                    stt("dve", xb[:, hh * 512:(hh + 1) * 512], xb[:, hh * 512:(hh + 1) * 512], ALPHA,
                        ypair[hh], ALU.mult, ALU.add, [r_y[hh]], [r_x[tb]])
                st6 = sm[:, c0:c0 + 12].rearrange("p (a b) -> p a b", a=2)
                for hh in range(2):
                    s.op("dve", lambda hh=hh: nc.vector.bn_stats(out=st6[:, hh, :], in_=xb[:, hh * 512:(hh + 1) * 512]), [r_x[tb]], [r_s])
                s.op("dve", lambda: nc.vector.bn_aggr(out=sm[:, c0 + 12:c0 + 14], in_=st6), [r_s], [r_s])
                ts("dve", sm[:, c0 + 14:c0 + 15], sm[:, c0 + 13:c0 + 14], LN_EPS, None, ALU.add, None, [r_s], [r_s])
                tt("pool", sm[:, c0 + 15:c0 + 16], sm[:, c0 + 14:c0 + 15], NEGH, ALU.pow, [r_s, r_smc], [r_s])
                stt("dve", sm[:, c0 + 16:c0 + 17], sm[:, c0 + 12:c0 + 13], -1.0, sm[:, c0 + 15:c0 + 16], ALU.mult, ALU.mult, [r_s], [r_s])
                act(xb, xb, AF.Identity, [r_x[tb], r_s], [r_x[tb]], bias=sm[:, c0 + 16:c0 + 17], scale=sm[:, c0 + 15:c0 + 16])
                tt("dve", xb, xb, gbt[:, 0:1024], ALU.mult, [r_x[tb], r_gbt], [r_x[tb]])
                tt("pool", xb, xb, gbt[:, 1024:2048], ALU.add, [r_x[tb], r_gbt], [r_x[tb]])

            ALPHA = (2.0 * DEPTH) ** 0.25
            ATT_SCALE = 128 ** -0.5
            W_SCALE = (8 ** -0.5) * (64 ** -0.5)

            def chk(name):
                if cfg.stop == name:
                    raise _Stop()

            for sq in range(NSEQ):
              try:
                xv = x_d[sq].rearrange("(b p) d -> p b d", p=128)
                for b in range(NB):
                    s.dma("sp", x_sb[:, b, :], xv[:, b, :], writes=[r_x[b]])
                s.dma("sp", posi[:], pos_d[sq], writes=[r_pos])
                copy("dve", posf[:], posi[:], [r_pos], [r_pos])
                r_ang = res("ang")
                for b in range(NB):
                    A_, K_, Rs, Rc = ang[:, 0, :], ang[:, 1, :], ang[:, 2, :], ang[:, 3, :]
                    ts("dve", A_, invf, posf[:, b:b + 1], None, ALU.mult, None, [r_cst, r_pos], [r_ang])
                    for which, dst in ((0, Rs), (1, Rc)):
                        src = A_
                        if which == 1:
                            ts("dve", Rc, A_, math.pi / 2, None, ALU.add, None, [r_ang], [r_ang])
                            src = Rc
                        ts("dve", K_, src, 1.0 / TWO_PI, MAGIC, ALU.mult, ALU.add, [r_ang], [r_ang])
                        ts("dve", K_, K_, MAGIC, None, ALU.subtract, None, [r_ang], [r_ang])
                        stt("dve", dst, K_, -C1, src, ALU.mult, ALU.add, [r_ang], [r_ang])
                        stt("dve", dst, K_, -C2, dst, ALU.mult, ALU.add, [r_ang], [r_ang])
                        ts("dve", dst, dst, PI_LO, -PI_LO, ALU.min, ALU.max, [r_ang], [r_ang])
                    act(cs_sin[:, b, :], Rs, AF.Sin, [r_ang], [r_cs])
                    act(cs_cos[:, b, :], Rc, AF.Sin, [r_ang], [r_cs])
                chk('setup')
                for l in range(DEPTH):
                    cv = Carver()
                    attTall = cv.take(4 * S).rearrange("p (k t) -> p k t", k=4)
                    PERSIST = cv.off
                    KT = cv.take(4 * S).rearrange("p (h t) -> p h t", h=4)
                    Vg = cv.take(NB * 4 * 130).rearrange("p (b h e) -> p b h e", b=NB, h=4)
                    ikT = cv.take(S)
                    wA = cv.take(KC * 1096)
                    xbf_ = cv.take(1024)
                    xT_ = cv.take(1024)
                    xbf = [xbf_, xbf_]
                    xT = [xT_, xT_]
                    tA = cv.take(1024, F32)
                    tB = cv.take(1024, F32)
                    rbuf = [cv.take(1024, F32) for _ in range(2)]
                    qiq_ = cv.take(1024)
                    qiq = [qiq_, qiq_]
                    ebuf2 = cv.take(S)
                    attb_ = cv.take(512)
                    attb = [attb_, attb_]
                    qT = [cv.take(512).rearrange("p (h t) -> p h t", h=4) for _ in range(3)]
                    iqT = [cv.take(512).rearrange("p (h t) -> p h t", h=4) for _ in range(2)]
                    Ibuf2 = [cv.take(2 * S, F32) for _ in range(2)]
                    jk = cv.take(16)
                    mb_ = cv.take(S)
                    mb = [mb_, mb_]
                    PT2 = cv.take(S)
                    ebuf = cv.take(S)
                    PTs_ = cv.take(S)
                    PTs = [PTs_, PTs_]
                    r_attTall = [res("attTall%d" % b) for b in range(NB)]
                    r_KT = [res("KT%d" % b) for b in range(NB)]
                    r_V = [res("V%d" % b) for b in range(NB)]
                    r_ikT = [res("ikT%d" % b) for b in range(NB)]
                    r_xbf = [res("xbf0"), res("xbf0")]
                    r_xT = [res("xT0"), res("xT0")]
                    if S >= 1024:
                        xbfB = [xbf_, PT2[:, 0:1024]]
                        xTB = [xT_, PTs_[:, 0:1024]]
                        r_xbfB = [r_xbf[0], res("PT2")]
                        r_xTB = [r_xT[0], res("PTs0")]
                    else:
                        xbfB, xTB, r_xbfB, r_xTB = xbf, xT, r_xbf, r_xT
                    r_tA, r_tB = res("tA"), res("tB")
                    r_wA = res("wA")
                    r_I2, r_e = [res("I0"), res("I1")], res("e")
                    r_jk = res("jk")
                    r_mb = [res("mb0"), res("mb0")]
                    r_mbT = [res("mbT0"), res("mbT1")]
                    r_PT = [res("PTs0"), res("PT2")]
                    r_nbq = [res("nbq%d" % i) for i in range(3)]
                    r_k2, r_k2m = res("k2"), res("k2m")
                    r_PTs = [res("PTs0"), res("PTs0")]
                    r_rb = [res("rb0"), res("rb1")]
                    r_qbf = [res("qbf0"), res("qbf0")]
                    r_iqb = [res("iqb0"), res("iqb0")]
                    r_e2, r_jk2, r_sb2, r_jk3 = res("e2"), res("jk2"), res("sb2"), res("jk3")
                    r_attb = [res("attb0"), res("attb0")]
                    r_kb = r_qbf
                    r_ikb = r_iqb
                    r_qT = [res("qT0"), res("qT1"), res("qT2")]
                    r_iqT = [res("iqT0"), res("iqT1")]

                    wKV = wA[:, 0:KC * 1096].rearrange("p (k c) -> p k c", k=KC)
                    r_wKVg = [res("wKVg%d" % i) for i in range(3)]
                    r_wQg = [res("wQg%d" % i) for i in range(2)]
                    for i, (c0, c1) in enumerate(((0, 512), (512, 1024), (1024, 1096))):
                        s.dma("pool", wKV[:, :, c0:c1], wkv_d[l][:, :, c0:c1], writes=[r_wKVg[i]] + r_wQg)
                    for tb in range(NB):
                        s.op("pool", lambda tb=tb: nc.gpsimd.memset(Vg[:, tb, :, 128:130], 1.0), [], [r_V[tb]])

                    def blkB(tb):
                        p = tb % 2
                        b0 = 4 * p
                        make_xT(tb, xbfB[p], xTB[p], r_xbfB[p], r_xTB[p], b0)
                        xt, r_xt = xTB[p], r_xTB[p]
                        mm(bankf(b0 + 1), [(xt[:, k * 128:(k + 1) * 128], wKV[:, k, 0:512]) for k in range(KC)], [r_xt, r_wKVg[0]], [r_bk[b0 + 1]])
                        mm(bankf(b0 + 2), [(xt[:, k * 128:(k + 1) * 128], wKV[:, k, 512:1024]) for k in range(KC)], [r_xt, r_wKVg[1]], [r_bk[b0 + 2]])
                        mm(bankf(b0 + 3)[:, 0:72], [(xt[:, k * 128:(k + 1) * 128], wKV[:, k, 1024:1096]) for k in range(KC)], [r_xt, r_wKVg[2]], [r_bk[b0 + 3]])
                        yield
                        kbf = qiq[p][:, 0:512]
                        ikb = qiq[p][:, 512:640]
                        rope(bankf(b0 + 1), [r_bk[b0 + 1]], 4, 64, cs_cos[:, tb, :], cs_sin[:, tb, :], kbf, r_kb[p], tA, tB, r_tA, r_tB)
                        copy("act", Vg[:, tb, :, 0:128], bankf(b0 + 2).rearrange("p (h e) -> p h e", h=4), [r_bk[b0 + 2]], [r_V[tb]])
                        for h in range(4):
                            s.op("act", lambda h=h: nc.scalar.activation(out=jk[:, 2:3].to_broadcast([128, 128]), in_=kbf[:, h * 128:(h + 1) * 128], func=AF.Square, accum_out=K2[:, tb * 4 + h:tb * 4 + h + 1]), [r_kb[p]], [r_k2, r_jk3])
                        rope(bankf(b0 + 3)[:, 0:64], [r_bk[b0 + 3]], 1, 32, cs_cos[:, tb, 0:64:2], cs_sin[:, tb, 0:64:2], ikb[:, 0:64], r_ikb[p], tA, tB, r_tA, r_tB)
                        copy("pool", ikb[:, 64:128], ikb[:, 0:64], [r_ikb[p]], [r_ikb[p]])
                        ts("dve", wsb[:, tb, :], bankf(b0 + 3)[:, 64:72], W_SCALE, None, ALU.mult, None, [r_bk[b0 + 3]], [r_wsb])
                        yield
                        pb = bankb(b0)
                        for h in range(4):
                            tr(pb[:, h * 128:(h + 1) * 128], kbf[:, h * 128:(h + 1) * 128], [r_kb[p]], [r_bk[b0]])
                        tr(pb[:, 512:640], ikb, [r_ikb[p]], [r_bk[b0]])
                        copy("act", KT[:, :, tb * 128:(tb + 1) * 128], pb[:, 0:512].rearrange("p (h t) -> p h t", h=4), [r_bk[b0]], [r_KT[tb]])
                        copy("act", ikT[:, tb * 128:(tb + 1) * 128], pb[:, 512:640], [r_bk[b0]], [r_ikT[tb]])

                    pipeline([blkB(tb) for tb in range(NB)], 2)
                    K2P = sm2[:, 88:92]
                    s.op("dve", lambda: nc.vector.tensor_reduce(out=K2P, in_=K2[:, 0:NB * 4].rearrange("p (b h) -> p h b", h=4), axis=AX.X, op=ALU.max), [r_k2], [r_k2m])
                    allmax4(K2P, r_k2m, K2M, r_k2m, 0, 0)
                    chk('B')
                    wQ = wA[:, 0:KC * 1024].rearrange("p (k c) -> p k c", k=KC)
                    for i in range(2):
                        s.dma("pool", wQ[:, :, i * 512:(i + 1) * 512], wq_d[l][:, :, i * 512:(i + 1) * 512], writes=[r_wQg[i]] + r_wKVg)
                    mb_free = [True, True]
                    phase_of = {}

                    def frontC(tb):
                        p = tb % 2
                        q3 = tb % 3
                        Ib = Ibuf2[p]
                        r_I = r_I2[p]
                        L = (tb + 1) * 128
                        phase_of[p] = "s0"
                        r_sb = res("smbis%d" % p)
                        r_sa = res("smatt%d" % p)
                        cB = 64 + p * 16
                        cA = 96 + p * 16
                        stp = steps[:, p, :]
                        make_xT(tb, xbf[p], xT[p], r_xbf[p], r_xT[p], 0)
                        xt, r_xt = xT[p], r_xT[p]
                        mm(bankf(1), [(xt[:, k * 128:(k + 1) * 128], wQ[:, k, 0:512]) for k in range(KC)], [r_xt, r_wQg[0]], [r_bk[1]])
                        mm(bankf(2), [(xt[:, k * 128:(k + 1) * 128], wQ[:, k, 512:1024]) for k in range(KC)], [r_xt, r_wQg[1]], [r_bk[2]])
                        if phase_of.get(1 - p) == "bis":
                            yield
                        if phase_of.get(1 - p) == "bis":
                            yield
                        qbf = qiq[p][:, 0:512]
                        iqb = qiq[p][:, 512:1024]
                        rope(bankf(1), [r_bk[1]], 4, 64, cs_cos[:, tb, :], cs_sin[:, tb, :], qbf, r_qbf[p], tA, tB, r_tA, r_tB)
                        rope(bankf(2), [r_bk[2]], 8, 32, cs_cos[:, tb, 0:64:2], cs_sin[:, tb, 0:64:2], iqb, r_iqb[p], tA, tB, r_tA, r_tB)
                        if phase_of.get(1 - p) == "bis":
                            yield
                        pb = bankb(0)
                        for h in range(4):
                            tr(pb[:, h * 128:(h + 1) * 128], qbf[:, h * 128:(h + 1) * 128], [r_qbf[p]], [r_bk[0]])
                        for h in range(4):
                            tr(pb[:, 512 + h * 128:512 + (h + 1) * 128], iqb[:, h * 128:(h + 1) * 128], [r_iqb[p]], [r_bk[0]])
                        copy("act", qT[q3], pb[:, 0:512].rearrange("p (h t) -> p h t", h=4), [r_bk[0]], [r_qT[q3]])
                        copy("act", iqT[p], pb[:, 512:1024].rearrange("p (h t) -> p h t", h=4), [r_bk[0]], [r_iqT[p]])
                        Q2 = sm2[:, 120 + p * 4:124 + p * 4]
                        Q2M = sm2[:, 128 + p * 4:132 + p * 4]
                        NBQ = sm2[:, 136 + q3 * 4:140 + q3 * 4]
                        r_q2 = res("q2_%d" % p)
                        for h in range(4):
                            s.op("act", lambda h=h: nc.scalar.activation(out=jk[:, 2:3].to_broadcast([128, 128]), in_=qbf[:, h * 128:(h + 1) * 128], func=AF.Square, accum_out=Q2[:, h:h + 1]), [r_qbf[p]], [r_q2, r_jk3])
                        if phase_of.get(1 - p) == "bis":
                            yield
                        allmax4(Q2, r_q2, Q2M, r_q2, 2, 1 + p)
                        if phase_of.get(1 - p) == "bis":
                            yield
                        tt("dve", Q2M, Q2M, K2M, ALU.mult, [r_q2, r_k2m], [r_q2])
                        tt("pool", Q2M, Q2M, POSH4, ALU.pow, [r_q2, r_smc], [r_q2])
                        ts("dve", NBQ, Q2M, -ATT_SCALE, None, ALU.mult, None, [r_q2], [r_nbq[q3]])
                        yield
                        phase_of[p] = "idx"
                        nch = (L + 511) // 512
                        cnt = 0
                        for c in range(nch):
                            n = min(512, L - 512 * c)
                            kres = r_ikT[4 * c:4 * c + (n + 127) // 128]
                            for h in range(8):
                                m_, base = h // 2, 64 * (h % 2)
                                bk = 1 + cnt % 2
                                rb, r_rb_ = rbuf[cnt % 2], r_rb[cnt % 2]
                                cnt += 1
                                mm(bankf(bk)[:, 0:n], [(iqT[p][base:base + 64, m_, :], ikT[base:base + 64, c * 512:c * 512 + n])], [r_iqT[p]] + kres, [r_bk[bk]])
                                act(rb[:, 0:n], bankf(bk)[:, 0:n], AF.Relu, [r_bk[bk]], [r_rb_])
                                if h == 0:
                                    ts("dve", Ib[:, c * 512:c * 512 + n], rb[:, 0:n], wsb[:, tb, 0:1], None, ALU.mult, None, [r_rb_, r_wsb], [r_I])
                                else:
                                    stt("dve", Ib[:, c * 512:c * 512 + n], rb[:, 0:n], wsb[:, tb, h:h + 1], Ib[:, c * 512:c * 512 + n], ALU.mult, ALU.add, [r_rb_, r_wsb], [r_I])
                                if h == 3:
                                    yield
                            yield
                        phase_of[p] = "bis"
                        tt("dve", Ib[:, tb * 128:L], Ib[:, tb * 128:L], cb, ALU.add, [r_cst], [r_I])
                        if L > TOPK:
                            MX, MN, RG, T, CNT, DD = [sm[:, cB + i:cB + i + 1] for i in range(6)]
                            s.op("dve", lambda: nc.vector.tensor_reduce(out=MX, in_=Ib[:, 0:L], axis=AX.X, op=ALU.max), [r_I], [r_sb])
                            s.op("dve", lambda: nc.vector.tensor_reduce(out=MN, in_=Ib[:, 0:L - 128], axis=AX.X, op=ALU.min), [r_I], [r_sb])
                            tt("dve", RG, MX, MN, ALU.subtract, [r_sb], [r_sb])
                            ts("dve", RG, RG, 1.0 + 1.0 / 1024, None, ALU.mult, None, [r_sb], [r_sb])
                            tt("dve", T, MX, RG, ALU.subtract, [r_sb], [r_sb])
                            ts("dve", stp[:, 0:NITER + 2], p2[:, 0:NITER + 2], RG, None, ALU.mult, None, [r_sb, r_cst], [r_sb])
                            stt("dve", T, stp[:, 1:2], 1.0, T, ALU.mult, ALU.add, [r_sb], [r_sb])
                            if p == 1:
                                ts("dve", nstp[:, 0:NITER + 2], stp[:, 0:NITER + 2], -0.5, None, ALU.mult, None, [r_sb], [r_sb])
                            junk = jk[:, 0:1].to_broadcast([128, L])
                            if p == 0:
                                for i in range(NITER):
                                    ts("dve", junk, Ib[:, 0:L], T, None, ALU.is_gt, ALU.add, [r_I, r_sb], [r_jk], accum=CNT)
                                    s.op("dve", lambda: nc.vector.tensor_scalar(out=DD, in0=CNT, scalar1=TOPK - 0.5, scalar2=-0.5, op0=ALU.is_gt, op1=ALU.add), [r_jk], [r_sb])
                                    stt("dve", T, DD, stp[:, i + 1:i + 2], T, ALU.mult, ALU.add, [r_sb], [r_sb])
                                    yield
                            else:
                                NT = sm[:, cB + 6:cB + 7]
                                DS = sm[:, cB + 7:cB + 8]
                                junk2 = jk[:, 1:2].to_broadcast([128, L])
                                ts("dve", NT, T, -1.0, None, ALU.mult, None, [r_sb], [r_sb])
                                for i in range(NITER):
                                    s.op("act", lambda: nc.scalar.activation(out=junk2, in_=Ib[:, 0:L], func=AF.Sign, bias=NT, scale=1.0, accum_out=CNT), [r_I, r_sb], [r_jk2])
                                    s.op("act", lambda: nc.scalar.activation(out=DS, in_=CNT, func=AF.Sign, bias=BIASC[L], scale=1.0), [r_jk2, r_smc], [r_sb2])
                                    s.op("act", lambda i=i: nc.scalar.activation(out=NT, in_=DS, func=AF.Identity, bias=NT, scale=nstp[:, i + 1:i + 2]), [r_sb2, r_sb], [r_sb])
                                    yield
                                ts("dve", T, NT, -1.0, None, ALU.mult, None, [r_sb], [r_sb])
                            tt("dve", T, T, stp[:, NITER + 1:NITER + 2], ALU.subtract, [r_sb], [r_sb])
                            TAU = T
                            r_tau = r_sb
                        else:
                            TAU = TAUALL
                            r_tau = r_smc
                        while not mb_free[p]:
                            yield
                        mb_free[p] = False
                        ts("dve", mb[p][:, 0:L], Ib[:, 0:L], TAU, MASK_NEG, ALU.is_le, ALU.mult, [r_I, r_tau], [r_mb[p]])
                        mbT = [ebuf, ebuf2][p]
                        pb0 = bankb(0)
                        for half in range((tb + 8) // 8):
                            nblk = min(8, tb + 1 - 8 * half)
                            for j in range(nblk):
                                sbk = half * 8 + j
                                tr(pb0[:, j * 128:(j + 1) * 128], mb[p][:, sbk * 128:(sbk + 1) * 128], [r_mb[p]], [r_bk[0]])
                            copy("act", mbT[:, half * 1024:half * 1024 + nblk * 128], pb0[:, 0:nblk * 128], [r_bk[0]], [r_mbT[p]])

                    def backC(tb):
                        p = tb % 2
                        q3 = tb % 3
                        L = (tb + 1) * 128
                        nch = (L + 511) // 512
                        cA = 96 + p * 16
                        mbT = [ebuf, ebuf2][p]
                        PTb = [PTs[0], PT2]
                        NBQ = sm2[:, 136 + q3 * 4:140 + q3 * 4]

                        def hbank(h):
                            return 4 + 2 * (h % 2) if nch <= 2 else 4

                        def qk(h):
                            hb = hbank(h)
                            for sbk in range(tb + 1):
                                bkk = hb + sbk // 4
                                mm(ps[:, hb * 512 + sbk * 128:hb * 512 + (sbk + 1) * 128],
                                   [(KT[:, h, sbk * 128:(sbk + 1) * 128], qT[q3][:, h, :]), (ident[:], mbT[:, sbk * 128:(sbk + 1) * 128])],
                                   [r_qT[q3], r_mbT[p], r_ident, r_KT[sbk]], [r_bk[bkk]])

                        def sexp(h):
                            hb = hbank(h)
                            sc_ = ps[:, hb * 512:hb * 512 + L]
                            rsc = r_bk[hb:hb + nch]
                            act(PTb[h % 2][:, 0:L], sc_, AF.Exp, rsc + [r_nbq[q3]], [r_PT[h % 2]], bias=NBQ[:, h:h + 1], scale=ATT_SCALE)

                        def pv(h):
                            pts = PTb[h % 2]
                            RDa = sm[:, cA + 6 + (h % 2):cA + 7 + (h % 2)]
                            r_sr = res("smrd%d_%d" % (p, h % 2))
                            po = bankf(3)[:, 0:129]
                            mm(po, [(pts[:, sbk * 128:(sbk + 1) * 128], Vg[:, sbk, h, 0:129]) for sbk in range(tb + 1)],
                               [r_PT[h % 2]] + r_V[0:tb + 1], [r_bk[3]])
                            s.op("dve", lambda: nc.vector.reciprocal(out=RDa, in_=bankf(3)[:, 128:129]), [r_bk[3]], [r_sr])
                            ts("dve", attb[p][:, h * 128:(h + 1) * 128], bankf(3)[:, 0:128], RDa, None, ALU.mult, None, [r_bk[3], r_sr], [r_attb[p]])

                        qk(0)
                        sexp(0)
                        yield
                        for h in range(4):
                            if h + 1 < 4:
                                qk(h + 1)
                                sexp(h + 1)
                                yield
                            pv(h)
                            yield
                        pbb = bankb(3)
                        for h in range(4):
                            tr(pbb[:, h * 128:(h + 1) * 128], attb[p][:, h * 128:(h + 1) * 128], [r_attb[p]], [r_bk[3]])
                        copy("act", attTall[:, :, tb * 128:(tb + 1) * 128], pbb[:, 0:512].rearrange("p (h t) -> p h t", h=4), [r_bk[3]], [r_attTall[tb]])

                    fronts = []
                    ready = []
                    back = None
                    done_back = set()
                    next_tb = 0
                    while next_tb < NB or fronts or ready or back is not None:
                        while (next_tb < NB and len(fronts) < 2
                               and all(t % 2 != next_tb % 2 for t, _ in fronts)
                               and all(phase_of.get(t % 2) != "s0" for t, _ in fronts)
                               and (next_tb < 3 or (next_tb - 3) in done_back)):
                            fronts.append((next_tb, frontC(next_tb)))
                            next_tb += 1
                        if back is None and ready:
                            tbb = ready.pop(0)
                            back = (tbb, backC(tbb))
                        if back is not None:
                            try:
                                next(back[1])
                            except StopIteration:
                                mb_free[back[0] % 2] = True
                                done_back.add(back[0])
                                back = None
                        for item in list(fronts):
                            try:
                                next(item[1])
                            except StopIteration:
                                fronts.remove(item)
                                ready.append(item[0])
                    chk('C')
                    s.barrier(list(R.values()))
                    cv = Carver(PERSIST)
                    mixT = cv.take(4 * S).rearrange("p (k t) -> p k t", k=4)
                    PERSIST2 = cv.off
                    wR = cv.take(KC * 2048).rearrange("p (k c) -> p k c", k=KC)
                    xbf = [cv.take(1024) for _ in range(3)]
                    xT = [cv.take(1024) for _ in range(3)]
                    tA = cv.take(2048, F32)
                    tB = cv.take(2048, F32)
                    qk_r = cv.take(2048, F32)
                    qkb = [cv.take(1024) for _ in range(3)]
                    kdec = [cv.take(512) for _ in range(3)]
                    vbf = [cv.take(512) for _ in range(3)]
                    gsl = [cv.take(1024, F32) for _ in range(3)]
                    qkT = [cv.take(1024) for _ in range(3)]
                    stb = [cv.take(512) for _ in range(3)]
                    yn = cv.take(1024, F32)
                    y3 = [cv.take(512) for _ in range(3)]
                    state_f = cv.take(1024, F32)
                    state_bf = cv.take(512)
                    gbA = cv.take(1024, F32)
                    r_mixT = [res("mixT%d" % b) for b in range(NB)]
                    r_wR, r_gbA = res("wR"), res("gbA")
                    r_qk, r_yn = res("qk_r"), res("yn")
                    r_qkb = [res("qkb%d" % i) for i in range(3)]
                    r_kdec = [res("kdec%d" % i) for i in range(3)]
                    r_vbf = [res("vbf%d" % i) for i in range(3)]
                    r_gsl = [res("gsl%d" % i) for i in range(3)]
                    r_qkT = [res("qkT%d" % i) for i in range(3)]
                    r_stb = [res("stb%d" % i) for i in range(3)]
                    r_y3 = [res("y3%d" % i) for i in range(3)]
                    r_stf, r_stbf = res("state_f"), res("state_bf")
                    r_xbfA = [res("xbfA%d" % i) for i in range(3)]
                    r_xTA = [res("xTA%d" % i) for i in range(3)]
                    r_tA2, r_tB2 = res("tA2"), res("tB2")
                    s.dma("sp", gbA[:, 0:512], gret_d[l], writes=[r_gbA])
                    r_wRg = [res("wRg%d" % i) for i in range(4)]
                    for i in range(4):
                        s.dma("pool", wR[:, :, i * 512:(i + 1) * 512], wr_d[l][:, :, i * 512:(i + 1) * 512], writes=[r_wRg[i]])
                    s.op("dve", lambda: nc.vector.memset(state_f, 0.0), [], [r_stf])
                    s.op("pool", lambda: nc.gpsimd.memset(state_bf, 0.0), [], [r_stbf])
                    dct = cst[:, C_RT + 16:C_RT + 20].unsqueeze(2).to_broadcast([128, 4, 128])
                    kps = cst[:, C_RT + 4:C_RT + 8].unsqueeze(2).to_broadcast([128, 4, 128])
                    dks = cst[:, C_RT + 8:C_RT + 12].unsqueeze(2).to_broadcast([128, 4, 128])

                    def blkA(tb):
                        p = tb % 3
                        r_sg = res("smgn%d" % (tb % 2))
                        cG = 128 + (tb % 2) * 48
                        make_xT(tb, xbf[p], xT[p], r_xbfA[p], r_xTA[p], 0)
                        xt, r_xt = xT[p], r_xTA[p]
                        for j in range(4):
                            mm(bankf(1 + j), [(xt[:, k * 128:(k + 1) * 128], wR[:, k, j * 512:(j + 1) * 512]) for k in range(KC)], [r_xt, r_wRg[j]], [r_bk[1 + j]])
                        yield
                        rope(bankf(1, 2), [r_bk[1], r_bk[2]], 8, 64, cs_cos[:, tb, :], cs_sin[:, tb, :], qk_r, r_qk, tA, tB, r_tA2, r_tB2, e2="dve")
                        copy("act", vbf[p], bankf(3), [r_bk[3]], [r_vbf[p]])
                        act(gsl[p], bankf(4), AF.Silu, [r_bk[4]], [r_gsl[p]])
                        yield
                        copy("pool", qkb[p][:, 0:512], qk_r[:, 0:512], [r_qk], [r_qkb[p]])
                        k3 = qk_r[:, 512:1024].rearrange("p (h d) -> p h d", h=4)
                        tt("dve", qkb[p][:, 512:1024].rearrange("p (h d) -> p h d", h=4), k3, kps, ALU.mult, [r_qk, r_cst], [r_qkb[p]])
                        tt("pool", kdec[p].rearrange("p (h d) -> p h d", h=4), k3, dks, ALU.mult, [r_qk, r_cst], [r_kdec[p]])
                        pb = bankb(0)
                        for j in range(8):
                            tr(pb[:, j * 128:(j + 1) * 128], qkb[p][:, j * 128:(j + 1) * 128], [r_qkb[p]], [r_bk[0]])
                        copy("act", qkT[p], pb[:, :], [r_bk[0]], [r_qkT[p]])
                        yield
                        for h in range(4):
                            mm(bankf(5)[:, h * 128:(h + 1) * 128], [(qkT[p][:, 512 + h * 128:512 + (h + 1) * 128], qkT[p][:, h * 128:(h + 1) * 128])], [r_qkT[p]], [r_bk[5]])
                        tt("dve", stb[p].rearrange("p (h d) -> p h d", h=4), bankf(5).rearrange("p (h d) -> p h d", h=4),
                           tri.unsqueeze(1).to_broadcast([128, 4, 128]), ALU.mult, [r_bk[5], r_cst], [r_stb[p]])
                        yield
                        for h in range(4):
                            sl = slice(h * 128, (h + 1) * 128)
                            mm(bankf(6)[:, sl], [(stb[p][:, sl], vbf[p][:, sl]), (qkT[p][:, sl], state_bf[:, sl])], [r_stb[p], r_vbf[p], r_qkT[p], r_stbf], [r_bk[6]])
                        for h in range(4):
                            sl = slice(h * 128, (h + 1) * 128)
                            mm(bankf(7)[:, sl], [(kdec[p][:, sl], vbf[p][:, sl])], [r_kdec[p], r_vbf[p]], [r_bk[7]])
                        tt("pool", state_f.rearrange("p (h d) -> p h d", h=4), state_f.rearrange("p (h d) -> p h d", h=4), dct, ALU.mult, [r_cst], [r_stf])
                        tt("dve", state_f, state_f, bankf(7), ALU.add, [r_bk[7]], [r_stf])
                        copy("pool", state_bf, state_f, [r_stf], [r_stbf])
                        yield
                        st6 = sm[:, cG:cG + 24].rearrange("p (h a) -> p h a", h=4)
                        mv = sm[:, cG + 24:cG + 32].rearrange("p (h a) -> p h a", h=4)
                        for h in range(4):
                            s.op("dve", lambda h=h: nc.vector.bn_stats(out=st6[:, h, :], in_=bankf(6)[:, h * 128:(h + 1) * 128]), [r_bk[6]], [r_sg])
                        for h in range(4):
                            s.op("dve", lambda h=h: nc.vector.bn_aggr(out=mv[:, h, :], in_=st6[:, h, :]), [r_sg], [r_sg])
                        A4 = sm[:, cG + 32:cG + 36]
                        RS4 = sm[:, cG + 36:cG + 40]
                        SC4 = sm[:, cG + 40:cG + 44]
                        NB4 = sm[:, cG + 44:cG + 48]
                        tt("dve", A4, mv[:, :, 1], cst[:, C_RT + 12:C_RT + 16], ALU.mult, [r_sg, r_cst], [r_sg])
                        ts("dve", A4, A4, LN_EPS, None, ALU.add, None, [r_sg], [r_sg])
                        tt("pool", RS4, A4, NEGH4, ALU.pow, [r_sg, r_smc], [r_sg])
                        tt("dve", SC4, RS4, cst[:, C_RT:C_RT + 4], ALU.mult, [r_sg, r_cst], [r_sg])
                        stt("dve", NB4, mv[:, :, 0], -1.0, SC4, ALU.mult, ALU.mult, [r_sg], [r_sg])
                        for h in range(4):
                            sl = slice(h * 128, (h + 1) * 128)
                            act(yn[:, sl], bankf(6)[:, sl], AF.Identity, [r_bk[6], r_sg], [r_yn], bias=NB4[:, h:h + 1], scale=SC4[:, h:h + 1])
                        yield
                        tt("dve", yn, yn, gbA[:, 0:512], ALU.mult, [r_yn, r_gbA], [r_yn])
                        tt("dve", y3[p], yn, gsl[p], ALU.mult, [r_yn, r_gsl[p]], [r_y3[p]])
                        pb = bankb(0)
                        for h in range(4):
                            tr(pb[:, h * 128:(h + 1) * 128], y3[p][:, h * 128:(h + 1) * 128], [r_y3[p]], [r_bk[0]])
                        copy("act", mixT[:, :, tb * 128:(tb + 1) * 128], pb[:, 0:512].rearrange("p (h t) -> p h t", h=4), [r_bk[0]], [r_mixT[tb]])

                    pipeline([blkA(tb) for tb in range(NB)], 3)
                    chk('A')
                    s.barrier(list(R.values()))
                    cv = Carver(PERSIST2)
                    wO = cv.take(KC * 1024).rearrange("p (k c) -> p k c", k=KC)
                    gbm = cv.take(4096, F32)
                    r_wO, r_gbm = res("wO"), res("gbm")
                    r_wOg = [res("wOg%d" % i) for i in range(2)]
                    for i in range(2):
                        s.dma("pool", wO[:, :, i * 512:(i + 1) * 512], wout_d[l][:, :, i * 512:(i + 1) * 512], writes=[r_wOg[i]])
                    s.dma("sp", gbm[:, :], gmix_d[l], writes=[r_gbm])
                    for tb in range(NB):
                        b0 = 2 * (tb % 2)
                        for hh in range(2):
                            pairs = [(mixT[:, k, tb * 128:(tb + 1) * 128], wO[:, k, hh * 512:(hh + 1) * 512]) for k in range(4)]
                            pairs += [(attTall[:, k, tb * 128:(tb + 1) * 128], wO[:, 4 + k, hh * 512:(hh + 1) * 512]) for k in range(4)]
                            mm(bankf(b0 + hh), pairs, [r_mixT[tb], r_attTall[tb], r_wOg[hh]], [r_bk[b0 + hh]])
                        layer_norm_block(tb, [bankf(b0), bankf(b0 + 1)], [r_bk[b0], r_bk[b0 + 1]], gbm, r_gbm)
                    chk('C2')
                    s.barrier(list(R.values()))
                    cv = Carver()
                    NG = S // GT
                    nblk = GT // 128
                    ncol = GT // 512
                    x1T = [cv.take(KC * GT).rearrange("p (k t) -> p k t", k=KC) for _ in range(2)]
                    hT = cv.take(HC * GT).rearrange("p (c t) -> p c t", c=HC)
                    wD = cv.take(HC * 1024).rearrange("p (c n) -> p c n", c=HC)
                    wG = [cv.take(KC * 512).rearrange("p (k c) -> p k c", k=KC) for _ in range(3)]
                    xbf2 = [cv.take(1024) for _ in range(2)]
                    sgb = [cv.take(1024, F32) for _ in range(2)]
                    gbf = cv.take(4096, F32)
                    r_x1T = [[res("x1T%d_%d" % (q, i)) for i in range(nblk)] for q in range(2)]
                    r_hT = [res("hT%d_%d" % (c, t)) for c in range(HC) for t in range(ncol)]
                    r_wD = [res("wD%d" % i) for i in range(4)]
                    r_wG = [res("wG%d" % i) for i in range(3)]
                    r_xbf2 = [res("xbf2_0"), res("xbf2_1")]
                    r_sgl = [res("sg0"), res("sg1")]
                    r_gbf = res("gbf")
                    wdq = [(0, 6), (6, 12), (12, 18), (18, 22)]
                    for i, (c0, c1) in enumerate(wdq):
                        s.dma("pool", wD[:, c0:c1, :], wd_d[l][:, c0:c1, :], writes=[r_wD[i]])
                    s.dma("sp", gbf[:, :], gffn_d[l], writes=[r_gbf])
                    NP = HC // 2
                    gu = [0]

                    def load_gu(pr):
                        slot = gu[0] % 3
                        gu[0] += 1
                        s.dma("sp", wG[slot].rearrange("p k c -> p (k c)"), wgu_bf[l, pr], reads=[r_conv[l]], writes=[r_wG[slot]])
                        return slot

                    def prep_x1T(g):
                        q = g % 2
                        for j in range(nblk):
                            tb = g * nblk + j
                            i2 = j % 2
                            copy("act" if j % 2 == 0 else "dve", xbf2[i2], x_sb[:, tb, :], [r_x[tb]], [r_xbf2[i2]])
                            pb = bankb(i2)
                            for k in range(KC):
                                tr(pb[:, k * 128:(k + 1) * 128], xbf2[i2][:, k * 128:(k + 1) * 128], [r_xbf2[i2]], [r_bk[i2]])
                            copy("act", x1T[q][:, :, j * 128:(j + 1) * 128], pb[:, :].rearrange("p (k t) -> p k t", k=KC), [r_bk[i2]], [r_x1T[q][j]])

                    prep_x1T(0)
                    for g in range(NG):
                        q = g % 2
                        slots = [load_gu(0), load_gu(1)]
                        for pr in range(NP):
                            if pr + 2 < NP:
                                slots.append(load_gu(pr + 2))
                            slot = slots[pr]
                            for ci in range(2):
                                c = 2 * pr + ci
                                for th in range(ncol):
                                    xr = r_x1T[q][4 * th:4 * th + 4]
                                    qq = (c * ncol + th) % 2
                                    bG, bU = 2 + 2 * qq, 3 + 2 * qq
                                    mm(bankf(bG), [(wG[slot][:, k, ci * 256:ci * 256 + 128], x1T[q][:, k, th * 512:(th + 1) * 512]) for k in range(KC)], xr + [r_wG[slot]], [r_bk[bG]])
                                    mm(bankf(bU), [(wG[slot][:, k, ci * 256 + 128:ci * 256 + 256], x1T[q][:, k, th * 512:(th + 1) * 512]) for k in range(KC)], xr + [r_wG[slot]], [r_bk[bU]])
                                    act(sgb[qq], bankf(bG), AF.Silu, [r_bk[bG]], [r_sgl[qq]])
                                    tt("dve", hT[:, c, th * 512:(th + 1) * 512], sgb[qq], bankf(bU), ALU.mult, [r_sgl[qq], r_bk[bU]], [r_hT[c * ncol + th]])
                        if g + 1 < NG:
                            prep_x1T(g + 1)
                        for j in range(nblk):
                            tb = g * nblk + j
                            th = j // 4
                            hres = [r_hT[c * ncol + th] for c in range(HC)]
                            for hh in range(2):
                                mm(bankf(6 + hh), [(hT[:, c, j * 128:(j + 1) * 128], wD[:, c, hh * 512:(hh + 1) * 512]) for c in range(HC)],
                                   hres + r_wD, [r_bk[6 + hh]])
                            layer_norm_block(tb, [bankf(6), bankf(7)], [r_bk[6], r_bk[7]], gbf, r_gbf)
                    s.barrier(list(R.values()))
              except _Stop:
                pass
              if True:
                yv = y_d[sq].rearrange("(b p) d -> p b d", p=128)
                for b in range(NB):
                    s.dma("sp", yv[:, b, :], x_sb[:, b, :], reads=[r_x[b]], semres=r_stb_[b])
            for b in range(NB):
                s._wait("sp", ("dma", r_stb_[b], r_stb_[b].cnt))

        s1 = Sched(nc, es, None)
        emit(s1)
        s2 = Sched(nc, es, s1.needed)
        emit(s2)
    return nc


_CACHE = {}


def _prep_inputs(cfg, x, positions, w_in, ret_gn_gain, w_out, ln_mix_gain, ln_mix_bias,
                 w_gate_up, w_down, ln_ffn_gain, ln_ffn_bias, ncores):
    DEPTH = cfg.DEPTH
    f = lambda a: np.ascontiguousarray(np.asarray(a, dtype=np.float32))
    g_ret = np.ascontiguousarray(np.broadcast_to(f(ret_gn_gain)[:, None, :], (DEPTH, 128, 512)))
    g_mix = np.ascontiguousarray(np.broadcast_to(
        np.concatenate([f(ln_mix_gain), f(ln_mix_bias)], axis=1)[:, None, :], (DEPTH, 128, 2048)))
    g_ffn = np.ascontiguousarray(np.broadcast_to(
        np.concatenate([f(ln_ffn_gain), f(ln_ffn_bias)], axis=1)[:, None, :], (DEPTH, 128, 2048)))
    cst = make_consts(cfg)
    w_in = f(w_in)

    def tile_k(w):
        return np.ascontiguousarray(w.reshape(DEPTH, KC, 128, w.shape[-1]).transpose(0, 2, 1, 3))

    w_kv = tile_k(np.concatenate([w_in[:, :, OFF_AK:OFF_AK + 1024], w_in[:, :, OFF_IK:OFF_IK + 72]], axis=2))
    w_q = tile_k(np.concatenate([w_in[:, :, OFF_AQ:OFF_AQ + 512], w_in[:, :, OFF_IQ:OFF_IQ + 512]], axis=2))
    w_r = tile_k(w_in[:, :, 0:2048])
    w_o = tile_k(f(w_out))
    wgu = f(w_gate_up)
    g4 = wgu[:, :, 0:HID].reshape(DEPTH, KC, 128, HC // 2, 2, 128)
    u4 = wgu[:, :, HID:2 * HID].reshape(DEPTH, KC, 128, HC // 2, 2, 128)
    gu = np.stack([g4, u4], axis=5)
    gu = gu.transpose(0, 3, 2, 1, 4, 5, 6)
    w_gu = np.ascontiguousarray(gu.reshape(DEPTH, HC // 2, 128, KC * 512))
    w_d = np.ascontiguousarray(f(w_down).reshape(DEPTH, HC, 128, 1024).transpose(0, 2, 1, 3))
    x = f(x)
    pos = np.asarray(positions).astype(np.int32)
    shared = {"w_kv": w_kv, "w_q": w_q, "w_r": w_r, "w_o": w_o, "w_gu": w_gu, "w_d": w_d,
              "g_ret": g_ret, "g_mix": g_mix, "g_ffn": g_ffn, "cst": cst}
    maps = []
    for c in range(ncores):
        xs = x[c * cfg.NSEQ:(c + 1) * cfg.NSEQ]
        ps_ = pos[c * cfg.NSEQ:(c + 1) * cfg.NSEQ]
        ps_ = np.ascontiguousarray(ps_.reshape(cfg.NSEQ, cfg.NB, 128).transpose(0, 2, 1))
        m = dict(shared)
        m["x"] = np.ascontiguousarray(xs)
        m["pos"] = ps_
        maps.append(m)
    return maps


def run(cfg, ncores, **inputs):
    key = (cfg.S, cfg.NSEQ, cfg.DEPTH, cfg.NITER, cfg.stop)
    if key not in _CACHE:
        _CACHE[key] = build(cfg)
    nc = _CACHE[key]
    maps = _prep_inputs(cfg, ncores=ncores, **inputs)
    res = run_bass_kernel_spmd(nc, maps, core_ids=list(range(ncores)))
    return np.concatenate([r["y"] for r in res.results], axis=0)


def kernel(x, positions, w_in, ret_gn_gain, w_out, ln_mix_gain, ln_mix_bias,
           w_gate_up, w_down, ln_ffn_gain, ln_ffn_bias):
    cfg = Cfg(S=2048, NSEQ=2, DEPTH=2, NITER=14)
    out = run(cfg, 8, x=x, positions=positions, w_in=w_in, ret_gn_gain=ret_gn_gain, w_out=w_out,
              ln_mix_gain=ln_mix_gain, ln_mix_bias=ln_mix_bias, w_gate_up=w_gate_up, w_down=w_down,
              ln_ffn_gain=ln_ffn_gain, ln_ffn_bias=ln_ffn_bias)
    return out.astype(np.float32)
```

```python
import math
from contextlib import ExitStack
import numpy as np
import concourse.bass as bass
import concourse.mybir as mybir
from concourse.bass_utils import run_bass_kernel_spmd

F32 = mybir.dt.float32
BF16 = mybir.dt.bfloat16
I32 = mybir.dt.int32
ALU = mybir.AluOpType
AF = mybir.ActivationFunctionType
AX = mybir.AxisListType

D = 1024
KC = 8
HID = 2816
HC = 22
IN_COLS = 4168
OFF_RQ, OFF_RK, OFF_RV, OFF_RG, OFF_AQ, OFF_AK, OFF_AV, OFF_IQ, OFF_IK, OFF_IW = (
    0, 512, 1024, 1536, 2048, 2560, 3072, 3584, 4096, 4160)
LN_EPS = 1e-5
MAGIC = 12582912.0
TWO_PI = 2.0 * math.pi
C1 = 6.28125
C2 = TWO_PI - C1
PI_LO = 3.1415925
NEG_BIG = -1.0e30
MASK_NEG = -30000.0

C_ID, C_TRI, C_CB, C_RT, C_INVF, C_P2, C_END = 0, 128, 256, 384, 404, 468, 500


class Res:
    __slots__ = ("name", "w", "r", "sem", "cnt")

    def __init__(self, name):
        self.name = name
        self.w = None
        self.r = {}
        self.sem = None
        self.cnt = 0


class Sched:
    ENG = ("pe", "act", "dve", "pool", "sp")

    def __init__(self, nc, es, needed):
        self.nc = nc
        self.es = es
        self.rec = needed is None
        self.needed = set() if self.rec else needed
        self.idx = {e: 0 for e in self.ENG}
        self.sig = {e: 0 for e in self.ENG}
        self.cntof = {}
        self.waited = {}
        self.eng = dict(pe=nc.tensor, act=nc.scalar, dve=nc.vector, pool=nc.gpsimd, sp=nc.sync)
        self.sem = {}
        self.nsem = 0
        if not self.rec:
            for e in self.ENG:
                self.sem[e] = es.enter_context(nc.semaphore("sem_" + e))

    def _wait(self, e, tok):
        if tok is None:
            return
        if tok[0] == "eng":
            _, pe_, i = tok
            if pe_ == e and e in ("pe", "sp"):
                return
            if self.rec:
                self.needed.add((pe_, i))
                return
            val = self.cntof[(pe_, i)]
            key = (e, pe_)
            semh = self.sem[pe_]
        else:
            _, res, val = tok
            if self.rec:
                return
            key = (e, "dma", res.name)
            semh = res.sem
        if self.waited.get(key, 0) >= val:
            return
        self.waited[key] = val
        self.eng[e].wait_ge(semh, val)

    def _deps(self, e, reads, writes):
        for r in reads:
            self._wait(e, r.w)
        for w in writes:
            self._wait(e, w.w)
            for t in list(w.r.values()):
                self._wait(e, t)

    def op(self, e, fn, reads=(), writes=()):
        self._deps(e, reads, writes)
        i = self.idx[e]
        self.idx[e] += 1
        tok = ("eng", e, i)
        if not self.rec:
            ins = fn()
            if (e, i) in self.needed:
                ins.then_inc(self.sem[e], 1)
                self.sig[e] += 1
                self.cntof[(e, i)] = self.sig[e]
        for r in reads:
            r.r[e] = tok
        for w in writes:
            w.w = tok
            w.r = {}
        return tok

    def _getsem(self, res):
        if res.sem is None and not self.rec:
            res.sem = self.es.enter_context(self.nc.semaphore("dsem_%d" % self.nsem))
            self.nsem += 1

    def dma(self, q, out_ap, in_ap, reads=(), writes=(), semres=None):
        self._deps(q, reads, writes)
        self.idx[q] += 1
        tr = writes[0] if writes else semres
        self._getsem(tr)
        tr.cnt += 16
        tok = ("dma", tr, tr.cnt)
        if not self.rec:
            self.eng[q].dma_start(out=out_ap, in_=in_ap).then_inc(tr.sem, 16)
        for w in writes:
            w.w = tok
            w.r = {}
        for r in reads:
            r.r["dma_" + tr.name] = tok
        return tok

    def barrier(self, allres):
        toks = []
        for r in allres:
            if r.w is not None:
                toks.append(r.w)
            toks.extend(r.r.values())
        for e in ("pe", "act", "dve", "pool", "sp"):
            for t in toks:
                self._wait(e, t)


class _Stop(Exception):
    pass


class Cfg:
    stop = None

    def __init__(self, S=2048, NSEQ=2, DEPTH=2, NITER=20):
        self.S = S
        self.NSEQ = NSEQ
        self.DEPTH = DEPTH
        self.NB = S // 128
        self.TOPK = min(256, S // 4)
        self.NITER = NITER
        self.GT = min(512, S)


def make_consts(cfg):
    c = np.zeros((128, C_END), np.float32)
    p = np.arange(128)
    c[:, C_ID:C_ID + 128] = np.eye(128, dtype=np.float32)
    c[:, C_TRI:C_TRI + 128] = (p[None, :] >= p[:, None]).astype(np.float32)
    c[:, C_CB:C_CB + 128] = np.where(p[None, :] <= p[:, None], 0.0, NEG_BIG)
    for h in range(4):
        lg = math.log(1.0 - 2.0 ** (-5.0 - h))
        dq = np.exp(lg * (p + 1.0))
        c[:, C_RT + h] = dq
        c[:, C_RT + 4 + h] = np.exp(-lg * (p + 1.0)) * 128 ** -0.5
        c[:, C_RT + 8 + h] = np.exp(lg * (127.0 - p)) * 128 ** -0.5
        c[:, C_RT + 12 + h] = dq * dq
        c[:, C_RT + 16 + h] = (1.0 - 2.0 ** (-5.0 - h)) ** 128
    invf = (10000.0 ** (-np.arange(0, 128, 2, dtype=np.float32) / 128)).astype(np.float32)
    c[:, C_INVF:C_INVF + 64] = invf[None, :]
    c[:, C_P2:C_P2 + 32] = (2.0 ** (-np.arange(32, dtype=np.float64)))[None, :]
    return c


def pipeline(gens, depth):
    gens = list(gens)
    active = []
    nxt = 0
    while nxt < len(gens) or active:
        if nxt < len(gens) and len(active) < depth:
            active.append(gens[nxt])
            nxt += 1
        for g in list(active):
            try:
                next(g)
            except StopIteration:
                active.remove(g)


def build(cfg):
    S, NSEQ, DEPTH, NB, TOPK, NITER, GT = cfg.S, cfg.NSEQ, cfg.DEPTH, cfg.NB, cfg.TOPK, cfg.NITER, cfg.GT
    nc = bass.Bass("TRN2", target_bir_lowering=False)
    dt = nc.dram_tensor
    x_d = dt("x", [NSEQ, S, D], F32, kind="ExternalInput").ap()
    pos_d = dt("pos", [NSEQ, 128, NB], I32, kind="ExternalInput").ap()
    wkv_d = dt("w_kv", [DEPTH, 128, KC, 1096], F32, kind="ExternalInput").ap()
    wq_d = dt("w_q", [DEPTH, 128, KC, 1024], F32, kind="ExternalInput").ap()
    wr_d = dt("w_r", [DEPTH, 128, KC, 2048], F32, kind="ExternalInput").ap()
    wout_d = dt("w_o", [DEPTH, 128, KC, 1024], F32, kind="ExternalInput").ap()
    wgu_d = dt("w_gu", [DEPTH, HC // 2, 128, KC * 512], F32, kind="ExternalInput").ap()
    wd_d = dt("w_d", [DEPTH, 128, HC, 1024], F32, kind="ExternalInput").ap()
    gret_d = dt("g_ret", [DEPTH, 128, 512], F32, kind="ExternalInput").ap()
    gmix_d = dt("g_mix", [DEPTH, 128, 2048], F32, kind="ExternalInput").ap()
    gffn_d = dt("g_ffn", [DEPTH, 128, 2048], F32, kind="ExternalInput").ap()
    cst_d = dt("cst", [128, C_END], F32, kind="ExternalInput").ap()
    y_d = dt("y", [NSEQ, S, D], F32, kind="ExternalOutput").ap()
    wgu_bf = dt("wgu_bf", [DEPTH, HC // 2, 128, KC * 512], BF16, kind="Internal").ap()

    ARENA = 63 * 1024
    with ExitStack() as es:
        sb = lambda name, shape, dtp: es.enter_context(nc.sbuf_tensor(name, shape, dtp))
        x_sb = sb("x_sb", [128, NB, D], F32)
        cs_cos = sb("cs_cos", [128, NB, 64], F32)
        cs_sin = sb("cs_sin", [128, NB, 64], F32)
        cst = sb("cst_sb", [128, C_END], F32)
        ident = sb("ident", [128, 128], BF16)
        posi = sb("posi", [128, NB], I32)
        posf = sb("posf", [128, NB], F32)
        wsb = sb("wsb", [128, NB, 8], F32)
        sm = sb("sm", [128, 256], F32)
        steps = sb("steps", [128, 2, 32], F32)
        nstp = sb("nstp", [128, 32], F32)
        sm2 = sb("sm2", [128, 160], F32)
        ones4 = sb("ones4", [4, 128], F32)
        biasc_t = sb("biasc", [128, 32], F32)
        sb_late = lambda name, shape, dtp: biasc_t
        ang = sb("ang", [128, 4, 64], F32)
        arena = sb("arena", [128, ARENA], BF16)
        ps = es.enter_context(nc.psum_tensor("ps", [128, 4096], F32))

        def bankf(i, n=1):
            return ps[:, i * 512:(i + n) * 512]

        def bankb(i):
            return ps[:, i * 512:(i + 1) * 512].bitcast(BF16)

        class Carver:
            def __init__(self, off=0):
                self.off = off

            def take(self, n_bf16, dtype=BF16):
                n = (n_bf16 + 15) // 16 * 16
                ap = arena[:, self.off:self.off + n_bf16]
                self.off += n
                assert self.off <= ARENA, "arena overflow %d" % self.off
                if dtype == F32:
                    ap = ap.bitcast(F32)
                return ap

        def emit(s):
            R = {}

            def res(name):
                if name not in R:
                    R[name] = Res(name)
                return R[name]

            E = s.eng
            r_cst, r_ident, r_cs, r_pos, r_wsb = res("cst"), res("ident"), res("cs"), res("pos"), res("wsb")
            r_x = [res("x%d" % b) for b in range(NB)]
            r_bk = [res("bank%d" % i) for i in range(8)]
            r_stb_ = [res("store%d" % b) for b in range(NB)]
            r_smc = res("smc")

            def tt(e, out, in0, in1, op, reads, writes):
                return s.op(e, lambda: E[e].tensor_tensor(out=out, in0=in0, in1=in1, op=op), reads, writes)

            def ts(e, out, in0, s1, s2, op0, op1, reads, writes, accum=None):
                if op1 is None:
                    return s.op(e, lambda: E[e].tensor_scalar(out=out, in0=in0, scalar1=s1, scalar2=None, op0=op0), reads, writes)
                if accum is not None:
                    return s.op(e, lambda: E[e].tensor_scalar(out=out, in0=in0, scalar1=s1, scalar2=s2, op0=op0, op1=op1, accum_out=accum), reads, writes)
                return s.op(e, lambda: E[e].tensor_scalar(out=out, in0=in0, scalar1=s1, scalar2=s2, op0=op0, op1=op1), reads, writes)

            def stt(e, out, in0, sc, in1, op0, op1, reads, writes):
                return s.op(e, lambda: E[e].scalar_tensor_tensor(out=out, in0=in0, scalar=sc, in1=in1, op0=op0, op1=op1), reads, writes)

            def act(out, in_, func, reads, writes, bias=None, scale=None):
                kw = {}
                if bias is not None:
                    kw["bias"] = bias
                if scale is not None:
                    kw["scale"] = scale
                return s.op("act", lambda: nc.scalar.activation(out=out, in_=in_, func=func, **kw), reads, writes)

            def copy(e, out, in_, reads, writes):
                if e == "act":
                    return s.op("act", lambda: nc.scalar.copy(out=out, in_=in_), reads, writes)
                return s.op(e, lambda: E[e].tensor_copy(out=out, in_=in_), reads, writes)

            def mm(out, pairs, reads, writes):
                def f():
                    ins = None
                    n = len(pairs)
                    for i, (l, r) in enumerate(pairs):
                        ins = nc.tensor.matmul(out, lhsT=l, rhs=r, start=(i == 0), stop=(i == n - 1))
                    return ins
                return s.op("pe", f, reads, writes)

            def tr(out, in_, reads, writes):
                return s.op("pe", lambda: nc.tensor.transpose(out=out, in_=in_, identity=ident[:]), list(reads) + [r_ident], writes)

            s.dma("sp", cst[:], cst_d, writes=[r_cst])
            copy("dve", ident[:], cst[:, C_ID:C_ID + 128], [r_cst], [r_ident])
            cb = cst[:, C_CB:C_CB + 128]
            tri = cst[:, C_TRI:C_TRI + 128]
            invf = cst[:, C_INVF:C_INVF + 64]
            p2 = cst[:, C_P2:C_P2 + 32]
            s.op("dve", lambda: nc.vector.memset(sm[:, 0:8], -0.5), [], [r_smc])
            s.op("dve", lambda: nc.vector.memset(sm[:, 8:9], -1.0e29), [], [r_smc])
            biasc = sb_late("biasc", [128, 32], F32)
            BIASC = {}
            for tbq in range(NB):
                Lq = (tbq + 1) * 128
                s.op("dve", lambda tbq=tbq, Lq=Lq: nc.vector.memset(biasc[:, tbq:tbq + 1], float(-(2 * TOPK - Lq - 1))), [], [r_smc])
                BIASC[Lq] = biasc[:, tbq:tbq + 1]
            r_ones4 = res("ones4")
            s.op("dve", lambda: nc.vector.memset(ones4[:], 1.0), [], [r_ones4])
            s.op("dve", lambda: nc.vector.memset(sm2[:, 0:4], 0.5), [], [r_smc])
            POSH4 = sm2[:, 0:4]
            K2 = sm2[:, 16:16 + 64]
            K2M = sm2[:, 84:88]
            identf = cst[:, C_ID:C_ID + 128]

            def allmax4(src, r_src, dst, r_dst, bk, tmpc):
                r_t = res("amx%d" % tmpc)
                mx4 = sm2[0:4, 100 + tmpc * 8:101 + tmpc * 8]
                dg4 = sm2[0:4, 101 + tmpc * 8:105 + tmpc * 8]
                pT = bankf(bk)[0:4, 0:128]
                s.op("pe", lambda: nc.tensor.transpose(out=pT, in_=src, identity=identf), [r_src, r_cst], [r_bk[bk]])
                s.op("dve", lambda: nc.vector.tensor_reduce(out=mx4, in_=pT, axis=AX.X, op=ALU.max), [r_bk[bk]], [r_t])
                ts("dve", dg4, cst[0:4, C_ID:C_ID + 4], mx4, None, ALU.mult, None, [r_t, r_cst], [r_t])
                pB = bankf(bk)[:, 128:132]
                s.op("pe", lambda: nc.tensor.matmul(pB, lhsT=ones4[0:4, :], rhs=dg4, start=True, stop=True), [r_t, r_ones4], [r_bk[bk]])
                copy("dve", dst, pB, [r_bk[bk]], [r_dst])

            NEGH4 = sm[:, 0:4]
            NEGH = sm[:, 0:1]
            TAUALL = sm[:, 8:9]
            r_conv = [res("conv%d" % l) for l in range(DEPTH)]
            conv_done = [False]

            def issue_conv():
                if conv_done[0]:
                    return
                conv_done[0] = True
                for l_ in range(DEPTH):
                    for pr in range(HC // 2):
                        s.dma("pool", wgu_bf[l_, pr], wgu_d[l_, pr], semres=r_conv[l_])
                    r_conv[l_].w = ("dma", r_conv[l_], r_conv[l_].cnt)

            xrot = [0]

            def make_xT(tb, xbf, xT, r_xbf, r_xT, bk):
                e = "act" if xrot[0] % 2 == 0 else "dve"
                xrot[0] += 1
                copy(e, xbf, x_sb[:, tb, :], [r_x[tb]], [r_xbf])
                pb = bankb(bk)
                for k in range(KC):
                    tr(pb[:, k * 128:(k + 1) * 128], xbf[:, k * 128:(k + 1) * 128], [r_xbf], [r_bk[bk]])
                copy("act", xT, pb[:, :], [r_bk[bk]], [r_xT])

            def rope(src, r_src, G, half, cosap, sinap, out, r_out, tA, tB, r_tA, r_tB, e2="pool"):
                n = G * 2 * half
                s3 = src.rearrange("p (g d) -> p g d", d=half)
                cbb = cosap.unsqueeze(1).to_broadcast([128, 2 * G, half])
                sbb = sinap.unsqueeze(1).to_broadcast([128, 2 * G, half])
                tA3 = tA[:, 0:n].rearrange("p (g d) -> p g d", d=half)
                tB3 = tB[:, 0:n].rearrange("p (g d) -> p g d", d=half)
                tt("dve", tA3, s3, cbb, ALU.mult, list(r_src) + [r_cs], [r_tA])
                tt("dve", tB3, s3, sbb, ALU.mult, list(r_src) + [r_cs], [r_tB])
                tA4 = tA[:, 0:n].rearrange("p (g t d) -> p g t d", t=2, d=half)
                tB4 = tB[:, 0:n].rearrange("p (g t d) -> p g t d", t=2, d=half)
                o4 = out.rearrange("p (g t d) -> p g t d", t=2, d=half)
                tt(e2, o4[:, :, 0, :], tA4[:, :, 0, :], tB4[:, :, 1, :], ALU.subtract, [r_tA, r_tB], [r_out])
                tt(e2, o4[:, :, 1, :], tA4[:, :, 1, :], tB4[:, :, 0, :], ALU.add, [r_tA, r_tB], [r_out])

            lncnt = [0]

            def layer_norm_block(tb, ypair, r_y, gbt, r_gbt):
                par = lncnt[0] % 2
                lncnt[0] += 1
                c0 = 16 + par * 24
                r_s = res("smln%d" % par)
                xb = x_sb[:, tb, :]
                for hh in range(2):
                    stt("dve", xb[:, hh * 512:(hh + 1) * 512], xb[:, hh * 512:(hh + 1) * 512], ALPHA,
                        ypair[hh], ALU.mult, ALU.add, [r_y[hh]], [r_x[tb]])
                st6 = sm[:, c0:c0 + 12].rearrange("p (a b) -> p a b", a=2)
                for hh in range(2):
                    s.op("dve", lambda hh=hh: nc.vector.bn_stats(out=st6[:, hh, :], in_=xb[:, hh * 512:(hh + 1) * 512]), [r_x[tb]], [r_s])
                s.op("dve", lambda: nc.vector.bn_aggr(out=sm[:, c0 + 12:c0 + 14], in_=st6), [r_s], [r_s])
                ts("dve", sm[:, c0 + 14:c0 + 15], sm[:, c0 + 13:c0 + 14], LN_EPS, None, ALU.add, None, [r_s], [r_s])
                tt("pool", sm[:, c0 + 15:c0 + 16], sm[:, c0 + 14:c0 + 15], NEGH, ALU.pow, [r_s, r_smc], [r_s])
                stt("dve", sm[:, c0 + 16:c0 + 17], sm[:, c0 + 12:c0 + 13], -1.0, sm[:, c0 + 15:c0 + 16], ALU.mult, ALU.mult, [r_s], [r_s])
                act(xb, xb, AF.Identity, [r_x[tb], r_s], [r_x[tb]], bias=sm[:, c0 + 16:c0 + 17], scale=sm[:, c0 + 15:c0 + 16])
                tt("dve", xb, xb, gbt[:, 0:1024], ALU.mult, [r_x[tb], r_gbt], [r_x[tb]])
                tt("pool", xb, xb, gbt[:, 1024:2048], ALU.add, [r_x[tb], r_gbt], [r_x[tb]])

            ALPHA = (2.0 * DEPTH) ** 0.25
            ATT_SCALE = 128 ** -0.5
            W_SCALE = (8 ** -0.5) * (64 ** -0.5)

            def chk(name):
                if cfg.stop == name:
                    raise _Stop()

            for sq in range(NSEQ):
              try:
                xv = x_d[sq].rearrange("(b p) d -> p b d", p=128)
                for b in range(NB):
                    s.dma("sp", x_sb[:, b, :], xv[:, b, :], writes=[r_x[b]])
                s.dma("sp", posi[:], pos_d[sq], writes=[r_pos])
                copy("dve", posf[:], posi[:], [r_pos], [r_pos])
                r_ang = res("ang")
                for b in range(NB):
                    A_, K_, Rs, Rc = ang[:, 0, :], ang[:, 1, :], ang[:, 2, :], ang[:, 3, :]
                    ts("dve", A_, invf, posf[:, b:b + 1], None, ALU.mult, None, [r_cst, r_pos], [r_ang])
                    for which, dst in ((0, Rs), (1, Rc)):
                        src = A_
                        if which == 1:
                            ts("dve", Rc, A_, math.pi / 2, None, ALU.add, None, [r_ang], [r_ang])
                            src = Rc
                        ts("dve", K_, src, 1.0 / TWO_PI, MAGIC, ALU.mult, ALU.add, [r_ang], [r_ang])
                        ts("dve", K_, K_, MAGIC, None, ALU.subtract, None, [r_ang], [r_ang])
                        stt("dve", dst, K_, -C1, src, ALU.mult, ALU.add, [r_ang], [r_ang])
                        stt("dve", dst, K_, -C2, dst, ALU.mult, ALU.add, [r_ang], [r_ang])
                        ts("dve", dst, dst, PI_LO, -PI_LO, ALU.min, ALU.max, [r_ang], [r_ang])
                    act(cs_sin[:, b, :], Rs, AF.Sin, [r_ang], [r_cs])
                    act(cs_cos[:, b, :], Rc, AF.Sin, [r_ang], [r_cs])
                chk('setup')
                for l in range(DEPTH):
                    cv = Carver()
                    attTall = cv.take(4 * S).rearrange("p (k t) -> p k t", k=4)
                    PERSIST = cv.off
                    KT = cv.take(4 * S).rearrange("p (h t) -> p h t", h=4)
                    Vg = cv.take(NB * 4 * 130).rearrange("p (b h e) -> p b h e", b=NB, h=4)
                    ikT = cv.take(S)
                    wA = cv.take(KC * 1096)
                    xbf_ = cv.take(1024)
                    xT_ = cv.take(1024)
                    xbf = [xbf_, xbf_]
                    xT = [xT_, xT_]
                    tA = cv.take(1024, F32)
                    tB = cv.take(1024, F32)
                    rbuf = [cv.take(1024, F32) for _ in range(2)]
                    qiq_ = cv.take(1024)
                    qiq = [qiq_, qiq_]
                    ebuf2 = cv.take(S)
                    attb_ = cv.take(512)
                    attb = [attb_, attb_]
                    qT = [cv.take(512).rearrange("p (h t) -> p h t", h=4) for _ in range(3)]
                    iqT = [cv.take(512).rearrange("p (h t) -> p h t", h=4) for _ in range(2)]
                    Ibuf2 = [cv.take(2 * S, F32) for _ in range(2)]
                    jk = cv.take(16)
                    mb_ = cv.take(S)
                    mb = [mb_, mb_]
                    PT2 = cv.take(S)
                    ebuf = cv.take(S)
                    PTs_ = cv.take(S)
                    PTs = [PTs_, PTs_]
                    r_attTall = [res("attTall%d" % b) for b in range(NB)]
                    r_KT = [res("KT%d" % b) for b in range(NB)]
                    r_V = [res("V%d" % b) for b in range(NB)]
                    r_ikT = [res("ikT%d" % b) for b in range(NB)]
                    r_xbf = [res("xbf0"), res("xbf0")]
                    r_xT = [res("xT0"), res("xT0")]
                    if S >= 1024:
                        xbfB = [xbf_, PT2[:, 0:1024]]
                        xTB = [xT_, PTs_[:, 0:1024]]
                        r_xbfB = [r_xbf[0], res("PT2")]
                        r_xTB = [r_xT[0], res("PTs0")]
                    else:
                        xbfB, xTB, r_xbfB, r_xTB = xbf, xT, r_xbf, r_xT
                    r_tA, r_tB = res("tA"), res("tB")
                    r_wA = res("wA")
                    r_I2, r_e = [res("I0"), res("I1")], res("e")
                    r_jk = res("jk")
                    r_mb = [res("mb0"), res("mb0")]
                    r_mbT = [res("mbT0"), res("mbT1")]
                    r_PT = [res("PTs0"), res("PT2")]
                    r_nbq = [res("nbq%d" % i) for i in range(3)]
                    r_k2, r_k2m = res("k2"), res("k2m")
                    r_PTs = [res("PTs0"), res("PTs0")]
                    r_rb = [res("rb0"), res("rb1")]
                    r_qbf = [res("qbf0"), res("qbf0")]
                    r_iqb = [res("iqb0"), res("iqb0")]
                    r_e2, r_jk2, r_sb2, r_jk3 = res("e2"), res("jk2"), res("sb2"), res("jk3")
                    r_attb = [res("attb0"), res("attb0")]
                    r_kb = r_qbf
                    r_ikb = r_iqb
                    r_qT = [res("qT0"), res("qT1"), res("qT2")]
                    r_iqT = [res("iqT0"), res("iqT1")]

                    wKV = wA[:, 0:KC * 1096].rearrange("p (k c) -> p k c", k=KC)
                    r_wKVg = [res("wKVg%d" % i) for i in range(3)]
                    r_wQg = [res("wQg%d" % i) for i in range(2)]
                    for i, (c0, c1) in enumerate(((0, 512), (512, 1024), (1024, 1096))):
                        s.dma("pool", wKV[:, :, c0:c1], wkv_d[l][:, :, c0:c1], writes=[r_wKVg[i]] + r_wQg)
                    for tb in range(NB):
                        s.op("pool", lambda tb=tb: nc.gpsimd.memset(Vg[:, tb, :, 128:130], 1.0), [], [r_V[tb]])
                    issue_conv()

                    def blkB(tb):
                        p = tb % 2
                        b0 = 4 * p
                        make_xT(tb, xbfB[p], xTB[p], r_xbfB[p], r_xTB[p], b0)
                        xt, r_xt = xTB[p], r_xTB[p]
                        mm(bankf(b0 + 1), [(xt[:, k * 128:(k + 1) * 128], wKV[:, k, 0:512]) for k in range(KC)], [r_xt, r_wKVg[0]], [r_bk[b0 + 1]])
                        mm(bankf(b0 + 2), [(xt[:, k * 128:(k + 1) * 128], wKV[:, k, 512:1024]) for k in range(KC)], [r_xt, r_wKVg[1]], [r_bk[b0 + 2]])
                        mm(bankf(b0 + 3)[:, 0:72], [(xt[:, k * 128:(k + 1) * 128], wKV[:, k, 1024:1096]) for k in range(KC)], [r_xt, r_wKVg[2]], [r_bk[b0 + 3]])
                        yield
                        kbf = qiq[p][:, 0:512]
                        ikb = qiq[p][:, 512:640]
                        rope(bankf(b0 + 1), [r_bk[b0 + 1]], 4, 64, cs_cos[:, tb, :], cs_sin[:, tb, :], kbf, r_kb[p], tA, tB, r_tA, r_tB)
                        copy("act", Vg[:, tb, :, 0:128], bankf(b0 + 2).rearrange("p (h e) -> p h e", h=4), [r_bk[b0 + 2]], [r_V[tb]])
                        for h in range(4):
                            s.op("act", lambda h=h: nc.scalar.activation(out=jk[:, 2:3].to_broadcast([128, 128]), in_=kbf[:, h * 128:(h + 1) * 128], func=AF.Square, accum_out=K2[:, tb * 4 + h:tb * 4 + h + 1]), [r_kb[p]], [r_k2, r_jk3])
                        rope(bankf(b0 + 3)[:, 0:64], [r_bk[b0 + 3]], 1, 32, cs_cos[:, tb, 0:64:2], cs_sin[:, tb, 0:64:2], ikb[:, 0:64], r_ikb[p], tA, tB, r_tA, r_tB)
                        copy("pool", ikb[:, 64:128], ikb[:, 0:64], [r_ikb[p]], [r_ikb[p]])
                        ts("dve", wsb[:, tb, :], bankf(b0 + 3)[:, 64:72], W_SCALE, None, ALU.mult, None, [r_bk[b0 + 3]], [r_wsb])
                        yield
                        pb = bankb(b0)
                        for h in range(4):
                            tr(pb[:, h * 128:(h + 1) * 128], kbf[:, h * 128:(h + 1) * 128], [r_kb[p]], [r_bk[b0]])
                        tr(pb[:, 512:640], ikb, [r_ikb[p]], [r_bk[b0]])
                        copy("act", KT[:, :, tb * 128:(tb + 1) * 128], pb[:, 0:512].rearrange("p (h t) -> p h t", h=4), [r_bk[b0]], [r_KT[tb]])
                        copy("act", ikT[:, tb * 128:(tb + 1) * 128], pb[:, 512:640], [r_bk[b0]], [r_ikT[tb]])

                    pipeline([blkB(tb) for tb in range(NB)], 3)
                    K2P = sm2[:, 88:92]
                    s.op("dve", lambda: nc.vector.tensor_reduce(out=K2P, in_=K2[:, 0:NB * 4].rearrange("p (b h) -> p h b", h=4), axis=AX.X, op=ALU.max), [r_k2], [r_k2m])
                    allmax4(K2P, r_k2m, K2M, r_k2m, 0, 0)
                    chk('B')
                    wQ = wA[:, 0:KC * 1024].rearrange("p (k c) -> p k c", k=KC)
                    for i in range(2):
                        s.dma("pool", wQ[:, :, i * 512:(i + 1) * 512], wq_d[l][:, :, i * 512:(i + 1) * 512], writes=[r_wQg[i]] + r_wKVg)
                    mb_free = [True, True]
                    phase_of = {}

                    def frontC(tb):
                        p = tb % 2
                        q3 = tb % 3
                        Ib = Ibuf2[p]
                        r_I = r_I2[p]
                        L = (tb + 1) * 128
                        phase_of[p] = "s0"
                        r_sb = res("smbis%d" % p)
                        r_sa = res("smatt%d" % p)
                        cB = 64 + p * 16
                        cA = 96 + p * 16
                        stp = steps[:, p, :]
                        make_xT(tb, xbf[p], xT[p], r_xbf[p], r_xT[p], 0)
                        xt, r_xt = xT[p], r_xT[p]
                        mm(bankf(1), [(xt[:, k * 128:(k + 1) * 128], wQ[:, k, 0:512]) for k in range(KC)], [r_xt, r_wQg[0]], [r_bk[1]])
                        mm(bankf(2), [(xt[:, k * 128:(k + 1) * 128], wQ[:, k, 512:1024]) for k in range(KC)], [r_xt, r_wQg[1]], [r_bk[2]])
                        if phase_of.get(1 - p) == "bis":
                            yield
                        if phase_of.get(1 - p) == "bis":
                            yield
                        qbf = qiq[p][:, 0:512]
                        iqb = qiq[p][:, 512:1024]
                        rope(bankf(1), [r_bk[1]], 4, 64, cs_cos[:, tb, :], cs_sin[:, tb, :], qbf, r_qbf[p], tA, tB, r_tA, r_tB)
                        rope(bankf(2), [r_bk[2]], 8, 32, cs_cos[:, tb, 0:64:2], cs_sin[:, tb, 0:64:2], iqb, r_iqb[p], tA, tB, r_tA, r_tB)
                        if phase_of.get(1 - p) == "bis":
                            yield
                        pb = bankb(0)
                        for h in range(4):
                            tr(pb[:, h * 128:(h + 1) * 128], qbf[:, h * 128:(h + 1) * 128], [r_qbf[p]], [r_bk[0]])
                        for h in range(4):
                            tr(pb[:, 512 + h * 128:512 + (h + 1) * 128], iqb[:, h * 128:(h + 1) * 128], [r_iqb[p]], [r_bk[0]])
                        copy("act", qT[q3], pb[:, 0:512].rearrange("p (h t) -> p h t", h=4), [r_bk[0]], [r_qT[q3]])
                        copy("act", iqT[p], pb[:, 512:1024].rearrange("p (h t) -> p h t", h=4), [r_bk[0]], [r_iqT[p]])
                        Q2 = sm2[:, 120 + p * 4:124 + p * 4]
                        Q2M = sm2[:, 128 + p * 4:132 + p * 4]
                        NBQ = sm2[:, 136 + q3 * 4:140 + q3 * 4]
                        r_q2 = res("q2_%d" % p)
                        for h in range(4):
                            s.op("act", lambda h=h: nc.scalar.activation(out=jk[:, 2:3].to_broadcast([128, 128]), in_=qbf[:, h * 128:(h + 1) * 128], func=AF.Square, accum_out=Q2[:, h:h + 1]), [r_qbf[p]], [r_q2, r_jk3])
                        if phase_of.get(1 - p) == "bis":
                            yield
                        allmax4(Q2, r_q2, Q2M, r_q2, 2, 1 + p)
                        if phase_of.get(1 - p) == "bis":
                            yield
                        tt("dve", Q2M, Q2M, K2M, ALU.mult, [r_q2, r_k2m], [r_q2])
                        tt("pool", Q2M, Q2M, POSH4, ALU.pow, [r_q2, r_smc], [r_q2])
                        ts("dve", NBQ, Q2M, -ATT_SCALE, None, ALU.mult, None, [r_q2], [r_nbq[q3]])
                        yield
                        phase_of[p] = "idx"
                        nch = (L + 511) // 512
                        cnt = 0
                        for c in range(nch):
                            n = min(512, L - 512 * c)
                            kres = r_ikT[4 * c:4 * c + (n + 127) // 128]
                            for h in range(8):
                                m_, base = h // 2, 64 * (h % 2)
                                bk = 1 + cnt % 2
                                rb, r_rb_ = rbuf[cnt % 2], r_rb[cnt % 2]
                                cnt += 1
                                mm(bankf(bk)[:, 0:n], [(iqT[p][base:base + 64, m_, :], ikT[base:base + 64, c * 512:c * 512 + n])], [r_iqT[p]] + kres, [r_bk[bk]])
                                act(rb[:, 0:n], bankf(bk)[:, 0:n], AF.Relu, [r_bk[bk]], [r_rb_])
                                if h == 0:
                                    ts("dve", Ib[:, c * 512:c * 512 + n], rb[:, 0:n], wsb[:, tb, 0:1], None, ALU.mult, None, [r_rb_, r_wsb], [r_I])
                                else:
                                    stt("dve", Ib[:, c * 512:c * 512 + n], rb[:, 0:n], wsb[:, tb, h:h + 1], Ib[:, c * 512:c * 512 + n], ALU.mult, ALU.add, [r_rb_, r_wsb], [r_I])
                                if h == 3:
                                    yield
                            yield
                        phase_of[p] = "bis"
                        tt("dve", Ib[:, tb * 128:L], Ib[:, tb * 128:L], cb, ALU.add, [r_cst], [r_I])
                        if L > TOPK:
                            MX, MN, RG, T, CNT, DD = [sm[:, cB + i:cB + i + 1] for i in range(6)]
                            s.op("dve", lambda: nc.vector.tensor_reduce(out=MX, in_=Ib[:, 0:L], axis=AX.X, op=ALU.max), [r_I], [r_sb])
                            s.op("dve", lambda: nc.vector.tensor_reduce(out=MN, in_=Ib[:, 0:L - 128], axis=AX.X, op=ALU.min), [r_I], [r_sb])
                            tt("dve", RG, MX, MN, ALU.subtract, [r_sb], [r_sb])
                            ts("dve", RG, RG, 1.0 + 1.0 / 1024, None, ALU.mult, None, [r_sb], [r_sb])
                            tt("dve", T, MX, RG, ALU.subtract, [r_sb], [r_sb])
                            ts("dve", stp[:, 0:NITER + 2], p2[:, 0:NITER + 2], RG, None, ALU.mult, None, [r_sb, r_cst], [r_sb])
                            stt("dve", T, stp[:, 1:2], 1.0, T, ALU.mult, ALU.add, [r_sb], [r_sb])
                            if p == 1:
                                ts("dve", nstp[:, 0:NITER + 2], stp[:, 0:NITER + 2], -0.5, None, ALU.mult, None, [r_sb], [r_sb])
                            junk = jk[:, 0:1].to_broadcast([128, L])
                            if p == 0:
                                for i in range(NITER):
                                    ts("dve", junk, Ib[:, 0:L], T, None, ALU.is_gt, ALU.add, [r_I, r_sb], [r_jk], accum=CNT)
                                    s.op("dve", lambda: nc.vector.tensor_scalar(out=DD, in0=CNT, scalar1=TOPK - 0.5, scalar2=-0.5, op0=ALU.is_gt, op1=ALU.add), [r_jk], [r_sb])
                                    stt("dve", T, DD, stp[:, i + 1:i + 2], T, ALU.mult, ALU.add, [r_sb], [r_sb])
                                    yield
                            else:
                                NT = sm[:, cB + 6:cB + 7]
                                DS = sm[:, cB + 7:cB + 8]
                                junk2 = jk[:, 1:2].to_broadcast([128, L])
                                ts("dve", NT, T, -1.0, None, ALU.mult, None, [r_sb], [r_sb])
                                for i in range(NITER):
                                    s.op("act", lambda: nc.scalar.activation(out=junk2, in_=Ib[:, 0:L], func=AF.Sign, bias=NT, scale=1.0, accum_out=CNT), [r_I, r_sb], [r_jk2])
                                    s.op("act", lambda: nc.scalar.activation(out=DS, in_=CNT, func=AF.Sign, bias=BIASC[L], scale=1.0), [r_jk2, r_smc], [r_sb2])
                                    s.op("act", lambda i=i: nc.scalar.activation(out=NT, in_=DS, func=AF.Identity, bias=NT, scale=nstp[:, i + 1:i + 2]), [r_sb2, r_sb], [r_sb])
                                    yield
                                ts("dve", T, NT, -1.0, None, ALU.mult, None, [r_sb], [r_sb])
                            tt("dve", T, T, stp[:, NITER + 1:NITER + 2], ALU.subtract, [r_sb], [r_sb])
                            TAU = T
                            r_tau = r_sb
                        else:
                            TAU = TAUALL
                            r_tau = r_smc
                        while not mb_free[p]:
                            yield
                        mb_free[p] = False
                        ts("dve", mb[p][:, 0:L], Ib[:, 0:L], TAU, MASK_NEG, ALU.is_le, ALU.mult, [r_I, r_tau], [r_mb[p]])
                        mbT = [ebuf, ebuf2][p]
                        pb0 = bankb(0)
                        for half in range((tb + 8) // 8):
                            nblk = min(8, tb + 1 - 8 * half)
                            for j in range(nblk):
                                sbk = half * 8 + j
                                tr(pb0[:, j * 128:(j + 1) * 128], mb[p][:, sbk * 128:(sbk + 1) * 128], [r_mb[p]], [r_bk[0]])
                            copy("act", mbT[:, half * 1024:half * 1024 + nblk * 128], pb0[:, 0:nblk * 128], [r_bk[0]], [r_mbT[p]])

                    def backC(tb):
                        p = tb % 2
                        q3 = tb % 3
                        L = (tb + 1) * 128
                        nch = (L + 511) // 512
                        cA = 96 + p * 16
                        mbT = [ebuf, ebuf2][p]
                        PTb = [PTs[0], PT2]
                        NBQ = sm2[:, 136 + q3 * 4:140 + q3 * 4]

                        def hbank(h):
                            return 4 + 2 * (h % 2) if nch <= 2 else 4

                        def qk(h):
                            hb = hbank(h)
                            for sbk in range(tb + 1):
                                bkk = hb + sbk // 4
                                mm(ps[:, hb * 512 + sbk * 128:hb * 512 + (sbk + 1) * 128],
                                   [(KT[:, h, sbk * 128:(sbk + 1) * 128], qT[q3][:, h, :]), (ident[:], mbT[:, sbk * 128:(sbk + 1) * 128])],
                                   [r_qT[q3], r_mbT[p], r_ident, r_KT[sbk]], [r_bk[bkk]])

                        def sexp(h):
                            hb = hbank(h)
                            sc_ = ps[:, hb * 512:hb * 512 + L]
                            rsc = r_bk[hb:hb + nch]
                            act(PTb[h % 2][:, 0:L], sc_, AF.Exp, rsc + [r_nbq[q3]], [r_PT[h % 2]], bias=NBQ[:, h:h + 1], scale=ATT_SCALE)

                        def pv(h):
                            pts = PTb[h % 2]
                            RDa = sm[:, cA + 6 + (h % 2):cA + 7 + (h % 2)]
                            r_sr = res("smrd%d_%d" % (p, h % 2))
                            po = bankf(3)[:, 0:129]
                            mm(po, [(pts[:, sbk * 128:(sbk + 1) * 128], Vg[:, sbk, h, 0:129]) for sbk in range(tb + 1)],
                               [r_PT[h % 2]] + r_V[0:tb + 1], [r_bk[3]])
                            s.op("dve", lambda: nc.vector.reciprocal(out=RDa, in_=bankf(3)[:, 128:129]), [r_bk[3]], [r_sr])
                            ts("dve", attb[p][:, h * 128:(h + 1) * 128], bankf(3)[:, 0:128], RDa, None, ALU.mult, None, [r_bk[3], r_sr], [r_attb[p]])

                        qk(0)
                        sexp(0)
                        yield
                        for h in range(4):
                            if h + 1 < 4:
                                qk(h + 1)
                                sexp(h + 1)
                                yield
                            pv(h)
                            yield
                        pbb = bankb(3)
                        for h in range(4):
                            tr(pbb[:, h * 128:(h + 1) * 128], attb[p][:, h * 128:(h + 1) * 128], [r_attb[p]], [r_bk[3]])
                        copy("act", attTall[:, :, tb * 128:(tb + 1) * 128], pbb[:, 0:512].rearrange("p (h t) -> p h t", h=4), [r_bk[3]], [r_attTall[tb]])

                    fronts = []
                    ready = []
                    back = None
                    done_back = set()
                    next_tb = 0
                    while next_tb < NB or fronts or ready or back is not None:
                        while (next_tb < NB and len(fronts) < 2
                               and all(t % 2 != next_tb % 2 for t, _ in fronts)
                               and all(phase_of.get(t % 2) != "s0" for t, _ in fronts)
                               and (next_tb < 3 or (next_tb - 3) in done_back)):
                            fronts.append((next_tb, frontC(next_tb)))
                            next_tb += 1
                        if back is None and ready:
                            tbb = ready.pop(0)
                            back = (tbb, backC(tbb))
                        if back is not None:
                            try:
                                next(back[1])
                            except StopIteration:
                                mb_free[back[0] % 2] = True
                                done_back.add(back[0])
                                back = None
                        for item in list(fronts):
                            try:
                                next(item[1])
                            except StopIteration:
                                fronts.remove(item)
                                ready.append(item[0])
                    chk('C')
                    s.barrier(list(R.values()))
                    cv = Carver(PERSIST)
                    mixT = cv.take(4 * S).rearrange("p (k t) -> p k t", k=4)
                    PERSIST2 = cv.off
                    wR = cv.take(KC * 2048).rearrange("p (k c) -> p k c", k=KC)
                    xbf = [cv.take(1024) for _ in range(3)]
                    xT = [cv.take(1024) for _ in range(3)]
                    tA = cv.take(2048, F32)
                    tB = cv.take(2048, F32)
                    qk_r = cv.take(2048, F32)
                    qkb = [cv.take(1024) for _ in range(3)]
                    kdec = [cv.take(512) for _ in range(3)]
                    vbf = [cv.take(512) for _ in range(3)]
                    gsl = [cv.take(1024, F32) for _ in range(3)]
                    qkT = [cv.take(1024) for _ in range(3)]
                    stb = [cv.take(512) for _ in range(3)]
                    yn = cv.take(1024, F32)
                    y3 = [cv.take(512) for _ in range(3)]
                    state_f = cv.take(1024, F32)
                    state_bf = cv.take(512)
                    gbA = cv.take(1024, F32)
                    r_mixT = [res("mixT%d" % b) for b in range(NB)]
                    r_wR, r_gbA = res("wR"), res("gbA")
                    r_qk, r_yn = res("qk_r"), res("yn")
                    r_qkb = [res("qkb%d" % i) for i in range(3)]
                    r_kdec = [res("kdec%d" % i) for i in range(3)]
                    r_vbf = [res("vbf%d" % i) for i in range(3)]
                    r_gsl = [res("gsl%d" % i) for i in range(3)]
                    r_qkT = [res("qkT%d" % i) for i in range(3)]
                    r_stb = [res("stb%d" % i) for i in range(3)]
                    r_y3 = [res("y3%d" % i) for i in range(3)]
                    r_stf, r_stbf = res("state_f"), res("state_bf")
                    r_xbfA = [res("xbfA%d" % i) for i in range(3)]
                    r_xTA = [res("xTA%d" % i) for i in range(3)]
                    r_tA2, r_tB2 = res("tA2"), res("tB2")
                    s.dma("sp", gbA[:, 0:512], gret_d[l], writes=[r_gbA])
                    r_wRg = [res("wRg%d" % i) for i in range(4)]
                    for i in range(4):
                        s.dma("pool", wR[:, :, i * 512:(i + 1) * 512], wr_d[l][:, :, i * 512:(i + 1) * 512], writes=[r_wRg[i]])
                    s.op("dve", lambda: nc.vector.memset(state_f, 0.0), [], [r_stf])
                    s.op("pool", lambda: nc.gpsimd.memset(state_bf, 0.0), [], [r_stbf])
                    dct = cst[:, C_RT + 16:C_RT + 20].unsqueeze(2).to_broadcast([128, 4, 128])
                    kps = cst[:, C_RT + 4:C_RT + 8].unsqueeze(2).to_broadcast([128, 4, 128])
                    dks = cst[:, C_RT + 8:C_RT + 12].unsqueeze(2).to_broadcast([128, 4, 128])

                    def blkA(tb):
                        p = tb % 3
                        r_sg = res("smgn%d" % (tb % 2))
                        cG = 128 + (tb % 2) * 48
                        make_xT(tb, xbf[p], xT[p], r_xbfA[p], r_xTA[p], 0)
                        xt, r_xt = xT[p], r_xTA[p]
                        for j in range(4):
                            mm(bankf(1 + j), [(xt[:, k * 128:(k + 1) * 128], wR[:, k, j * 512:(j + 1) * 512]) for k in range(KC)], [r_xt, r_wRg[j]], [r_bk[1 + j]])
                        yield
                        rope(bankf(1, 2), [r_bk[1], r_bk[2]], 8, 64, cs_cos[:, tb, :], cs_sin[:, tb, :], qk_r, r_qk, tA, tB, r_tA2, r_tB2, e2="dve")
                        copy("act", vbf[p], bankf(3), [r_bk[3]], [r_vbf[p]])
                        act(gsl[p], bankf(4), AF.Silu, [r_bk[4]], [r_gsl[p]])
                        yield
                        copy("act", qkb[p][:, 0:512], qk_r[:, 0:512], [r_qk], [r_qkb[p]])
                        k3 = qk_r[:, 512:1024].rearrange("p (h d) -> p h d", h=4)
                        tt("dve", qkb[p][:, 512:1024].rearrange("p (h d) -> p h d", h=4), k3, kps, ALU.mult, [r_qk, r_cst], [r_qkb[p]])
                        tt("pool", kdec[p].rearrange("p (h d) -> p h d", h=4), k3, dks, ALU.mult, [r_qk, r_cst], [r_kdec[p]])
                        pb = bankb(0)
                        for j in range(8):
                            tr(pb[:, j * 128:(j + 1) * 128], qkb[p][:, j * 128:(j + 1) * 128], [r_qkb[p]], [r_bk[0]])
                        copy("act", qkT[p], pb[:, :], [r_bk[0]], [r_qkT[p]])
                        yield
                        for h in range(4):
                            mm(bankf(5)[:, h * 128:(h + 1) * 128], [(qkT[p][:, 512 + h * 128:512 + (h + 1) * 128], qkT[p][:, h * 128:(h + 1) * 128])], [r_qkT[p]], [r_bk[5]])
                        tt("dve", stb[p].rearrange("p (h d) -> p h d", h=4), bankf(5).rearrange("p (h d) -> p h d", h=4),
                           tri.unsqueeze(1).to_broadcast([128, 4, 128]), ALU.mult, [r_bk[5], r_cst], [r_stb[p]])
                        yield
                        for h in range(4):
                            sl = slice(h * 128, (h + 1) * 128)
                            mm(bankf(6)[:, sl], [(stb[p][:, sl], vbf[p][:, sl]), (qkT[p][:, sl], state_bf[:, sl])], [r_stb[p], r_vbf[p], r_qkT[p], r_stbf], [r_bk[6]])
                        for h in range(4):
                            sl = slice(h * 128, (h + 1) * 128)
                            mm(bankf(7)[:, sl], [(kdec[p][:, sl], vbf[p][:, sl])], [r_kdec[p], r_vbf[p]], [r_bk[7]])
                        tt("pool", state_f.rearrange("p (h d) -> p h d", h=4), state_f.rearrange("p (h d) -> p h d", h=4), dct, ALU.mult, [r_cst], [r_stf])
                        tt("dve", state_f, state_f, bankf(7), ALU.add, [r_bk[7]], [r_stf])
                        copy("dve", state_bf, state_f, [r_stf], [r_stbf])
                        yield
                        st6 = sm[:, cG:cG + 24].rearrange("p (h a) -> p h a", h=4)
                        mv = sm[:, cG + 24:cG + 32].rearrange("p (h a) -> p h a", h=4)
                        for h in range(4):
                            s.op("dve", lambda h=h: nc.vector.bn_stats(out=st6[:, h, :], in_=bankf(6)[:, h * 128:(h + 1) * 128]), [r_bk[6]], [r_sg])
                        for h in range(4):
                            s.op("dve", lambda h=h: nc.vector.bn_aggr(out=mv[:, h, :], in_=st6[:, h, :]), [r_sg], [r_sg])
                        A4 = sm[:, cG + 32:cG + 36]
                        RS4 = sm[:, cG + 36:cG + 40]
                        SC4 = sm[:, cG + 40:cG + 44]
                        NB4 = sm[:, cG + 44:cG + 48]
                        tt("dve", A4, mv[:, :, 1], cst[:, C_RT + 12:C_RT + 16], ALU.mult, [r_sg, r_cst], [r_sg])
                        ts("dve", A4, A4, LN_EPS, None, ALU.add, None, [r_sg], [r_sg])
                        tt("pool", RS4, A4, NEGH4, ALU.pow, [r_sg, r_smc], [r_sg])
                        tt("dve", SC4, RS4, cst[:, C_RT:C_RT + 4], ALU.mult, [r_sg, r_cst], [r_sg])
                        stt("dve", NB4, mv[:, :, 0], -1.0, SC4, ALU.mult, ALU.mult, [r_sg], [r_sg])
                        for h in range(4):
                            sl = slice(h * 128, (h + 1) * 128)
                            act(yn[:, sl], bankf(6)[:, sl], AF.Identity, [r_bk[6], r_sg], [r_yn], bias=NB4[:, h:h + 1], scale=SC4[:, h:h + 1])
                        yield
                        tt("dve", yn, yn, gbA[:, 0:512], ALU.mult, [r_yn, r_gbA], [r_yn])
                        tt("dve", y3[p], yn, gsl[p], ALU.mult, [r_yn, r_gsl[p]], [r_y3[p]])
                        pb = bankb(0)
                        for h in range(4):
                            tr(pb[:, h * 128:(h + 1) * 128], y3[p][:, h * 128:(h + 1) * 128], [r_y3[p]], [r_bk[0]])
                        copy("act", mixT[:, :, tb * 128:(tb + 1) * 128], pb[:, 0:512].rearrange("p (h t) -> p h t", h=4), [r_bk[0]], [r_mixT[tb]])

                    pipeline([blkA(tb) for tb in range(NB)], 3)
                    chk('A')
                    cv = Carver(PERSIST2)
                    wO = cv.take(KC * 1024).rearrange("p (k c) -> p k c", k=KC)
                    gbm = cv.take(4096, F32)
                    r_wO, r_gbm = res("wO"), res("gbm")
                    r_wOg = [res("wOg%d" % i) for i in range(2)]
                    for i in range(2):
                        s.dma("pool", wO[:, :, i * 512:(i + 1) * 512], wout_d[l][:, :, i * 512:(i + 1) * 512], writes=[r_wOg[i]] + r_wRg)
                    s.dma("sp", gbm[:, :], gmix_d[l], writes=[r_gbm] + r_wRg)
                    for tb in range(NB):
                        b0 = 2 * (tb % 2)
                        for hh in range(2):
                            pairs = [(mixT[:, k, tb * 128:(tb + 1) * 128], wO[:, k, hh * 512:(hh + 1) * 512]) for k in range(4)]
                            pairs += [(attTall[:, k, tb * 128:(tb + 1) * 128], wO[:, 4 + k, hh * 512:(hh + 1) * 512]) for k in range(4)]
                            mm(bankf(b0 + hh), pairs, [r_mixT[tb], r_attTall[tb], r_wOg[hh]], [r_bk[b0 + hh]])
                        layer_norm_block(tb, [bankf(b0), bankf(b0 + 1)], [r_bk[b0], r_bk[b0 + 1]], gbm, r_gbm)
                    chk('C2')
                    s.barrier(list(R.values()))
                    cv = Carver()
                    NG = S // GT
                    nblk = GT // 128
                    ncol = GT // 512
                    x1T = [cv.take(KC * GT).rearrange("p (k t) -> p k t", k=KC) for _ in range(2)]
                    hT = cv.take(HC * GT).rearrange("p (c t) -> p c t", c=HC)
                    wD = cv.take(HC * 1024).rearrange("p (c n) -> p c n", c=HC)
                    wG = [cv.take(KC * 512).rearrange("p (k c) -> p k c", k=KC) for _ in range(3)]
                    xbf2 = [cv.take(1024) for _ in range(2)]
                    sgb = [cv.take(1024, F32) for _ in range(2)]
                    gbf = cv.take(4096, F32)
                    r_x1T = [[res("x1T%d_%d" % (q, i)) for i in range(nblk)] for q in range(2)]
                    r_hT = [res("hT%d_%d" % (c, t)) for c in range(HC) for t in range(ncol)]
                    r_wD = [res("wD%d" % i) for i in range(4)]
                    r_wG = [res("wG%d" % i) for i in range(3)]
                    r_xbf2 = [res("xbf2_0"), res("xbf2_1")]
                    r_sgl = [res("sg0"), res("sg1")]
                    r_gbf = res("gbf")
                    wdq = [(0, 6), (6, 12), (12, 18), (18, 22)]
                    for i, (c0, c1) in enumerate(wdq):
                        s.dma("pool", wD[:, c0:c1, :], wd_d[l][:, c0:c1, :], writes=[r_wD[i]])
                    s.dma("sp", gbf[:, :], gffn_d[l], writes=[r_gbf])
                    NP = HC // 2
                    gu = [0]

                    def load_gu(pr):
                        slot = gu[0] % 3
                        gu[0] += 1
                        s.dma("sp", wG[slot].rearrange("p k c -> p (k c)"), wgu_bf[l, pr], reads=[r_conv[l]], writes=[r_wG[slot]])
                        return slot

                    def prep_x1T(g):
                        q = g % 2
                        for j in range(nblk):
                            tb = g * nblk + j
                            i2 = j % 2
                            copy("act" if j % 2 == 0 else "dve", xbf2[i2], x_sb[:, tb, :], [r_x[tb]], [r_xbf2[i2]])
                            pb = bankb(i2)
                            for k in range(KC):
                                tr(pb[:, k * 128:(k + 1) * 128], xbf2[i2][:, k * 128:(k + 1) * 128], [r_xbf2[i2]], [r_bk[i2]])
                            copy("act", x1T[q][:, :, j * 128:(j + 1) * 128], pb[:, :].rearrange("p (k t) -> p k t", k=KC), [r_bk[i2]], [r_x1T[q][j]])

                    prep_x1T(0)
                    for g in range(NG):
                        q = g % 2
                        slots = [load_gu(0), load_gu(1)]
                        for pr in range(NP):
                            if pr + 2 < NP:
                                slots.append(load_gu(pr + 2))
                            slot = slots[pr]
                            for ci in range(2):
                                c = 2 * pr + ci
                                for th in range(ncol):
                                    xr = r_x1T[q][4 * th:4 * th + 4]
                                    qq = (c * ncol + th) % 2
                                    bG, bU = 2 + 2 * qq, 3 + 2 * qq
                                    mm(bankf(bG), [(wG[slot][:, k, ci * 256:ci * 256 + 128], x1T[q][:, k, th * 512:(th + 1) * 512]) for k in range(KC)], xr + [r_wG[slot]], [r_bk[bG]])
                                    mm(bankf(bU), [(wG[slot][:, k, ci * 256 + 128:ci * 256 + 256], x1T[q][:, k, th * 512:(th + 1) * 512]) for k in range(KC)], xr + [r_wG[slot]], [r_bk[bU]])
                                    act(sgb[qq], bankf(bG), AF.Silu, [r_bk[bG]], [r_sgl[qq]])
                                    tt("dve", hT[:, c, th * 512:(th + 1) * 512], sgb[qq], bankf(bU), ALU.mult, [r_sgl[qq], r_bk[bU]], [r_hT[c * ncol + th]])
                        if g + 1 < NG:
                            prep_x1T(g + 1)
                        for j in range(nblk):
                            tb = g * nblk + j
                            th = j // 4
                            hres = [r_hT[c * ncol + th] for c in range(HC)]
                            for hh in range(2):
                                mm(bankf(6 + hh), [(hT[:, c, j * 128:(j + 1) * 128], wD[:, c, hh * 512:(hh + 1) * 512]) for c in range(HC)],
                                   hres + r_wD, [r_bk[6 + hh]])
                            layer_norm_block(tb, [bankf(6), bankf(7)], [r_bk[6], r_bk[7]], gbf, r_gbf)
                    s.barrier(list(R.values()))
              except _Stop:
                pass
              if True:
                yv = y_d[sq].rearrange("(b p) d -> p b d", p=128)
                for b in range(NB):
                    s.dma("sp", yv[:, b, :], x_sb[:, b, :], reads=[r_x[b]], semres=r_stb_[b])
            for b in range(NB):
                s._wait("sp", ("dma", r_stb_[b], r_stb_[b].cnt))

        s1 = Sched(nc, es, None)
        emit(s1)
        s2 = Sched(nc, es, s1.needed)
        emit(s2)
    return nc


_CACHE = {}


def _prep_inputs(cfg, x, positions, w_in, ret_gn_gain, w_out, ln_mix_gain, ln_mix_bias,
                 w_gate_up, w_down, ln_ffn_gain, ln_ffn_bias, ncores):
    DEPTH = cfg.DEPTH
    f = lambda a: np.ascontiguousarray(np.asarray(a, dtype=np.float32))
    g_ret = np.ascontiguousarray(np.broadcast_to(f(ret_gn_gain)[:, None, :], (DEPTH, 128, 512)))
    g_mix = np.ascontiguousarray(np.broadcast_to(
        np.concatenate([f(ln_mix_gain), f(ln_mix_bias)], axis=1)[:, None, :], (DEPTH, 128, 2048)))
    g_ffn = np.ascontiguousarray(np.broadcast_to(
        np.concatenate([f(ln_ffn_gain), f(ln_ffn_bias)], axis=1)[:, None, :], (DEPTH, 128, 2048)))
    cst = make_consts(cfg)
    w_in = f(w_in)

    def tile_k(w):
        return np.ascontiguousarray(w.reshape(DEPTH, KC, 128, w.shape[-1]).transpose(0, 2, 1, 3))

    w_kv = tile_k(np.concatenate([w_in[:, :, OFF_AK:OFF_AK + 1024], w_in[:, :, OFF_IK:OFF_IK + 72]], axis=2))
    w_q = tile_k(np.concatenate([w_in[:, :, OFF_AQ:OFF_AQ + 512], w_in[:, :, OFF_IQ:OFF_IQ + 512]], axis=2))
    w_r = tile_k(w_in[:, :, 0:2048])
    w_o = tile_k(f(w_out))
    wgu = f(w_gate_up)
    g4 = wgu[:, :, 0:HID].reshape(DEPTH, KC, 128, HC // 2, 2, 128)
    u4 = wgu[:, :, HID:2 * HID].reshape(DEPTH, KC, 128, HC // 2, 2, 128)
    gu = np.stack([g4, u4], axis=5)
    gu = gu.transpose(0, 3, 2, 1, 4, 5, 6)
    w_gu = np.ascontiguousarray(gu.reshape(DEPTH, HC // 2, 128, KC * 512))
    w_d = np.ascontiguousarray(f(w_down).reshape(DEPTH, HC, 128, 1024).transpose(0, 2, 1, 3))
    x = f(x)
    pos = np.asarray(positions).astype(np.int32)
    shared = {"w_kv": w_kv, "w_q": w_q, "w_r": w_r, "w_o": w_o, "w_gu": w_gu, "w_d": w_d,
              "g_ret": g_ret, "g_mix": g_mix, "g_ffn": g_ffn, "cst": cst}
    maps = []
    for c in range(ncores):
        xs = x[c * cfg.NSEQ:(c + 1) * cfg.NSEQ]
        ps_ = pos[c * cfg.NSEQ:(c + 1) * cfg.NSEQ]
        ps_ = np.ascontiguousarray(ps_.reshape(cfg.NSEQ, cfg.NB, 128).transpose(0, 2, 1))
        m = dict(shared)
        m["x"] = np.ascontiguousarray(xs)
        m["pos"] = ps_
        maps.append(m)
    return maps


def run(cfg, ncores, **inputs):
    key = (cfg.S, cfg.NSEQ, cfg.DEPTH, cfg.NITER, cfg.stop)
    if key not in _CACHE:
        _CACHE[key] = build(cfg)
    nc = _CACHE[key]
    maps = _prep_inputs(cfg, ncores=ncores, **inputs)
    res = run_bass_kernel_spmd(nc, maps, core_ids=list(range(ncores)))
    return np.concatenate([r["y"] for r in res.results], axis=0)


def kernel(x, positions, w_in, ret_gn_gain, w_out, ln_mix_gain, ln_mix_bias,
           w_gate_up, w_down, ln_ffn_gain, ln_ffn_bias):
    cfg = Cfg(S=2048, NSEQ=2, DEPTH=2, NITER=14)
    out = run(cfg, 8, x=x, positions=positions, w_in=w_in, ret_gn_gain=ret_gn_gain, w_out=w_out,
              ln_mix_gain=ln_mix_gain, ln_mix_bias=ln_mix_bias, w_gate_up=w_gate_up, w_down=w_down,
              ln_ffn_gain=ln_ffn_gain, ln_ffn_bias=ln_ffn_bias)
    return out.astype(np.float32)
```
